# Optimizing a Trainium2 kernel written in Bass

```python
import jax, jax.numpy as jnp
from jax import lax
import numpy as np

D_MODEL = 1024
BATCH = 8
SEQ = 4096
DEPTH = 2

HEAD_DIM = 128
N_MEM_TOK = 256
MEM_HEADS = 4
MEM_W = MEM_HEADS * HEAD_DIM
A_GROUPS = ((128, 1), (512, 4), (2048, 16))
N_A_GROUPS = len(A_GROUPS)
A_HEADS = 8
A_W = A_HEADS * HEAD_DIM
B_Q_HEADS = 8
B_KV_HEADS = 2
B_GROUP = B_Q_HEADS // B_KV_HEADS
B_W = B_Q_HEADS * HEAD_DIM
B_KV_W = B_KV_HEADS * HEAD_DIM
MIX_W = A_W
BRANCH_W = MIX_W + MEM_W
IN_A = 3 * N_A_GROUPS * A_W + MEM_W + BRANCH_W
IN_B = B_W + 2 * B_KV_W + MEM_W + BRANCH_W
ROPE_THETA = 500000.0
ROT_DIM_A = HEAD_DIM // 4
AXIAL_THETA = 10000.0
GRID_W = 64
Q_BLOCK = 128
EPS = 1e-6
N_A = (DEPTH + 1) // 2
N_B = DEPTH // 2

kernel_name = 'hybrid_dilated_axial_gqa_encoder'


def rmsnorm(x, g):
    xf = x.astype(jnp.float32)
    y = xf * lax.rsqrt(jnp.mean(xf * xf, axis=-1, keepdims=True) + EPS)
    return (y * g.astype(jnp.float32)).astype(x.dtype)


def rope_angles(pos, dim, theta):
    inv = theta ** (-jnp.arange(0, dim, 2, dtype=jnp.float32) / dim)
    return pos.astype(jnp.float32)[:, None] * inv[None, :]


def apply_rope(x, ang):
    half = ang.shape[-1]
    rd = 2 * half
    shape = (1, ang.shape[0]) + (1,) * (x.ndim - 3) + (half,)
    cos = jnp.cos(ang).reshape(shape)
    sin = jnp.sin(ang).reshape(shape)
    xf = x.astype(jnp.float32)
    x1 = xf[..., :half]
    x2 = xf[..., half:rd]
    out = jnp.concatenate([x1 * cos - x2 * sin, x2 * cos + x1 * sin, xf[..., rd:]], axis=-1)
    return out.astype(x.dtype)


def dilated_window_attention(q, k, v, window, dilation):
    Bn, S, H, E = q.shape
    r = window // (2 * dilation)
    L = S // dilation
    nb = -(-L // r)
    Lp = nb * r

    def sub(a):
        return a.reshape(Bn, L, dilation, H, E).transpose(0, 2, 1, 3, 4)

    qb = jnp.pad(sub(q), ((0, 0), (0, 0), (0, Lp - L), (0, 0), (0, 0))).reshape(Bn, dilation, nb, r, H, E)

    def key_blocks(a):
        ap = jnp.pad(sub(a), ((0, 0), (0, 0), (r, Lp - L + r), (0, 0), (0, 0)))
        ap = ap.reshape(Bn, dilation, nb + 2, r, H, E)
        return jnp.concatenate([ap[:, :, :-2], ap[:, :, 1:-1], ap[:, :, 2:]], axis=3)

    kb = key_blocks(k)
    vb = key_blocks(v)
    qi = jnp.arange(nb)[:, None, None] * r + jnp.arange(r)[None, :, None]
    kj = (jnp.arange(nb)[:, None, None] - 1) * r + jnp.arange(3 * r)[None, None, :]
    valid = (jnp.abs(qi - kj) <= r) & (kj >= 0) & (kj < L)
    s = jnp.einsum('bdnqhe,bdnkhe->bdnhqk', qb, kb).astype(jnp.float32)
    s = jnp.where(valid[None, None, :, None], s, -jnp.inf)
    m = jnp.max(s, axis=-1, keepdims=True)
    p = jnp.exp(s - m)
    den = jnp.sum(p, axis=-1, keepdims=True)
    o = jnp.einsum('bdnhqk,bdnkhe->bdnqhe', (p / den).astype(v.dtype), vb)
    lse = (m + jnp.log(den))[..., 0]
    o = o.reshape(Bn, dilation, Lp, H, E)[:, :, :L].transpose(0, 2, 1, 3, 4).reshape(Bn, S, H, E)
    lse = lse.transpose(0, 1, 2, 4, 3).reshape(Bn, dilation, Lp, H)[:, :, :L]
    lse = lse.transpose(0, 2, 1, 3).reshape(Bn, S, H)
    return o, lse


def mixer_dilated(qkv, qn_g, kn_g, ang):
    Bn, S, _ = qkv.shape
    qkv = qkv.reshape(Bn, S, 3, N_A_GROUPS, A_HEADS, HEAD_DIM)
    q = rmsnorm(qkv[:, :, 0], qn_g[:, None, :])
    k = rmsnorm(qkv[:, :, 1], kn_g[:, None, :])
    v = qkv[:, :, 2]
    q = apply_rope(q, ang) * (HEAD_DIM ** -0.5)
    k = apply_rope(k, ang)
    outs, lses = [], []
    for g, (window, dilation) in enumerate(A_GROUPS):
        o_g, l_g = dilated_window_attention(q[:, :, g], k[:, :, g], v[:, :, g], window, dilation)
        outs.append(o_g)
        lses.append(l_g)
    o = jnp.stack(outs, axis=2)
    w = jax.nn.softmax(jnp.stack(lses, axis=2), axis=2)
    o = jnp.sum(w[..., None].astype(o.dtype) * o, axis=2)
    return o.reshape(Bn, S, A_W)


def mixer_axial_gqa(q, k, v, qn_g, kn_g, ang):
    Bn, S, _ = q.shape
    q = rmsnorm(q.reshape(Bn, S, B_Q_HEADS, HEAD_DIM), qn_g)
    k = rmsnorm(k.reshape(Bn, S, B_KV_HEADS, HEAD_DIM), kn_g)
    v = v.reshape(Bn, S, B_KV_HEADS, HEAD_DIM)
    q = apply_rope(q, ang) * (HEAD_DIM ** -0.5)
    k = apply_rope(k, ang)
    n_qb = S // Q_BLOCK
    qb = q.reshape(Bn, n_qb, Q_BLOCK, B_KV_HEADS, B_GROUP, HEAD_DIM).transpose(1, 0, 2, 3, 4, 5)

    def block(qi):
        s = jnp.einsum('bqhge,bshe->bhgqs', qi, k).astype(jnp.float32)
        p = jax.nn.softmax(s, axis=-1).astype(v.dtype)
        return jnp.einsum('bhgqs,bshe->bqhge', p, v)

    o = lax.map(block, qb)
    return o.transpose(1, 0, 2, 3, 4, 5).reshape(Bn, S, B_W)


def memory_attention(q_mem, mem_h, w_kv, qn_g, kn_g):
    Bn, S, _ = q_mem.shape
    N = mem_h.shape[1]
    q = rmsnorm(q_mem.reshape(Bn, S, MEM_HEADS, HEAD_DIM), qn_g) * (HEAD_DIM ** -0.5)
    kv = jnp.einsum('bnd,de->bne', mem_h, w_kv).reshape(Bn, N, 2, MEM_HEADS, HEAD_DIM)
    k = rmsnorm(kv[:, :, 0], kn_g)
    v = kv[:, :, 1]
    s = jnp.einsum('bqhe,bkhe->bhqk', q, k).astype(jnp.float32)
    p = jax.nn.softmax(s, axis=-1).astype(v.dtype)
    return jnp.einsum('bhqk,bkhe->bqhe', p, v).reshape(Bn, S, MEM_W)


def setup_inputs(seed: int = 0) -> dict:
    key = jax.random.key(seed)
    ks = jax.random.split(key, 16)
    nrm = jax.random.normal
    f32 = jnp.float32
    return {
        'x': nrm(ks[0], (BATCH, SEQ, D_MODEL), f32),
        'mem': nrm(ks[1], (BATCH, N_MEM_TOK, D_MODEL), f32),
        'norm_g': 1.0 + 0.01 * nrm(ks[2], (DEPTH, D_MODEL), f32),
        'mem_norm_g': 1.0 + 0.01 * nrm(ks[3], (DEPTH, D_MODEL), f32),
        'w_mem_kv': nrm(ks[4], (DEPTH, D_MODEL, 2 * MEM_W), f32) * D_MODEL ** -0.5,
        'mem_qn_g': 1.0 + 0.01 * nrm(ks[5], (DEPTH, HEAD_DIM), f32),
        'mem_kn_g': 1.0 + 0.01 * nrm(ks[6], (DEPTH, HEAD_DIM), f32),
        'w_out': nrm(ks[7], (DEPTH, BRANCH_W, D_MODEL), f32) * BRANCH_W ** -0.5,
        'w_in_a': nrm(ks[8], (N_A, D_MODEL, IN_A), f32) * D_MODEL ** -0.5,
        'qn_a': 1.0 + 0.01 * nrm(ks[9], (N_A, N_A_GROUPS, HEAD_DIM), f32),
        'kn_a': 1.0 + 0.01 * nrm(ks[10], (N_A, N_A_GROUPS, HEAD_DIM), f32),
        'w_in_b': nrm(ks[11], (N_B, D_MODEL, IN_B), f32) * D_MODEL ** -0.5,
        'qn_b': 1.0 + 0.01 * nrm(ks[12], (N_B, HEAD_DIM), f32),
        'kn_b': 1.0 + 0.01 * nrm(ks[13], (N_B, HEAD_DIM), f32),
    }


def reference(x, mem, norm_g, mem_norm_g, w_mem_kv, mem_qn_g, mem_kn_g, w_out,
              w_in_a, qn_a, kn_a, w_in_b, qn_b, kn_b):
    S = x.shape[1]
    ROWS = S // GRID_W
    pos = jnp.arange(S, dtype=jnp.int32)
    row = jnp.repeat(jnp.arange(ROWS, dtype=jnp.int32), GRID_W)
    col = jnp.tile(jnp.arange(GRID_W, dtype=jnp.int32), ROWS)
    ang_a = rope_angles(pos, ROT_DIM_A, ROPE_THETA)
    ang_b = jnp.concatenate([rope_angles(row, HEAD_DIM // 2, AXIAL_THETA),
                             rope_angles(col, HEAD_DIM // 2, AXIAL_THETA)], axis=-1)
    for i in range(DEPTH):
        h = rmsnorm(x, norm_g[i])
        mem_h = rmsnorm(mem, mem_norm_g[i])
        j = i // 2
        if i % 2 == 0:
            proj = jnp.einsum('bsd,de->bse', h, w_in_a[j])
            n_qkv = 3 * N_A_GROUPS * A_W
            qkv = proj[..., :n_qkv]
            q_mem = proj[..., n_qkv:n_qkv + MEM_W]
            gate = proj[..., n_qkv + MEM_W:]
            o_mix = mixer_dilated(qkv, qn_a[j], kn_a[j], ang_a)
        else:
            proj = jnp.einsum('bsd,de->bse', h, w_in_b[j])
            c1 = B_W
            c2 = c1 + B_KV_W
            c3 = c2 + B_KV_W
            c4 = c3 + MEM_W
            o_mix = mixer_axial_gqa(proj[..., :c1], proj[..., c1:c2], proj[..., c2:c3],
                                    qn_b[j], kn_b[j], ang_b)
            q_mem = proj[..., c3:c4]
            gate = proj[..., c4:]
        o_mem = memory_attention(q_mem, mem_h, w_mem_kv[i], mem_qn_g[i], mem_kn_g[i])
        y = jnp.concatenate([o_mix, o_mem], axis=-1) * jax.nn.silu(gate)
        x = x + jnp.einsum('bse,ed->bsd', y, w_out[i])
    return x
```

```python
import numpy as np
import concourse.bass as bass
import concourse.mybir as mybir
from concourse.bass_utils import run_bass_kernel_spmd

F32 = mybir.dt.float32
BF16 = mybir.dt.bfloat16
ALU = mybir.AluOpType
ACT = mybir.ActivationFunctionType
AX = mybir.AxisListType

ENGS = ['pe', 'act', 'dve', 'pool', 'sp']

S_TOK = 4096
NT = 32
D = 1024
E = 128
EPS = 1e-6
SCALE = float(E) ** -0.5
A_GROUPS = ((128, 1), (512, 4), (2048, 16))
IN_A = 11264
IN_B = 3584


class Sched:
    def __init__(self):
        self.ops = []
        self.state = {}
        self.last_on = {e: None for e in ENGS}
        self.dma_last = {}

    def op(self, eng, fn, reads=(), writes=(), dma=None):
        i = len(self.ops)
        deps = set()
        for k in reads:
            excl = isinstance(k, tuple) and k[0] == 'ps'
            st = self.state.setdefault(k, [None, []])
            if st[0] is not None:
                deps.add(st[0])
            if excl:
                for r in st[1]:
                    if self.ops[r]['eng'] != eng:
                        deps.add(r)
        for k in writes:
            st = self.state.setdefault(k, [None, []])
            if st[0] is not None:
                deps.add(st[0])
            for r in st[1]:
                deps.add(r)
        for k in reads:
            self.state[k][1].append(i)
        for k in writes:
            self.state[k] = [i, []]
        if dma is not None:
            p = self.dma_last.get(dma)
            if p is not None:
                deps.add(p)
            self.dma_last[dma] = i
        deps.discard(i)
        if eng == 'pe':
            deps = {d for d in deps if self.ops[d]['eng'] != 'pe'}
        best = {}
        keep = set()
        for d in deps:
            od = self.ops[d]
            if od['dma'] is not None:
                keep.add(d)
            elif d > best.get(od['eng'], -1):
                best[od['eng']] = d
        deps = keep | set(best.values())
        self.ops.append(dict(eng=eng, fn=fn, deps=deps, dma=dma))
        if fn is not None:
            self.last_on[eng] = i
        return i

    def barrier(self):
        lasts = [v for v in self.last_on.values() if v is not None]
        lasts += list(self.dma_last.values())
        lasts = set(lasts)
        for e in ENGS:
            deps = {d for d in lasts if not (self.ops[d]['eng'] == e and self.ops[d]['dma'] is None)}
            self.ops.append(dict(eng=e, fn=None, deps=deps, dma=None))
        self.state = {}

    def emit(self, nc):
        ops = self.ops
        needed = set()
        for o in ops:
            needed |= o['deps']
        dma_keys = []
        seen = set()
        for o in ops:
            if o['dma'] is not None and o['dma'] not in seen:
                seen.add(o['dma'])
                dma_keys.append(o['dma'])
        sem_ctx = []
        sems = {}
        for n_i, name in enumerate(['pe', 'act', 'dve', 'pool'] + [('dma', k) for k in dma_keys]):
            cm = nc.semaphore("s%d" % n_i)
            sems[name] = cm.__enter__()
            sem_ctx.append(cm)
        cnt = {}
        for i, o in enumerate(ops):
            if o['dma'] is not None:
                key = ('dma', o['dma'])
                cnt[key] = cnt.get(key, 0) + 16
                o['sem'], o['val'], o['inc'] = key, cnt[key], 16
            elif i in needed:
                assert o['fn'] is not None
                key = o['eng']
                cnt[key] = cnt.get(key, 0) + 1
                o['sem'], o['val'], o['inc'] = key, cnt[key], 1
            else:
                o['sem'] = None
        self.maxval = dict(cnt)

        def run(eng_name, e):
            waited = {}
            for o in ops:
                if o['eng'] != eng_name:
                    continue
                need = {}
                for d in o['deps']:
                    od = ops[d]
                    s, v = od['sem'], od['val']
                    if v > need.get(s, 0):
                        need[s] = v
                for s, v in need.items():
                    if waited.get(s, 0) < v:
                        e.wait_ge(sems[s], v)
                        waited[s] = v
                if o['fn'] is None:
                    continue
                ins = o['fn'](e)
                if o['sem'] is not None:
                    ins.then_inc(sems[o['sem']], o['inc'])

        with nc.Block() as block:
            @block.tensor
            def _(e):
                run('pe', e)

            @block.scalar
            def _(e):
                run('act', e)

            @block.vector
            def _(e):
                run('dve', e)

            @block.gpsimd
            def _(e):
                run('pool', e)

            @block.sync
            def _(e):
                run('sp', e)
        for cm in reversed(sem_ctx):
            cm.__exit__(None, None, None)


def _mask_tables():
    masks = []
    index = {}
    table = {}
    kk = np.arange(128)[:, None]
    qq = np.arange(128)[None, :]
    for g, (window, dil) in enumerate(A_GROUPS):
        hw = window // 2
        dmax = hw // 128 + (1 if hw % 128 else 0)
        dmax = max(dmax, 1)
        for dl in range(-dmax, dmax + 1):
            diff = 128 * dl + kk - qq
            m = ((diff % dil) == 0) & (np.abs(diff) <= hw)
            if not m.any():
                continue
            key = m.tobytes()
            if key not in index:
                index[key] = len(masks)
                masks.append(m.astype(np.float32))
            table[(g, dl)] = index[key]
    return np.stack(masks, axis=1), table


MASKS_NP, MASK_TABLE = _mask_tables()
NM = MASKS_NP.shape[1]


def build(n_layers=2, debug=False, stop=None):
    stop_spec = stop
    nc = bass.Bass("TRN2", target_bir_lowering=False)
    dk = "ExternalOutput" if debug else "Internal"
    x_in = nc.dram_tensor("x", [S_TOK, D], F32, kind="ExternalInput").ap()
    mem_in = nc.dram_tensor("mem", [256, D], F32, kind="ExternalInput").ap()
    w_in_a = nc.dram_tensor("w_in_a", [D, IN_A], F32, kind="ExternalInput").ap()
    w_in_b = nc.dram_tensor("w_in_b", [D, IN_B], F32, kind="ExternalInput").ap()
    w_out = nc.dram_tensor("w_out", [2, 1536, D], F32, kind="ExternalInput").ap()
    w_kv = nc.dram_tensor("w_mem_kv", [2, D, 1024], F32, kind="ExternalInput").ap()
    ng_in = nc.dram_tensor("ng", [128, 32], F32, kind="ExternalInput").ap()
    gains_in = nc.dram_tensor("gains", [128, 12 * 128], F32, kind="ExternalInput").ap()
    ropeA_in = nc.dram_tensor("ropeA", [128, 2 * 32 * 16], F32, kind="ExternalInput").ap()
    ropeB_in = nc.dram_tensor("ropeB", [128, 2 * 32 * 64], F32, kind="ExternalInput").ap()
    ident_in = nc.dram_tensor("ident", [128, 128], F32, kind="ExternalInput").ap()
    masks_in = nc.dram_tensor("masks", [128, NM * 128], F32, kind="ExternalInput").ap()
    out = nc.dram_tensor("y_out", [S_TOK, D], F32, kind="ExternalOutput").ap()
    x1_scr = nc.dram_tensor("x1_scr", [S_TOK, D], F32, kind=dk).ap()
    o_scr = nc.dram_tensor("o_scr", [S_TOK, D], BF16, kind=dk).ap()

    S = Sched()
    R_N = 6 * 4096 + 3 * 32 * 130
    import contextlib
    with contextlib.ExitStack() as es:
        def sb(name, shape, dt):
            return es.enter_context(nc.sbuf_tensor(name, shape, dt))

        hT = sb("hT", [128, 8 * 4096], BF16)
        R = sb("R", [128, R_N], BF16)
        Wbuf = sb("Wbuf", [128, 8 * 1152], BF16)
        Wbf = Wbuf.bitcast(F32)
        Wst = sb("Wst", [128, 2 * 8 * 128], F32)
        ropeA = sb("ropeA_t", [128, 2 * 32 * 16], F32)
        ropeBt = ropeA[:, 0:256]
        gains = sb("gains_t", [128, 12 * 128], F32)
        maskb = sb("maskb", [128, NM * 128], BF16)
        idf = sb("idf", [128, 128], F32)
        idb = sb("idb", [128, 128], BF16)
        ngt = sb("ngt", [128, 32], F32)
        mh = sb("mh", [128, 16], F32)
        KmT = sb("KmT", [128, 4 * 256], BF16)
        Vm = sb("Vm", [128, 2 * 4 * 130], BF16)
        ssA = sb("ssA", [128, 2 * 8], F32)
        msA = sb("msA", [128, 2 * 8], F32)
        rsA = sb("rsA", [128, 2 * 8], F32)
        sq = sb("sq", [128, 2 * 768], F32)
        tmpf = sb("tmpf", [128, 2 * 768], F32)
        U = sb("U", [128, 2560], F32)
        Ub = U.bitcast(BF16)
        qn = sb("qn", [128, 2 * 768], BF16)
        pt = sb("pt", [128, 3 * 512], BF16)
        rc = sb("rc", [128, 8], F32)
        ob = Ub[:, 3072:3584]
        th = Wbf[:, 0:1536]
        ot = Ub[:, 3072:5120]
        Wbb = Wbuf
        om = Wbb[:, 3072:3584]
        yb = Wbb[:, 3584:5120]
        yT = Ub[:, 0:3072]
        qmT = Wbb[:, 5120:5632]

        PSALL = es.enter_context(nc.psum_tensor("psall", [128, 4096], F32))
        PSBALL = PSALL.bitcast(BF16)
        PS = [PSALL[:, i * 512:(i + 1) * 512] for i in range(8)]
        PSB = [PSBALL[:, i * 1024:(i + 1) * 1024] for i in range(8)]
        PJ = [0, 1]
        SC = [2, 3]
        OA = [4, 5]
        TP = [6, 7]

        def pk(i):
            return ('ps', i)

        Rf = R.bitcast(F32)
        QK_OFF = 0
        V_OFF = 6 * 4096
        XT_OFF_F = 28672 // 2
        def xt_ap(b):
            return Rf[:, XT_OFF_F + b * 1024: XT_OFF_F + (b + 1) * 1024]

        def x1t_ap(b):
            return Rf[:, XT_OFF_F + 2048 + b * 1024: XT_OFF_F + 2048 + (b + 1) * 1024]

        def qk_ap(slot, t0, n=128):
            o = QK_OFF + slot * 4096 + t0
            return R[:, o:o + n]

        def v_ap(slot, kb, n=129):
            o = V_OFF + (slot * 32 + kb) * 130
            return R[:, o:o + n]

        def hT_ap(c, t):
            o = c * 4096 + t * 128
            return hT[:, o:o + 128]

        def wst_ap(slot, nchunk=8):
            return Wst[:, slot * 1024: slot * 1024 + nchunk * 128].rearrange("p (c n) -> p c n", n=128)

        S.op('sp', lambda e: e.dma_start(out=idf[:], in_=ident_in), writes=['idf'], dma='c0')
        S.op('sp', lambda e: e.dma_start(out=gains[:], in_=gains_in), writes=['gains'], dma='c1')
        S.op('sp', lambda e: e.dma_start(out=ngt[:], in_=ng_in), writes=['ngt'], dma='c2')
        S.op('sp', lambda e: e.dma_start(out=ropeA[:], in_=ropeA_in), writes=['ropeA'], dma='c3')
        S.op('sp', lambda e: e.dma_start(out=Rf[:, 0:NM * 128], in_=masks_in), writes=['mstage'], dma='c4')
        S.op('pool', lambda e: e.tensor_copy(out=idb[:], in_=idf[:]), reads=['idf'], writes=['idb'])
        S.op('pool', lambda e: e.tensor_scalar(out=maskb[:], in0=Rf[:, 0:NM * 128], scalar1=-1.0, scalar2=30000.0,
                                               op0=ALU.add, op1=ALU.mult), reads=['mstage'], writes=['maskb'])
        S.op('pool', lambda e: e.memset(mh[:], -0.5), writes=['mh'])
        S.barrier()
        stopped = (stop == 'init')

        wcount = [0]

        def load_weight_piece(src_ap, nchunk, dst_ap, dst_key):
            slot = wcount[0] % 2
            wcount[0] += 1
            S.op('sp', lambda e: e.dma_start(out=wst_ap(slot, nchunk), in_=src_ap.rearrange("(c p) n -> p c n", p=128)),
                 writes=[('wst', slot)], dma=('wst', slot))
            S.op('pool', lambda e: e.tensor_copy(out=dst_ap, in_=wst_ap(slot, nchunk)),
                 reads=[('wst', slot)], writes=[dst_key])

        sidx = [0]

        def rms_stats(src_ap, n_groups, glen, src_keys, src_is_psum, inv_n):
            b = sidx[0] % 2
            sidx[0] += 1
            n = n_groups * glen
            sq_ap = sq[:, b * 768: b * 768 + n]
            ss_ap = ssA[:, b * 8: b * 8 + n_groups]
            ms_ap = msA[:, b * 8: b * 8 + n_groups]
            rs_ap = rsA[:, b * 8: b * 8 + n_groups]
            S.op('act', lambda e: e.activation(out=sq_ap, in_=src_ap, func=ACT.Square),
                 reads=src_keys, writes=[('sq', b), ('sqb', b)])
            S.op('dve', lambda e: e.tensor_reduce(out=ss_ap, in_=sq_ap.rearrange("p (a b) -> p a b", a=n_groups),
                                                  axis=AX.X, op=ALU.add),
                 reads=[('sq', b), ('sqb', b)], writes=[('ss', b)])
            S.op('dve', lambda e: e.tensor_scalar(out=ms_ap, in0=ss_ap, scalar1=inv_n, scalar2=EPS,
                                                  op0=ALU.mult, op1=ALU.add),
                 reads=[('ss', b)], writes=[('ms', b)])
            S.op('pool', lambda e: e.tensor_tensor(out=rs_ap, in0=ms_ap, in1=mh[:, 0:n_groups], op=ALU.pow),
                 reads=[('ms', b), 'mh'], writes=[('rs', b)])
            return rs_ap, ('rs', b)

        def norm_transpose_tile(src_dram_ap, t, gcol, dst_fn, dst_keys, tcount):
            b = tcount % 2
            xt = xt_ap(b)
            S.op('sp', lambda e: e.dma_start(out=xt, in_=src_dram_ap), writes=[('xt', b)], dma=('xt', b))
            ss_ap = ssA[:, b * 8: b * 8 + 1]
            ms_ap = msA[:, b * 8: b * 8 + 1]
            rs_ap = rsA[:, b * 8: b * 8 + 1]
            S.op('act', lambda e: e.activation(out=sq[:, 0:1024], in_=xt, func=ACT.Square, accum_out=ss_ap),
                 reads=[('xt', b)], writes=[('sq', 0), ('sq', 1), ('sqb', 0), ('sqb', 1), ('ss', b)])
            S.op('dve', lambda e: e.tensor_scalar(out=ms_ap, in0=ss_ap, scalar1=1.0 / D, scalar2=EPS,
                                                  op0=ALU.mult, op1=ALU.add),
                 reads=[('ss', b)], writes=[('ms', b)])
            S.op('pool', lambda e: e.tensor_tensor(out=rs_ap, in0=ms_ap, in1=mh[:, 0:1], op=ALU.pow),
                 reads=[('ms', b), 'mh'], writes=[('rs', b)])
            S.op('dve', lambda e: e.tensor_scalar(out=xt, in0=xt, scalar1=rs_ap, scalar2=None, op0=ALU.mult),
                 reads=[('xt', b), ('rs', b)], writes=[('xt', b)])
            for half in range(2):
                bank = PJ[half]
                for j in range(4):
                    c = half * 4 + j
                    S.op('pe', lambda e, c=c, j=j, bank=bank: e.transpose(out=PS[bank][:, j * 128:(j + 1) * 128],
                                                                          in_=xt[:, c * 128:(c + 1) * 128], identity=idf[:]),
                         reads=[('xt', b), 'idf'], writes=[pk(bank)])
                for j in range(4):
                    c = half * 4 + j
                    eng = 'dve' if j % 2 == 0 else 'act'
                    if eng == 'dve':
                        S.op('dve', lambda e, c=c, j=j, bank=bank: e.tensor_scalar(
                            out=dst_fn(c), in0=PS[bank][:, j * 128:(j + 1) * 128],
                            scalar1=ngt[:, gcol + c: gcol + c + 1], scalar2=None, op0=ALU.mult),
                            reads=[pk(bank), 'ngt'], writes=dst_keys(c))
                    else:
                        S.op('act', lambda e, c=c, j=j, bank=bank: e.activation(
                            out=dst_fn(c), in_=PS[bank][:, j * 128:(j + 1) * 128], func=ACT.Copy,
                            scale=ngt[:, gcol + c: gcol + c + 1]),
                            reads=[pk(bank), 'ngt'], writes=dst_keys(c))

        def proj_head(pbase, nqk, nv, gain_ops, t, rope, ropekey, vslot0, ucount):
            banks = sorted(set((pbase * 512 + i * 128) // 512 for i in range(nqk + nv)))
            bkeys = [pk(bk) for bk in banks]
            c0 = pbase * 512
            src = PSALL[:, c0:c0 + nqk * 128]
            rs_ap, rs_key = rms_stats(src, nqk, 128, bkeys, True, 1.0 / E)
            b = ucount % 2
            tf = tmpf[:, b * 768: b * 768 + nqk * 128]
            tf3 = tf.rearrange("p (a b) -> p a b", a=nqk)
            TK = [('tmpf', b)]
            S.op('dve', lambda e: e.tensor_tensor(out=tf3, in0=src.rearrange("p (a b) -> p a b", a=nqk),
                                                  in1=rs_ap.unsqueeze(2).to_broadcast([128, nqk, 128]), op=ALU.mult),
                 reads=bkeys + [rs_key], writes=TK)
            for vi in range(nv):
                S.op('act', lambda e, vi=vi: e.activation(out=v_ap(vslot0 + vi, t, 128),
                                                          in_=PSALL[:, c0 + (nqk + vi) * 128: c0 + (nqk + vi + 1) * 128], func=ACT.Copy),
                     reads=bkeys, writes=[('v', vslot0 + vi, t)])
            return dict(nqk=nqk, gain_ops=gain_ops, t=t, rope=rope, ropekey=ropekey, ucount=ucount, b=b, tf3=tf3, TK=TK)

        def proj_tail(cx):
            nqk, gain_ops, t, rope, ropekey, ucount, b, tf3, TK = (cx[k_] for k_ in
                                                                   ('nqk', 'gain_ops', 't', 'rope', 'ropekey', 'ucount', 'b', 'tf3', 'TK'))
            b3 = ucount % 3
            qn_ap = qn[:, b3 * 768: b3 * 768 + nqk * 128] if b3 < 2 else Ub[:, 3584:3584 + nqk * 128]
            qn3 = qn_ap.rearrange("p (a b) -> p a b", a=nqk)
            QA, QB, QC = ('qn3', b3, 'a'), ('qn3', b3, 'b'), ('qn3', b3, 'c')
            cx['qn_ap'] = qn_ap
            cx['QK3'] = [QA, QB, QC]
            for (b0, nb_, gsrc) in gain_ops:
                S.op('dve', lambda e, b0=b0, nb_=nb_, gsrc=gsrc: e.tensor_tensor(out=tf3[:, b0:b0 + nb_, :], in0=tf3[:, b0:b0 + nb_, :],
                                                                            in1=gsrc, op=ALU.mult),
                     reads=TK + ['gains'], writes=TK)
            cos_ap, sin_ap, Rr = rope
            cosb = cos_ap.unsqueeze(1).to_broadcast([128, nqk, Rr])
            sinb = sin_ap.unsqueeze(1).to_broadcast([128, nqk, Rr])
            x1 = tf3[:, :, 0:Rr]
            x2 = tf3[:, :, Rr:2 * Rr]
            nr = nqk * Rr
            ra3 = sq[:, b * 768: b * 768 + nr].rearrange("p (a b) -> p a b", a=nqk)
            rb3 = sq[:, b * 768 + 384: b * 768 + 384 + nr].rearrange("p (a b) -> p a b", a=nqk)
            ra4 = U[:, b * 768: b * 768 + nr].rearrange("p (a b) -> p a b", a=nqk)
            rb4 = U[:, b * 768 + 384: b * 768 + 384 + nr].rearrange("p (a b) -> p a b", a=nqk)
            SQK = ('sq', b)
            S.op('dve', lambda e: e.tensor_tensor(out=ra3, in0=x1, in1=cosb, op=ALU.mult),
                 reads=TK + [ropekey], writes=[SQK])
            S.op('pool', lambda e: e.tensor_tensor(out=rb3, in0=x2, in1=sinb, op=ALU.mult),
                 reads=TK + [ropekey], writes=[('sqb', b)])
            S.op('dve', lambda e: e.tensor_tensor(out=ra4, in0=x2, in1=cosb, op=ALU.mult),
                 reads=TK + [ropekey], writes=[('ra4', b)])
            S.op('pool', lambda e: e.tensor_tensor(out=rb4, in0=x1, in1=sinb, op=ALU.mult),
                 reads=TK + [ropekey], writes=[('rb4', b)])
            S.op('dve', lambda e: e.tensor_tensor(out=qn3[:, :, 0:Rr], in0=ra3, in1=rb3, op=ALU.subtract),
                 reads=[SQK, ('sqb', b)], writes=[QA])
            S.op('pool', lambda e: e.tensor_tensor(out=qn3[:, :, Rr:2 * Rr], in0=ra4, in1=rb4, op=ALU.add),
                 reads=[('ra4', b), ('rb4', b)], writes=[QB])
            if 2 * Rr < 128:
                S.op('act', lambda e: e.activation(out=qn3[:, :, 2 * Rr:128], in_=tf3[:, :, 2 * Rr:128], func=ACT.Copy),
                     reads=TK, writes=[QC])

        def proj_tail_b(cx):
            nqk, t, ucount, qn_ap = cx['nqk'], cx['t'], cx['ucount'], cx['qn_ap']
            QA, QB, QC = cx['QK3']
            tb = TP[ucount % 2]
            for i in range(nqk):
                S.op('pe', lambda e, i=i: e.transpose(out=PSB[tb][:, i * 128:(i + 1) * 128],
                                                     in_=qn_ap[:, i * 128:(i + 1) * 128], identity=idb[:]),
                     reads=[QA, QB, QC, 'idb'], writes=[pk(tb)])
            qkdst = R[:, 0:nqk * 4096].rearrange("p (s k) -> p s k", s=nqk)[:, :, t * 128:(t + 1) * 128]
            S.op('act', lambda e: e.activation(out=qkdst, in_=PSB[tb][:, 0:nqk * 128].rearrange("p (s k) -> p s k", s=nqk),
                                               func=ACT.Copy),
                 reads=[pk(tb)], writes=[('qk', sl, t) for sl in range(nqk)])

        acount = [0]
        ptcount = [0]
        sccount = [0]
        fcount = [0]
        pend = []
        att_cfg = dict(banks=[2, 3], depth=1)

        def att_flush_one():
            q = pend.pop(0)
            pb, ab = q['pb'], q['ab']
            acc = PS[ab][:, 0:129]
            for (i, v_ap_, v_keys, first, last) in q['pv']:
                S.op('pe', lambda e, i=i, v_ap_=v_ap_, first=first, last=last, pb=pb, acc=acc: e.matmul(
                    acc, lhsT=pt[:, pb * 512 + i * 128: pb * 512 + (i + 1) * 128], rhs=v_ap_, start=first, stop=last),
                    reads=[('pt', pb)] + list(v_keys), writes=[pk(ab)])
            if q['fin'] is not None:
                q['fin'](ab)

        def att_flush_all():
            while pend:
                att_flush_one()

        def attention_job(blocks, fin):
            oa_ = att_cfg.get('oa', OA)
            ab = oa_[acount[0] % len(oa_)]
            acount[0] += 1
            nb = len(blocks)
            bi = 0
            while bi < nb:
                quad = blocks[bi:bi + 4]
                n = len(quad)
                banks = att_cfg['banks']
                sb_ = banks[sccount[0] % len(banks)]
                sccount[0] += 1
                pb = ptcount[0] % 3
                ptcount[0] += 1
                for i, blk in enumerate(quad):
                    q_ap, q_keys, k_ap, k_keys, v_ap_, v_keys, mi = blk
                    S.op('pe', lambda e, i=i, k_ap=k_ap, q_ap=q_ap, sb_=sb_, mi=mi: e.matmul(
                        PS[sb_][:, i * 128:(i + 1) * 128], lhsT=k_ap, rhs=q_ap, start=True, stop=(mi is None)),
                         reads=list(q_keys) + list(k_keys), writes=[pk(sb_)])
                    if mi is not None:
                        S.op('pe', lambda e, i=i, sb_=sb_, mi=mi: e.matmul(
                            PS[sb_][:, i * 128:(i + 1) * 128], lhsT=idb[:], rhs=maskb[:, mi * 128:(mi + 1) * 128], start=False, stop=True),
                             reads=['idb', 'maskb'], writes=[pk(sb_)])
                p_ap = pt[:, pb * 512: pb * 512 + n * 128]
                S.op('act', lambda e, p_ap=p_ap, sb_=sb_, n=n: e.activation(out=p_ap, in_=PS[sb_][:, 0:n * 128], func=ACT.Exp, scale=SCALE),
                     reads=[pk(sb_)], writes=[('pt', pb)])
                pv = []
                for i, blk in enumerate(quad):
                    gi = bi + i
                    pv.append((i, blk[4], blk[5], gi == 0, gi == nb - 1))
                bi += n
                pend.append(dict(pb=pb, ab=ab, pv=pv, fin=(fin if bi >= nb else None)))
                while len(pend) > att_cfg['depth']:
                    att_flush_one()

        for l in range(n_layers):
            if stopped:
                break
            if stop_spec is not None and ':' in stop_spec:
                stop = stop_spec.split(':')[1] if int(stop_spec.split(':')[0]) == l else None
            x_src = x_in if l == 0 else x1_scr
            x_dst = x1_scr if (l == 0 and n_layers > 1) else out
            w_in = w_in_a if l == 0 else w_in_b
            g_q_mem = 8 + l
            g_k_mem = 10 + l

            Wkv = R[:, 0:8192].rearrange("p (c n) -> p c n", c=8)
            memT = R[:, 8192:8192 + 2048]
            for piece in range(8):
                load_weight_piece(w_kv[l, :, piece * 128:(piece + 1) * 128], 8,
                                  Wkv[:, :, piece * 128:(piece + 1) * 128], ('wkv', piece))
            for mt in range(2):
                norm_transpose_tile(mem_in[mt * 128:(mt + 1) * 128, :], mt, 16 + l * 8,
                                    lambda c, mt=mt: memT[:, c * 256 + mt * 128: c * 256 + (mt + 1) * 128],
                                    lambda c, mt=mt: [('memT', mt, c)], mt)
            S.op('pool', lambda e: e.memset(Vm[:], 1.0), writes=['Vm'])
            for mt in range(2):
                for half in range(2):
                    bank = PJ[half]
                    for c in range(8):
                        S.op('pe', lambda e, c=c, half=half, bank=bank, mt=mt: e.matmul(
                            PS[bank][:, 0:512], lhsT=memT[:, c * 256 + mt * 128: c * 256 + (mt + 1) * 128],
                            rhs=Wkv[:, c, half * 512:(half + 1) * 512], start=(c == 0), stop=(c == 7)),
                            reads=[('memT', mt, c)] + [('wkv', half * 4 + j) for j in range(4)], writes=[pk(bank)])
                bank = PJ[0]
                src = PS[bank][:, 0:512]
                rs_ap, rs_key = rms_stats(src, 4, 128, [pk(bank)], True, 1.0 / E)
                tf3 = tmpf[:, 0:512].rearrange("p (a b) -> p a b", a=4)
                S.op('dve', lambda e, src=src, rs_ap=rs_ap, tf3=tf3: e.tensor_tensor(
                    out=tf3, in0=src.rearrange("p (a b) -> p a b", a=4),
                    in1=rs_ap.unsqueeze(2).to_broadcast([128, 4, 128]), op=ALU.mult),
                    reads=[pk(bank), rs_key], writes=[('tmpf', 0, 0)])
                qn3 = qn[:, 0:512].rearrange("p (a b) -> p a b", a=4)
                gsrc = gains[:, g_k_mem * 128:(g_k_mem + 1) * 128].unsqueeze(1).to_broadcast([128, 4, 128])
                S.op('pool', lambda e, qn3=qn3, tf3=tf3, gsrc=gsrc: e.tensor_tensor(out=qn3, in0=tf3, in1=gsrc, op=ALU.mult),
                     reads=[('tmpf', 0, 0), 'gains'], writes=[('qn', 0, 'a')])
                tb = TP[0]
                for hd in range(4):
                    S.op('pe', lambda e, hd=hd: e.transpose(out=PSB[tb][:, hd * 128:(hd + 1) * 128],
                                                           in_=qn[:, hd * 128:(hd + 1) * 128], identity=idb[:]),
                         reads=[('qn', 0, 'a'), 'idb'], writes=[pk(tb)])
                S.op('act', lambda e, mt=mt: e.activation(
                    out=KmT[:].rearrange("p (h k) -> p h k", h=4)[:, :, mt * 128:(mt + 1) * 128],
                    in_=PSB[tb][:, 0:512].rearrange("p (h k) -> p h k", h=4), func=ACT.Copy),
                    reads=[pk(tb)], writes=['KmT'])
                bank = PJ[1]
                S.op('dve', lambda e, mt=mt, bank=bank: e.tensor_copy(
                    out=Vm[:, mt * 520:(mt + 1) * 520].rearrange("p (h k) -> p h k", h=4)[:, :, 0:128],
                    in_=PS[bank][:, 0:512].rearrange("p (h k) -> p h k", h=4)),
                    reads=[pk(bank)], writes=['Vm'])
            S.barrier()
            if stop == 'M':
                break

            for t in range(NT):
                norm_transpose_tile(x_src[t * 128:(t + 1) * 128, :], t, l * 8,
                                    lambda c, t=t: hT_ap(c, t), lambda c, t=t: [('hT', t, c)], t)
            S.barrier()
            if stop == 'p1':
                break

            S.op('pool', lambda e: e.memset(R[:, V_OFF:R_N], 1.0), writes=['Vall'])
            S.barrier()
            if l == 0:
                iters = [dict(cols=[((s_ * 3 + g) * 8 + h) * 128 for s_ in range(3) for g in range(3)], nqk=6, nv=3, hd=h)
                         for h in range(8)]
                gain_ops = [(0, 6, gains[:, 0:768].rearrange("p (a b) -> p a b", a=6))]
            else:
                iters = [dict(cols=[(kvh * 4 + j) * 128 for j in range(4)] + [1024 + kvh * 128, 1280 + kvh * 128], nqk=5, nv=1, hd=kvh)
                         for kvh in range(2)]
                gain_ops = [(0, 4, gains[:, 6 * 128:7 * 128].unsqueeze(1).to_broadcast([128, 4, 128])),
                            (4, 1, gains[:, 7 * 128:8 * 128].unsqueeze(1))]
            att_cfg['banks'] = [0, 1, 2, 3]
            att_cfg['depth'] = 2

            def emit_iter_load(it):
                ncols = (it['nqk'] + it['nv']) * 128
                Wv_ = Wbuf[:, 0:8 * ncols].rearrange("p (c n) -> p c n", c=8)
                for bi_, col in enumerate(it['cols']):
                    load_weight_piece(w_in[:, col:col + 128], 8, Wv_[:, :, bi_ * 128:(bi_ + 1) * 128], ('wbuf', bi_))

            emit_iter_load(iters[0])
            tilecount = 0
            for it_i, it in enumerate(iters):
                nqk, nv = it['nqk'], it['nv']
                nblk = nqk + nv
                ncols = nblk * 128
                Wv = Wbuf[:, 0:8 * ncols].rearrange("p (c n) -> p c n", c=8)
                prev_cx = None
                prev2_cx = None
                for t in range(NT):
                    pbase = 0 if tilecount % 2 == 0 else 3
                    c0 = 0
                    while c0 < ncols:
                        n_ = min(512, ncols - c0)
                        bank = pbase + c0 // 512
                        wkeys = [('wbuf', j) for j in range(c0 // 128, (c0 + n_) // 128)]
                        for c in range(8):
                            S.op('pe', lambda e, c=c, t=t, c0=c0, n_=n_, pbase=pbase, Wv=Wv: e.matmul(
                                PSALL[:, pbase * 512 + c0: pbase * 512 + c0 + n_], lhsT=hT_ap(c, t), rhs=Wv[:, c, c0:c0 + n_],
                                start=(c == 0), stop=(c == 7)),
                                reads=[('hT', t, c)] + wkeys, writes=[pk(bank)])
                        c0 += n_
                    if l == 0:
                        rope = (ropeA[:, t * 16:(t + 1) * 16], ropeA[:, 512 + t * 16: 512 + (t + 1) * 16], 16)
                        ropekey = 'ropeA'
                    else:
                        rbuf = tilecount % 2
                        S.op('sp', lambda e, rbuf=rbuf, t=t: e.dma_start(
                            out=ropeBt[:, rbuf * 128:(rbuf + 1) * 128].rearrange("p (a b) -> p a b", a=2),
                            in_=ropeB_in.rearrange("p (a t b) -> p a t b", a=2, t=32)[:, :, t, :]),
                            writes=[('ropeB', rbuf)], dma=('ropeB', rbuf))
                        rope = (ropeBt[:, rbuf * 128: rbuf * 128 + 64], ropeBt[:, rbuf * 128 + 64: rbuf * 128 + 128], 64)
                        ropekey = ('ropeB', rbuf)
                    if prev_cx is not None:
                        proj_tail(prev_cx)
                    if prev2_cx is not None:
                        proj_tail_b(prev2_cx)
                    prev2_cx = prev_cx
                    prev_cx = proj_head(pbase, nqk, nv, gain_ops, t, rope, ropekey, 0, tilecount)
                    tilecount += 1
                proj_tail(prev_cx)
                proj_tail_b(prev2_cx)
                proj_tail_b(prev_cx)
                if it_i + 1 < len(iters):
                    emit_iter_load(iters[it_i + 1])
                if stop == 'p2proj':
                    continue
                hd = it['hd']
                jobs = []
                if l == 0:
                    for T in range(NT):
                        blocks = []
                        for g, (window, dil) in enumerate(A_GROUPS):
                            for dl in range(-9, 10):
                                if (g, dl) not in MASK_TABLE:
                                    continue
                                kb = T + dl
                                if kb < 0 or kb >= NT:
                                    continue
                                blocks.append((qk_ap(g, T * 128), [('qk', g, T)], qk_ap(3 + g, kb * 128), [('qk', 3 + g, kb)],
                                               v_ap(g, kb), [('v', g, kb)], MASK_TABLE[(g, dl)]))
                        jobs.append((blocks, T, hd))
                else:
                    for j in range(4):
                        for T in range(NT):
                            blocks = []
                            for kb in range(NT):
                                blocks.append((qk_ap(j, T * 128), [('qk', j, T)], qk_ap(4, kb * 128), [('qk', 4, kb)],
                                               v_ap(0, kb), [('v', 0, kb)], None))
                            jobs.append((blocks, T, hd * 4 + j))
                for blocks, T, hcol in jobs:
                    def fin(ab, T=T, hcol=hcol):
                        ob_i = fcount[0] % 4
                        fcount[0] += 1
                        rc_ap = rc[:, ob_i:ob_i + 1]
                        ob_ap = ob[:, ob_i * 128:(ob_i + 1) * 128]
                        S.op('dve', lambda e: e.reciprocal(out=rc_ap, in_=PS[ab][:, 128:129]),
                             reads=[pk(ab)], writes=[('rc', ob_i)])
                        S.op('dve', lambda e: e.tensor_scalar(out=ob_ap, in0=PS[ab][:, 0:128], scalar1=rc_ap, scalar2=None,
                                                              op0=ALU.mult),
                             reads=[pk(ab), ('rc', ob_i)], writes=[('ob', ob_i)])
                        S.op('sp', lambda e: e.dma_start(out=o_scr[T * 128:(T + 1) * 128, hcol * 128:(hcol + 1) * 128], in_=ob_ap),
                             reads=[('ob', ob_i)], writes=[('oscr', T)], dma=('ob', ob_i))
                    attention_job(blocks, fin)
                att_flush_all()
            att_cfg['banks'] = [2, 3]
            att_cfg['depth'] = 1
            S.barrier()
            if stop in ('p2', 'p2proj'):
                break

            Wg = R[:, 0:16384].rearrange("p (c n) -> p c n", c=8)
            Wo = R[:, 16384:28672].rearrange("p (c n) -> p c n", c=12)
            gcol0 = 9216 if l == 0 else 1536
            for piece in range(16):
                load_weight_piece(w_in[:, gcol0 + piece * 128: gcol0 + (piece + 1) * 128], 8,
                                  Wg[:, :, piece * 128:(piece + 1) * 128], ('wg', piece))
            for piece in range(8):
                load_weight_piece(w_out[l, 0:1024, piece * 128:(piece + 1) * 128], 8,
                                  Wo[:, 0:8, piece * 128:(piece + 1) * 128], ('wo', 0, piece))
                load_weight_piece(w_out[l, 1024:1536, piece * 128:(piece + 1) * 128], 4,
                                  Wo[:, 8:12, piece * 128:(piece + 1) * 128], ('wo', 1, piece))
            wg_keys = [('wg', p_) for p_ in range(16)]
            wo_keys = [('wo', a_, p_) for a_ in range(2) for p_ in range(8)]
            KmT3 = KmT[:].rearrange("p (h k) -> p h k", h=4)
            att_cfg['banks'] = [2, 0]
            att_cfg['depth'] = 1
            att_cfg['oa'] = [4]

            def gate_chain(t, j, gb, o_j, o_keys):
                th_ap = th[:, j * 512:(j + 1) * 512]
                S.op('act', lambda e: e.activation(out=th_ap, in_=PS[gb][:, 0:512], func=ACT.Tanh, scale=0.5),
                     reads=[pk(gb)], writes=[('th', j)])
                S.op('dve', lambda e: e.scalar_tensor_tensor(out=th_ap, in0=th_ap, scalar=1.0, in1=PS[gb][:, 0:512],
                                                             op0=ALU.add, op1=ALU.mult),
                     reads=[pk(gb), ('th', j)], writes=[('th', j)])
                yb_ap = yb[:, j * 512:(j + 1) * 512]
                S.op('dve', lambda e: e.scalar_tensor_tensor(out=yb_ap, in0=th_ap, scalar=0.5, in1=o_j, op0=ALU.mult, op1=ALU.mult),
                     reads=[('th', j)] + o_keys, writes=[('yb', j)])

            def gate_proj(t, j, gb):
                for c in range(8):
                    S.op('pe', lambda e, c=c: e.matmul(PS[gb][:, 0:512], lhsT=hT_ap(c, t), rhs=Wg[:, c, 512 + j * 512: 1024 + j * 512],
                                                       start=(c == 0), stop=(c == 7)),
                         reads=[('hT', t, c)] + wg_keys[4 + 4 * j: 8 + 4 * j], writes=[pk(gb)])

            def y_transposes(t, j, tpb, eng):
                b = t % 2
                yb_ap = yb[:, j * 512:(j + 1) * 512]
                yTb = yT[:, b * 1536:(b + 1) * 1536]
                for i in range(4):
                    S.op('pe', lambda e, i=i: e.transpose(out=PSB[tpb][:, i * 128:(i + 1) * 128],
                                                         in_=yb_ap[:, i * 128:(i + 1) * 128], identity=idb[:]),
                         reads=[('yb', j), 'idb'], writes=[pk(tpb)])
                if eng == 'act':
                    S.op('act', lambda e: e.activation(out=yTb[:, j * 512:(j + 1) * 512], in_=PSB[tpb][:, 0:512], func=ACT.Copy),
                         reads=[pk(tpb)], writes=[('yT', b, j)])
                else:
                    S.op('dve', lambda e: e.tensor_copy(out=yTb[:, j * 512:(j + 1) * 512], in_=PSB[tpb][:, 0:512]),
                         reads=[pk(tpb)], writes=[('yT', b, j)])

            def stage_A(t):
                b = t % 2
                xt = xt_ap(b)
                ot_ap = ot[:, b * 1024:(b + 1) * 1024]
                S.op('sp', lambda e, x_src=x_src: e.dma_start(out=xt, in_=x_src[t * 128:(t + 1) * 128, :]),
                     writes=[('xt', b)], dma=('xt', b))
                S.op('sp', lambda e: e.dma_start(out=ot_ap, in_=o_scr[t * 128:(t + 1) * 128, :]),
                     reads=[('oscr', t)], writes=[('ot', b)], dma=('ot', b))
                bank = 0
                for c in range(8):
                    S.op('pe', lambda e, c=c: e.matmul(PS[bank][:, 0:512], lhsT=hT_ap(c, t), rhs=Wg[:, c, 0:512],
                                                       start=(c == 0), stop=(c == 7)),
                         reads=[('hT', t, c)] + wg_keys[0:4], writes=[pk(bank)])
                gate_proj(t, 0, 1)
                gate_proj(t, 1, 3)
                src = PS[bank][:, 0:512]
                rs_ap, rs_key = rms_stats(src, 4, 128, [pk(bank)], True, 1.0 / E)
                tf3 = tmpf[:, b * 768: b * 768 + 512].rearrange("p (a b) -> p a b", a=4)
                S.op('dve', lambda e: e.tensor_tensor(out=tf3, in0=src.rearrange("p (a b) -> p a b", a=4),
                                                      in1=rs_ap.unsqueeze(2).to_broadcast([128, 4, 128]), op=ALU.mult),
                     reads=[pk(bank), rs_key], writes=[('tmpf', b, 0)])
                qn3 = qn[:, b * 768: b * 768 + 512].rearrange("p (a b) -> p a b", a=4)
                gsrc = gains[:, g_q_mem * 128:(g_q_mem + 1) * 128].unsqueeze(1).to_broadcast([128, 4, 128])
                S.op('pool', lambda e: e.tensor_tensor(out=qn3, in0=tf3, in1=gsrc, op=ALU.mult),
                     reads=[('tmpf', b, 0), 'gains'], writes=[('qn', b, 'a')])
                gate_chain(t, 0, 1, ot_ap[:, 0:512], [('ot', b)])
                gate_chain(t, 1, 3, ot_ap[:, 512:1024], [('ot', b)])

            def stage_B(t):
                b = t % 2
                qn_ap = qn[:, b * 768: b * 768 + 512]
                tb = 6
                for hd in range(4):
                    S.op('pe', lambda e, hd=hd: e.transpose(out=PSB[tb][:, hd * 128:(hd + 1) * 128],
                                                           in_=qn_ap[:, hd * 128:(hd + 1) * 128], identity=idb[:]),
                         reads=[('qn', b, 'a'), 'idb'], writes=[pk(tb)])
                S.op('act', lambda e: e.activation(out=qmT[:], in_=PSB[tb][:, 0:512], func=ACT.Copy),
                     reads=[pk(tb)], writes=['qmT'])
                y_transposes(t, 0, 7, 'dve')
                gate_proj(t, 2, 5)
                th2 = th[:, 1024:1536]
                S.op('act', lambda e: e.activation(out=th2, in_=PS[5][:, 0:512], func=ACT.Tanh, scale=0.5),
                     reads=[pk(5)], writes=[('th', 2)])
                S.op('dve', lambda e: e.scalar_tensor_tensor(out=th2, in0=th2, scalar=1.0, in1=PS[5][:, 0:512],
                                                             op0=ALU.add, op1=ALU.mult),
                     reads=[pk(5), ('th', 2)], writes=[('th', 2)])
                y_transposes(t, 1, 6, 'act')
                for hd in range(4):
                    blocks = []
                    for kb in range(2):
                        blocks.append((qmT[:, hd * 128:(hd + 1) * 128], ['qmT'], KmT3[:, hd, kb * 128:(kb + 1) * 128], ['KmT'],
                                       Vm[:, (kb * 4 + hd) * 130:(kb * 4 + hd) * 130 + 129], ['Vm'], None))

                    def fin(ab, hd=hd):
                        ob_i = fcount[0] % 4
                        fcount[0] += 1
                        rc_ap = rc[:, 4 + ob_i:5 + ob_i]
                        S.op('dve', lambda e: e.reciprocal(out=rc_ap, in_=PS[ab][:, 128:129]),
                             reads=[pk(ab)], writes=[('rcm', ob_i)])
                        S.op('act', lambda e: e.activation(out=om[:, hd * 128:(hd + 1) * 128], in_=PS[ab][:, 0:128], func=ACT.Copy,
                                                           scale=rc_ap),
                             reads=[pk(ab), ('rcm', ob_i)], writes=[('om', hd)])
                    attention_job(blocks, fin)
                att_flush_all()
                yb2 = yb[:, 1024:1536]
                S.op('dve', lambda e: e.scalar_tensor_tensor(out=yb2, in0=th2, scalar=0.5, in1=om[:, 0:512], op0=ALU.mult, op1=ALU.mult),
                     reads=[('th', 2)] + [('om', hd) for hd in range(4)], writes=[('yb', 2)])
                y_transposes(t, 2, 7, 'act')

            def stage_C(t):
                b = t % 2
                xt = xt_ap(b)
                x1t = x1t_ap(b)
                yTb = yT[:, b * 1536:(b + 1) * 1536]
                for nb_ in range(2):
                    obk = [2, 0][nb_]
                    for cc in range(12):
                        S.op('pe', lambda e, cc=cc, nb_=nb_, obk=obk: e.matmul(
                            PS[obk][:, 0:512], lhsT=yTb[:, cc * 128:(cc + 1) * 128], rhs=Wo[:, cc, nb_ * 512:(nb_ + 1) * 512],
                            start=(cc == 0), stop=(cc == 11)),
                            reads=[('yT', b, cc // 4)] + wo_keys, writes=[pk(obk)])
                    S.op('dve', lambda e, nb_=nb_, obk=obk: e.tensor_tensor(
                        out=x1t[:, nb_ * 512:(nb_ + 1) * 512], in0=PS[obk][:, 0:512], in1=xt[:, nb_ * 512:(nb_ + 1) * 512], op=ALU.add),
                        reads=[pk(obk), ('xt', b)], writes=[('x1t', b, nb_)])
                S.op('sp', lambda e, x_dst=x_dst: e.dma_start(out=x_dst[t * 128:(t + 1) * 128, :], in_=x1t),
                     reads=[('x1t', b, 0), ('x1t', b, 1)], writes=[('xdst', t)], dma=('x1t', b))

            def _bind(f, **kw):
                return f

            if stop != 'p3w':
                stage_A(0)
                for t in range(NT):
                    stage_B(t)
                    if t + 1 < NT:
                        stage_A(t + 1)
                    stage_C(t)
            att_cfg['oa'] = OA
            S.barrier()
        S.emit(nc)
    return nc, S


def _host_consts(norm_g, mem_norm_g, mem_qn_g, mem_kn_g, qn_a, kn_a, qn_b, kn_b):
    ng = np.zeros((128, 32), np.float32)
    for l in range(2):
        ng[:, l * 8:(l + 1) * 8] = norm_g[l].reshape(8, 128).T
        ng[:, 16 + l * 8:16 + (l + 1) * 8] = mem_norm_g[l].reshape(8, 128).T
    rows = [qn_a[0, 0], qn_a[0, 1], qn_a[0, 2], kn_a[0, 0], kn_a[0, 1], kn_a[0, 2], qn_b[0], kn_b[0],
            mem_qn_g[0], mem_qn_g[1], mem_kn_g[0], mem_kn_g[1]]
    gains = np.ascontiguousarray(np.broadcast_to(np.concatenate(rows)[None, :], (128, 12 * 128))).astype(np.float32)
    pos = np.arange(S_TOK, dtype=np.float32)
    invA = (np.float32(500000.0) ** (-np.arange(0, 32, 2, dtype=np.float32) / np.float32(32))).astype(np.float32)
    angA = pos[:, None] * invA[None, :]
    row = np.repeat(np.arange(64, dtype=np.float32), 64)
    col = np.tile(np.arange(64, dtype=np.float32), 64)
    invB = (np.float32(10000.0) ** (-np.arange(0, 64, 2, dtype=np.float32) / np.float32(64))).astype(np.float32)
    angB = np.concatenate([row[:, None] * invB[None, :], col[:, None] * invB[None, :]], axis=-1)

    def lay(a):
        return a.reshape(32, 128, -1).transpose(1, 0, 2)
    ropeA = np.stack([lay(np.cos(angA)), lay(np.sin(angA))], axis=1).reshape(128, -1).astype(np.float32)
    ropeB = np.stack([lay(np.cos(angB)), lay(np.sin(angB))], axis=1).reshape(128, -1).astype(np.float32)
    return dict(ng=ng, gains=gains, ropeA=np.ascontiguousarray(ropeA), ropeB=np.ascontiguousarray(ropeB),
                ident=np.eye(128, dtype=np.float32), masks=np.ascontiguousarray(MASKS_NP.reshape(128, -1)))


_NC_CACHE = {}


def kernel(x, mem, norm_g, mem_norm_g, w_mem_kv, mem_qn_g, mem_kn_g, w_out,
           w_in_a, qn_a, kn_a, w_in_b, qn_b, kn_b):
    f = lambda a: np.ascontiguousarray(np.asarray(a, dtype=np.float32))
    x = f(x)
    mem = f(mem)
    consts = _host_consts(f(norm_g), f(mem_norm_g), f(mem_qn_g), f(mem_kn_g), f(qn_a), f(kn_a), f(qn_b), f(kn_b))
    shared = dict(w_in_a=f(w_in_a)[0], w_in_b=f(w_in_b)[0], w_out=f(w_out), w_mem_kv=f(w_mem_kv), **consts)
    if 'nc' not in _NC_CACHE:
        _NC_CACHE['nc'] = build()[0]
    nc = _NC_CACHE['nc']
    n = x.shape[0]
    in_maps = [dict(x=x[b], mem=mem[b], **shared) for b in range(n)]
    res = run_bass_kernel_spmd(nc, in_maps, core_ids=list(range(n)))
    return np.stack([np.asarray(r["y_out"], dtype=np.float32) for r in res.results], axis=0)
```

```python
import numpy as np
import concourse.bass as bass
import concourse.mybir as mybir
from concourse.bass_utils import run_bass_kernel_spmd

F32 = mybir.dt.float32
BF16 = mybir.dt.bfloat16
ALU = mybir.AluOpType
ACT = mybir.ActivationFunctionType
AX = mybir.AxisListType

ENGS = ['pe', 'act', 'dve', 'pool', 'sp']

S_TOK = 4096
NT = 32
D = 1024
E = 128
EPS = 1e-6
SCALE = float(E) ** -0.5
A_GROUPS = ((128, 1), (512, 4), (2048, 16))
IN_A = 11264
IN_B = 3584


class Sched:
    def __init__(self):
        self.ops = []
        self.state = {}
        self.last_on = {e: None for e in ENGS}
        self.dma_last = {}

    def op(self, eng, fn, reads=(), writes=(), dma=None):
        i = len(self.ops)
        deps = set()
        for k in reads:
            excl = isinstance(k, tuple) and k[0] == 'ps'
            st = self.state.setdefault(k, [None, []])
            if st[0] is not None:
                deps.add(st[0])
            if excl:
                for r in st[1]:
                    if self.ops[r]['eng'] != eng:
                        deps.add(r)
        for k in writes:
            st = self.state.setdefault(k, [None, []])
            if st[0] is not None:
                deps.add(st[0])
            for r in st[1]:
                deps.add(r)
        for k in reads:
            self.state[k][1].append(i)
        for k in writes:
            self.state[k] = [i, []]
        if dma is not None:
            p = self.dma_last.get(dma)
            if p is not None:
                deps.add(p)
            self.dma_last[dma] = i
        deps.discard(i)
        if eng == 'pe':
            deps = {d for d in deps if self.ops[d]['eng'] != 'pe'}
        best = {}
        keep = set()
        for d in deps:
            od = self.ops[d]
            if od['dma'] is not None:
                keep.add(d)
            elif d > best.get(od['eng'], -1):
                best[od['eng']] = d
        deps = keep | set(best.values())
        self.ops.append(dict(eng=eng, fn=fn, deps=deps, dma=dma))
        if fn is not None:
            self.last_on[eng] = i
        return i

    def barrier(self):
        lasts = [v for v in self.last_on.values() if v is not None]
        lasts += list(self.dma_last.values())
        lasts = set(lasts)
        for e in ENGS:
            deps = {d for d in lasts if not (self.ops[d]['eng'] == e and self.ops[d]['dma'] is None)}
            self.ops.append(dict(eng=e, fn=None, deps=deps, dma=None))
        self.state = {}

    def emit(self, nc):
        ops = self.ops
        needed = set()
        for o in ops:
            needed |= o['deps']
        dma_keys = []
        seen = set()
        for o in ops:
            if o['dma'] is not None and o['dma'] not in seen:
                seen.add(o['dma'])
                dma_keys.append(o['dma'])
        sem_ctx = []
        sems = {}
        for n_i, name in enumerate(['pe', 'act', 'dve', 'pool'] + [('dma', k) for k in dma_keys]):
            cm = nc.semaphore("s%d" % n_i)
            sems[name] = cm.__enter__()
            sem_ctx.append(cm)
        cnt = {}
        for i, o in enumerate(ops):
            if o['dma'] is not None:
                key = ('dma', o['dma'])
                cnt[key] = cnt.get(key, 0) + 16
                o['sem'], o['val'], o['inc'] = key, cnt[key], 16
            elif i in needed:
                assert o['fn'] is not None
                key = o['eng']
                cnt[key] = cnt.get(key, 0) + 1
                o['sem'], o['val'], o['inc'] = key, cnt[key], 1
            else:
                o['sem'] = None
        self.maxval = dict(cnt)

        def run(eng_name, e):
            waited = {}
            for o in ops:
                if o['eng'] != eng_name:
                    continue
                need = {}
                for d in o['deps']:
                    od = ops[d]
                    s, v = od['sem'], od['val']
                    if v > need.get(s, 0):
                        need[s] = v
                for s, v in need.items():
                    if waited.get(s, 0) < v:
                        e.wait_ge(sems[s], v)
                        waited[s] = v
                if o['fn'] is None:
                    continue
                ins = o['fn'](e)
                if o['sem'] is not None:
                    ins.then_inc(sems[o['sem']], o['inc'])

        with nc.Block() as block:
            @block.tensor
            def _(e):
                run('pe', e)

            @block.scalar
            def _(e):
                run('act', e)

            @block.vector
            def _(e):
                run('dve', e)

            @block.gpsimd
            def _(e):
                run('pool', e)

            @block.sync
            def _(e):
                run('sp', e)
        for cm in reversed(sem_ctx):
            cm.__exit__(None, None, None)


def _mask_tables():
    masks = []
    index = {}
    table = {}
    kk = np.arange(128)[:, None]
    qq = np.arange(128)[None, :]
    for g, (window, dil) in enumerate(A_GROUPS):
        hw = window // 2
        dmax = hw // 128 + (1 if hw % 128 else 0)
        dmax = max(dmax, 1)
        for dl in range(-dmax, dmax + 1):
            diff = 128 * dl + kk - qq
            m = ((diff % dil) == 0) & (np.abs(diff) <= hw)
            if not m.any():
                continue
            key = m.tobytes()
            if key not in index:
                index[key] = len(masks)
                masks.append(m.astype(np.float32))
            table[(g, dl)] = index[key]
    return np.stack(masks, axis=1), table


MASKS_NP, MASK_TABLE = _mask_tables()
NM = MASKS_NP.shape[1]


def build(n_layers=2, debug=False, stop=None):
    stop_spec = stop
    nc = bass.Bass("TRN2", target_bir_lowering=False)
    dk = "ExternalOutput" if debug else "Internal"
    x_in = nc.dram_tensor("x", [S_TOK, D], F32, kind="ExternalInput").ap()
    mem_in = nc.dram_tensor("mem", [256, D], F32, kind="ExternalInput").ap()
    w_in_a = nc.dram_tensor("w_in_a", [D, IN_A], F32, kind="ExternalInput").ap()
    w_in_b = nc.dram_tensor("w_in_b", [D, IN_B], F32, kind="ExternalInput").ap()
    w_out = nc.dram_tensor("w_out", [2, 1536, D], F32, kind="ExternalInput").ap()
    w_kv = nc.dram_tensor("w_mem_kv", [2, D, 1024], F32, kind="ExternalInput").ap()
    ng_in = nc.dram_tensor("ng", [128, 32], F32, kind="ExternalInput").ap()
    gains_in = nc.dram_tensor("gains", [128, 12 * 128], F32, kind="ExternalInput").ap()
    ropeA_in = nc.dram_tensor("ropeA", [128, 2 * 32 * 16], F32, kind="ExternalInput").ap()
    ropeB_in = nc.dram_tensor("ropeB", [128, 2 * 32 * 64], F32, kind="ExternalInput").ap()
    ident_in = nc.dram_tensor("ident", [128, 128], F32, kind="ExternalInput").ap()
    masks_in = nc.dram_tensor("masks", [128, NM * 128], F32, kind="ExternalInput").ap()
    out = nc.dram_tensor("y_out", [S_TOK, D], F32, kind="ExternalOutput").ap()
    x1_scr = nc.dram_tensor("x1_scr", [S_TOK, D], F32, kind=dk).ap()
    o_scr = nc.dram_tensor("o_scr", [S_TOK, D], BF16, kind=dk).ap()

    S = Sched()
    R_N = 6 * 4096 + 3 * 32 * 130
    import contextlib
    with contextlib.ExitStack() as es:
        def sb(name, shape, dt):
            return es.enter_context(nc.sbuf_tensor(name, shape, dt))

        hT = sb("hT", [128, 8 * 4096], BF16)
        R = sb("R", [128, R_N], BF16)
        Wbuf = sb("Wbuf", [128, 8 * 1152], BF16)
        Wbf = Wbuf.bitcast(F32)
        Wst = sb("Wst", [128, 2 * 8 * 128], F32)
        ropeA = sb("ropeA_t", [128, 2 * 32 * 16], F32)
        ropeBt = ropeA[:, 0:256]
        gains = sb("gains_t", [128, 12 * 128], F32)
        maskb = sb("maskb", [128, NM * 128], BF16)
        idf = sb("idf", [128, 128], F32)
        idb = sb("idb", [128, 128], BF16)
        ngt = sb("ngt", [128, 32], F32)
        mh = sb("mh", [128, 16], F32)
        KmT = sb("KmT", [128, 4 * 256], BF16)
        Vm = sb("Vm", [128, 2 * 4 * 130], BF16)
        ssA = sb("ssA", [128, 2 * 8], F32)
        msA = sb("msA", [128, 2 * 8], F32)
        rsA = sb("rsA", [128, 2 * 8], F32)
        sq = sb("sq", [128, 2 * 768], F32)
        tmpf = sb("tmpf", [128, 2 * 768], F32)
        U = sb("U", [128, 2560], F32)
        Ub = U.bitcast(BF16)
        qn = sb("qn", [128, 2 * 768], BF16)
        pt = sb("pt", [128, 3 * 512], BF16)
        rc = sb("rc", [128, 8], F32)
        ob = Ub[:, 3072:3584]
        th = Wbf[:, 0:1536]
        ot = Ub[:, 3072:5120]
        Wbb = Wbuf
        om = Wbb[:, 3072:3584]
        yb = Wbb[:, 3584:5120]
        yT = Ub[:, 0:3072]
        qmT = Wbb[:, 5120:5632]

        PSALL = es.enter_context(nc.psum_tensor("psall", [128, 4096], F32))
        PSBALL = PSALL.bitcast(BF16)
        PS = [PSALL[:, i * 512:(i + 1) * 512] for i in range(8)]
        PSB = [PSBALL[:, i * 1024:(i + 1) * 1024] for i in range(8)]
        PJ = [0, 1]
        SC = [2, 3]
        OA = [4, 5]
        TP = [6, 7]

        def pk(i):
            return ('ps', i)

        Rf = R.bitcast(F32)
        QK_OFF = 0
        V_OFF = 6 * 4096
        XT_OFF_F = 28672 // 2
        def xt_ap(b):
            return Rf[:, XT_OFF_F + b * 1024: XT_OFF_F + (b + 1) * 1024]

        def x1t_ap(b):
            return Rf[:, XT_OFF_F + 2048 + b * 1024: XT_OFF_F + 2048 + (b + 1) * 1024]

        def qk_ap(slot, t0, n=128):
            o = QK_OFF + slot * 4096 + t0
            return R[:, o:o + n]

        def v_ap(slot, kb, n=129):
            o = V_OFF + (slot * 32 + kb) * 130
            return R[:, o:o + n]

        def hT_ap(c, t):
            o = c * 4096 + t * 128
            return hT[:, o:o + 128]

        def wst_ap(slot, nchunk=8):
            return Wst[:, slot * 1024: slot * 1024 + nchunk * 128].rearrange("p (c n) -> p c n", n=128)

        S.op('sp', lambda e: e.dma_start(out=idf[:], in_=ident_in), writes=['idf'], dma='c0')
        S.op('sp', lambda e: e.dma_start(out=gains[:], in_=gains_in), writes=['gains'], dma='c1')
        S.op('sp', lambda e: e.dma_start(out=ngt[:], in_=ng_in), writes=['ngt'], dma='c2')
        S.op('sp', lambda e: e.dma_start(out=ropeA[:], in_=ropeA_in), writes=['ropeA'], dma='c3')
        S.op('sp', lambda e: e.dma_start(out=Rf[:, 0:NM * 128], in_=masks_in), writes=['mstage'], dma='c4')
        S.op('pool', lambda e: e.tensor_copy(out=idb[:], in_=idf[:]), reads=['idf'], writes=['idb'])
        S.op('pool', lambda e: e.tensor_scalar(out=maskb[:], in0=Rf[:, 0:NM * 128], scalar1=-1.0, scalar2=30000.0,
                                               op0=ALU.add, op1=ALU.mult), reads=['mstage'], writes=['maskb'])
        S.op('pool', lambda e: e.memset(mh[:, 0:8], -0.5), writes=['mh'])
        S.op('pool', lambda e: e.memset(mh[:, 8:16], EPS), writes=['mhe'])
        S.barrier()
        stopped = (stop == 'init')

        wcount = [0]

        def load_weight_piece(src_ap, nchunk, dst_ap, dst_key):
            slot = wcount[0] % 2
            wcount[0] += 1
            S.op('sp', lambda e: e.dma_start(out=wst_ap(slot, nchunk), in_=src_ap.rearrange("(c p) n -> p c n", p=128)),
                 writes=[('wst', slot)], dma=('wst', slot))
            S.op('pool', lambda e: e.tensor_copy(out=dst_ap, in_=wst_ap(slot, nchunk)),
                 reads=[('wst', slot)], writes=[dst_key])

        sidx = [0]

        def rms_stats(src_ap, n_groups, glen, src_keys, src_is_psum, inv_n, use_ln=False):
            b = sidx[0] % 2
            sidx[0] += 1
            n = n_groups * glen
            sq_ap = sq[:, b * 768: b * 768 + n]
            ss_ap = ssA[:, b * 8: b * 8 + n_groups]
            ms_ap = msA[:, b * 8: b * 8 + n_groups]
            rs_ap = rsA[:, b * 8: b * 8 + n_groups]
            S.op('act', lambda e: e.activation(out=sq_ap, in_=src_ap, func=ACT.Square),
                 reads=src_keys, writes=[('sq', b), ('sqb', b)])
            S.op('dve', lambda e: e.tensor_reduce(out=ss_ap, in_=sq_ap.rearrange("p (a b) -> p a b", a=n_groups),
                                                  axis=AX.X, op=ALU.add),
                 reads=[('sq', b), ('sqb', b)], writes=[('ss', b)])
            if use_ln:
                S.op('act', lambda e: e.activation(out=ms_ap, in_=ss_ap, func=ACT.Ln, scale=inv_n, bias=mh[:, 8:9]),
                     reads=[('ss', b), 'mhe'], writes=[('ms', b)])
                S.op('act', lambda e: e.activation(out=rs_ap, in_=ms_ap, func=ACT.Exp, scale=-0.5),
                     reads=[('ms', b)], writes=[('rs', b)])
                return rs_ap, ('rs', b)
            S.op('dve', lambda e: e.tensor_scalar(out=ms_ap, in0=ss_ap, scalar1=inv_n, scalar2=EPS,
                                                  op0=ALU.mult, op1=ALU.add),
                 reads=[('ss', b)], writes=[('ms', b)])
            S.op('pool', lambda e: e.tensor_tensor(out=rs_ap, in0=ms_ap, in1=mh[:, 0:n_groups], op=ALU.pow),
                 reads=[('ms', b), 'mh'], writes=[('rs', b)])
            return rs_ap, ('rs', b)

        ntt_pending = [None]

        def ntt_flush():
            if ntt_pending[0] is not None:
                ntt_pending[0]()
                ntt_pending[0] = None

        def norm_transpose_tile(src_dram_ap, t, gcol, dst_fn, dst_keys, tcount):
            b = tcount % 2
            xt = xt_ap(b)
            S.op('sp', lambda e: e.dma_start(out=xt, in_=src_dram_ap), writes=[('xt', b)], dma=('xt', b))
            ss_ap = ssA[:, b * 8: b * 8 + 1]
            ms_ap = msA[:, b * 8: b * 8 + 1]
            rs_ap = rsA[:, b * 8: b * 8 + 1]
            S.op('act', lambda e: e.activation(out=sq[:, 0:1024], in_=xt, func=ACT.Square, accum_out=ss_ap),
                 reads=[('xt', b)], writes=[('sq', 0), ('sq', 1), ('sqb', 0), ('sqb', 1), ('ss', b)])
            S.op('act', lambda e: e.activation(out=ms_ap, in_=ss_ap, func=ACT.Ln, scale=1.0 / D, bias=mh[:, 8:9]),
                 reads=[('ss', b), 'mhe'], writes=[('ms', b)])
            S.op('act', lambda e: e.activation(out=rs_ap, in_=ms_ap, func=ACT.Exp, scale=-0.5),
                 reads=[('ms', b)], writes=[('rs', b)])
            S.op('dve', lambda e: e.tensor_scalar(out=xt, in0=xt, scalar1=rs_ap, scalar2=None, op0=ALU.mult),
                 reads=[('xt', b), ('rs', b)], writes=[('xt', b)])

            def tail():
                for half in range(2):
                    bank = (0 if b == 0 else 2) + half
                    for j in range(4):
                        c = half * 4 + j
                        S.op('pe', lambda e, c=c, j=j, bank=bank: e.transpose(out=PS[bank][:, j * 128:(j + 1) * 128],
                                                                              in_=xt[:, c * 128:(c + 1) * 128], identity=idf[:]),
                             reads=[('xt', b), 'idf'], writes=[pk(bank)])
                    gsrc = ngt[:, gcol + half * 4: gcol + half * 4 + 4].unsqueeze(2).to_broadcast([128, 4, 128])
                    S.op('dve', lambda e, half=half, bank=bank, gsrc=gsrc: e.tensor_tensor(
                        out=dst_fn(half), in0=PS[bank][:, 0:512].rearrange("p (c k) -> p c k", c=4), in1=gsrc, op=ALU.mult),
                        reads=[pk(bank), 'ngt'], writes=dst_keys(half))
            ntt_flush()
            ntt_pending[0] = tail

        def proj_head(pbase, nqk, nv, gain_ops, t, rope, ropekey, vslot0, ucount):
            banks = sorted(set((pbase * 512 + i * 128) // 512 for i in range(nqk + nv)))
            bkeys = [pk(bk) for bk in banks]
            c0 = pbase * 512
            src = PSALL[:, c0:c0 + nqk * 128]
            for vi in range(nv):
                S.op('act', lambda e, vi=vi: e.activation(out=v_ap(vslot0 + vi, t, 128),
                                                          in_=PSALL[:, c0 + (nqk + vi) * 128: c0 + (nqk + vi + 1) * 128], func=ACT.Copy),
                     reads=bkeys, writes=[('v', vslot0 + vi, t)])
            rs_ap, rs_key = rms_stats(src, nqk, 128, bkeys, True, 1.0 / E, use_ln=True)
            b = ucount % 2
            tf = tmpf[:, b * 768: b * 768 + nqk * 128]
            tf3 = tf.rearrange("p (a b) -> p a b", a=nqk)
            TK = [('tmpf', b)]
            S.op('dve', lambda e: e.tensor_tensor(out=tf3, in0=src.rearrange("p (a b) -> p a b", a=nqk),
                                                  in1=rs_ap.unsqueeze(2).to_broadcast([128, nqk, 128]), op=ALU.mult),
                 reads=bkeys + [rs_key], writes=TK)
            return dict(nqk=nqk, gain_ops=gain_ops, t=t, rope=rope, ropekey=ropekey, ucount=ucount, b=b, tf3=tf3, TK=TK)

        def proj_tail(cx):
            nqk, gain_ops, t, rope, ropekey, ucount, b, tf3, TK = (cx[k_] for k_ in
                                                                   ('nqk', 'gain_ops', 't', 'rope', 'ropekey', 'ucount', 'b', 'tf3', 'TK'))
            b3 = ucount % 3
            qn_ap = qn[:, b3 * 768: b3 * 768 + nqk * 128] if b3 < 2 else Ub[:, 3584:3584 + nqk * 128]
            qn3 = qn_ap.rearrange("p (a b) -> p a b", a=nqk)
            QA, QB, QC = ('qn3', b3, 'a'), ('qn3', b3, 'b'), ('qn3', b3, 'c')
            cx['qn_ap'] = qn_ap
            cx['QK3'] = [QA, QB, QC]
            for (b0, nb_, gsrc) in gain_ops:
                S.op('dve', lambda e, b0=b0, nb_=nb_, gsrc=gsrc: e.tensor_tensor(out=tf3[:, b0:b0 + nb_, :], in0=tf3[:, b0:b0 + nb_, :],
                                                                            in1=gsrc, op=ALU.mult),
                     reads=TK + ['gains'], writes=TK)
            cos_ap, sin_ap, Rr = rope
            cosb = cos_ap.unsqueeze(1).to_broadcast([128, nqk, Rr])
            sinb = sin_ap.unsqueeze(1).to_broadcast([128, nqk, Rr])
            x1 = tf3[:, :, 0:Rr]
            x2 = tf3[:, :, Rr:2 * Rr]
            nr = nqk * Rr
            ra3 = sq[:, b * 768: b * 768 + nr].rearrange("p (a b) -> p a b", a=nqk)
            rb3 = sq[:, b * 768 + 384: b * 768 + 384 + nr].rearrange("p (a b) -> p a b", a=nqk)
            ra4 = U[:, b * 768: b * 768 + nr].rearrange("p (a b) -> p a b", a=nqk)
            rb4 = U[:, b * 768 + 384: b * 768 + 384 + nr].rearrange("p (a b) -> p a b", a=nqk)
            SQK = ('sq', b)
            S.op('dve', lambda e: e.tensor_tensor(out=ra3, in0=x1, in1=cosb, op=ALU.mult),
                 reads=TK + [ropekey], writes=[SQK])
            S.op('pool', lambda e: e.tensor_tensor(out=rb3, in0=x2, in1=sinb, op=ALU.mult),
                 reads=TK + [ropekey], writes=[('sqb', b)])
            S.op('dve', lambda e: e.tensor_tensor(out=ra4, in0=x2, in1=cosb, op=ALU.mult),
                 reads=TK + [ropekey], writes=[('ra4', b)])
            S.op('pool', lambda e: e.tensor_tensor(out=rb4, in0=x1, in1=sinb, op=ALU.mult),
                 reads=TK + [ropekey], writes=[('rb4', b)])
            S.op('dve', lambda e: e.tensor_tensor(out=qn3[:, :, 0:Rr], in0=ra3, in1=rb3, op=ALU.subtract),
                 reads=[SQK, ('sqb', b)], writes=[QA])
            S.op('pool', lambda e: e.tensor_tensor(out=qn3[:, :, Rr:2 * Rr], in0=ra4, in1=rb4, op=ALU.add),
                 reads=[('ra4', b), ('rb4', b)], writes=[QB])
            if 2 * Rr < 128:
                S.op('act', lambda e: e.activation(out=qn3[:, :, 2 * Rr:128], in_=tf3[:, :, 2 * Rr:128], func=ACT.Copy),
                     reads=TK, writes=[QC])

        def proj_tail_b(cx):
            nqk, t, ucount, qn_ap = cx['nqk'], cx['t'], cx['ucount'], cx['qn_ap']
            QA, QB, QC = cx['QK3']
            tb = TP[ucount % 2]
            for i in range(nqk):
                S.op('pe', lambda e, i=i: e.transpose(out=PSB[tb][:, i * 128:(i + 1) * 128],
                                                     in_=qn_ap[:, i * 128:(i + 1) * 128], identity=idb[:]),
                     reads=[QA, QB, QC, 'idb'], writes=[pk(tb)])
            qkdst = R[:, 0:nqk * 4096].rearrange("p (s k) -> p s k", s=nqk)[:, :, t * 128:(t + 1) * 128]
            S.op('act', lambda e: e.activation(out=qkdst, in_=PSB[tb][:, 0:nqk * 128].rearrange("p (s k) -> p s k", s=nqk),
                                               func=ACT.Copy),
                 reads=[pk(tb)], writes=[('qk', sl, t) for sl in range(nqk)])

        acount = [0]
        ptcount = [0]
        sccount = [0]
        fcount = [0]
        pend = []
        att_cfg = dict(banks=[2, 3], depth=1)

        def att_flush_one():
            q = pend.pop(0)
            pb, ab = q['pb'], q['ab']
            acc = PS[ab][:, 0:129]
            for (i, v_ap_, v_keys, first, last) in q['pv']:
                S.op('pe', lambda e, i=i, v_ap_=v_ap_, first=first, last=last, pb=pb, acc=acc: e.matmul(
                    acc, lhsT=pt[:, pb * 512 + i * 128: pb * 512 + (i + 1) * 128], rhs=v_ap_, start=first, stop=last),
                    reads=[('pt', pb)] + list(v_keys), writes=[pk(ab)])
            if q['fin'] is not None:
                q['fin'](ab)

        def att_flush_all():
            while pend:
                att_flush_one()

        def attention_job(blocks, fin):
            oa_ = att_cfg.get('oa', OA)
            ab = oa_[acount[0] % len(oa_)]
            acount[0] += 1
            nb = len(blocks)
            bi = 0
            while bi < nb:
                quad = blocks[bi:bi + 4]
                n = len(quad)
                banks = att_cfg['banks']
                sb_ = banks[sccount[0] % len(banks)]
                sccount[0] += 1
                pb = ptcount[0] % 3
                ptcount[0] += 1
                for i, blk in enumerate(quad):
                    q_ap, q_keys, k_ap, k_keys, v_ap_, v_keys, mi = blk
                    S.op('pe', lambda e, i=i, k_ap=k_ap, q_ap=q_ap, sb_=sb_, mi=mi: e.matmul(
                        PS[sb_][:, i * 128:(i + 1) * 128], lhsT=k_ap, rhs=q_ap, start=True, stop=(mi is None)),
                         reads=list(q_keys) + list(k_keys), writes=[pk(sb_)])
                    if mi is not None:
                        S.op('pe', lambda e, i=i, sb_=sb_, mi=mi: e.matmul(
                            PS[sb_][:, i * 128:(i + 1) * 128], lhsT=idb[:], rhs=maskb[:, mi * 128:(mi + 1) * 128], start=False, stop=True),
                             reads=['idb', 'maskb'], writes=[pk(sb_)])
                p_ap = pt[:, pb * 512: pb * 512 + n * 128]
                S.op('act', lambda e, p_ap=p_ap, sb_=sb_, n=n: e.activation(out=p_ap, in_=PS[sb_][:, 0:n * 128], func=ACT.Exp, scale=SCALE),
                     reads=[pk(sb_)], writes=[('pt', pb)])
                pv = []
                for i, blk in enumerate(quad):
                    gi = bi + i
                    pv.append((i, blk[4], blk[5], gi == 0, gi == nb - 1))
                bi += n
                pend.append(dict(pb=pb, ab=ab, pv=pv, fin=(fin if bi >= nb else None)))
                while len(pend) > att_cfg['depth']:
                    att_flush_one()

        for l in range(n_layers):
            if stopped:
                break
            if stop_spec is not None and ':' in stop_spec:
                stop = stop_spec.split(':')[1] if int(stop_spec.split(':')[0]) == l else None
            x_src = x_in if l == 0 else x1_scr
            x_dst = x1_scr if (l == 0 and n_layers > 1) else out
            w_in = w_in_a if l == 0 else w_in_b
            g_q_mem = 8 + l
            g_k_mem = 10 + l

            Wkv = R[:, 0:8192].rearrange("p (c n) -> p c n", c=8)
            memT = R[:, 8192:8192 + 2048]
            for piece in range(8):
                load_weight_piece(w_kv[l, :, piece * 128:(piece + 1) * 128], 8,
                                  Wkv[:, :, piece * 128:(piece + 1) * 128], ('wkv', piece))
            for mt in range(2):
                norm_transpose_tile(mem_in[mt * 128:(mt + 1) * 128, :], mt, 16 + l * 8,
                                    lambda half, mt=mt: memT[:, half * 1024:(half + 1) * 1024].rearrange(
                                        "p (c k) -> p c k", c=4)[:, :, mt * 128:(mt + 1) * 128],
                                    lambda half, mt=mt: [('memT', mt, half * 4 + j_) for j_ in range(4)], mt)
            ntt_flush()
            S.op('pool', lambda e: e.memset(Vm[:], 1.0), writes=['Vm'])
            for mt in range(2):
                for half in range(2):
                    bank = PJ[half]
                    for c in range(8):
                        S.op('pe', lambda e, c=c, half=half, bank=bank, mt=mt: e.matmul(
                            PS[bank][:, 0:512], lhsT=memT[:, c * 256 + mt * 128: c * 256 + (mt + 1) * 128],
                            rhs=Wkv[:, c, half * 512:(half + 1) * 512], start=(c == 0), stop=(c == 7)),
                            reads=[('memT', mt, c)] + [('wkv', half * 4 + j) for j in range(4)], writes=[pk(bank)])
                bank = PJ[0]
                src = PS[bank][:, 0:512]
                rs_ap, rs_key = rms_stats(src, 4, 128, [pk(bank)], True, 1.0 / E)
                tf3 = tmpf[:, 0:512].rearrange("p (a b) -> p a b", a=4)
                S.op('dve', lambda e, src=src, rs_ap=rs_ap, tf3=tf3: e.tensor_tensor(
                    out=tf3, in0=src.rearrange("p (a b) -> p a b", a=4),
                    in1=rs_ap.unsqueeze(2).to_broadcast([128, 4, 128]), op=ALU.mult),
                    reads=[pk(bank), rs_key], writes=[('tmpf', 0, 0)])
                qn3 = qn[:, 0:512].rearrange("p (a b) -> p a b", a=4)
                gsrc = gains[:, g_k_mem * 128:(g_k_mem + 1) * 128].unsqueeze(1).to_broadcast([128, 4, 128])
                S.op('pool', lambda e, qn3=qn3, tf3=tf3, gsrc=gsrc: e.tensor_tensor(out=qn3, in0=tf3, in1=gsrc, op=ALU.mult),
                     reads=[('tmpf', 0, 0), 'gains'], writes=[('qn', 0, 'a')])
                tb = TP[0]
                for hd in range(4):
                    S.op('pe', lambda e, hd=hd: e.transpose(out=PSB[tb][:, hd * 128:(hd + 1) * 128],
                                                           in_=qn[:, hd * 128:(hd + 1) * 128], identity=idb[:]),
                         reads=[('qn', 0, 'a'), 'idb'], writes=[pk(tb)])
                S.op('act', lambda e, mt=mt: e.activation(
                    out=KmT[:].rearrange("p (h k) -> p h k", h=4)[:, :, mt * 128:(mt + 1) * 128],
                    in_=PSB[tb][:, 0:512].rearrange("p (h k) -> p h k", h=4), func=ACT.Copy),
                    reads=[pk(tb)], writes=['KmT'])
                bank = PJ[1]
                S.op('dve', lambda e, mt=mt, bank=bank: e.tensor_copy(
                    out=Vm[:, mt * 520:(mt + 1) * 520].rearrange("p (h k) -> p h k", h=4)[:, :, 0:128],
                    in_=PS[bank][:, 0:512].rearrange("p (h k) -> p h k", h=4)),
                    reads=[pk(bank)], writes=['Vm'])
            if stop == 'M':
                S.barrier()
                break

            for t in range(NT):
                norm_transpose_tile(x_src[t * 128:(t + 1) * 128, :], t, l * 8,
                                    lambda half, t=t: hT[:, half * 16384:(half + 1) * 16384].rearrange(
                                        "p (c k) -> p c k", c=4)[:, :, t * 128:(t + 1) * 128],
                                    lambda half, t=t: [('hT', t, half * 4 + j_) for j_ in range(4)], t)
            ntt_flush()
            S.barrier()
            if stop == 'p1':
                break

            S.op('pool', lambda e: e.memset(R[:, V_OFF:R_N], 1.0), writes=['Vall'])
            S.barrier()
            if l == 0:
                iters = [dict(cols=[((s_ * 3 + g) * 8 + h) * 128 for s_ in range(3) for g in range(3)], nqk=6, nv=3, hd=h)
                         for h in range(8)]
                gain_ops = [(0, 6, gains[:, 0:768].rearrange("p (a b) -> p a b", a=6))]
            else:
                iters = [dict(cols=[(kvh * 4 + j) * 128 for j in range(4)] + [1024 + kvh * 128, 1280 + kvh * 128], nqk=5, nv=1, hd=kvh)
                         for kvh in range(2)]
                gain_ops = [(0, 4, gains[:, 6 * 128:7 * 128].unsqueeze(1).to_broadcast([128, 4, 128])),
                            (4, 1, gains[:, 7 * 128:8 * 128].unsqueeze(1))]
            att_cfg['banks'] = [0, 1, 2, 3]
            att_cfg['depth'] = 2

            def emit_iter_load(it):
                ncols = (it['nqk'] + it['nv']) * 128
                Wv_ = Wbuf[:, 0:8 * ncols].rearrange("p (c n) -> p c n", c=8)
                for bi_, col in enumerate(it['cols']):
                    load_weight_piece(w_in[:, col:col + 128], 8, Wv_[:, :, bi_ * 128:(bi_ + 1) * 128], ('wbuf', bi_))

            emit_iter_load(iters[0])
            tilecount = 0
            for it_i, it in enumerate(iters):
                nqk, nv = it['nqk'], it['nv']
                nblk = nqk + nv
                ncols = nblk * 128
                Wv = Wbuf[:, 0:8 * ncols].rearrange("p (c n) -> p c n", c=8)
                prev_cx = None
                prev2_cx = None
                for t in range(NT):
                    pbase = 0 if tilecount % 2 == 0 else 3
                    c0 = 0
                    while c0 < ncols:
                        n_ = min(512, ncols - c0)
                        bank = pbase + c0 // 512
                        wkeys = [('wbuf', j) for j in range(c0 // 128, (c0 + n_) // 128)]
                        for c in range(8):
                            S.op('pe', lambda e, c=c, t=t, c0=c0, n_=n_, pbase=pbase, Wv=Wv: e.matmul(
                                PSALL[:, pbase * 512 + c0: pbase * 512 + c0 + n_], lhsT=hT_ap(c, t), rhs=Wv[:, c, c0:c0 + n_],
                                start=(c == 0), stop=(c == 7)),
                                reads=[('hT', t, c)] + wkeys, writes=[pk(bank)])
                        c0 += n_
                    if l == 0:
                        rope = (ropeA[:, t * 16:(t + 1) * 16], ropeA[:, 512 + t * 16: 512 + (t + 1) * 16], 16)
                        ropekey = 'ropeA'
                    else:
                        rbuf = tilecount % 2
                        S.op('sp', lambda e, rbuf=rbuf, t=t: e.dma_start(
                            out=ropeBt[:, rbuf * 128:(rbuf + 1) * 128].rearrange("p (a b) -> p a b", a=2),
                            in_=ropeB_in.rearrange("p (a t b) -> p a t b", a=2, t=32)[:, :, t, :]),
                            writes=[('ropeB', rbuf)], dma=('ropeB', rbuf))
                        rope = (ropeBt[:, rbuf * 128: rbuf * 128 + 64], ropeBt[:, rbuf * 128 + 64: rbuf * 128 + 128], 64)
                        ropekey = ('ropeB', rbuf)
                    if prev_cx is not None:
                        proj_tail(prev_cx)
                    if prev2_cx is not None:
                        proj_tail_b(prev2_cx)
                    prev2_cx = prev_cx
                    prev_cx = proj_head(pbase, nqk, nv, gain_ops, t, rope, ropekey, 0, tilecount)
                    tilecount += 1
                proj_tail(prev_cx)
                proj_tail_b(prev2_cx)
                proj_tail_b(prev_cx)
                if it_i + 1 < len(iters):
                    emit_iter_load(iters[it_i + 1])
                if stop == 'p2proj':
                    continue
                hd = it['hd']
                jobs = []
                if l == 0:
                    for T in range(NT):
                        blocks = []
                        for g, (window, dil) in enumerate(A_GROUPS):
                            for dl in range(-9, 10):
                                if (g, dl) not in MASK_TABLE:
                                    continue
                                kb = T + dl
                                if kb < 0 or kb >= NT:
                                    continue
                                blocks.append((qk_ap(g, T * 128), [('qk', g, T)], qk_ap(3 + g, kb * 128), [('qk', 3 + g, kb)],
                                               v_ap(g, kb), [('v', g, kb)], MASK_TABLE[(g, dl)]))
                        jobs.append((blocks, T, hd))
                else:
                    for j in range(4):
                        for T in range(NT):
                            blocks = []
                            for kb in range(NT):
                                blocks.append((qk_ap(j, T * 128), [('qk', j, T)], qk_ap(4, kb * 128), [('qk', 4, kb)],
                                               v_ap(0, kb), [('v', 0, kb)], None))
                            jobs.append((blocks, T, hd * 4 + j))
                for blocks, T, hcol in jobs:
                    def fin(ab, T=T, hcol=hcol):
                        ob_i = fcount[0] % 4
                        fcount[0] += 1
                        rc_ap = rc[:, ob_i:ob_i + 1]
                        ob_ap = ob[:, ob_i * 128:(ob_i + 1) * 128]
                        S.op('dve', lambda e: e.reciprocal(out=rc_ap, in_=PS[ab][:, 128:129]),
                             reads=[pk(ab)], writes=[('rc', ob_i)])
                        S.op('dve', lambda e: e.tensor_scalar(out=ob_ap, in0=PS[ab][:, 0:128], scalar1=rc_ap, scalar2=None,
                                                              op0=ALU.mult),
                             reads=[pk(ab), ('rc', ob_i)], writes=[('ob', ob_i)])
                        S.op('sp', lambda e: e.dma_start(out=o_scr[T * 128:(T + 1) * 128, hcol * 128:(hcol + 1) * 128], in_=ob_ap),
                             reads=[('ob', ob_i)], writes=[('oscr', T)], dma=('ob', ob_i))
                    attention_job(blocks, fin)
                att_flush_all()
            att_cfg['banks'] = [2, 3]
            att_cfg['depth'] = 1
            S.barrier()
            if stop in ('p2', 'p2proj'):
                break

            Wg = R[:, 0:16384].rearrange("p (c n) -> p c n", c=8)
            Wo = R[:, 16384:28672].rearrange("p (c n) -> p c n", c=12)
            gcol0 = 9216 if l == 0 else 1536
            for piece in range(16):
                load_weight_piece(w_in[:, gcol0 + piece * 128: gcol0 + (piece + 1) * 128], 8,
                                  Wg[:, :, piece * 128:(piece + 1) * 128], ('wg', piece))
            for piece in range(8):
                load_weight_piece(w_out[l, 0:1024, piece * 128:(piece + 1) * 128], 8,
                                  Wo[:, 0:8, piece * 128:(piece + 1) * 128], ('wo', 0, piece))
                load_weight_piece(w_out[l, 1024:1536, piece * 128:(piece + 1) * 128], 4,
                                  Wo[:, 8:12, piece * 128:(piece + 1) * 128], ('wo', 1, piece))
            wg_keys = [('wg', p_) for p_ in range(16)]
            wo_keys = [('wo', a_, p_) for a_ in range(2) for p_ in range(8)]
            KmT3 = KmT[:].rearrange("p (h k) -> p h k", h=4)
            att_cfg['banks'] = [2, 0]
            att_cfg['depth'] = 1
            att_cfg['oa'] = [4]

            def gate_chain(t, j, gb, o_j, o_keys):
                th_ap = th[:, j * 512:(j + 1) * 512]
                S.op('act', lambda e: e.activation(out=th_ap, in_=PS[gb][:, 0:512], func=ACT.Tanh, scale=0.5),
                     reads=[pk(gb)], writes=[('th', j)])
                S.op('dve', lambda e: e.scalar_tensor_tensor(out=th_ap, in0=th_ap, scalar=1.0, in1=PS[gb][:, 0:512],
                                                             op0=ALU.add, op1=ALU.mult),
                     reads=[pk(gb), ('th', j)], writes=[('th', j)])
                yb_ap = yb[:, j * 512:(j + 1) * 512]
                S.op('dve', lambda e: e.scalar_tensor_tensor(out=yb_ap, in0=th_ap, scalar=0.5, in1=o_j, op0=ALU.mult, op1=ALU.mult),
                     reads=[('th', j)] + o_keys, writes=[('yb', j)])

            def gate_proj(t, j, gb):
                for c in range(8):
                    S.op('pe', lambda e, c=c: e.matmul(PS[gb][:, 0:512], lhsT=hT_ap(c, t), rhs=Wg[:, c, 512 + j * 512: 1024 + j * 512],
                                                       start=(c == 0), stop=(c == 7)),
                         reads=[('hT', t, c)] + wg_keys[4 + 4 * j: 8 + 4 * j], writes=[pk(gb)])

            def y_transposes(t, j, tpb, eng):
                b = t % 2
                yb_ap = yb[:, j * 512:(j + 1) * 512]
                yTb = yT[:, b * 1536:(b + 1) * 1536]
                for i in range(4):
                    S.op('pe', lambda e, i=i: e.transpose(out=PSB[tpb][:, i * 128:(i + 1) * 128],
                                                         in_=yb_ap[:, i * 128:(i + 1) * 128], identity=idb[:]),
                         reads=[('yb', j), 'idb'], writes=[pk(tpb)])
                if eng == 'act':
                    S.op('act', lambda e: e.activation(out=yTb[:, j * 512:(j + 1) * 512], in_=PSB[tpb][:, 0:512], func=ACT.Copy),
                         reads=[pk(tpb)], writes=[('yT', b, j)])
                else:
                    S.op('dve', lambda e: e.tensor_copy(out=yTb[:, j * 512:(j + 1) * 512], in_=PSB[tpb][:, 0:512]),
                         reads=[pk(tpb)], writes=[('yT', b, j)])

            def stage_A(t):
                b = t % 2
                xt = xt_ap(b)
                ot_ap = ot[:, b * 1024:(b + 1) * 1024]
                S.op('sp', lambda e, x_src=x_src: e.dma_start(out=xt, in_=x_src[t * 128:(t + 1) * 128, :]),
                     writes=[('xt', b)], dma=('xt', b))
                S.op('sp', lambda e: e.dma_start(out=ot_ap, in_=o_scr[t * 128:(t + 1) * 128, :]),
                     reads=[('oscr', t)], writes=[('ot', b)], dma=('ot', b))
                bank = 0
                for c in range(8):
                    S.op('pe', lambda e, c=c: e.matmul(PS[bank][:, 0:512], lhsT=hT_ap(c, t), rhs=Wg[:, c, 0:512],
                                                       start=(c == 0), stop=(c == 7)),
                         reads=[('hT', t, c)] + wg_keys[0:4], writes=[pk(bank)])
                gate_proj(t, 0, 1)
                gate_proj(t, 1, 3)
                src = PS[bank][:, 0:512]
                rs_ap, rs_key = rms_stats(src, 4, 128, [pk(bank)], True, 1.0 / E)
                tf3 = tmpf[:, b * 768: b * 768 + 512].rearrange("p (a b) -> p a b", a=4)
                S.op('dve', lambda e: e.tensor_tensor(out=tf3, in0=src.rearrange("p (a b) -> p a b", a=4),
                                                      in1=rs_ap.unsqueeze(2).to_broadcast([128, 4, 128]), op=ALU.mult),
                     reads=[pk(bank), rs_key], writes=[('tmpf', b, 0)])
                qn3 = qn[:, b * 768: b * 768 + 512].rearrange("p (a b) -> p a b", a=4)
                gsrc = gains[:, g_q_mem * 128:(g_q_mem + 1) * 128].unsqueeze(1).to_broadcast([128, 4, 128])
                S.op('pool', lambda e: e.tensor_tensor(out=qn3, in0=tf3, in1=gsrc, op=ALU.mult),
                     reads=[('tmpf', b, 0), 'gains'], writes=[('qn', b, 'a')])
                gate_chain(t, 0, 1, ot_ap[:, 0:512], [('ot', b)])
                gate_chain(t, 1, 3, ot_ap[:, 512:1024], [('ot', b)])

            def stage_B(t):
                b = t % 2
                qn_ap = qn[:, b * 768: b * 768 + 512]
                tb = 6
                for hd in range(4):
                    S.op('pe', lambda e, hd=hd: e.transpose(out=PSB[tb][:, hd * 128:(hd + 1) * 128],
                                                           in_=qn_ap[:, hd * 128:(hd + 1) * 128], identity=idb[:]),
                         reads=[('qn', b, 'a'), 'idb'], writes=[pk(tb)])
                S.op('act', lambda e: e.activation(out=qmT[:], in_=PSB[tb][:, 0:512], func=ACT.Copy),
                     reads=[pk(tb)], writes=['qmT'])
                y_transposes(t, 0, 7, 'dve')
                gate_proj(t, 2, 5)
                th2 = th[:, 1024:1536]
                S.op('act', lambda e: e.activation(out=th2, in_=PS[5][:, 0:512], func=ACT.Tanh, scale=0.5),
                     reads=[pk(5)], writes=[('th', 2)])
                S.op('dve', lambda e: e.scalar_tensor_tensor(out=th2, in0=th2, scalar=1.0, in1=PS[5][:, 0:512],
                                                             op0=ALU.add, op1=ALU.mult),
                     reads=[pk(5), ('th', 2)], writes=[('th', 2)])
                y_transposes(t, 1, 6, 'act')
                for hd in range(4):
                    blocks = []
                    for kb in range(2):
                        blocks.append((qmT[:, hd * 128:(hd + 1) * 128], ['qmT'], KmT3[:, hd, kb * 128:(kb + 1) * 128], ['KmT'],
                                       Vm[:, (kb * 4 + hd) * 130:(kb * 4 + hd) * 130 + 129], ['Vm'], None))

                    def fin(ab, hd=hd):
                        ob_i = fcount[0] % 4
                        fcount[0] += 1
                        rc_ap = rc[:, 4 + ob_i:5 + ob_i]
                        S.op('dve', lambda e: e.reciprocal(out=rc_ap, in_=PS[ab][:, 128:129]),
                             reads=[pk(ab)], writes=[('rcm', ob_i)])
                        S.op('act', lambda e: e.activation(out=om[:, hd * 128:(hd + 1) * 128], in_=PS[ab][:, 0:128], func=ACT.Copy,
                                                           scale=rc_ap),
                             reads=[pk(ab), ('rcm', ob_i)], writes=[('om', hd)])
                    attention_job(blocks, fin)
                att_flush_all()
                yb2 = yb[:, 1024:1536]
                S.op('dve', lambda e: e.scalar_tensor_tensor(out=yb2, in0=th2, scalar=0.5, in1=om[:, 0:512], op0=ALU.mult, op1=ALU.mult),
                     reads=[('th', 2)] + [('om', hd) for hd in range(4)], writes=[('yb', 2)])
                y_transposes(t, 2, 7, 'act')

            def stage_C(t):
                b = t % 2
                xt = xt_ap(b)
                x1t = x1t_ap(b)
                yTb = yT[:, b * 1536:(b + 1) * 1536]
                for nb_ in range(2):
                    obk = [2, 0][nb_]
                    for cc in range(12):
                        S.op('pe', lambda e, cc=cc, nb_=nb_, obk=obk: e.matmul(
                            PS[obk][:, 0:512], lhsT=yTb[:, cc * 128:(cc + 1) * 128], rhs=Wo[:, cc, nb_ * 512:(nb_ + 1) * 512],
                            start=(cc == 0), stop=(cc == 11)),
                            reads=[('yT', b, cc // 4)] + wo_keys, writes=[pk(obk)])
                    S.op('dve', lambda e, nb_=nb_, obk=obk: e.tensor_tensor(
                        out=x1t[:, nb_ * 512:(nb_ + 1) * 512], in0=PS[obk][:, 0:512], in1=xt[:, nb_ * 512:(nb_ + 1) * 512], op=ALU.add),
                        reads=[pk(obk), ('xt', b)], writes=[('x1t', b, nb_)])
                S.op('sp', lambda e, x_dst=x_dst: e.dma_start(out=x_dst[t * 128:(t + 1) * 128, :], in_=x1t),
                     reads=[('x1t', b, 0), ('x1t', b, 1)], writes=[('xdst', t)], dma=('x1t', b))

            def _bind(f, **kw):
                return f

            if stop != 'p3w':
                stage_A(0)
                for t in range(NT):
                    stage_B(t)
                    if t + 1 < NT:
                        stage_A(t + 1)
                    stage_C(t)
            att_cfg['oa'] = OA
            S.barrier()
        S.emit(nc)
    return nc, S


def _host_consts(norm_g, mem_norm_g, mem_qn_g, mem_kn_g, qn_a, kn_a, qn_b, kn_b):
    ng = np.zeros((128, 32), np.float32)
    for l in range(2):
        ng[:, l * 8:(l + 1) * 8] = norm_g[l].reshape(8, 128).T
        ng[:, 16 + l * 8:16 + (l + 1) * 8] = mem_norm_g[l].reshape(8, 128).T
    rows = [qn_a[0, 0], qn_a[0, 1], qn_a[0, 2], kn_a[0, 0], kn_a[0, 1], kn_a[0, 2], qn_b[0], kn_b[0],
            mem_qn_g[0], mem_qn_g[1], mem_kn_g[0], mem_kn_g[1]]
    gains = np.ascontiguousarray(np.broadcast_to(np.concatenate(rows)[None, :], (128, 12 * 128))).astype(np.float32)
    pos = np.arange(S_TOK, dtype=np.float32)
    invA = (np.float32(500000.0) ** (-np.arange(0, 32, 2, dtype=np.float32) / np.float32(32))).astype(np.float32)
    angA = pos[:, None] * invA[None, :]
    row = np.repeat(np.arange(64, dtype=np.float32), 64)
    col = np.tile(np.arange(64, dtype=np.float32), 64)
    invB = (np.float32(10000.0) ** (-np.arange(0, 64, 2, dtype=np.float32) / np.float32(64))).astype(np.float32)
    angB = np.concatenate([row[:, None] * invB[None, :], col[:, None] * invB[None, :]], axis=-1)

    def lay(a):
        return a.reshape(32, 128, -1).transpose(1, 0, 2)
    ropeA = np.stack([lay(np.cos(angA)), lay(np.sin(angA))], axis=1).reshape(128, -1).astype(np.float32)
    ropeB = np.stack([lay(np.cos(angB)), lay(np.sin(angB))], axis=1).reshape(128, -1).astype(np.float32)
    return dict(ng=ng, gains=gains, ropeA=np.ascontiguousarray(ropeA), ropeB=np.ascontiguousarray(ropeB),
                ident=np.eye(128, dtype=np.float32), masks=np.ascontiguousarray(MASKS_NP.reshape(128, -1)))


_NC_CACHE = {}


def kernel(x, mem, norm_g, mem_norm_g, w_mem_kv, mem_qn_g, mem_kn_g, w_out,
           w_in_a, qn_a, kn_a, w_in_b, qn_b, kn_b):
    f = lambda a: np.ascontiguousarray(np.asarray(a, dtype=np.float32))
    x = f(x)
    mem = f(mem)
    consts = _host_consts(f(norm_g), f(mem_norm_g), f(mem_qn_g), f(mem_kn_g), f(qn_a), f(kn_a), f(qn_b), f(kn_b))
    shared = dict(w_in_a=f(w_in_a)[0], w_in_b=f(w_in_b)[0], w_out=f(w_out), w_mem_kv=f(w_mem_kv), **consts)
    if 'nc' not in _NC_CACHE:
        _NC_CACHE['nc'] = build()[0]
    nc = _NC_CACHE['nc']
    n = x.shape[0]
    in_maps = [dict(x=x[b], mem=mem[b], **shared) for b in range(n)]
    res = run_bass_kernel_spmd(nc, in_maps, core_ids=list(range(n)))
    return np.stack([np.asarray(r["y_out"], dtype=np.float32) for r in res.results], axis=0)
```

```python
import numpy as np
import concourse.bass as bass
import concourse.mybir as mybir
from concourse.bass_utils import run_bass_kernel_spmd

F32 = mybir.dt.float32
BF16 = mybir.dt.bfloat16
ALU = mybir.AluOpType
ACT = mybir.ActivationFunctionType
AX = mybir.AxisListType

ENGS = ['pe', 'act', 'dve', 'pool', 'sp']

S_TOK = 4096
NT = 32
D = 1024
E = 128
EPS = 1e-6
SCALE = float(E) ** -0.5
A_GROUPS = ((128, 1), (512, 4), (2048, 16))
IN_A = 11264
IN_B = 3584

PROJ_N = 512


class Sched:
    def __init__(self):
        self.ops = []
        self.state = {}
        self.last_on = {e: None for e in ENGS}
        self.dma_last = {}

    def op(self, eng, fn, reads=(), writes=(), dma=None):
        i = len(self.ops)
        deps = set()
        for k in reads:
            excl = isinstance(k, tuple) and k[0] == 'ps'
            st = self.state.setdefault(k, [None, []])
            if st[0] is not None:
                deps.add(st[0])
            if excl:
                for r in st[1]:
                    if self.ops[r]['eng'] != eng:
                        deps.add(r)
        for k in writes:
            st = self.state.setdefault(k, [None, []])
            if st[0] is not None:
                deps.add(st[0])
            for r in st[1]:
                deps.add(r)
        for k in reads:
            self.state[k][1].append(i)
        for k in writes:
            self.state[k] = [i, []]
        if dma is not None:
            p = self.dma_last.get(dma)
            if p is not None:
                deps.add(p)
            self.dma_last[dma] = i
        deps.discard(i)
        if eng == 'pe':
            deps = {d for d in deps if self.ops[d]['eng'] != 'pe'}
        best = {}
        keep = set()
        for d in deps:
            od = self.ops[d]
            if od['dma'] is not None:
                keep.add(d)
            elif d > best.get(od['eng'], -1):
                best[od['eng']] = d
        deps = keep | set(best.values())
        self.ops.append(dict(eng=eng, fn=fn, deps=deps, dma=dma))
        if fn is not None:
            self.last_on[eng] = i
        return i

    def barrier(self):
        lasts = [v for v in self.last_on.values() if v is not None]
        lasts += list(self.dma_last.values())
        lasts = set(lasts)
        for e in ENGS:
            deps = {d for d in lasts if not (self.ops[d]['eng'] == e and self.ops[d]['dma'] is None)}
            self.ops.append(dict(eng=e, fn=None, deps=deps, dma=None))
        self.state = {}

    def emit(self, nc):
        ops = self.ops
        needed = set()
        for o in ops:
            needed |= o['deps']
        dma_keys = []
        seen = set()
        for o in ops:
            if o['dma'] is not None and o['dma'] not in seen:
                seen.add(o['dma'])
                dma_keys.append(o['dma'])
        sem_ctx = []
        sems = {}
        for n_i, name in enumerate(['pe', 'act', 'dve', 'pool'] + [('dma', k) for k in dma_keys]):
            cm = nc.semaphore("s%d" % n_i)
            sems[name] = cm.__enter__()
            sem_ctx.append(cm)
        cnt = {}
        for i, o in enumerate(ops):
            if o['dma'] is not None:
                key = ('dma', o['dma'])
                cnt[key] = cnt.get(key, 0) + 16
                o['sem'], o['val'], o['inc'] = key, cnt[key], 16
            elif i in needed:
                assert o['fn'] is not None
                key = o['eng']
                cnt[key] = cnt.get(key, 0) + 1
                o['sem'], o['val'], o['inc'] = key, cnt[key], 1
            else:
                o['sem'] = None
        self.maxval = dict(cnt)

        def run(eng_name, e):
            waited = {}
            for o in ops:
                if o['eng'] != eng_name:
                    continue
                need = {}
                for d in o['deps']:
                    od = ops[d]
                    s, v = od['sem'], od['val']
                    if v > need.get(s, 0):
                        need[s] = v
                for s, v in need.items():
                    if waited.get(s, 0) < v:
                        e.wait_ge(sems[s], v)
                        waited[s] = v
                if o['fn'] is None:
                    continue
                ins = o['fn'](e)
                if o['sem'] is not None:
                    ins.then_inc(sems[o['sem']], o['inc'])

        with nc.Block() as block:
            @block.tensor
            def _(e):
                run('pe', e)

            @block.scalar
            def _(e):
                run('act', e)

            @block.vector
            def _(e):
                run('dve', e)

            @block.gpsimd
            def _(e):
                run('pool', e)

            @block.sync
            def _(e):
                run('sp', e)
        for cm in reversed(sem_ctx):
            cm.__exit__(None, None, None)


def _mask_tables():
    masks = []
    index = {}
    table = {}
    kk = np.arange(128)[:, None]
    qq = np.arange(128)[None, :]
    for g, (window, dil) in enumerate(A_GROUPS):
        hw = window // 2
        dmax = hw // 128 + (1 if hw % 128 else 0)
        dmax = max(dmax, 1)
        for dl in range(-dmax, dmax + 1):
            diff = 128 * dl + kk - qq
            m = ((diff % dil) == 0) & (np.abs(diff) <= hw)
            if not m.any():
                continue
            key = m.tobytes()
            if key not in index:
                index[key] = len(masks)
                masks.append(m.astype(np.float32))
            table[(g, dl)] = index[key]
    return np.stack(masks, axis=1), table


MASKS_NP, MASK_TABLE = _mask_tables()
NM = MASKS_NP.shape[1]


def build(n_layers=2, debug=False, stop=None):
    stop_spec = stop
    nc = bass.Bass("TRN2", target_bir_lowering=False)
    dk = "ExternalOutput" if debug else "Internal"
    x_in = nc.dram_tensor("x", [S_TOK, D], F32, kind="ExternalInput").ap()
    mem_in = nc.dram_tensor("mem", [256, D], F32, kind="ExternalInput").ap()
    w_in_a = nc.dram_tensor("w_in_a", [D, IN_A], F32, kind="ExternalInput").ap()
    w_in_b = nc.dram_tensor("w_in_b", [D, IN_B], F32, kind="ExternalInput").ap()
    w_out = nc.dram_tensor("w_out", [2, 1536, D], F32, kind="ExternalInput").ap()
    w_kv = nc.dram_tensor("w_mem_kv", [2, D, 1024], F32, kind="ExternalInput").ap()
    ng_in = nc.dram_tensor("ng", [128, 32], F32, kind="ExternalInput").ap()
    gains_in = nc.dram_tensor("gains", [128, 12 * 128], F32, kind="ExternalInput").ap()
    ropeA_in = nc.dram_tensor("ropeA", [128, 2 * 32 * 16], F32, kind="ExternalInput").ap()
    ropeB_in = nc.dram_tensor("ropeB", [128, 2 * 32 * 64], F32, kind="ExternalInput").ap()
    ident_in = nc.dram_tensor("ident", [128, 128], F32, kind="ExternalInput").ap()
    masks_in = nc.dram_tensor("masks", [128, NM * 128], F32, kind="ExternalInput").ap()
    out = nc.dram_tensor("y_out", [S_TOK, D], F32, kind="ExternalOutput").ap()
    x1_scr = nc.dram_tensor("x1_scr", [S_TOK, D], F32, kind=dk).ap()
    o_scr = nc.dram_tensor("o_scr", [S_TOK, D], BF16, kind=dk).ap()

    S = Sched()
    R_N = 6 * 4096 + 3 * 32 * 130
    import contextlib
    with contextlib.ExitStack() as es:
        def sb(name, shape, dt):
            return es.enter_context(nc.sbuf_tensor(name, shape, dt))

        hT = sb("hT", [128, 8 * 4096], BF16)
        R = sb("R", [128, R_N], BF16)
        Wbuf = sb("Wbuf", [128, 8 * 1152], BF16)
        Wbf = Wbuf.bitcast(F32)
        Wst = sb("Wst", [128, 2 * 8 * 128], F32)
        ropeA = sb("ropeA_t", [128, 2 * 32 * 16], F32)
        ropeBt = ropeA[:, 0:256]
        gains = sb("gains_t", [128, 12 * 128], F32)
        maskb = sb("maskb", [128, NM * 128], BF16)
        idf = sb("idf", [128, 128], F32)
        idb = sb("idb", [128, 128], BF16)
        ngt = sb("ngt", [128, 32], F32)
        mh = sb("mh", [128, 16], F32)
        KmT = sb("KmT", [128, 4 * 256], BF16)
        Vm = sb("Vm", [128, 2 * 4 * 130], BF16)
        ssA = sb("ssA", [128, 2 * 8], F32)
        msA = sb("msA", [128, 2 * 8], F32)
        rsA = sb("rsA", [128, 2 * 8], F32)
        sq = sb("sq", [128, 2 * 768], F32)
        tmpf = sb("tmpf", [128, 2 * 768], F32)
        U = sb("U", [128, 2560], F32)
        Ub = U.bitcast(BF16)
        qn = sb("qn", [128, 2 * 768], BF16)
        pt = sb("pt", [128, 3 * 512], BF16)
        rc = sb("rc", [128, 8], F32)
        ob = Ub[:, 3072:3584]
        th = Wbf[:, 0:1536]
        ot = Ub[:, 3072:5120]
        Wbb = Wbuf
        om = Wbb[:, 3072:3584]
        yb = Wbb[:, 3584:5120]
        yT = Ub[:, 0:3072]
        qmT = Wbb[:, 5120:5632]

        PSALL = es.enter_context(nc.psum_tensor("psall", [128, 4096], F32))
        PSBALL = PSALL.bitcast(BF16)
        PS = [PSALL[:, i * 512:(i + 1) * 512] for i in range(8)]
        PSB = [PSBALL[:, i * 1024:(i + 1) * 1024] for i in range(8)]
        PJ = [0, 1]
        SC = [2, 3]
        OA = [4, 5]
        TP = [6, 7]

        def pk(i):
            return ('ps', i)

        Rf = R.bitcast(F32)
        QK_OFF = 0
        V_OFF = 6 * 4096
        XT_OFF_F = 28672 // 2
        def xt_ap(b):
            return Rf[:, XT_OFF_F + b * 1024: XT_OFF_F + (b + 1) * 1024]

        def x1t_ap(b):
            return Rf[:, XT_OFF_F + 2048 + b * 1024: XT_OFF_F + 2048 + (b + 1) * 1024]

        def qk_ap(slot, t0, n=128):
            o = QK_OFF + slot * 4096 + t0
            return R[:, o:o + n]

        def v_ap(slot, kb, n=129):
            o = V_OFF + (slot * 32 + kb) * 130
            return R[:, o:o + n]

        def hT_ap(c, t):
            o = c * 4096 + t * 128
            return hT[:, o:o + 128]

        def wst_ap(slot, nchunk=8):
            return Wst[:, slot * 1024: slot * 1024 + nchunk * 128].rearrange("p (c n) -> p c n", n=128)

        S.op('sp', lambda e: e.dma_start(out=idf[:], in_=ident_in), writes=['idf'], dma='c0')
        S.op('sp', lambda e: e.dma_start(out=gains[:], in_=gains_in), writes=['gains'], dma='c1')
        S.op('sp', lambda e: e.dma_start(out=ngt[:], in_=ng_in), writes=['ngt'], dma='c2')
        S.op('sp', lambda e: e.dma_start(out=ropeA[:], in_=ropeA_in), writes=['ropeA'], dma='c3')
        S.op('sp', lambda e: e.dma_start(out=Rf[:, 0:NM * 128], in_=masks_in), writes=['mstage'], dma='c4')
        S.op('pool', lambda e: e.tensor_copy(out=idb[:], in_=idf[:]), reads=['idf'], writes=['idb'])
        S.op('pool', lambda e: e.tensor_scalar(out=maskb[:], in0=Rf[:, 0:NM * 128], scalar1=-1.0, scalar2=30000.0,
                                               op0=ALU.add, op1=ALU.mult), reads=['mstage'], writes=['maskb'])
        S.op('pool', lambda e: e.memset(mh[:, 0:8], -0.5), writes=['mh'])
        S.op('pool', lambda e: e.memset(mh[:, 8:16], EPS), writes=['mhe'])
        S.barrier()
        stopped = (stop == 'init')

        wcount = [0]

        def load_weight_piece(src_ap, nchunk, dst_ap, dst_key):
            slot = wcount[0] % 2
            wcount[0] += 1
            S.op('sp', lambda e: e.dma_start(out=wst_ap(slot, nchunk), in_=src_ap.rearrange("(c p) n -> p c n", p=128)),
                 writes=[('wst', slot)], dma=('wst', slot))
            S.op('pool', lambda e: e.tensor_copy(out=dst_ap, in_=wst_ap(slot, nchunk)),
                 reads=[('wst', slot)], writes=[dst_key])

        sidx = [0]

        def rms_stats(src_ap, n_groups, glen, src_keys, src_is_psum, inv_n, use_ln=False):
            b = sidx[0] % 2
            sidx[0] += 1
            n = n_groups * glen
            sq_ap = sq[:, b * 768: b * 768 + n]
            ss_ap = ssA[:, b * 8: b * 8 + n_groups]
            ms_ap = msA[:, b * 8: b * 8 + n_groups]
            rs_ap = rsA[:, b * 8: b * 8 + n_groups]
            S.op('act', lambda e: e.activation(out=sq_ap, in_=src_ap, func=ACT.Square),
                 reads=src_keys, writes=[('sq', b), ('sqb', b)])
            S.op('dve', lambda e: e.tensor_reduce(out=ss_ap, in_=sq_ap.rearrange("p (a b) -> p a b", a=n_groups),
                                                  axis=AX.X, op=ALU.add),
                 reads=[('sq', b), ('sqb', b)], writes=[('ss', b)])
            if use_ln:
                S.op('act', lambda e: e.activation(out=ms_ap, in_=ss_ap, func=ACT.Ln, scale=inv_n, bias=mh[:, 8:9]),
                     reads=[('ss', b), 'mhe'], writes=[('ms', b)])
                S.op('act', lambda e: e.activation(out=rs_ap, in_=ms_ap, func=ACT.Exp, scale=-0.5),
                     reads=[('ms', b)], writes=[('rs', b)])
                return rs_ap, ('rs', b)
            S.op('dve', lambda e: e.tensor_scalar(out=ms_ap, in0=ss_ap, scalar1=inv_n, scalar2=EPS,
                                                  op0=ALU.mult, op1=ALU.add),
                 reads=[('ss', b)], writes=[('ms', b)])
            S.op('pool', lambda e: e.tensor_tensor(out=rs_ap, in0=ms_ap, in1=mh[:, 0:n_groups], op=ALU.pow),
                 reads=[('ms', b), 'mh'], writes=[('rs', b)])
            return rs_ap, ('rs', b)

        ntt_pending = [None]

        def ntt_flush():
            if ntt_pending[0] is not None:
                ntt_pending[0]()
                ntt_pending[0] = None

        def norm_transpose_tile(src_dram_ap, t, gcol, dst_fn, dst_keys, tcount):
            b = tcount % 2
            xt = xt_ap(b)
            S.op('sp', lambda e: e.dma_start(out=xt, in_=src_dram_ap), writes=[('xt', b)], dma=('xt', b))
            ss_ap = ssA[:, b * 8: b * 8 + 1]
            ms_ap = msA[:, b * 8: b * 8 + 1]
            rs_ap = rsA[:, b * 8: b * 8 + 1]
            S.op('act', lambda e: e.activation(out=sq[:, 0:1024], in_=xt, func=ACT.Square, accum_out=ss_ap),
                 reads=[('xt', b)], writes=[('sq', 0), ('sq', 1), ('sqb', 0), ('sqb', 1), ('ss', b)])
            S.op('act', lambda e: e.activation(out=ms_ap, in_=ss_ap, func=ACT.Ln, scale=1.0 / D, bias=mh[:, 8:9]),
                 reads=[('ss', b), 'mhe'], writes=[('ms', b)])
            S.op('act', lambda e: e.activation(out=rs_ap, in_=ms_ap, func=ACT.Exp, scale=-0.5),
                 reads=[('ms', b)], writes=[('rs', b)])
            S.op('dve', lambda e: e.tensor_scalar(out=xt, in0=xt, scalar1=rs_ap, scalar2=None, op0=ALU.mult),
                 reads=[('xt', b), ('rs', b)], writes=[('xt', b)])

            def tail():
                for half in range(2):
                    bank = (0 if b == 0 else 2) + half
                    for j in range(4):
                        c = half * 4 + j
                        S.op('pe', lambda e, c=c, j=j, bank=bank: e.transpose(out=PS[bank][:, j * 128:(j + 1) * 128],
                                                                              in_=xt[:, c * 128:(c + 1) * 128], identity=idf[:]),
                             reads=[('xt', b), 'idf'], writes=[pk(bank)])
                    gsrc = ngt[:, gcol + half * 4: gcol + half * 4 + 4].unsqueeze(2).to_broadcast([128, 4, 128])
                    S.op('dve', lambda e, half=half, bank=bank, gsrc=gsrc: e.tensor_tensor(
                        out=dst_fn(half), in0=PS[bank][:, 0:512].rearrange("p (c k) -> p c k", c=4), in1=gsrc, op=ALU.mult),
                        reads=[pk(bank), 'ngt'], writes=dst_keys(half))
            ntt_flush()
            ntt_pending[0] = tail

        def proj_head(pbase, nqk, nv, gain_ops, t, rope, ropekey, vslot0, ucount):
            banks = sorted(set((pbase * 512 + i * 128) // 512 for i in range(nqk + nv)))
            bkeys = [pk(bk) for bk in banks]
            c0 = pbase * 512
            src = PSALL[:, c0:c0 + nqk * 128]
            for vi in range(nv):
                S.op('act', lambda e, vi=vi: e.activation(out=v_ap(vslot0 + vi, t, 128),
                                                          in_=PSALL[:, c0 + (nqk + vi) * 128: c0 + (nqk + vi + 1) * 128], func=ACT.Copy),
                     reads=bkeys, writes=[('v', vslot0 + vi, t)])
            sb_i = sidx[0] % 2
            sidx[0] += 1
            n = nqk * 128
            sq_ap = sq[:, sb_i * 768: sb_i * 768 + n]
            ss_ap = ssA[:, sb_i * 8: sb_i * 8 + nqk]
            S.op('act', lambda e: e.activation(out=sq_ap, in_=src, func=ACT.Square),
                 reads=bkeys, writes=[('sq', sb_i), ('sqb', sb_i)])
            S.op('dve', lambda e: e.tensor_reduce(out=ss_ap, in_=sq_ap.rearrange("p (a b) -> p a b", a=nqk),
                                                  axis=AX.X, op=ALU.add),
                 reads=[('sq', sb_i), ('sqb', sb_i)], writes=[('ss', sb_i)])
            b = ucount % 2
            tf = tmpf[:, b * 768: b * 768 + nqk * 128]
            tf3 = tf.rearrange("p (a b) -> p a b", a=nqk)
            TK = [('tmpf', b)]
            return dict(nqk=nqk, gain_ops=gain_ops, t=t, rope=rope, ropekey=ropekey, ucount=ucount, b=b, tf3=tf3, TK=TK,
                        sb_i=sb_i, src=src, bkeys=bkeys)

        def proj_head2(cx):
            nqk, sb_i, src, bkeys, tf3, TK = (cx[k_] for k_ in ('nqk', 'sb_i', 'src', 'bkeys', 'tf3', 'TK'))
            ss_ap = ssA[:, sb_i * 8: sb_i * 8 + nqk]
            ms_ap = msA[:, sb_i * 8: sb_i * 8 + nqk]
            rs_ap = rsA[:, sb_i * 8: sb_i * 8 + nqk]
            S.op('act', lambda e: e.activation(out=ms_ap, in_=ss_ap, func=ACT.Ln, scale=1.0 / E, bias=mh[:, 8:9]),
                 reads=[('ss', sb_i), 'mhe'], writes=[('ms', sb_i)])
            S.op('act', lambda e: e.activation(out=rs_ap, in_=ms_ap, func=ACT.Exp, scale=-0.5),
                 reads=[('ms', sb_i)], writes=[('rs', sb_i)])
            S.op('dve', lambda e: e.tensor_tensor(out=tf3, in0=src.rearrange("p (a b) -> p a b", a=nqk),
                                                  in1=rs_ap.unsqueeze(2).to_broadcast([128, nqk, 128]), op=ALU.mult),
                 reads=bkeys + [('rs', sb_i)], writes=TK)

        def proj_tail(cx):
            nqk, gain_ops, t, rope, ropekey, ucount, b, tf3, TK = (cx[k_] for k_ in
                                                                   ('nqk', 'gain_ops', 't', 'rope', 'ropekey', 'ucount', 'b', 'tf3', 'TK'))
            b3 = ucount % 3
            qn_ap = qn[:, b3 * 768: b3 * 768 + nqk * 128] if b3 < 2 else Ub[:, 3584:3584 + nqk * 128]
            qn3 = qn_ap.rearrange("p (a b) -> p a b", a=nqk)
            QA, QB, QC = ('qn3', b3, 'a'), ('qn3', b3, 'b'), ('qn3', b3, 'c')
            cx['qn_ap'] = qn_ap
            cx['QK3'] = [QA, QB, QC]
            for (b0, nb_, gsrc) in gain_ops:
                S.op('dve', lambda e, b0=b0, nb_=nb_, gsrc=gsrc: e.tensor_tensor(out=tf3[:, b0:b0 + nb_, :], in0=tf3[:, b0:b0 + nb_, :],
                                                                            in1=gsrc, op=ALU.mult),
                     reads=TK + ['gains'], writes=TK)
            cos_ap, sin_ap, Rr = rope
            cosb = cos_ap.unsqueeze(1).to_broadcast([128, nqk, Rr])
            sinb = sin_ap.unsqueeze(1).to_broadcast([128, nqk, Rr])
            x1 = tf3[:, :, 0:Rr]
            x2 = tf3[:, :, Rr:2 * Rr]
            nr = nqk * Rr
            ra3 = sq[:, b * 768: b * 768 + nr].rearrange("p (a b) -> p a b", a=nqk)
            rb3 = sq[:, b * 768 + 384: b * 768 + 384 + nr].rearrange("p (a b) -> p a b", a=nqk)
            ra4 = U[:, b * 768: b * 768 + nr].rearrange("p (a b) -> p a b", a=nqk)
            rb4 = U[:, b * 768 + 384: b * 768 + 384 + nr].rearrange("p (a b) -> p a b", a=nqk)
            SQK = ('sq', b)
            S.op('dve', lambda e: e.tensor_tensor(out=ra3, in0=x1, in1=cosb, op=ALU.mult),
                 reads=TK + [ropekey], writes=[SQK])
            S.op('pool', lambda e: e.tensor_tensor(out=rb3, in0=x2, in1=sinb, op=ALU.mult),
                 reads=TK + [ropekey], writes=[('sqb', b)])
            S.op('dve', lambda e: e.tensor_tensor(out=ra4, in0=x2, in1=cosb, op=ALU.mult),
                 reads=TK + [ropekey], writes=[('ra4', b)])
            S.op('pool', lambda e: e.tensor_tensor(out=rb4, in0=x1, in1=sinb, op=ALU.mult),
                 reads=TK + [ropekey], writes=[('rb4', b)])
            S.op('dve', lambda e: e.tensor_tensor(out=qn3[:, :, 0:Rr], in0=ra3, in1=rb3, op=ALU.subtract),
                 reads=[SQK, ('sqb', b)], writes=[QA])
            S.op('pool', lambda e: e.tensor_tensor(out=qn3[:, :, Rr:2 * Rr], in0=ra4, in1=rb4, op=ALU.add),
                 reads=[('ra4', b), ('rb4', b)], writes=[QB])
            if 2 * Rr < 128:
                S.op('pool', lambda e: e.tensor_copy(out=qn3[:, :, 2 * Rr:128], in_=tf3[:, :, 2 * Rr:128]),
                     reads=TK, writes=[QC])

        def proj_tail_b(cx):
            nqk, t, ucount, qn_ap = cx['nqk'], cx['t'], cx['ucount'], cx['qn_ap']
            QA, QB, QC = cx['QK3']
            tb = TP[ucount % 2]
            for i in range(nqk):
                S.op('pe', lambda e, i=i: e.transpose(out=PSB[tb][:, i * 128:(i + 1) * 128],
                                                     in_=qn_ap[:, i * 128:(i + 1) * 128], identity=idb[:]),
                     reads=[QA, QB, QC, 'idb'], writes=[pk(tb)])
            qkdst = R[:, 0:nqk * 4096].rearrange("p (s k) -> p s k", s=nqk)[:, :, t * 128:(t + 1) * 128]
            S.op('act', lambda e: e.activation(out=qkdst, in_=PSB[tb][:, 0:nqk * 128].rearrange("p (s k) -> p s k", s=nqk),
                                               func=ACT.Copy),
                 reads=[pk(tb)], writes=[('qk', sl, t) for sl in range(nqk)])

        acount = [0]
        ptcount = [0]
        sccount = [0]
        fcount = [0]
        pend = []
        att_cfg = dict(banks=[2, 3], depth=1)

        def att_flush_one():
            q = pend.pop(0)
            pb, ab = q['pb'], q['ab']
            acc = PS[ab][:, 0:129]
            for (i, v_ap_, v_keys, first, last) in q['pv']:
                S.op('pe', lambda e, i=i, v_ap_=v_ap_, first=first, last=last, pb=pb, acc=acc: e.matmul(
                    acc, lhsT=pt[:, pb * 512 + i * 128: pb * 512 + (i + 1) * 128], rhs=v_ap_, start=first, stop=last),
                    reads=[('pt', pb)] + list(v_keys), writes=[pk(ab)])
            if q['fin'] is not None:
                q['fin'](ab)

        def att_flush_all():
            while pend:
                att_flush_one()

        def attention_job(blocks, fin):
            oa_ = att_cfg.get('oa', OA)
            ab = oa_[acount[0] % len(oa_)]
            acount[0] += 1
            nb = len(blocks)
            bi = 0
            while bi < nb:
                quad = blocks[bi:bi + 4]
                n = len(quad)
                banks = att_cfg['banks']
                sb_ = banks[sccount[0] % len(banks)]
                sccount[0] += 1
                pb = ptcount[0] % 3
                ptcount[0] += 1
                for i, blk in enumerate(quad):
                    q_ap, q_keys, k_ap, k_keys, v_ap_, v_keys, mi = blk
                    S.op('pe', lambda e, i=i, k_ap=k_ap, q_ap=q_ap, sb_=sb_, mi=mi: e.matmul(
                        PS[sb_][:, i * 128:(i + 1) * 128], lhsT=k_ap, rhs=q_ap, start=True, stop=(mi is None)),
                         reads=list(q_keys) + list(k_keys), writes=[pk(sb_)])
                    if mi is not None:
                        S.op('pe', lambda e, i=i, sb_=sb_, mi=mi: e.matmul(
                            PS[sb_][:, i * 128:(i + 1) * 128], lhsT=idb[:], rhs=maskb[:, mi * 128:(mi + 1) * 128], start=False, stop=True),
                             reads=['idb', 'maskb'], writes=[pk(sb_)])
                p_ap = pt[:, pb * 512: pb * 512 + n * 128]
                S.op('act', lambda e, p_ap=p_ap, sb_=sb_, n=n: e.activation(out=p_ap, in_=PS[sb_][:, 0:n * 128], func=ACT.Exp, scale=SCALE),
                     reads=[pk(sb_)], writes=[('pt', pb)])
                pv = []
                for i, blk in enumerate(quad):
                    gi = bi + i
                    pv.append((i, blk[4], blk[5], gi == 0, gi == nb - 1))
                bi += n
                pend.append(dict(pb=pb, ab=ab, pv=pv, fin=(fin if bi >= nb else None)))
                while len(pend) > att_cfg['depth']:
                    att_flush_one()

        for l in range(n_layers):
            if stopped:
                break
            if stop_spec is not None and ':' in stop_spec:
                stop = stop_spec.split(':')[1] if int(stop_spec.split(':')[0]) == l else None
            x_src = x_in if l == 0 else x1_scr
            x_dst = x1_scr if (l == 0 and n_layers > 1) else out
            w_in = w_in_a if l == 0 else w_in_b
            g_q_mem = 8 + l
            g_k_mem = 10 + l

            Wkv = R[:, 0:8192].rearrange("p (c n) -> p c n", c=8)
            memT = R[:, 8192:8192 + 2048]
            for piece in range(8):
                load_weight_piece(w_kv[l, :, piece * 128:(piece + 1) * 128], 8,
                                  Wkv[:, :, piece * 128:(piece + 1) * 128], ('wkv', piece))
            for mt in range(2):
                norm_transpose_tile(mem_in[mt * 128:(mt + 1) * 128, :], mt, 16 + l * 8,
                                    lambda half, mt=mt: memT[:, half * 1024:(half + 1) * 1024].rearrange(
                                        "p (c k) -> p c k", c=4)[:, :, mt * 128:(mt + 1) * 128],
                                    lambda half, mt=mt: [('memT', mt, half * 4 + j_) for j_ in range(4)], mt)
            ntt_flush()
            S.op('pool', lambda e: e.memset(Vm[:], 1.0), writes=['Vm'])
            for mt in range(2):
                for half in range(2):
                    bank = PJ[half]
                    for c in range(8):
                        S.op('pe', lambda e, c=c, half=half, bank=bank, mt=mt: e.matmul(
                            PS[bank][:, 0:512], lhsT=memT[:, c * 256 + mt * 128: c * 256 + (mt + 1) * 128],
                            rhs=Wkv[:, c, half * 512:(half + 1) * 512], start=(c == 0), stop=(c == 7)),
                            reads=[('memT', mt, c)] + [('wkv', half * 4 + j) for j in range(4)], writes=[pk(bank)])
                bank = PJ[0]
                src = PS[bank][:, 0:512]
                rs_ap, rs_key = rms_stats(src, 4, 128, [pk(bank)], True, 1.0 / E)
                tf3 = tmpf[:, 0:512].rearrange("p (a b) -> p a b", a=4)
                S.op('dve', lambda e, src=src, rs_ap=rs_ap, tf3=tf3: e.tensor_tensor(
                    out=tf3, in0=src.rearrange("p (a b) -> p a b", a=4),
                    in1=rs_ap.unsqueeze(2).to_broadcast([128, 4, 128]), op=ALU.mult),
                    reads=[pk(bank), rs_key], writes=[('tmpf', 0, 0)])
                qn3 = qn[:, 0:512].rearrange("p (a b) -> p a b", a=4)
                gsrc = gains[:, g_k_mem * 128:(g_k_mem + 1) * 128].unsqueeze(1).to_broadcast([128, 4, 128])
                S.op('pool', lambda e, qn3=qn3, tf3=tf3, gsrc=gsrc: e.tensor_tensor(out=qn3, in0=tf3, in1=gsrc, op=ALU.mult),
                     reads=[('tmpf', 0, 0), 'gains'], writes=[('qn', 0, 'a')])
                tb = TP[0]
                for hd in range(4):
                    S.op('pe', lambda e, hd=hd: e.transpose(out=PSB[tb][:, hd * 128:(hd + 1) * 128],
                                                           in_=qn[:, hd * 128:(hd + 1) * 128], identity=idb[:]),
                         reads=[('qn', 0, 'a'), 'idb'], writes=[pk(tb)])
                S.op('act', lambda e, mt=mt: e.activation(
                    out=KmT[:].rearrange("p (h k) -> p h k", h=4)[:, :, mt * 128:(mt + 1) * 128],
                    in_=PSB[tb][:, 0:512].rearrange("p (h k) -> p h k", h=4), func=ACT.Copy),
                    reads=[pk(tb)], writes=['KmT'])
                bank = PJ[1]
                S.op('dve', lambda e, mt=mt, bank=bank: e.tensor_copy(
                    out=Vm[:, mt * 520:(mt + 1) * 520].rearrange("p (h k) -> p h k", h=4)[:, :, 0:128],
                    in_=PS[bank][:, 0:512].rearrange("p (h k) -> p h k", h=4)),
                    reads=[pk(bank)], writes=['Vm'])
            if stop == 'M':
                S.barrier()
                break

            for t in range(NT):
                norm_transpose_tile(x_src[t * 128:(t + 1) * 128, :], t, l * 8,
                                    lambda half, t=t: hT[:, half * 16384:(half + 1) * 16384].rearrange(
                                        "p (c k) -> p c k", c=4)[:, :, t * 128:(t + 1) * 128],
                                    lambda half, t=t: [('hT', t, half * 4 + j_) for j_ in range(4)], t)
            ntt_flush()
            S.barrier()
            if stop == 'p1':
                break

            S.op('pool', lambda e: e.memset(R[:, V_OFF:R_N], 1.0), writes=['Vall'])
            S.barrier()
            if l == 0:
                iters = [dict(cols=[((s_ * 3 + g) * 8 + h) * 128 for s_ in range(3) for g in range(3)], nqk=6, nv=3, hd=h)
                         for h in range(8)]
                gain_ops = [(0, 6, gains[:, 0:768].rearrange("p (a b) -> p a b", a=6))]
            else:
                iters = [dict(cols=[(kvh * 4 + j) * 128 for j in range(4)] + [1024 + kvh * 128, 1280 + kvh * 128], nqk=5, nv=1, hd=kvh)
                         for kvh in range(2)]
                gain_ops = [(0, 4, gains[:, 6 * 128:7 * 128].unsqueeze(1).to_broadcast([128, 4, 128])),
                            (4, 1, gains[:, 7 * 128:8 * 128].unsqueeze(1))]
            att_cfg['banks'] = [0, 1, 2, 3]
            att_cfg['depth'] = 2

            def emit_iter_load(it):
                ncols = (it['nqk'] + it['nv']) * 128
                Wv_ = Wbuf[:, 0:8 * ncols].rearrange("p (c n) -> p c n", c=8)
                for bi_, col in enumerate(it['cols']):
                    load_weight_piece(w_in[:, col:col + 128], 8, Wv_[:, :, bi_ * 128:(bi_ + 1) * 128], ('wbuf', bi_))

            emit_iter_load(iters[0])
            tilecount = 0
            for it_i, it in enumerate(iters):
                nqk, nv = it['nqk'], it['nv']
                nblk = nqk + nv
                ncols = nblk * 128
                Wv = Wbuf[:, 0:8 * ncols].rearrange("p (c n) -> p c n", c=8)
                prev_cx = None
                prev2_cx = None
                for t in range(NT):
                    pbase = 0 if tilecount % 2 == 0 else 3
                    c0 = 0
                    while c0 < ncols:
                        n_ = min(PROJ_N, 512 - (c0 % 512), ncols - c0)
                        bank = pbase + c0 // 512
                        wkeys = [('wbuf', j) for j in range(c0 // 128, (c0 + n_) // 128)]
                        for c in range(8):
                            S.op('pe', lambda e, c=c, t=t, c0=c0, n_=n_, pbase=pbase, Wv=Wv: e.matmul(
                                PSALL[:, pbase * 512 + c0: pbase * 512 + c0 + n_], lhsT=hT_ap(c, t), rhs=Wv[:, c, c0:c0 + n_],
                                start=(c == 0), stop=(c == 7)),
                                reads=[('hT', t, c)] + wkeys, writes=[pk(bank)])
                        c0 += n_
                    if l == 0:
                        rope = (ropeA[:, t * 16:(t + 1) * 16], ropeA[:, 512 + t * 16: 512 + (t + 1) * 16], 16)
                        ropekey = 'ropeA'
                    else:
                        rbuf = tilecount % 2
                        S.op('sp', lambda e, rbuf=rbuf, t=t: e.dma_start(
                            out=ropeBt[:, rbuf * 128:(rbuf + 1) * 128].rearrange("p (a b) -> p a b", a=2),
                            in_=ropeB_in.rearrange("p (a t b) -> p a t b", a=2, t=32)[:, :, t, :]),
                            writes=[('ropeB', rbuf)], dma=('ropeB', rbuf))
                        rope = (ropeBt[:, rbuf * 128: rbuf * 128 + 64], ropeBt[:, rbuf * 128 + 64: rbuf * 128 + 128], 64)
                        ropekey = ('ropeB', rbuf)
                    if prev_cx is not None:
                        proj_tail(prev_cx)
                    cx_ = proj_head(pbase, nqk, nv, gain_ops, t, rope, ropekey, 0, tilecount)
                    if prev2_cx is not None:
                        proj_tail_b(prev2_cx)
                    proj_head2(cx_)
                    prev2_cx = prev_cx
                    prev_cx = cx_
                    tilecount += 1
                proj_tail(prev_cx)
                proj_tail_b(prev2_cx)
                proj_tail_b(prev_cx)
                if it_i + 1 < len(iters):
                    emit_iter_load(iters[it_i + 1])
                if stop == 'p2proj':
                    continue
                hd = it['hd']
                jobs = []
                if l == 0:
                    for T in range(NT):
                        blocks = []
                        for g, (window, dil) in enumerate(A_GROUPS):
                            for dl in range(-9, 10):
                                if (g, dl) not in MASK_TABLE:
                                    continue
                                kb = T + dl
                                if kb < 0 or kb >= NT:
                                    continue
                                blocks.append((qk_ap(g, T * 128), [('qk', g, T)], qk_ap(3 + g, kb * 128), [('qk', 3 + g, kb)],
                                               v_ap(g, kb), [('v', g, kb)], MASK_TABLE[(g, dl)]))
                        jobs.append((blocks, T, hd))
                else:
                    for j in range(4):
                        for T in range(NT):
                            blocks = []
                            for kb in range(NT):
                                blocks.append((qk_ap(j, T * 128), [('qk', j, T)], qk_ap(4, kb * 128), [('qk', 4, kb)],
                                               v_ap(0, kb), [('v', 0, kb)], None))
                            jobs.append((blocks, T, hd * 4 + j))
                for blocks, T, hcol in jobs:
                    def fin(ab, T=T, hcol=hcol):
                        ob_i = fcount[0] % 4
                        fcount[0] += 1
                        rc_ap = rc[:, ob_i:ob_i + 1]
                        ob_ap = ob[:, ob_i * 128:(ob_i + 1) * 128]
                        S.op('dve', lambda e: e.reciprocal(out=rc_ap, in_=PS[ab][:, 128:129]),
                             reads=[pk(ab)], writes=[('rc', ob_i)])
                        S.op('dve', lambda e: e.tensor_scalar(out=ob_ap, in0=PS[ab][:, 0:128], scalar1=rc_ap, scalar2=None,
                                                              op0=ALU.mult),
                             reads=[pk(ab), ('rc', ob_i)], writes=[('ob', ob_i)])
                        S.op('sp', lambda e: e.dma_start(out=o_scr[T * 128:(T + 1) * 128, hcol * 128:(hcol + 1) * 128], in_=ob_ap),
                             reads=[('ob', ob_i)], writes=[('oscr', T)], dma=('ob', ob_i))
                    attention_job(blocks, fin)
                att_flush_all()
            att_cfg['banks'] = [2, 3]
            att_cfg['depth'] = 1
            S.barrier()
            if stop in ('p2', 'p2proj'):
                break

            Wg = R[:, 0:16384].rearrange("p (c n) -> p c n", c=8)
            Wo = R[:, 16384:28672].rearrange("p (c n) -> p c n", c=12)
            gcol0 = 9216 if l == 0 else 1536
            for piece in range(16):
                load_weight_piece(w_in[:, gcol0 + piece * 128: gcol0 + (piece + 1) * 128], 8,
                                  Wg[:, :, piece * 128:(piece + 1) * 128], ('wg', piece))
            for piece in range(8):
                load_weight_piece(w_out[l, 0:1024, piece * 128:(piece + 1) * 128], 8,
                                  Wo[:, 0:8, piece * 128:(piece + 1) * 128], ('wo', 0, piece))
                load_weight_piece(w_out[l, 1024:1536, piece * 128:(piece + 1) * 128], 4,
                                  Wo[:, 8:12, piece * 128:(piece + 1) * 128], ('wo', 1, piece))
            wg_keys = [('wg', p_) for p_ in range(16)]
            wo_keys = [('wo', a_, p_) for a_ in range(2) for p_ in range(8)]
            KmT3 = KmT[:].rearrange("p (h k) -> p h k", h=4)
            att_cfg['banks'] = [2, 0, 5]
            att_cfg['depth'] = 2
            att_cfg['oa'] = [4, 7]

            def gate_chain(t, j, gb, o_j, o_keys):
                th_ap = th[:, j * 512:(j + 1) * 512]
                S.op('act', lambda e: e.activation(out=th_ap, in_=PS[gb][:, 0:512], func=ACT.Tanh, scale=0.5),
                     reads=[pk(gb)], writes=[('th', j)])
                S.op('dve', lambda e: e.scalar_tensor_tensor(out=th_ap, in0=th_ap, scalar=1.0, in1=PS[gb][:, 0:512],
                                                             op0=ALU.add, op1=ALU.mult),
                     reads=[pk(gb), ('th', j)], writes=[('th', j)])
                yb_ap = yb[:, j * 512:(j + 1) * 512]
                S.op('dve', lambda e: e.scalar_tensor_tensor(out=yb_ap, in0=th_ap, scalar=0.5, in1=o_j, op0=ALU.mult, op1=ALU.mult),
                     reads=[('th', j)] + o_keys, writes=[('yb', j)])

            def gate_proj(t, j, gb):
                for c in range(8):
                    S.op('pe', lambda e, c=c: e.matmul(PS[gb][:, 0:512], lhsT=hT_ap(c, t), rhs=Wg[:, c, 512 + j * 512: 1024 + j * 512],
                                                       start=(c == 0), stop=(c == 7)),
                         reads=[('hT', t, c)] + wg_keys[4 + 4 * j: 8 + 4 * j], writes=[pk(gb)])

            def y_transposes(t, j, tpb, eng):
                b = t % 2
                yb_ap = yb[:, j * 512:(j + 1) * 512]
                yTb = yT[:, b * 1536:(b + 1) * 1536]
                for i in range(4):
                    S.op('pe', lambda e, i=i: e.transpose(out=PSB[tpb][:, i * 128:(i + 1) * 128],
                                                         in_=yb_ap[:, i * 128:(i + 1) * 128], identity=idb[:]),
                         reads=[('yb', j), 'idb'], writes=[pk(tpb)])
                if eng == 'act':
                    S.op('act', lambda e: e.activation(out=yTb[:, j * 512:(j + 1) * 512], in_=PSB[tpb][:, 0:512], func=ACT.Copy),
                         reads=[pk(tpb)], writes=[('yT', b, j)])
                else:
                    S.op('dve', lambda e: e.tensor_copy(out=yTb[:, j * 512:(j + 1) * 512], in_=PSB[tpb][:, 0:512]),
                         reads=[pk(tpb)], writes=[('yT', b, j)])

            def stage_A(t):
                b = t % 2
                xt = xt_ap(b)
                ot_ap = ot[:, b * 1024:(b + 1) * 1024]
                S.op('sp', lambda e, x_src=x_src: e.dma_start(out=xt, in_=x_src[t * 128:(t + 1) * 128, :]),
                     writes=[('xt', b)], dma=('xt', b))
                S.op('sp', lambda e: e.dma_start(out=ot_ap, in_=o_scr[t * 128:(t + 1) * 128, :]),
                     reads=[('oscr', t)], writes=[('ot', b)], dma=('ot', b))
                bank = 0
                for c in range(8):
                    S.op('pe', lambda e, c=c: e.matmul(PS[bank][:, 0:512], lhsT=hT_ap(c, t), rhs=Wg[:, c, 0:512],
                                                       start=(c == 0), stop=(c == 7)),
                         reads=[('hT', t, c)] + wg_keys[0:4], writes=[pk(bank)])
                gate_proj(t, 0, 1)
                gate_proj(t, 1, 3)
                src = PS[bank][:, 0:512]
                rs_ap, rs_key = rms_stats(src, 4, 128, [pk(bank)], True, 1.0 / E)
                tf3 = tmpf[:, b * 768: b * 768 + 512].rearrange("p (a b) -> p a b", a=4)
                S.op('dve', lambda e: e.tensor_tensor(out=tf3, in0=src.rearrange("p (a b) -> p a b", a=4),
                                                      in1=rs_ap.unsqueeze(2).to_broadcast([128, 4, 128]), op=ALU.mult),
                     reads=[pk(bank), rs_key], writes=[('tmpf', b, 0)])
                qn3 = qn[:, b * 768: b * 768 + 512].rearrange("p (a b) -> p a b", a=4)
                gsrc = gains[:, g_q_mem * 128:(g_q_mem + 1) * 128].unsqueeze(1).to_broadcast([128, 4, 128])
                S.op('pool', lambda e: e.tensor_tensor(out=qn3, in0=tf3, in1=gsrc, op=ALU.mult),
                     reads=[('tmpf', b, 0), 'gains'], writes=[('qn', b, 'a')])
                gate_chain(t, 0, 1, ot_ap[:, 0:512], [('ot', b)])
                gate_chain(t, 1, 3, ot_ap[:, 512:1024], [('ot', b)])

            def stage_B(t):
                b = t % 2
                qn_ap = qn[:, b * 768: b * 768 + 512]
                tb = 6
                for hd in range(4):
                    S.op('pe', lambda e, hd=hd: e.transpose(out=PSB[tb][:, hd * 128:(hd + 1) * 128],
                                                           in_=qn_ap[:, hd * 128:(hd + 1) * 128], identity=idb[:]),
                         reads=[('qn', b, 'a'), 'idb'], writes=[pk(tb)])
                S.op('act', lambda e: e.activation(out=qmT[:], in_=PSB[tb][:, 0:512], func=ACT.Copy),
                     reads=[pk(tb)], writes=['qmT'])
                y_transposes(t, 0, 7, 'dve')
                gate_proj(t, 2, 1)
                th2 = th[:, 1024:1536]
                S.op('act', lambda e: e.activation(out=th2, in_=PS[1][:, 0:512], func=ACT.Tanh, scale=0.5),
                     reads=[pk(1)], writes=[('th', 2)])
                S.op('dve', lambda e: e.scalar_tensor_tensor(out=th2, in0=th2, scalar=1.0, in1=PS[1][:, 0:512],
                                                             op0=ALU.add, op1=ALU.mult),
                     reads=[pk(1), ('th', 2)], writes=[('th', 2)])
                y_transposes(t, 1, 6, 'act')
                for hd in range(4):
                    blocks = []
                    for kb in range(2):
                        blocks.append((qmT[:, hd * 128:(hd + 1) * 128], ['qmT'], KmT3[:, hd, kb * 128:(kb + 1) * 128], ['KmT'],
                                       Vm[:, (kb * 4 + hd) * 130:(kb * 4 + hd) * 130 + 129], ['Vm'], None))

                    def fin(ab, hd=hd):
                        ob_i = fcount[0] % 4
                        fcount[0] += 1
                        rc_ap = rc[:, 4 + ob_i:5 + ob_i]
                        S.op('dve', lambda e: e.reciprocal(out=rc_ap, in_=PS[ab][:, 128:129]),
                             reads=[pk(ab)], writes=[('rcm', ob_i)])
                        S.op('act', lambda e: e.activation(out=om[:, hd * 128:(hd + 1) * 128], in_=PS[ab][:, 0:128], func=ACT.Copy,
                                                           scale=rc_ap),
                             reads=[pk(ab), ('rcm', ob_i)], writes=[('om', hd)])
                    attention_job(blocks, fin)
                att_flush_all()
                yb2 = yb[:, 1024:1536]
                S.op('dve', lambda e: e.scalar_tensor_tensor(out=yb2, in0=th2, scalar=0.5, in1=om[:, 0:512], op0=ALU.mult, op1=ALU.mult),
                     reads=[('th', 2)] + [('om', hd) for hd in range(4)], writes=[('yb', 2)])
                y_transposes(t, 2, 7, 'act')

            def stage_C(t):
                b = t % 2
                xt = xt_ap(b)
                x1t = x1t_ap(b)
                yTb = yT[:, b * 1536:(b + 1) * 1536]
                for nb_ in range(2):
                    obk = [2, 0][nb_]
                    for cc in range(12):
                        S.op('pe', lambda e, cc=cc, nb_=nb_, obk=obk: e.matmul(
                            PS[obk][:, 0:512], lhsT=yTb[:, cc * 128:(cc + 1) * 128], rhs=Wo[:, cc, nb_ * 512:(nb_ + 1) * 512],
                            start=(cc == 0), stop=(cc == 11)),
                            reads=[('yT', b, cc // 4)] + wo_keys, writes=[pk(obk)])
                    S.op('dve', lambda e, nb_=nb_, obk=obk: e.tensor_tensor(
                        out=x1t[:, nb_ * 512:(nb_ + 1) * 512], in0=PS[obk][:, 0:512], in1=xt[:, nb_ * 512:(nb_ + 1) * 512], op=ALU.add),
                        reads=[pk(obk), ('xt', b)], writes=[('x1t', b, nb_)])
                S.op('sp', lambda e, x_dst=x_dst: e.dma_start(out=x_dst[t * 128:(t + 1) * 128, :], in_=x1t),
                     reads=[('x1t', b, 0), ('x1t', b, 1)], writes=[('xdst', t)], dma=('x1t', b))

            def _bind(f, **kw):
                return f

            if stop != 'p3w':
                stage_A(0)
                for t in range(NT):
                    stage_B(t)
                    if t + 1 < NT:
                        stage_A(t + 1)
                    stage_C(t)
            att_cfg['oa'] = OA
            S.barrier()
        S.emit(nc)
    return nc, S


def _host_consts(norm_g, mem_norm_g, mem_qn_g, mem_kn_g, qn_a, kn_a, qn_b, kn_b):
    ng = np.zeros((128, 32), np.float32)
    for l in range(2):
        ng[:, l * 8:(l + 1) * 8] = norm_g[l].reshape(8, 128).T
        ng[:, 16 + l * 8:16 + (l + 1) * 8] = mem_norm_g[l].reshape(8, 128).T
    rows = [qn_a[0, 0], qn_a[0, 1], qn_a[0, 2], kn_a[0, 0], kn_a[0, 1], kn_a[0, 2], qn_b[0], kn_b[0],
            mem_qn_g[0], mem_qn_g[1], mem_kn_g[0], mem_kn_g[1]]
    gains = np.ascontiguousarray(np.broadcast_to(np.concatenate(rows)[None, :], (128, 12 * 128))).astype(np.float32)
    pos = np.arange(S_TOK, dtype=np.float32)
    invA = (np.float32(500000.0) ** (-np.arange(0, 32, 2, dtype=np.float32) / np.float32(32))).astype(np.float32)
    angA = pos[:, None] * invA[None, :]
    row = np.repeat(np.arange(64, dtype=np.float32), 64)
    col = np.tile(np.arange(64, dtype=np.float32), 64)
    invB = (np.float32(10000.0) ** (-np.arange(0, 64, 2, dtype=np.float32) / np.float32(64))).astype(np.float32)
    angB = np.concatenate([row[:, None] * invB[None, :], col[:, None] * invB[None, :]], axis=-1)

    def lay(a):
        return a.reshape(32, 128, -1).transpose(1, 0, 2)
    ropeA = np.stack([lay(np.cos(angA)), lay(np.sin(angA))], axis=1).reshape(128, -1).astype(np.float32)
    ropeB = np.stack([lay(np.cos(angB)), lay(np.sin(angB))], axis=1).reshape(128, -1).astype(np.float32)
    return dict(ng=ng, gains=gains, ropeA=np.ascontiguousarray(ropeA), ropeB=np.ascontiguousarray(ropeB),
                ident=np.eye(128, dtype=np.float32), masks=np.ascontiguousarray(MASKS_NP.reshape(128, -1)))


_NC_CACHE = {}


def kernel(x, mem, norm_g, mem_norm_g, w_mem_kv, mem_qn_g, mem_kn_g, w_out,
           w_in_a, qn_a, kn_a, w_in_b, qn_b, kn_b):
    f = lambda a: np.ascontiguousarray(np.asarray(a, dtype=np.float32))
    x = f(x)
    mem = f(mem)
    consts = _host_consts(f(norm_g), f(mem_norm_g), f(mem_qn_g), f(mem_kn_g), f(qn_a), f(kn_a), f(qn_b), f(kn_b))
    shared = dict(w_in_a=f(w_in_a)[0], w_in_b=f(w_in_b)[0], w_out=f(w_out), w_mem_kv=f(w_mem_kv), **consts)
    if 'nc' not in _NC_CACHE:
        _NC_CACHE['nc'] = build()[0]
    nc = _NC_CACHE['nc']
    n = x.shape[0]
    in_maps = [dict(x=x[b], mem=mem[b], **shared) for b in range(n)]
    res = run_bass_kernel_spmd(nc, in_maps, core_ids=list(range(n)))
    return np.stack([np.asarray(r["y_out"], dtype=np.float32) for r in res.results], axis=0)
```

```python
import numpy as np
import concourse.bass as bass
import concourse.mybir as mybir
from concourse.bass_utils import run_bass_kernel_spmd

F32 = mybir.dt.float32
BF16 = mybir.dt.bfloat16
ALU = mybir.AluOpType
ACT = mybir.ActivationFunctionType
AX = mybir.AxisListType

ENGS = ['pe', 'act', 'dve', 'pool', 'sp']

S_TOK = 4096
NT = 32
D = 1024
E = 128
EPS = 1e-6
SCALE = float(E) ** -0.5
A_GROUPS = ((128, 1), (512, 4), (2048, 16))
IN_A = 11264
IN_B = 3584

PROJ_N = 512


class Sched:
    def __init__(self):
        self.ops = []
        self.state = {}
        self.last_on = {e: None for e in ENGS}
        self.dma_last = {}

    def op(self, eng, fn, reads=(), writes=(), dma=None):
        i = len(self.ops)
        deps = set()
        for k in reads:
            excl = isinstance(k, tuple) and k[0] == 'ps'
            st = self.state.setdefault(k, [None, []])
            if st[0] is not None:
                deps.add(st[0])
            if excl:
                for r in st[1]:
                    if self.ops[r]['eng'] != eng:
                        deps.add(r)
        for k in writes:
            st = self.state.setdefault(k, [None, []])
            if st[0] is not None:
                deps.add(st[0])
            for r in st[1]:
                deps.add(r)
        for k in reads:
            self.state[k][1].append(i)
        for k in writes:
            self.state[k] = [i, []]
        if dma is not None:
            p = self.dma_last.get(dma)
            if p is not None:
                deps.add(p)
            self.dma_last[dma] = i
        deps.discard(i)
        if eng == 'pe':
            deps = {d for d in deps if self.ops[d]['eng'] != 'pe'}
        best = {}
        keep = set()
        for d in deps:
            od = self.ops[d]
            if od['dma'] is not None:
                keep.add(d)
            elif d > best.get(od['eng'], -1):
                best[od['eng']] = d
        deps = keep | set(best.values())
        self.ops.append(dict(eng=eng, fn=fn, deps=deps, dma=dma))
        if fn is not None:
            self.last_on[eng] = i
        return i

    def barrier(self):
        lasts = [v for v in self.last_on.values() if v is not None]
        lasts += list(self.dma_last.values())
        lasts = set(lasts)
        for e in ENGS:
            deps = {d for d in lasts if not (self.ops[d]['eng'] == e and self.ops[d]['dma'] is None)}
            self.ops.append(dict(eng=e, fn=None, deps=deps, dma=None))
        self.state = {}

    def emit(self, nc):
        ops = self.ops
        needed = set()
        for o in ops:
            needed |= o['deps']
        dma_keys = []
        seen = set()
        for o in ops:
            if o['dma'] is not None and o['dma'] not in seen:
                seen.add(o['dma'])
                dma_keys.append(o['dma'])
        sem_ctx = []
        sems = {}
        for n_i, name in enumerate(['pe', 'act', 'dve', 'pool'] + [('dma', k) for k in dma_keys]):
            cm = nc.semaphore("s%d" % n_i)
            sems[name] = cm.__enter__()
            sem_ctx.append(cm)
        cnt = {}
        for i, o in enumerate(ops):
            if o['dma'] is not None:
                key = ('dma', o['dma'])
                cnt[key] = cnt.get(key, 0) + 16
                o['sem'], o['val'], o['inc'] = key, cnt[key], 16
            elif i in needed:
                assert o['fn'] is not None
                key = o['eng']
                cnt[key] = cnt.get(key, 0) + 1
                o['sem'], o['val'], o['inc'] = key, cnt[key], 1
            else:
                o['sem'] = None
        self.maxval = dict(cnt)

        def run(eng_name, e):
            waited = {}
            for o in ops:
                if o['eng'] != eng_name:
                    continue
                need = {}
                for d in o['deps']:
                    od = ops[d]
                    s, v = od['sem'], od['val']
                    if v > need.get(s, 0):
                        need[s] = v
                for s, v in need.items():
                    if waited.get(s, 0) < v:
                        e.wait_ge(sems[s], v)
                        waited[s] = v
                if o['fn'] is None:
                    continue
                ins = o['fn'](e)
                if o['sem'] is not None:
                    ins.then_inc(sems[o['sem']], o['inc'])

        with nc.Block() as block:
            @block.tensor
            def _(e):
                run('pe', e)

            @block.scalar
            def _(e):
                run('act', e)

            @block.vector
            def _(e):
                run('dve', e)

            @block.gpsimd
            def _(e):
                run('pool', e)

            @block.sync
            def _(e):
                run('sp', e)
        for cm in reversed(sem_ctx):
            cm.__exit__(None, None, None)


def _mask_tables():
    masks = []
    index = {}
    table = {}
    kk = np.arange(128)[:, None]
    qq = np.arange(128)[None, :]
    for g, (window, dil) in enumerate(A_GROUPS):
        hw = window // 2
        dmax = hw // 128 + (1 if hw % 128 else 0)
        dmax = max(dmax, 1)
        for dl in range(-dmax, dmax + 1):
            diff = 128 * dl + kk - qq
            m = ((diff % dil) == 0) & (np.abs(diff) <= hw)
            if not m.any():
                continue
            key = m.tobytes()
            if key not in index:
                index[key] = len(masks)
                masks.append(m.astype(np.float32))
            table[(g, dl)] = index[key]
    return np.stack(masks, axis=1), table


MASKS_NP, MASK_TABLE = _mask_tables()
NM = MASKS_NP.shape[1]


def build(n_layers=2, debug=False, stop=None):
    stop_spec = stop
    nc = bass.Bass("TRN2", target_bir_lowering=False)
    dk = "ExternalOutput" if debug else "Internal"
    x_in = nc.dram_tensor("x", [S_TOK, D], F32, kind="ExternalInput").ap()
    mem_in = nc.dram_tensor("mem", [256, D], F32, kind="ExternalInput").ap()
    w_in_a = nc.dram_tensor("w_in_a", [D, IN_A], F32, kind="ExternalInput").ap()
    w_in_b = nc.dram_tensor("w_in_b", [D, IN_B], F32, kind="ExternalInput").ap()
    w_out = nc.dram_tensor("w_out", [2, 1536, D], F32, kind="ExternalInput").ap()
    w_kv = nc.dram_tensor("w_mem_kv", [2, D, 1024], F32, kind="ExternalInput").ap()
    ng_in = nc.dram_tensor("ng", [128, 32], F32, kind="ExternalInput").ap()
    gains_in = nc.dram_tensor("gains", [128, 12 * 128], F32, kind="ExternalInput").ap()
    ropeA_in = nc.dram_tensor("ropeA", [128, 2 * 32 * 16], F32, kind="ExternalInput").ap()
    ropeB_in = nc.dram_tensor("ropeB", [128, 2 * 32 * 64], F32, kind="ExternalInput").ap()
    ident_in = nc.dram_tensor("ident", [128, 128], F32, kind="ExternalInput").ap()
    masks_in = nc.dram_tensor("masks", [128, NM * 128], F32, kind="ExternalInput").ap()
    out = nc.dram_tensor("y_out", [S_TOK, D], F32, kind="ExternalOutput").ap()
    x1_scr = nc.dram_tensor("x1_scr", [S_TOK, D], F32, kind=dk).ap()
    o_scr = nc.dram_tensor("o_scr", [S_TOK, D], BF16, kind=dk).ap()

    S = Sched()
    R_N = 6 * 4096 + 3 * 32 * 130
    import contextlib
    with contextlib.ExitStack() as es:
        def sb(name, shape, dt):
            return es.enter_context(nc.sbuf_tensor(name, shape, dt))

        hT = sb("hT", [128, 8 * 4096], BF16)
        R = sb("R", [128, R_N], BF16)
        Wbuf = sb("Wbuf", [128, 8 * 1152], BF16)
        Wbf = Wbuf.bitcast(F32)
        Wst = sb("Wst", [128, 2 * 8 * 128], F32)
        ropeA = sb("ropeA_t", [128, 2 * 32 * 16], F32)
        ropeBt = ropeA[:, 0:256]
        gains = sb("gains_t", [128, 12 * 128], F32)
        maskb = sb("maskb", [128, NM * 128], BF16)
        idf = sb("idf", [128, 128], F32)
        idb = sb("idb", [128, 128], BF16)
        ngt = sb("ngt", [128, 32], F32)
        mh = sb("mh", [128, 16], F32)
        KmT = sb("KmT", [128, 4 * 256], BF16)
        Vm = sb("Vm", [128, 2 * 4 * 130], BF16)
        ssA = sb("ssA", [128, 2 * 8], F32)
        msA = sb("msA", [128, 2 * 8], F32)
        rsA = sb("rsA", [128, 2 * 8], F32)
        sq = sb("sq", [128, 2 * 768], F32)
        tmpf = sb("tmpf", [128, 2 * 768], F32)
        U = sb("U", [128, 2560], F32)
        Ub = U.bitcast(BF16)
        qn = sb("qn", [128, 2 * 768], BF16)
        pt = sb("pt", [128, 3 * 512], BF16)
        rc = sb("rc", [128, 8], F32)
        ob = Ub[:, 3072:3584]
        th = Wbf[:, 0:1536]
        ot = Ub[:, 3072:5120]
        Wbb = Wbuf
        om = Wbb[:, 3072:3584]
        yb = Wbb[:, 3584:5120]
        yT = Ub[:, 0:3072]
        qmT = Wbb[:, 5120:5632]

        PSALL = es.enter_context(nc.psum_tensor("psall", [128, 4096], F32))
        PSBALL = PSALL.bitcast(BF16)
        PS = [PSALL[:, i * 512:(i + 1) * 512] for i in range(8)]
        PSB = [PSBALL[:, i * 1024:(i + 1) * 1024] for i in range(8)]
        PJ = [0, 1]
        SC = [2, 3]
        OA = [4, 5]
        TP = [6, 7]

        def pk(i):
            return ('ps', i)

        Rf = R.bitcast(F32)
        QK_OFF = 0
        V_OFF = 6 * 4096
        XT_OFF_F = 28672 // 2
        def xt_ap(b):
            return Rf[:, XT_OFF_F + b * 1024: XT_OFF_F + (b + 1) * 1024]

        def x1t_ap(b):
            return Rf[:, XT_OFF_F + 2048 + b * 1024: XT_OFF_F + 2048 + (b + 1) * 1024]

        def qk_ap(slot, t0, n=128):
            o = QK_OFF + slot * 4096 + t0
            return R[:, o:o + n]

        def v_ap(slot, kb, n=129):
            o = V_OFF + (slot * 32 + kb) * 130
            return R[:, o:o + n]

        def hT_ap(c, t):
            o = c * 4096 + t * 128
            return hT[:, o:o + 128]

        def wst_ap(slot, nchunk=8):
            return Wst[:, slot * 1024: slot * 1024 + nchunk * 128].rearrange("p (c n) -> p c n", n=128)

        S.op('sp', lambda e: e.dma_start(out=idf[:], in_=ident_in), writes=['idf'], dma='c0')
        S.op('sp', lambda e: e.dma_start(out=gains[:], in_=gains_in), writes=['gains'], dma='c1')
        S.op('sp', lambda e: e.dma_start(out=ngt[:], in_=ng_in), writes=['ngt'], dma='c2')
        S.op('sp', lambda e: e.dma_start(out=ropeA[:], in_=ropeA_in), writes=['ropeA'], dma='c3')
        S.op('sp', lambda e: e.dma_start(out=Rf[:, 0:NM * 128], in_=masks_in), writes=['mstage'], dma='c4')
        S.op('pool', lambda e: e.tensor_copy(out=idb[:], in_=idf[:]), reads=['idf'], writes=['idb'])
        S.op('pool', lambda e: e.tensor_scalar(out=maskb[:], in0=Rf[:, 0:NM * 128], scalar1=-1.0, scalar2=30000.0,
                                               op0=ALU.add, op1=ALU.mult), reads=['mstage'], writes=['maskb'])
        S.op('pool', lambda e: e.memset(mh[:, 0:8], -0.5), writes=['mh'])
        S.op('pool', lambda e: e.memset(mh[:, 8:16], EPS), writes=['mhe'])
        S.barrier()
        stopped = (stop == 'init')

        wcount = [0]

        def load_weight_piece(src_ap, nchunk, dst_ap, dst_key):
            slot = wcount[0] % 2
            wcount[0] += 1
            S.op('sp', lambda e: e.dma_start(out=wst_ap(slot, nchunk), in_=src_ap.rearrange("(c p) n -> p c n", p=128)),
                 writes=[('wst', slot)], dma=('wst', slot))
            S.op('pool', lambda e: e.tensor_copy(out=dst_ap, in_=wst_ap(slot, nchunk)),
                 reads=[('wst', slot)], writes=[dst_key])

        sidx = [0]

        def rms_stats(src_ap, n_groups, glen, src_keys, src_is_psum, inv_n, use_ln=False):
            b = sidx[0] % 2
            sidx[0] += 1
            n = n_groups * glen
            sq_ap = sq[:, b * 768: b * 768 + n]
            ss_ap = ssA[:, b * 8: b * 8 + n_groups]
            ms_ap = msA[:, b * 8: b * 8 + n_groups]
            rs_ap = rsA[:, b * 8: b * 8 + n_groups]
            S.op('act', lambda e: e.activation(out=sq_ap, in_=src_ap, func=ACT.Square),
                 reads=src_keys, writes=[('sq', b), ('sqb', b)])
            S.op('dve', lambda e: e.tensor_reduce(out=ss_ap, in_=sq_ap.rearrange("p (a b) -> p a b", a=n_groups),
                                                  axis=AX.X, op=ALU.add),
                 reads=[('sq', b), ('sqb', b)], writes=[('ss', b)])
            if use_ln:
                S.op('act', lambda e: e.activation(out=ms_ap, in_=ss_ap, func=ACT.Ln, scale=inv_n, bias=mh[:, 8:9]),
                     reads=[('ss', b), 'mhe'], writes=[('ms', b)])
                S.op('act', lambda e: e.activation(out=rs_ap, in_=ms_ap, func=ACT.Exp, scale=-0.5),
                     reads=[('ms', b)], writes=[('rs', b)])
                return rs_ap, ('rs', b)
            S.op('dve', lambda e: e.tensor_scalar(out=ms_ap, in0=ss_ap, scalar1=inv_n, scalar2=EPS,
                                                  op0=ALU.mult, op1=ALU.add),
                 reads=[('ss', b)], writes=[('ms', b)])
            S.op('pool', lambda e: e.tensor_tensor(out=rs_ap, in0=ms_ap, in1=mh[:, 0:n_groups], op=ALU.pow),
                 reads=[('ms', b), 'mh'], writes=[('rs', b)])
            return rs_ap, ('rs', b)

        ntt_pending = [None]

        def ntt_flush():
            if ntt_pending[0] is not None:
                ntt_pending[0]()
                ntt_pending[0] = None

        def norm_transpose_tile(src_dram_ap, t, gcol, dst_fn, dst_keys, tcount):
            b = tcount % 2
            xt = xt_ap(b)
            S.op('sp', lambda e: e.dma_start(out=xt, in_=src_dram_ap), writes=[('xt', b)], dma=('xt', b))
            ss_ap = ssA[:, b * 8: b * 8 + 1]
            ms_ap = msA[:, b * 8: b * 8 + 1]
            rs_ap = rsA[:, b * 8: b * 8 + 1]
            S.op('act', lambda e: e.activation(out=sq[:, 0:1024], in_=xt, func=ACT.Square, accum_out=ss_ap),
                 reads=[('xt', b)], writes=[('sq', 0), ('sq', 1), ('sqb', 0), ('sqb', 1), ('ss', b)])
            S.op('act', lambda e: e.activation(out=ms_ap, in_=ss_ap, func=ACT.Ln, scale=1.0 / D, bias=mh[:, 8:9]),
                 reads=[('ss', b), 'mhe'], writes=[('ms', b)])
            S.op('act', lambda e: e.activation(out=rs_ap, in_=ms_ap, func=ACT.Exp, scale=-0.5),
                 reads=[('ms', b)], writes=[('rs', b)])
            S.op('dve', lambda e: e.tensor_scalar(out=xt, in0=xt, scalar1=rs_ap, scalar2=None, op0=ALU.mult),
                 reads=[('xt', b), ('rs', b)], writes=[('xt', b)])

            def tail():
                for half in range(2):
                    bank = (0 if b == 0 else 2) + half
                    for j in range(4):
                        c = half * 4 + j
                        S.op('pe', lambda e, c=c, j=j, bank=bank: e.transpose(out=PS[bank][:, j * 128:(j + 1) * 128],
                                                                              in_=xt[:, c * 128:(c + 1) * 128], identity=idf[:]),
                             reads=[('xt', b), 'idf'], writes=[pk(bank)])
                    gsrc = ngt[:, gcol + half * 4: gcol + half * 4 + 4].unsqueeze(2).to_broadcast([128, 4, 128])
                    S.op('dve', lambda e, half=half, bank=bank, gsrc=gsrc: e.tensor_tensor(
                        out=dst_fn(half), in0=PS[bank][:, 0:512].rearrange("p (c k) -> p c k", c=4), in1=gsrc, op=ALU.mult),
                        reads=[pk(bank), 'ngt'], writes=dst_keys(half))
            ntt_flush()
            ntt_pending[0] = tail

        def proj_head(pbase, nqk, nv, gain_ops, t, rope, ropekey, vslot0, ucount):
            banks = sorted(set((pbase * 512 + i * 128) // 512 for i in range(nqk + nv)))
            bkeys = [pk(bk) for bk in banks]
            c0 = pbase * 512
            src = PSALL[:, c0:c0 + nqk * 128]
            for vi in range(nv):
                S.op('act', lambda e, vi=vi: e.activation(out=v_ap(vslot0 + vi, t, 128),
                                                          in_=PSALL[:, c0 + (nqk + vi) * 128: c0 + (nqk + vi + 1) * 128], func=ACT.Copy),
                     reads=bkeys, writes=[('v', vslot0 + vi, t)])
            sb_i = sidx[0] % 2
            sidx[0] += 1
            n = nqk * 128
            sq_ap = sq[:, sb_i * 768: sb_i * 768 + n]
            ss_ap = ssA[:, sb_i * 8: sb_i * 8 + nqk]
            S.op('act', lambda e: e.activation(out=sq_ap, in_=src, func=ACT.Square),
                 reads=bkeys, writes=[('sq', sb_i), ('sqb', sb_i)])
            S.op('dve', lambda e: e.tensor_reduce(out=ss_ap, in_=sq_ap.rearrange("p (a b) -> p a b", a=nqk),
                                                  axis=AX.X, op=ALU.add),
                 reads=[('sq', sb_i), ('sqb', sb_i)], writes=[('ss', sb_i)])
            b = ucount % 2
            tf = tmpf[:, b * 768: b * 768 + nqk * 128]
            tf3 = tf.rearrange("p (a b) -> p a b", a=nqk)
            TK = [('tmpf', b)]
            return dict(nqk=nqk, gain_ops=gain_ops, t=t, rope=rope, ropekey=ropekey, ucount=ucount, b=b, tf3=tf3, TK=TK,
                        sb_i=sb_i, src=src, bkeys=bkeys)

        def proj_head2(cx):
            nqk, sb_i, src, bkeys, tf3, TK = (cx[k_] for k_ in ('nqk', 'sb_i', 'src', 'bkeys', 'tf3', 'TK'))
            ss_ap = ssA[:, sb_i * 8: sb_i * 8 + nqk]
            ms_ap = msA[:, sb_i * 8: sb_i * 8 + nqk]
            rs_ap = rsA[:, sb_i * 8: sb_i * 8 + nqk]
            S.op('act', lambda e: e.activation(out=ms_ap, in_=ss_ap, func=ACT.Ln, scale=1.0 / E, bias=mh[:, 8:9]),
                 reads=[('ss', sb_i), 'mhe'], writes=[('ms', sb_i)])
            S.op('act', lambda e: e.activation(out=rs_ap, in_=ms_ap, func=ACT.Exp, scale=-0.5),
                 reads=[('ms', sb_i)], writes=[('rs', sb_i)])
            S.op('dve', lambda e: e.tensor_tensor(out=tf3, in0=src.rearrange("p (a b) -> p a b", a=nqk),
                                                  in1=rs_ap.unsqueeze(2).to_broadcast([128, nqk, 128]), op=ALU.mult),
                 reads=bkeys + [('rs', sb_i)], writes=TK)

        def proj_tail(cx):
            nqk, gain_ops, t, rope, ropekey, ucount, b, tf3, TK = (cx[k_] for k_ in
                                                                   ('nqk', 'gain_ops', 't', 'rope', 'ropekey', 'ucount', 'b', 'tf3', 'TK'))
            b3 = ucount % 3
            qn_ap = qn[:, b3 * 768: b3 * 768 + nqk * 128] if b3 < 2 else Ub[:, 3584:3584 + nqk * 128]
            qn3 = qn_ap.rearrange("p (a b) -> p a b", a=nqk)
            QA, QB, QC = ('qn3', b3, 'a'), ('qn3', b3, 'b'), ('qn3', b3, 'c')
            cx['qn_ap'] = qn_ap
            cx['QK3'] = [QA, QB, QC]
            for (b0, nb_, gsrc) in gain_ops:
                S.op('dve', lambda e, b0=b0, nb_=nb_, gsrc=gsrc: e.tensor_tensor(out=tf3[:, b0:b0 + nb_, :], in0=tf3[:, b0:b0 + nb_, :],
                                                                            in1=gsrc, op=ALU.mult),
                     reads=TK + ['gains'], writes=TK)
            cos_ap, sin_ap, Rr = rope
            cosb = cos_ap.unsqueeze(1).to_broadcast([128, nqk, Rr])
            sinb = sin_ap.unsqueeze(1).to_broadcast([128, nqk, Rr])
            x1 = tf3[:, :, 0:Rr]
            x2 = tf3[:, :, Rr:2 * Rr]
            nr = nqk * Rr
            ra3 = sq[:, b * 768: b * 768 + nr].rearrange("p (a b) -> p a b", a=nqk)
            rb3 = sq[:, b * 768 + 384: b * 768 + 384 + nr].rearrange("p (a b) -> p a b", a=nqk)
            ra4 = U[:, b * 768: b * 768 + nr].rearrange("p (a b) -> p a b", a=nqk)
            rb4 = U[:, b * 768 + 384: b * 768 + 384 + nr].rearrange("p (a b) -> p a b", a=nqk)
            SQK = ('sq', b)
            S.op('dve', lambda e: e.tensor_tensor(out=ra3, in0=x1, in1=cosb, op=ALU.mult),
                 reads=TK + [ropekey], writes=[SQK])
            S.op('pool', lambda e: e.tensor_tensor(out=rb3, in0=x2, in1=sinb, op=ALU.mult),
                 reads=TK + [ropekey], writes=[('sqb', b)])
            S.op('dve', lambda e: e.tensor_tensor(out=ra4, in0=x2, in1=cosb, op=ALU.mult),
                 reads=TK + [ropekey], writes=[('ra4', b)])
            S.op('pool', lambda e: e.tensor_tensor(out=rb4, in0=x1, in1=sinb, op=ALU.mult),
                 reads=TK + [ropekey], writes=[('rb4', b)])
            S.op('dve', lambda e: e.tensor_tensor(out=qn3[:, :, 0:Rr], in0=ra3, in1=rb3, op=ALU.subtract),
                 reads=[SQK, ('sqb', b)], writes=[QA])
            S.op('pool', lambda e: e.tensor_tensor(out=qn3[:, :, Rr:2 * Rr], in0=ra4, in1=rb4, op=ALU.add),
                 reads=[('ra4', b), ('rb4', b)], writes=[QB])
            if 2 * Rr < 128:
                S.op('pool', lambda e: e.tensor_copy(out=qn3[:, :, 2 * Rr:128], in_=tf3[:, :, 2 * Rr:128]),
                     reads=TK, writes=[QC])

        def proj_tail_b(cx):
            nqk, t, ucount, qn_ap = cx['nqk'], cx['t'], cx['ucount'], cx['qn_ap']
            QA, QB, QC = cx['QK3']
            tb = TP[ucount % 2]
            for i in range(nqk):
                S.op('pe', lambda e, i=i: e.transpose(out=PSB[tb][:, i * 128:(i + 1) * 128],
                                                     in_=qn_ap[:, i * 128:(i + 1) * 128], identity=idb[:]),
                     reads=[QA, QB, QC, 'idb'], writes=[pk(tb)])
            qkdst = R[:, 0:nqk * 4096].rearrange("p (s k) -> p s k", s=nqk)[:, :, t * 128:(t + 1) * 128]
            S.op('act', lambda e: e.activation(out=qkdst, in_=PSB[tb][:, 0:nqk * 128].rearrange("p (s k) -> p s k", s=nqk),
                                               func=ACT.Copy),
                 reads=[pk(tb)], writes=[('qk', sl, t) for sl in range(nqk)])

        acount = [0]
        ptcount = [0]
        sccount = [0]
        fcount = [0]
        pend = []
        att_cfg = dict(banks=[2, 3], depth=1)

        def att_flush_one():
            q = pend.pop(0)
            pb, ab = q['pb'], q['ab']
            acc = PS[ab][:, 0:129]
            for (i, v_ap_, v_keys, first, last) in q['pv']:
                S.op('pe', lambda e, i=i, v_ap_=v_ap_, first=first, last=last, pb=pb, acc=acc: e.matmul(
                    acc, lhsT=pt[:, pb * 512 + i * 128: pb * 512 + (i + 1) * 128], rhs=v_ap_, start=first, stop=last),
                    reads=[('pt', pb)] + list(v_keys), writes=[pk(ab)])
            if q['fin'] is not None:
                q['fin'](ab)

        def att_flush_all():
            while pend:
                att_flush_one()

        def attention_job(blocks, fin):
            oa_ = att_cfg.get('oa', OA)
            ab = oa_[acount[0] % len(oa_)]
            acount[0] += 1
            nb = len(blocks)
            bi = 0
            while bi < nb:
                quad = blocks[bi:bi + 4]
                n = len(quad)
                banks = att_cfg['banks']
                sb_ = banks[sccount[0] % len(banks)]
                sccount[0] += 1
                pb = ptcount[0] % 3
                ptcount[0] += 1
                for i, blk in enumerate(quad):
                    q_ap, q_keys, k_ap, k_keys, v_ap_, v_keys, mi = blk
                    S.op('pe', lambda e, i=i, k_ap=k_ap, q_ap=q_ap, sb_=sb_, mi=mi: e.matmul(
                        PS[sb_][:, i * 128:(i + 1) * 128], lhsT=k_ap, rhs=q_ap, start=True, stop=(mi is None)),
                         reads=list(q_keys) + list(k_keys), writes=[pk(sb_)])
                    if mi is not None:
                        S.op('pe', lambda e, i=i, sb_=sb_, mi=mi: e.matmul(
                            PS[sb_][:, i * 128:(i + 1) * 128], lhsT=idb[:], rhs=maskb[:, mi * 128:(mi + 1) * 128], start=False, stop=True),
                             reads=['idb', 'maskb'], writes=[pk(sb_)])
                p_ap = pt[:, pb * 512: pb * 512 + n * 128]
                S.op('act', lambda e, p_ap=p_ap, sb_=sb_, n=n: e.activation(out=p_ap, in_=PS[sb_][:, 0:n * 128], func=ACT.Exp, scale=SCALE),
                     reads=[pk(sb_)], writes=[('pt', pb)])
                pv = []
                for i, blk in enumerate(quad):
                    gi = bi + i
                    pv.append((i, blk[4], blk[5], gi == 0, gi == nb - 1))
                bi += n
                pend.append(dict(pb=pb, ab=ab, pv=pv, fin=(fin if bi >= nb else None)))
                while len(pend) > att_cfg['depth']:
                    att_flush_one()

        for l in range(n_layers):
            if stopped:
                break
            if stop_spec is not None and ':' in stop_spec:
                stop = stop_spec.split(':')[1] if int(stop_spec.split(':')[0]) == l else None
            x_src = x_in if l == 0 else x1_scr
            x_dst = x1_scr if (l == 0 and n_layers > 1) else out
            w_in = w_in_a if l == 0 else w_in_b
            g_q_mem = 8 + l
            g_k_mem = 10 + l

            Wkv = R[:, 0:8192].rearrange("p (c n) -> p c n", c=8)
            memT = R[:, 8192:8192 + 2048]
            for piece in range(8):
                load_weight_piece(w_kv[l, :, piece * 128:(piece + 1) * 128], 8,
                                  Wkv[:, :, piece * 128:(piece + 1) * 128], ('wkv', piece))
            for mt in range(2):
                norm_transpose_tile(mem_in[mt * 128:(mt + 1) * 128, :], mt, 16 + l * 8,
                                    lambda half, mt=mt: memT[:, half * 1024:(half + 1) * 1024].rearrange(
                                        "p (c k) -> p c k", c=4)[:, :, mt * 128:(mt + 1) * 128],
                                    lambda half, mt=mt: [('memT', mt, half * 4 + j_) for j_ in range(4)], mt)
            ntt_flush()
            S.op('pool', lambda e: e.memset(Vm[:], 1.0), writes=['Vm'])
            for mt in range(2):
                for half in range(2):
                    bank = PJ[half]
                    for c in range(8):
                        S.op('pe', lambda e, c=c, half=half, bank=bank, mt=mt: e.matmul(
                            PS[bank][:, 0:512], lhsT=memT[:, c * 256 + mt * 128: c * 256 + (mt + 1) * 128],
                            rhs=Wkv[:, c, half * 512:(half + 1) * 512], start=(c == 0), stop=(c == 7)),
                            reads=[('memT', mt, c)] + [('wkv', half * 4 + j) for j in range(4)], writes=[pk(bank)])
                bank = PJ[0]
                src = PS[bank][:, 0:512]
                rs_ap, rs_key = rms_stats(src, 4, 128, [pk(bank)], True, 1.0 / E)
                tf3 = tmpf[:, 0:512].rearrange("p (a b) -> p a b", a=4)
                S.op('dve', lambda e, src=src, rs_ap=rs_ap, tf3=tf3: e.tensor_tensor(
                    out=tf3, in0=src.rearrange("p (a b) -> p a b", a=4),
                    in1=rs_ap.unsqueeze(2).to_broadcast([128, 4, 128]), op=ALU.mult),
                    reads=[pk(bank), rs_key], writes=[('tmpf', 0, 0)])
                qn3 = qn[:, 0:512].rearrange("p (a b) -> p a b", a=4)
                gsrc = gains[:, g_k_mem * 128:(g_k_mem + 1) * 128].unsqueeze(1).to_broadcast([128, 4, 128])
                S.op('pool', lambda e, qn3=qn3, tf3=tf3, gsrc=gsrc: e.tensor_tensor(out=qn3, in0=tf3, in1=gsrc, op=ALU.mult),
                     reads=[('tmpf', 0, 0), 'gains'], writes=[('qn', 0, 'a')])
                tb = TP[0]
                for hd in range(4):
                    S.op('pe', lambda e, hd=hd: e.transpose(out=PSB[tb][:, hd * 128:(hd + 1) * 128],
                                                           in_=qn[:, hd * 128:(hd + 1) * 128], identity=idb[:]),
                         reads=[('qn', 0, 'a'), 'idb'], writes=[pk(tb)])
                S.op('act', lambda e, mt=mt: e.activation(
                    out=KmT[:].rearrange("p (h k) -> p h k", h=4)[:, :, mt * 128:(mt + 1) * 128],
                    in_=PSB[tb][:, 0:512].rearrange("p (h k) -> p h k", h=4), func=ACT.Copy),
                    reads=[pk(tb)], writes=['KmT'])
                bank = PJ[1]
                S.op('dve', lambda e, mt=mt, bank=bank: e.tensor_copy(
                    out=Vm[:, mt * 520:(mt + 1) * 520].rearrange("p (h k) -> p h k", h=4)[:, :, 0:128],
                    in_=PS[bank][:, 0:512].rearrange("p (h k) -> p h k", h=4)),
                    reads=[pk(bank)], writes=['Vm'])
            if stop == 'M':
                S.barrier()
                break

            for t in range(NT):
                norm_transpose_tile(x_src[t * 128:(t + 1) * 128, :], t, l * 8,
                                    lambda half, t=t: hT[:, half * 16384:(half + 1) * 16384].rearrange(
                                        "p (c k) -> p c k", c=4)[:, :, t * 128:(t + 1) * 128],
                                    lambda half, t=t: [('hT', t, half * 4 + j_) for j_ in range(4)], t)
            ntt_flush()
            S.barrier()
            if stop == 'p1':
                break

            S.op('pool', lambda e: e.memset(R[:, V_OFF:R_N], 1.0), writes=['Vall'])
            S.barrier()
            if l == 0:
                iters = [dict(cols=[((s_ * 3 + g) * 8 + h) * 128 for s_ in range(3) for g in range(3)], nqk=6, nv=3, hd=h)
                         for h in range(8)]
                gain_ops = [(0, 6, gains[:, 0:768].rearrange("p (a b) -> p a b", a=6))]
            else:
                iters = [dict(cols=[(kvh * 4 + j) * 128 for j in range(4)] + [1024 + kvh * 128, 1280 + kvh * 128], nqk=5, nv=1, hd=kvh)
                         for kvh in range(2)]
                gain_ops = [(0, 4, gains[:, 6 * 128:7 * 128].unsqueeze(1).to_broadcast([128, 4, 128])),
                            (4, 1, gains[:, 7 * 128:8 * 128].unsqueeze(1))]
            att_cfg['banks'] = [0, 1, 2, 3]
            att_cfg['depth'] = 2

            def emit_iter_load(it):
                ncols = (it['nqk'] + it['nv']) * 128
                Wv_ = Wbuf[:, 0:8 * ncols].rearrange("p (c n) -> p c n", c=8)
                for bi_, col in enumerate(it['cols']):
                    load_weight_piece(w_in[:, col:col + 128], 8, Wv_[:, :, bi_ * 128:(bi_ + 1) * 128], ('wbuf', bi_))

            emit_iter_load(iters[0])
            tilecount = 0
            for it_i, it in enumerate(iters):
                nqk, nv = it['nqk'], it['nv']
                nblk = nqk + nv
                ncols = nblk * 128
                Wv = Wbuf[:, 0:8 * ncols].rearrange("p (c n) -> p c n", c=8)
                prev_cx = None
                prev2_cx = None
                for t in range(NT):
                    pbase = 0 if tilecount % 2 == 0 else 3
                    c0 = 0
                    while c0 < ncols:
                        n_ = min(PROJ_N, 512 - (c0 % 512), ncols - c0)
                        bank = pbase + c0 // 512
                        wkeys = [('wbuf', j) for j in range(c0 // 128, (c0 + n_) // 128)]
                        for c in range(8):
                            S.op('pe', lambda e, c=c, t=t, c0=c0, n_=n_, pbase=pbase, Wv=Wv: e.matmul(
                                PSALL[:, pbase * 512 + c0: pbase * 512 + c0 + n_], lhsT=hT_ap(c, t), rhs=Wv[:, c, c0:c0 + n_],
                                start=(c == 0), stop=(c == 7)),
                                reads=[('hT', t, c)] + wkeys, writes=[pk(bank)])
                        c0 += n_
                    if l == 0:
                        rope = (ropeA[:, t * 16:(t + 1) * 16], ropeA[:, 512 + t * 16: 512 + (t + 1) * 16], 16)
                        ropekey = 'ropeA'
                    else:
                        rbuf = tilecount % 2
                        S.op('sp', lambda e, rbuf=rbuf, t=t: e.dma_start(
                            out=ropeBt[:, rbuf * 128:(rbuf + 1) * 128].rearrange("p (a b) -> p a b", a=2),
                            in_=ropeB_in.rearrange("p (a t b) -> p a t b", a=2, t=32)[:, :, t, :]),
                            writes=[('ropeB', rbuf)], dma=('ropeB', rbuf))
                        rope = (ropeBt[:, rbuf * 128: rbuf * 128 + 64], ropeBt[:, rbuf * 128 + 64: rbuf * 128 + 128], 64)
                        ropekey = ('ropeB', rbuf)
                    if prev_cx is not None:
                        proj_tail(prev_cx)
                    cx_ = proj_head(pbase, nqk, nv, gain_ops, t, rope, ropekey, 0, tilecount)
                    if prev2_cx is not None:
                        proj_tail_b(prev2_cx)
                    proj_head2(cx_)
                    prev2_cx = prev_cx
                    prev_cx = cx_
                    tilecount += 1
                proj_tail(prev_cx)
                proj_tail_b(prev2_cx)
                proj_tail_b(prev_cx)
                if it_i + 1 < len(iters):
                    emit_iter_load(iters[it_i + 1])
                if stop == 'p2proj':
                    continue
                hd = it['hd']
                jobs = []
                if l == 0:
                    for T in range(NT):
                        blocks = []
                        for g, (window, dil) in enumerate(A_GROUPS):
                            for dl in range(-9, 10):
                                if (g, dl) not in MASK_TABLE:
                                    continue
                                kb = T + dl
                                if kb < 0 or kb >= NT:
                                    continue
                                blocks.append((qk_ap(g, T * 128), [('qk', g, T)], qk_ap(3 + g, kb * 128), [('qk', 3 + g, kb)],
                                               v_ap(g, kb), [('v', g, kb)], MASK_TABLE[(g, dl)]))
                        jobs.append((blocks, T, hd))
                else:
                    for j in range(4):
                        for T in range(NT):
                            blocks = []
                            for kb in range(NT):
                                blocks.append((qk_ap(j, T * 128), [('qk', j, T)], qk_ap(4, kb * 128), [('qk', 4, kb)],
                                               v_ap(0, kb), [('v', 0, kb)], None))
                            jobs.append((blocks, T, hd * 4 + j))
                for blocks, T, hcol in jobs:
                    def fin(ab, T=T, hcol=hcol):
                        ob_i = fcount[0] % 4
                        fcount[0] += 1
                        rc_ap = rc[:, ob_i:ob_i + 1]
                        ob_ap = ob[:, ob_i * 128:(ob_i + 1) * 128]
                        S.op('dve', lambda e: e.reciprocal(out=rc_ap, in_=PS[ab][:, 128:129]),
                             reads=[pk(ab)], writes=[('rc', ob_i)])
                        S.op('dve', lambda e: e.tensor_scalar(out=ob_ap, in0=PS[ab][:, 0:128], scalar1=rc_ap, scalar2=None,
                                                              op0=ALU.mult),
                             reads=[pk(ab), ('rc', ob_i)], writes=[('ob', ob_i)])
                        S.op('sp', lambda e: e.dma_start(out=o_scr[T * 128:(T + 1) * 128, hcol * 128:(hcol + 1) * 128], in_=ob_ap),
                             reads=[('ob', ob_i)], writes=[('oscr', T)], dma=('ob', ob_i))
                    attention_job(blocks, fin)
                att_flush_all()
            att_cfg['banks'] = [2, 3]
            att_cfg['depth'] = 1
            S.barrier()
            if stop in ('p2', 'p2proj'):
                break

            Wg = R[:, 0:16384].rearrange("p (c n) -> p c n", c=8)
            Wo = R[:, 16384:28672].rearrange("p (c n) -> p c n", c=12)
            gcol0 = 9216 if l == 0 else 1536
            for piece in range(16):
                load_weight_piece(w_in[:, gcol0 + piece * 128: gcol0 + (piece + 1) * 128], 8,
                                  Wg[:, :, piece * 128:(piece + 1) * 128], ('wg', piece))
            for piece in range(8):
                load_weight_piece(w_out[l, 0:1024, piece * 128:(piece + 1) * 128], 8,
                                  Wo[:, 0:8, piece * 128:(piece + 1) * 128], ('wo', 0, piece))
                load_weight_piece(w_out[l, 1024:1536, piece * 128:(piece + 1) * 128], 4,
                                  Wo[:, 8:12, piece * 128:(piece + 1) * 128], ('wo', 1, piece))
            wg_keys = [('wg', p_) for p_ in range(16)]
            wo_keys = [('wo', a_, p_) for a_ in range(2) for p_ in range(8)]
            KmT3 = KmT[:].rearrange("p (h k) -> p h k", h=4)
            att_cfg['banks'] = [2, 0, 5]
            att_cfg['depth'] = 2
            att_cfg['oa'] = [4, 7]

            def gate_chain(t, j, gb, o_j, o_keys):
                th_ap = th[:, j * 512:(j + 1) * 512]
                S.op('act', lambda e: e.activation(out=th_ap, in_=PS[gb][:, 0:512], func=ACT.Tanh, scale=0.5),
                     reads=[pk(gb)], writes=[('th', j)])
                S.op('dve', lambda e: e.scalar_tensor_tensor(out=th_ap, in0=th_ap, scalar=1.0, in1=PS[gb][:, 0:512],
                                                             op0=ALU.add, op1=ALU.mult),
                     reads=[pk(gb), ('th', j)], writes=[('th', j)])
                yb_ap = yb[:, j * 512:(j + 1) * 512]
                S.op('dve', lambda e: e.scalar_tensor_tensor(out=yb_ap, in0=th_ap, scalar=0.5, in1=o_j, op0=ALU.mult, op1=ALU.mult),
                     reads=[('th', j)] + o_keys, writes=[('yb', j)])

            def gate_proj(t, j, gb):
                for c in range(8):
                    S.op('pe', lambda e, c=c: e.matmul(PS[gb][:, 0:512], lhsT=hT_ap(c, t), rhs=Wg[:, c, 512 + j * 512: 1024 + j * 512],
                                                       start=(c == 0), stop=(c == 7)),
                         reads=[('hT', t, c)] + wg_keys[4 + 4 * j: 8 + 4 * j], writes=[pk(gb)])

            def y_transposes(t, j, tpb, eng):
                b = t % 2
                yb_ap = yb[:, j * 512:(j + 1) * 512]
                yTb = yT[:, b * 1536:(b + 1) * 1536]
                for i in range(4):
                    S.op('pe', lambda e, i=i: e.transpose(out=PSB[tpb][:, i * 128:(i + 1) * 128],
                                                         in_=yb_ap[:, i * 128:(i + 1) * 128], identity=idb[:]),
                         reads=[('yb', j), 'idb'], writes=[pk(tpb)])
                if eng == 'act':
                    S.op('act', lambda e: e.activation(out=yTb[:, j * 512:(j + 1) * 512], in_=PSB[tpb][:, 0:512], func=ACT.Copy),
                         reads=[pk(tpb)], writes=[('yT', b, j)])
                else:
                    S.op('dve', lambda e: e.tensor_copy(out=yTb[:, j * 512:(j + 1) * 512], in_=PSB[tpb][:, 0:512]),
                         reads=[pk(tpb)], writes=[('yT', b, j)])

            def stage_A(t):
                b = t % 2
                xt = xt_ap(b)
                ot_ap = ot[:, b * 1024:(b + 1) * 1024]
                S.op('sp', lambda e, x_src=x_src: e.dma_start(out=xt, in_=x_src[t * 128:(t + 1) * 128, :]),
                     writes=[('xt', b)], dma=('xt', b))
                S.op('sp', lambda e: e.dma_start(out=ot_ap, in_=o_scr[t * 128:(t + 1) * 128, :]),
                     reads=[('oscr', t)], writes=[('ot', b)], dma=('ot', b))
                bank = 0
                for c in range(8):
                    S.op('pe', lambda e, c=c: e.matmul(PS[bank][:, 0:512], lhsT=hT_ap(c, t), rhs=Wg[:, c, 0:512],
                                                       start=(c == 0), stop=(c == 7)),
                         reads=[('hT', t, c)] + wg_keys[0:4], writes=[pk(bank)])
                gate_proj(t, 0, 1)
                gate_proj(t, 1, 3)
                src = PS[bank][:, 0:512]
                rs_ap, rs_key = rms_stats(src, 4, 128, [pk(bank)], True, 1.0 / E)
                tf3 = tmpf[:, b * 768: b * 768 + 512].rearrange("p (a b) -> p a b", a=4)
                S.op('dve', lambda e: e.tensor_tensor(out=tf3, in0=src.rearrange("p (a b) -> p a b", a=4),
                                                      in1=rs_ap.unsqueeze(2).to_broadcast([128, 4, 128]), op=ALU.mult),
                     reads=[pk(bank), rs_key], writes=[('tmpf', b, 0)])
                qn3 = qn[:, b * 768: b * 768 + 512].rearrange("p (a b) -> p a b", a=4)
                gsrc = gains[:, g_q_mem * 128:(g_q_mem + 1) * 128].unsqueeze(1).to_broadcast([128, 4, 128])
                S.op('pool', lambda e: e.tensor_tensor(out=qn3, in0=tf3, in1=gsrc, op=ALU.mult),
                     reads=[('tmpf', b, 0), 'gains'], writes=[('qn', b, 'a')])
                gate_chain(t, 0, 1, ot_ap[:, 0:512], [('ot', b)])
                gate_chain(t, 1, 3, ot_ap[:, 512:1024], [('ot', b)])

            def stage_B(t):
                b = t % 2
                qn_ap = qn[:, b * 768: b * 768 + 512]
                tb = 6
                for hd in range(4):
                    S.op('pe', lambda e, hd=hd: e.transpose(out=PSB[tb][:, hd * 128:(hd + 1) * 128],
                                                           in_=qn_ap[:, hd * 128:(hd + 1) * 128], identity=idb[:]),
                         reads=[('qn', b, 'a'), 'idb'], writes=[pk(tb)])
                S.op('act', lambda e: e.activation(out=qmT[:], in_=PSB[tb][:, 0:512], func=ACT.Copy),
                     reads=[pk(tb)], writes=['qmT'])
                y_transposes(t, 0, 7, 'dve')
                gate_proj(t, 2, 1)
                th2 = th[:, 1024:1536]
                S.op('act', lambda e: e.activation(out=th2, in_=PS[1][:, 0:512], func=ACT.Tanh, scale=0.5),
                     reads=[pk(1)], writes=[('th', 2)])
                S.op('dve', lambda e: e.scalar_tensor_tensor(out=th2, in0=th2, scalar=1.0, in1=PS[1][:, 0:512],
                                                             op0=ALU.add, op1=ALU.mult),
                     reads=[pk(1), ('th', 2)], writes=[('th', 2)])
                y_transposes(t, 1, 6, 'act')
                for hd in range(4):
                    blocks = []
                    for kb in range(2):
                        blocks.append((qmT[:, hd * 128:(hd + 1) * 128], ['qmT'], KmT3[:, hd, kb * 128:(kb + 1) * 128], ['KmT'],
                                       Vm[:, (kb * 4 + hd) * 130:(kb * 4 + hd) * 130 + 129], ['Vm'], None))

                    def fin(ab, hd=hd):
                        ob_i = fcount[0] % 4
                        fcount[0] += 1
                        rc_ap = rc[:, 4 + ob_i:5 + ob_i]
                        S.op('dve', lambda e: e.reciprocal(out=rc_ap, in_=PS[ab][:, 128:129]),
                             reads=[pk(ab)], writes=[('rcm', ob_i)])
                        S.op('act', lambda e: e.activation(out=om[:, hd * 128:(hd + 1) * 128], in_=PS[ab][:, 0:128], func=ACT.Copy,
                                                           scale=rc_ap),
                             reads=[pk(ab), ('rcm', ob_i)], writes=[('om', hd)])
                    attention_job(blocks, fin)
                att_flush_all()
                yb2 = yb[:, 1024:1536]
                S.op('dve', lambda e: e.scalar_tensor_tensor(out=yb2, in0=th2, scalar=0.5, in1=om[:, 0:512], op0=ALU.mult, op1=ALU.mult),
                     reads=[('th', 2)] + [('om', hd) for hd in range(4)], writes=[('yb', 2)])

            def stage_B2(t):
                y_transposes(t, 2, 7, 'act')

            def stage_C(t):
                b = t % 2
                xt = xt_ap(b)
                x1t = x1t_ap(b)
                yTb = yT[:, b * 1536:(b + 1) * 1536]
                for nb_ in range(2):
                    obk = [2, 0][nb_]
                    for cc in range(12):
                        S.op('pe', lambda e, cc=cc, nb_=nb_, obk=obk: e.matmul(
                            PS[obk][:, 0:512], lhsT=yTb[:, cc * 128:(cc + 1) * 128], rhs=Wo[:, cc, nb_ * 512:(nb_ + 1) * 512],
                            start=(cc == 0), stop=(cc == 11)),
                            reads=[('yT', b, cc // 4)] + wo_keys, writes=[pk(obk)])
                    S.op('dve', lambda e, nb_=nb_, obk=obk: e.tensor_tensor(
                        out=x1t[:, nb_ * 512:(nb_ + 1) * 512], in0=PS[obk][:, 0:512], in1=xt[:, nb_ * 512:(nb_ + 1) * 512], op=ALU.add),
                        reads=[pk(obk), ('xt', b)], writes=[('x1t', b, nb_)])
                S.op('sp', lambda e, x_dst=x_dst: e.dma_start(out=x_dst[t * 128:(t + 1) * 128, :], in_=x1t),
                     reads=[('x1t', b, 0), ('x1t', b, 1)], writes=[('xdst', t)], dma=('x1t', b))

            def _bind(f, **kw):
                return f

            if stop != 'p3w':
                stage_A(0)
                for t in range(NT):
                    stage_B(t)
                    if t + 1 < NT:
                        stage_A(t + 1)
                    stage_B2(t)
                    stage_C(t)
            att_cfg['oa'] = OA
            S.barrier()
        S.emit(nc)
    return nc, S


def _host_consts(norm_g, mem_norm_g, mem_qn_g, mem_kn_g, qn_a, kn_a, qn_b, kn_b):
    ng = np.zeros((128, 32), np.float32)
    for l in range(2):
        ng[:, l * 8:(l + 1) * 8] = norm_g[l].reshape(8, 128).T
        ng[:, 16 + l * 8:16 + (l + 1) * 8] = mem_norm_g[l].reshape(8, 128).T
    rows = [qn_a[0, 0], qn_a[0, 1], qn_a[0, 2], kn_a[0, 0], kn_a[0, 1], kn_a[0, 2], qn_b[0], kn_b[0],
            mem_qn_g[0], mem_qn_g[1], mem_kn_g[0], mem_kn_g[1]]
    gains = np.ascontiguousarray(np.broadcast_to(np.concatenate(rows)[None, :], (128, 12 * 128))).astype(np.float32)
    pos = np.arange(S_TOK, dtype=np.float32)
    invA = (np.float32(500000.0) ** (-np.arange(0, 32, 2, dtype=np.float32) / np.float32(32))).astype(np.float32)
    angA = pos[:, None] * invA[None, :]
    row = np.repeat(np.arange(64, dtype=np.float32), 64)
    col = np.tile(np.arange(64, dtype=np.float32), 64)
    invB = (np.float32(10000.0) ** (-np.arange(0, 64, 2, dtype=np.float32) / np.float32(64))).astype(np.float32)
    angB = np.concatenate([row[:, None] * invB[None, :], col[:, None] * invB[None, :]], axis=-1)

    def lay(a):
        return a.reshape(32, 128, -1).transpose(1, 0, 2)
    ropeA = np.stack([lay(np.cos(angA)), lay(np.sin(angA))], axis=1).reshape(128, -1).astype(np.float32)
    ropeB = np.stack([lay(np.cos(angB)), lay(np.sin(angB))], axis=1).reshape(128, -1).astype(np.float32)
    return dict(ng=ng, gains=gains, ropeA=np.ascontiguousarray(ropeA), ropeB=np.ascontiguousarray(ropeB),
                ident=np.eye(128, dtype=np.float32), masks=np.ascontiguousarray(MASKS_NP.reshape(128, -1)))


_NC_CACHE = {}


def kernel(x, mem, norm_g, mem_norm_g, w_mem_kv, mem_qn_g, mem_kn_g, w_out,
           w_in_a, qn_a, kn_a, w_in_b, qn_b, kn_b):
    f = lambda a: np.ascontiguousarray(np.asarray(a, dtype=np.float32))
    x = f(x)
    mem = f(mem)
    consts = _host_consts(f(norm_g), f(mem_norm_g), f(mem_qn_g), f(mem_kn_g), f(qn_a), f(kn_a), f(qn_b), f(kn_b))
    shared = dict(w_in_a=f(w_in_a)[0], w_in_b=f(w_in_b)[0], w_out=f(w_out), w_mem_kv=f(w_mem_kv), **consts)
    if 'nc' not in _NC_CACHE:
        _NC_CACHE['nc'] = build()[0]
    nc = _NC_CACHE['nc']
    n = x.shape[0]
    in_maps = [dict(x=x[b], mem=mem[b], **shared) for b in range(n)]
    res = run_bass_kernel_spmd(nc, in_maps, core_ids=list(range(n)))
    return np.stack([np.asarray(r["y_out"], dtype=np.float32) for r in res.results], axis=0)
```

```python
import numpy as np
import concourse.bass as bass
import concourse.mybir as mybir
from concourse.bass_utils import run_bass_kernel_spmd

F32 = mybir.dt.float32
BF16 = mybir.dt.bfloat16
ALU = mybir.AluOpType
ACT = mybir.ActivationFunctionType
AX = mybir.AxisListType

ENGS = ['pe', 'act', 'dve', 'pool', 'sp']

S_TOK = 4096
NT = 32
D = 1024
E = 128
EPS = 1e-6
SCALE = float(E) ** -0.5
A_GROUPS = ((128, 1), (512, 4), (2048, 16))
IN_A = 11264
IN_B = 3584

PROJ_N = 512


class Sched:
    def __init__(self):
        self.ops = []
        self.state = {}
        self.last_on = {e: None for e in ENGS}
        self.dma_last = {}

    def op(self, eng, fn, reads=(), writes=(), dma=None):
        i = len(self.ops)
        deps = set()
        for k in reads:
            excl = isinstance(k, tuple) and k[0] == 'ps'
            st = self.state.setdefault(k, [None, []])
            if st[0] is not None:
                deps.add(st[0])
            if excl:
                for r in st[1]:
                    if self.ops[r]['eng'] != eng:
                        deps.add(r)
        for k in writes:
            st = self.state.setdefault(k, [None, []])
            if st[0] is not None:
                deps.add(st[0])
            for r in st[1]:
                deps.add(r)
        for k in reads:
            self.state[k][1].append(i)
        for k in writes:
            self.state[k] = [i, []]
        if dma is not None:
            p = self.dma_last.get(dma)
            if p is not None:
                deps.add(p)
            self.dma_last[dma] = i
        deps.discard(i)
        if eng == 'pe':
            deps = {d for d in deps if self.ops[d]['eng'] != 'pe'}
        best = {}
        keep = set()
        for d in deps:
            od = self.ops[d]
            if od['dma'] is not None:
                keep.add(d)
            elif d > best.get(od['eng'], -1):
                best[od['eng']] = d
        deps = keep | set(best.values())
        self.ops.append(dict(eng=eng, fn=fn, deps=deps, dma=dma))
        if fn is not None:
            self.last_on[eng] = i
        return i

    def barrier(self):
        lasts = [v for v in self.last_on.values() if v is not None]
        lasts += list(self.dma_last.values())
        lasts = set(lasts)
        for e in ENGS:
            deps = {d for d in lasts if not (self.ops[d]['eng'] == e and self.ops[d]['dma'] is None)}
            self.ops.append(dict(eng=e, fn=None, deps=deps, dma=None))
        self.state = {}

    def emit(self, nc):
        ops = self.ops
        needed = set()
        for o in ops:
            needed |= o['deps']
        dma_keys = []
        seen = set()
        for o in ops:
            if o['dma'] is not None and o['dma'] not in seen:
                seen.add(o['dma'])
                dma_keys.append(o['dma'])
        sem_ctx = []
        sems = {}
        for n_i, name in enumerate(['pe', 'act', 'dve', 'pool'] + [('dma', k) for k in dma_keys]):
            cm = nc.semaphore("s%d" % n_i)
            sems[name] = cm.__enter__()
            sem_ctx.append(cm)
        cnt = {}
        for i, o in enumerate(ops):
            if o['dma'] is not None:
                key = ('dma', o['dma'])
                cnt[key] = cnt.get(key, 0) + 16
                o['sem'], o['val'], o['inc'] = key, cnt[key], 16
            elif i in needed:
                assert o['fn'] is not None
                key = o['eng']
                cnt[key] = cnt.get(key, 0) + 1
                o['sem'], o['val'], o['inc'] = key, cnt[key], 1
            else:
                o['sem'] = None
        self.maxval = dict(cnt)

        def run(eng_name, e):
            waited = {}
            for o in ops:
                if o['eng'] != eng_name:
                    continue
                need = {}
                for d in o['deps']:
                    od = ops[d]
                    s, v = od['sem'], od['val']
                    if v > need.get(s, 0):
                        need[s] = v
                for s, v in need.items():
                    if waited.get(s, 0) < v:
                        e.wait_ge(sems[s], v)
                        waited[s] = v
                if o['fn'] is None:
                    continue
                ins = o['fn'](e)
                if o['sem'] is not None:
                    ins.then_inc(sems[o['sem']], o['inc'])

        with nc.Block() as block:
            @block.tensor
            def _(e):
                run('pe', e)

            @block.scalar
            def _(e):
                run('act', e)

            @block.vector
            def _(e):
                run('dve', e)

            @block.gpsimd
            def _(e):
                run('pool', e)

            @block.sync
            def _(e):
                run('sp', e)
        for cm in reversed(sem_ctx):
            cm.__exit__(None, None, None)


def _mask_tables():
    masks = []
    index = {}
    table = {}
    kk = np.arange(128)[:, None]
    qq = np.arange(128)[None, :]
    for g, (window, dil) in enumerate(A_GROUPS):
        hw = window // 2
        dmax = hw // 128 + (1 if hw % 128 else 0)
        dmax = max(dmax, 1)
        for dl in range(-dmax, dmax + 1):
            diff = 128 * dl + kk - qq
            m = ((diff % dil) == 0) & (np.abs(diff) <= hw)
            if not m.any():
                continue
            key = m.tobytes()
            if key not in index:
                index[key] = len(masks)
                masks.append(m.astype(np.float32))
            table[(g, dl)] = index[key]
    return np.stack(masks, axis=1), table


MASKS_NP, MASK_TABLE = _mask_tables()
NM = MASKS_NP.shape[1]


def build(n_layers=2, debug=False, stop=None):
    stop_spec = stop
    nc = bass.Bass("TRN2", target_bir_lowering=False)
    dk = "ExternalOutput" if debug else "Internal"
    x_in = nc.dram_tensor("x", [S_TOK, D], F32, kind="ExternalInput").ap()
    mem_in = nc.dram_tensor("mem", [256, D], F32, kind="ExternalInput").ap()
    w_in_a = nc.dram_tensor("w_in_a", [D, IN_A], F32, kind="ExternalInput").ap()
    w_in_b = nc.dram_tensor("w_in_b", [D, IN_B], F32, kind="ExternalInput").ap()
    w_out = nc.dram_tensor("w_out", [2, 1536, D], F32, kind="ExternalInput").ap()
    w_kv = nc.dram_tensor("w_mem_kv", [2, D, 1024], F32, kind="ExternalInput").ap()
    ng_in = nc.dram_tensor("ng", [128, 32], F32, kind="ExternalInput").ap()
    gains_in = nc.dram_tensor("gains", [128, 12 * 128], F32, kind="ExternalInput").ap()
    ropeA_in = nc.dram_tensor("ropeA", [128, 2 * 32 * 16], F32, kind="ExternalInput").ap()
    ropeB_in = nc.dram_tensor("ropeB", [128, 2 * 32 * 64], F32, kind="ExternalInput").ap()
    ident_in = nc.dram_tensor("ident", [128, 128], F32, kind="ExternalInput").ap()
    masks_in = nc.dram_tensor("masks", [128, NM * 128], F32, kind="ExternalInput").ap()
    out = nc.dram_tensor("y_out", [S_TOK, D], F32, kind="ExternalOutput").ap()
    x1_scr = nc.dram_tensor("x1_scr", [S_TOK, D], F32, kind=dk).ap()
    o_scr = nc.dram_tensor("o_scr", [S_TOK, D], BF16, kind=dk).ap()

    S = Sched()
    R_N = 6 * 4096 + 3 * 32 * 130
    import contextlib
    with contextlib.ExitStack() as es:
        def sb(name, shape, dt):
            return es.enter_context(nc.sbuf_tensor(name, shape, dt))

        hT = sb("hT", [128, 8 * 4096], BF16)
        R = sb("R", [128, R_N], BF16)
        Wbuf = sb("Wbuf", [128, 8 * 1152], BF16)
        Wbf = Wbuf.bitcast(F32)
        Wst = sb("Wst", [128, 2 * 8 * 128], F32)
        ropeA = sb("ropeA_t", [128, 2 * 32 * 16], F32)
        ropeBt = ropeA[:, 0:256]
        gains = sb("gains_t", [128, 12 * 128], F32)
        maskb = sb("maskb", [128, NM * 128], BF16)
        idf = sb("idf", [128, 128], F32)
        idb = sb("idb", [128, 128], BF16)
        ngt = sb("ngt", [128, 32], F32)
        mh = sb("mh", [128, 16], F32)
        KmT = sb("KmT", [128, 4 * 256], BF16)
        Vm = sb("Vm", [128, 2 * 4 * 130], BF16)
        ssA = sb("ssA", [128, 2 * 8], F32)
        msA = sb("msA", [128, 2 * 8], F32)
        rsA = sb("rsA", [128, 2 * 8], F32)
        sq = sb("sq", [128, 2 * 768], F32)
        tmpf = sb("tmpf", [128, 2 * 768], F32)
        U = sb("U", [128, 2560], F32)
        Ub = U.bitcast(BF16)
        qn = sb("qn", [128, 2 * 768], BF16)
        pt = sb("pt", [128, 3 * 512], BF16)
        rc = sb("rc", [128, 8], F32)
        ob = Ub[:, 3072:3584]
        th = Wbf[:, 0:1536]
        ot = Ub[:, 3072:5120]
        Wbb = Wbuf
        om = Wbb[:, 3072:3584]
        yb = Wbb[:, 3584:5120]
        yT = Ub[:, 0:3072]
        qmT = Wbb[:, 5120:5632]

        PSALL = es.enter_context(nc.psum_tensor("psall", [128, 4096], F32))
        PSBALL = PSALL.bitcast(BF16)
        PS = [PSALL[:, i * 512:(i + 1) * 512] for i in range(8)]
        PSB = [PSBALL[:, i * 1024:(i + 1) * 1024] for i in range(8)]
        PJ = [0, 1]
        SC = [2, 3]
        OA = [4, 5]
        TP = [6, 7]

        def pk(i):
            return ('ps', i)

        Rf = R.bitcast(F32)
        QK_OFF = 0
        V_OFF = 6 * 4096
        XT_OFF_F = 28672 // 2
        def xt_ap(b):
            return Rf[:, XT_OFF_F + b * 1024: XT_OFF_F + (b + 1) * 1024]

        def x1t_ap(b):
            return Rf[:, XT_OFF_F + 2048 + b * 1024: XT_OFF_F + 2048 + (b + 1) * 1024]

        def qk_ap(slot, t0, n=128):
            o = QK_OFF + slot * 4096 + t0
            return R[:, o:o + n]

        def v_ap(slot, kb, n=129):
            o = V_OFF + (slot * 32 + kb) * 130
            return R[:, o:o + n]

        def hT_ap(c, t):
            o = c * 4096 + t * 128
            return hT[:, o:o + 128]

        def wst_ap(slot, nchunk=8):
            return Wst[:, slot * 1024: slot * 1024 + nchunk * 128].rearrange("p (c n) -> p c n", n=128)

        S.op('sp', lambda e: e.dma_start(out=idf[:], in_=ident_in), writes=['idf'], dma='c0')
        S.op('sp', lambda e: e.dma_start(out=gains[:], in_=gains_in), writes=['gains'], dma='c1')
        S.op('sp', lambda e: e.dma_start(out=ngt[:], in_=ng_in), writes=['ngt'], dma='c2')
        S.op('sp', lambda e: e.dma_start(out=ropeA[:], in_=ropeA_in), writes=['ropeA'], dma='c3')
        S.op('sp', lambda e: e.dma_start(out=Rf[:, 0:NM * 128], in_=masks_in), writes=['mstage'], dma='c4')
        S.op('pool', lambda e: e.tensor_copy(out=idb[:], in_=idf[:]), reads=['idf'], writes=['idb'])
        S.op('pool', lambda e: e.tensor_scalar(out=maskb[:], in0=Rf[:, 0:NM * 128], scalar1=-1.0, scalar2=30000.0,
                                               op0=ALU.add, op1=ALU.mult), reads=['mstage'], writes=['maskb'])
        S.op('pool', lambda e: e.memset(mh[:, 0:8], -0.5), writes=['mh'])
        S.op('pool', lambda e: e.memset(mh[:, 8:16], EPS), writes=['mhe'])
        S.barrier()
        stopped = (stop == 'init')

        wcount = [0]

        def load_weight_piece(src_ap, nchunk, dst_ap, dst_key):
            slot = wcount[0] % 2
            wcount[0] += 1
            S.op('sp', lambda e: e.dma_start(out=wst_ap(slot, nchunk), in_=src_ap.rearrange("(c p) n -> p c n", p=128)),
                 writes=[('wst', slot)], dma=('wst', slot))
            S.op('pool', lambda e: e.tensor_copy(out=dst_ap, in_=wst_ap(slot, nchunk)),
                 reads=[('wst', slot)], writes=[dst_key])

        rcount = [0]

        def load_weight_rows(src_ap, dst_ap, dst_key):
            slot = wcount[0] % 2
            wcount[0] += 1
            eng = 'dve' if rcount[0] % 2 == 0 else 'act'
            rcount[0] += 1
            st = Wst[:, slot * 1024:(slot + 1) * 1024]
            S.op('sp', lambda e: e.dma_start(out=st, in_=src_ap), writes=[('wst', slot)], dma=('wst', slot))
            if eng == 'dve':
                S.op('dve', lambda e: e.tensor_copy(out=dst_ap, in_=st), reads=[('wst', slot)], writes=[dst_key])
            else:
                S.op('act', lambda e: e.activation(out=dst_ap, in_=st, func=ACT.Copy), reads=[('wst', slot)], writes=[dst_key])

        sidx = [0]

        def rms_stats(src_ap, n_groups, glen, src_keys, src_is_psum, inv_n, use_ln=False):
            b = sidx[0] % 2
            sidx[0] += 1
            n = n_groups * glen
            sq_ap = sq[:, b * 768: b * 768 + n]
            ss_ap = ssA[:, b * 8: b * 8 + n_groups]
            ms_ap = msA[:, b * 8: b * 8 + n_groups]
            rs_ap = rsA[:, b * 8: b * 8 + n_groups]
            S.op('act', lambda e: e.activation(out=sq_ap, in_=src_ap, func=ACT.Square),
                 reads=src_keys, writes=[('sq', b), ('sqb', b)])
            S.op('dve', lambda e: e.tensor_reduce(out=ss_ap, in_=sq_ap.rearrange("p (a b) -> p a b", a=n_groups),
                                                  axis=AX.X, op=ALU.add),
                 reads=[('sq', b), ('sqb', b)], writes=[('ss', b)])
            if use_ln:
                S.op('act', lambda e: e.activation(out=ms_ap, in_=ss_ap, func=ACT.Ln, scale=inv_n, bias=mh[:, 8:9]),
                     reads=[('ss', b), 'mhe'], writes=[('ms', b)])
                S.op('act', lambda e: e.activation(out=rs_ap, in_=ms_ap, func=ACT.Exp, scale=-0.5),
                     reads=[('ms', b)], writes=[('rs', b)])
                return rs_ap, ('rs', b)
            S.op('dve', lambda e: e.tensor_scalar(out=ms_ap, in0=ss_ap, scalar1=inv_n, scalar2=EPS,
                                                  op0=ALU.mult, op1=ALU.add),
                 reads=[('ss', b)], writes=[('ms', b)])
            S.op('pool', lambda e: e.tensor_tensor(out=rs_ap, in0=ms_ap, in1=mh[:, 0:n_groups], op=ALU.pow),
                 reads=[('ms', b), 'mh'], writes=[('rs', b)])
            return rs_ap, ('rs', b)

        ntt_pending = [None]

        def ntt_flush():
            if ntt_pending[0] is not None:
                ntt_pending[0]()
                ntt_pending[0] = None

        def norm_transpose_tile(src_dram_ap, t, gcol, dst_fn, dst_keys, tcount):
            b = tcount % 2
            xt = xt_ap(b)
            S.op('sp', lambda e: e.dma_start(out=xt, in_=src_dram_ap), writes=[('xt', b)], dma=('xt', b))
            ss_ap = ssA[:, b * 8: b * 8 + 1]
            ms_ap = msA[:, b * 8: b * 8 + 1]
            rs_ap = rsA[:, b * 8: b * 8 + 1]
            S.op('act', lambda e: e.activation(out=sq[:, 0:1024], in_=xt, func=ACT.Square, accum_out=ss_ap),
                 reads=[('xt', b)], writes=[('sq', 0), ('sq', 1), ('sqb', 0), ('sqb', 1), ('ss', b)])
            S.op('act', lambda e: e.activation(out=ms_ap, in_=ss_ap, func=ACT.Ln, scale=1.0 / D, bias=mh[:, 8:9]),
                 reads=[('ss', b), 'mhe'], writes=[('ms', b)])
            S.op('act', lambda e: e.activation(out=rs_ap, in_=ms_ap, func=ACT.Exp, scale=-0.5),
                 reads=[('ms', b)], writes=[('rs', b)])
            S.op('dve', lambda e: e.tensor_scalar(out=xt, in0=xt, scalar1=rs_ap, scalar2=None, op0=ALU.mult),
                 reads=[('xt', b), ('rs', b)], writes=[('xt', b)])

            def tail():
                for half in range(2):
                    bank = (0 if b == 0 else 2) + half
                    for j in range(4):
                        c = half * 4 + j
                        S.op('pe', lambda e, c=c, j=j, bank=bank: e.transpose(out=PS[bank][:, j * 128:(j + 1) * 128],
                                                                              in_=xt[:, c * 128:(c + 1) * 128], identity=idf[:]),
                             reads=[('xt', b), 'idf'], writes=[pk(bank)])
                    gsrc = ngt[:, gcol + half * 4: gcol + half * 4 + 4].unsqueeze(2).to_broadcast([128, 4, 128])
                    S.op('dve', lambda e, half=half, bank=bank, gsrc=gsrc: e.tensor_tensor(
                        out=dst_fn(half), in0=PS[bank][:, 0:512].rearrange("p (c k) -> p c k", c=4), in1=gsrc, op=ALU.mult),
                        reads=[pk(bank), 'ngt'], writes=dst_keys(half))
            ntt_flush()
            ntt_pending[0] = tail

        def proj_head(pbase, nqk, nv, gain_ops, t, rope, ropekey, vslot0, ucount):
            banks = sorted(set((pbase * 512 + i * 128) // 512 for i in range(nqk + nv)))
            bkeys = [pk(bk) for bk in banks]
            c0 = pbase * 512
            src = PSALL[:, c0:c0 + nqk * 128]
            for vi in range(nv):
                S.op('act', lambda e, vi=vi: e.activation(out=v_ap(vslot0 + vi, t, 128),
                                                          in_=PSALL[:, c0 + (nqk + vi) * 128: c0 + (nqk + vi + 1) * 128], func=ACT.Copy),
                     reads=bkeys, writes=[('v', vslot0 + vi, t)])
            sb_i = sidx[0] % 2
            sidx[0] += 1
            n = nqk * 128
            sq_ap = sq[:, sb_i * 768: sb_i * 768 + n]
            ss_ap = ssA[:, sb_i * 8: sb_i * 8 + nqk]
            S.op('act', lambda e: e.activation(out=sq_ap, in_=src, func=ACT.Square),
                 reads=bkeys, writes=[('sq', sb_i), ('sqb', sb_i)])
            S.op('dve', lambda e: e.tensor_reduce(out=ss_ap, in_=sq_ap.rearrange("p (a b) -> p a b", a=nqk),
                                                  axis=AX.X, op=ALU.add),
                 reads=[('sq', sb_i), ('sqb', sb_i)], writes=[('ss', sb_i)])
            b = ucount % 2
            tf = tmpf[:, b * 768: b * 768 + nqk * 128]
            tf3 = tf.rearrange("p (a b) -> p a b", a=nqk)
            TK = [('tmpf', b)]
            return dict(nqk=nqk, gain_ops=gain_ops, t=t, rope=rope, ropekey=ropekey, ucount=ucount, b=b, tf3=tf3, TK=TK,
                        sb_i=sb_i, src=src, bkeys=bkeys)

        def proj_head2(cx):
            nqk, sb_i, src, bkeys, tf3, TK = (cx[k_] for k_ in ('nqk', 'sb_i', 'src', 'bkeys', 'tf3', 'TK'))
            ss_ap = ssA[:, sb_i * 8: sb_i * 8 + nqk]
            ms_ap = msA[:, sb_i * 8: sb_i * 8 + nqk]
            rs_ap = rsA[:, sb_i * 8: sb_i * 8 + nqk]
            S.op('act', lambda e: e.activation(out=ms_ap, in_=ss_ap, func=ACT.Ln, scale=1.0 / E, bias=mh[:, 8:9]),
                 reads=[('ss', sb_i), 'mhe'], writes=[('ms', sb_i)])
            S.op('act', lambda e: e.activation(out=rs_ap, in_=ms_ap, func=ACT.Exp, scale=-0.5),
                 reads=[('ms', sb_i)], writes=[('rs', sb_i)])
            S.op('dve', lambda e: e.tensor_tensor(out=tf3, in0=src.rearrange("p (a b) -> p a b", a=nqk),
                                                  in1=rs_ap.unsqueeze(2).to_broadcast([128, nqk, 128]), op=ALU.mult),
                 reads=bkeys + [('rs', sb_i)], writes=TK)

        def proj_tail(cx):
            nqk, gain_ops, t, rope, ropekey, ucount, b, tf3, TK = (cx[k_] for k_ in
                                                                   ('nqk', 'gain_ops', 't', 'rope', 'ropekey', 'ucount', 'b', 'tf3', 'TK'))
            b3 = ucount % 3
            qn_ap = qn[:, b3 * 768: b3 * 768 + nqk * 128] if b3 < 2 else Ub[:, 3584:3584 + nqk * 128]
            qn3 = qn_ap.rearrange("p (a b) -> p a b", a=nqk)
            QA, QB, QC = ('qn3', b3, 'a'), ('qn3', b3, 'b'), ('qn3', b3, 'c')
            cx['qn_ap'] = qn_ap
            cx['QK3'] = [QA, QB, QC]
            for (b0, nb_, gsrc) in gain_ops:
                S.op('dve', lambda e, b0=b0, nb_=nb_, gsrc=gsrc: e.tensor_tensor(out=tf3[:, b0:b0 + nb_, :], in0=tf3[:, b0:b0 + nb_, :],
                                                                            in1=gsrc, op=ALU.mult),
                     reads=TK + ['gains'], writes=TK)
            cos_ap, sin_ap, Rr = rope
            cosb = cos_ap.unsqueeze(1).to_broadcast([128, nqk, Rr])
            sinb = sin_ap.unsqueeze(1).to_broadcast([128, nqk, Rr])
            x1 = tf3[:, :, 0:Rr]
            x2 = tf3[:, :, Rr:2 * Rr]
            nr = nqk * Rr
            ra3 = sq[:, b * 768: b * 768 + nr].rearrange("p (a b) -> p a b", a=nqk)
            rb3 = sq[:, b * 768 + 384: b * 768 + 384 + nr].rearrange("p (a b) -> p a b", a=nqk)
            ra4 = U[:, b * 768: b * 768 + nr].rearrange("p (a b) -> p a b", a=nqk)
            rb4 = U[:, b * 768 + 384: b * 768 + 384 + nr].rearrange("p (a b) -> p a b", a=nqk)
            SQK = ('sq', b)
            S.op('dve', lambda e: e.tensor_tensor(out=ra3, in0=x1, in1=cosb, op=ALU.mult),
                 reads=TK + [ropekey], writes=[SQK])
            S.op('pool', lambda e: e.tensor_tensor(out=rb3, in0=x2, in1=sinb, op=ALU.mult),
                 reads=TK + [ropekey], writes=[('sqb', b)])
            S.op('dve', lambda e: e.tensor_tensor(out=ra4, in0=x2, in1=cosb, op=ALU.mult),
                 reads=TK + [ropekey], writes=[('ra4', b)])
            S.op('pool', lambda e: e.tensor_tensor(out=rb4, in0=x1, in1=sinb, op=ALU.mult),
                 reads=TK + [ropekey], writes=[('rb4', b)])
            S.op('dve', lambda e: e.tensor_tensor(out=qn3[:, :, 0:Rr], in0=ra3, in1=rb3, op=ALU.subtract),
                 reads=[SQK, ('sqb', b)], writes=[QA])
            S.op('pool', lambda e: e.tensor_tensor(out=qn3[:, :, Rr:2 * Rr], in0=ra4, in1=rb4, op=ALU.add),
                 reads=[('ra4', b), ('rb4', b)], writes=[QB])
            if 2 * Rr < 128:
                S.op('pool', lambda e: e.tensor_copy(out=qn3[:, :, 2 * Rr:128], in_=tf3[:, :, 2 * Rr:128]),
                     reads=TK, writes=[QC])

        def proj_tail_b(cx):
            nqk, t, ucount, qn_ap = cx['nqk'], cx['t'], cx['ucount'], cx['qn_ap']
            QA, QB, QC = cx['QK3']
            tb = TP[ucount % 2]
            for i in range(nqk):
                S.op('pe', lambda e, i=i: e.transpose(out=PSB[tb][:, i * 128:(i + 1) * 128],
                                                     in_=qn_ap[:, i * 128:(i + 1) * 128], identity=idb[:]),
                     reads=[QA, QB, QC, 'idb'], writes=[pk(tb)])
            qkdst = R[:, 0:nqk * 4096].rearrange("p (s k) -> p s k", s=nqk)[:, :, t * 128:(t + 1) * 128]
            S.op('act', lambda e: e.activation(out=qkdst, in_=PSB[tb][:, 0:nqk * 128].rearrange("p (s k) -> p s k", s=nqk),
                                               func=ACT.Copy),
                 reads=[pk(tb)], writes=[('qk', sl, t) for sl in range(nqk)])

        acount = [0]
        ptcount = [0]
        sccount = [0]
        fcount = [0]
        pend = []
        att_cfg = dict(banks=[2, 3], depth=1)

        def att_flush_one():
            q = pend.pop(0)
            pb, ab = q['pb'], q['ab']
            p_base = q['p_base']
            acc = PS[ab][:, 0:129]
            for (i, v_ap_, v_keys, first, last) in q['pv']:
                S.op('pe', lambda e, i=i, v_ap_=v_ap_, first=first, last=last, p_base=p_base, acc=acc: e.matmul(
                    acc, lhsT=p_base[:, i * 128:(i + 1) * 128], rhs=v_ap_, start=first, stop=last),
                    reads=[('pt', pb)] + list(v_keys), writes=[pk(ab)])
            if q['fin'] is not None:
                q['fin'](ab)

        def att_flush_all():
            while pend:
                att_flush_one()

        def attention_job(blocks, fin):
            oa_ = att_cfg.get('oa', OA)
            ab = oa_[acount[0] % len(oa_)]
            acount[0] += 1
            nb = len(blocks)
            bi = 0
            grp = att_cfg.get('group', 4)
            while bi < nb:
                quad = blocks[bi:bi + grp]
                n = len(quad)
                banks = att_cfg['banks']
                sbk = banks[sccount[0] % len(banks)]
                sccount[0] += 1
                sbl = list(sbk) if isinstance(sbk, tuple) else [sbk]
                sb0 = sbl[0]
                skeys = [pk(x_) for x_ in sbl]
                pb = ptcount[0] % 3
                ptcount[0] += 1
                for i, blk in enumerate(quad):
                    q_ap, q_keys, k_ap, k_keys, v_ap_, v_keys, mi = blk
                    o_ap = PSALL[:, sb0 * 512 + i * 128: sb0 * 512 + (i + 1) * 128]
                    bkey = pk(sb0 + i // 4)
                    S.op('pe', lambda e, o_ap=o_ap, k_ap=k_ap, q_ap=q_ap, mi=mi: e.matmul(
                        o_ap, lhsT=k_ap, rhs=q_ap, start=True, stop=(mi is None)),
                         reads=list(q_keys) + list(k_keys), writes=[bkey])
                    if mi is not None:
                        S.op('pe', lambda e, o_ap=o_ap, mi=mi: e.matmul(
                            o_ap, lhsT=idb[:], rhs=maskb[:, mi * 128:(mi + 1) * 128], start=False, stop=True),
                             reads=['idb', 'maskb'], writes=[bkey])
                ptb = att_cfg.get('pt', None)
                if ptb is None:
                    p_base = pt[:, pb * 512:(pb + 1) * 512]
                else:
                    p_base = ptb(pb)
                p_ap = p_base[:, 0:n * 128]
                S.op('act', lambda e, p_ap=p_ap, sb0=sb0, n=n: e.activation(out=p_ap, in_=PSALL[:, sb0 * 512: sb0 * 512 + n * 128],
                                                                           func=ACT.Exp, scale=SCALE),
                     reads=skeys[:(n + 3) // 4], writes=[('pt', pb)])
                pv = []
                for i, blk in enumerate(quad):
                    gi = bi + i
                    pv.append((i, blk[4], blk[5], gi == 0, gi == nb - 1))
                bi += n
                pend.append(dict(pb=pb, ab=ab, pv=pv, fin=(fin if bi >= nb else None), p_base=p_base))
                while len(pend) > att_cfg['depth']:
                    att_flush_one()

        for l in range(n_layers):
            if stopped:
                break
            if stop_spec is not None and ':' in stop_spec:
                stop = stop_spec.split(':')[1] if int(stop_spec.split(':')[0]) == l else None
            x_src = x_in if l == 0 else x1_scr
            x_dst = x1_scr if (l == 0 and n_layers > 1) else out
            w_in = w_in_a if l == 0 else w_in_b
            g_q_mem = 8 + l
            g_k_mem = 10 + l

            Wkv = R[:, 0:8192].rearrange("p (c n) -> p c n", c=8)
            memT = R[:, 8192:8192 + 2048]
            for c_ in range(8):
                load_weight_rows(w_kv[l, c_ * 128:(c_ + 1) * 128, :], Wkv[:, c_, :], ('wkv', c_))
            for mt in range(2):
                norm_transpose_tile(mem_in[mt * 128:(mt + 1) * 128, :], mt, 16 + l * 8,
                                    lambda half, mt=mt: memT[:, half * 1024:(half + 1) * 1024].rearrange(
                                        "p (c k) -> p c k", c=4)[:, :, mt * 128:(mt + 1) * 128],
                                    lambda half, mt=mt: [('memT', mt, half * 4 + j_) for j_ in range(4)], mt)
            ntt_flush()
            S.op('pool', lambda e: e.memset(Vm[:], 1.0), writes=['Vm'])
            for mt in range(2):
                for half in range(2):
                    bank = PJ[half]
                    for c in range(8):
                        S.op('pe', lambda e, c=c, half=half, bank=bank, mt=mt: e.matmul(
                            PS[bank][:, 0:512], lhsT=memT[:, c * 256 + mt * 128: c * 256 + (mt + 1) * 128],
                            rhs=Wkv[:, c, half * 512:(half + 1) * 512], start=(c == 0), stop=(c == 7)),
                            reads=[('memT', mt, c), ('wkv', c)], writes=[pk(bank)])
                bank = PJ[0]
                src = PS[bank][:, 0:512]
                rs_ap, rs_key = rms_stats(src, 4, 128, [pk(bank)], True, 1.0 / E)
                tf3 = tmpf[:, 0:512].rearrange("p (a b) -> p a b", a=4)
                S.op('dve', lambda e, src=src, rs_ap=rs_ap, tf3=tf3: e.tensor_tensor(
                    out=tf3, in0=src.rearrange("p (a b) -> p a b", a=4),
                    in1=rs_ap.unsqueeze(2).to_broadcast([128, 4, 128]), op=ALU.mult),
                    reads=[pk(bank), rs_key], writes=[('tmpf', 0, 0)])
                qn3 = qn[:, 0:512].rearrange("p (a b) -> p a b", a=4)
                gsrc = gains[:, g_k_mem * 128:(g_k_mem + 1) * 128].unsqueeze(1).to_broadcast([128, 4, 128])
                S.op('pool', lambda e, qn3=qn3, tf3=tf3, gsrc=gsrc: e.tensor_tensor(out=qn3, in0=tf3, in1=gsrc, op=ALU.mult),
                     reads=[('tmpf', 0, 0), 'gains'], writes=[('qn', 0, 'a')])
                tb = TP[0]
                for hd in range(4):
                    S.op('pe', lambda e, hd=hd: e.transpose(out=PSB[tb][:, hd * 128:(hd + 1) * 128],
                                                           in_=qn[:, hd * 128:(hd + 1) * 128], identity=idb[:]),
                         reads=[('qn', 0, 'a'), 'idb'], writes=[pk(tb)])
                S.op('act', lambda e, mt=mt: e.activation(
                    out=KmT[:].rearrange("p (h k) -> p h k", h=4)[:, :, mt * 128:(mt + 1) * 128],
                    in_=PSB[tb][:, 0:512].rearrange("p (h k) -> p h k", h=4), func=ACT.Copy),
                    reads=[pk(tb)], writes=['KmT'])
                bank = PJ[1]
                S.op('dve', lambda e, mt=mt, bank=bank: e.tensor_copy(
                    out=Vm[:, mt * 520:(mt + 1) * 520].rearrange("p (h k) -> p h k", h=4)[:, :, 0:128],
                    in_=PS[bank][:, 0:512].rearrange("p (h k) -> p h k", h=4)),
                    reads=[pk(bank)], writes=['Vm'])
            if stop == 'M':
                S.barrier()
                break

            for t in range(NT):
                norm_transpose_tile(x_src[t * 128:(t + 1) * 128, :], t, l * 8,
                                    lambda half, t=t: hT[:, half * 16384:(half + 1) * 16384].rearrange(
                                        "p (c k) -> p c k", c=4)[:, :, t * 128:(t + 1) * 128],
                                    lambda half, t=t: [('hT', t, half * 4 + j_) for j_ in range(4)], t)
            ntt_flush()
            S.barrier()
            if stop == 'p1':
                break

            S.op('pool', lambda e: e.memset(R[:, V_OFF:R_N], 1.0), writes=['Vall'])
            S.barrier()
            if l == 0:
                iters = [dict(cols=[((s_ * 3 + g) * 8 + h) * 128 for s_ in range(3) for g in range(3)], nqk=6, nv=3, hd=h)
                         for h in range(8)]
                gain_ops = [(0, 6, gains[:, 0:768].rearrange("p (a b) -> p a b", a=6))]
            else:
                iters = [dict(cols=[(kvh * 4 + j) * 128 for j in range(4)] + [1024 + kvh * 128, 1280 + kvh * 128], nqk=5, nv=1, hd=kvh)
                         for kvh in range(2)]
                gain_ops = [(0, 4, gains[:, 6 * 128:7 * 128].unsqueeze(1).to_broadcast([128, 4, 128])),
                            (4, 1, gains[:, 7 * 128:8 * 128].unsqueeze(1))]
            if l == 0:
                att_cfg['banks'] = [0, 1, 2, 3]
                att_cfg['group'] = 4
                att_cfg['pt'] = None
            else:
                att_cfg['banks'] = [(0, 1), (2, 3), (6, 7)]
                att_cfg['group'] = 8
                att_cfg['pt'] = lambda pb_: R[:, 5 * 4096 + pb_ * 1024: 5 * 4096 + (pb_ + 1) * 1024]
            att_cfg['depth'] = 2

            def emit_iter_load(it):
                ncols = (it['nqk'] + it['nv']) * 128
                Wv_ = Wbuf[:, 0:8 * ncols].rearrange("p (c n) -> p c n", c=8)
                for bi_, col in enumerate(it['cols']):
                    load_weight_piece(w_in[:, col:col + 128], 8, Wv_[:, :, bi_ * 128:(bi_ + 1) * 128], ('wbuf', bi_))

            emit_iter_load(iters[0])
            tilecount = 0
            for it_i, it in enumerate(iters):
                nqk, nv = it['nqk'], it['nv']
                nblk = nqk + nv
                ncols = nblk * 128
                Wv = Wbuf[:, 0:8 * ncols].rearrange("p (c n) -> p c n", c=8)
                prev_cx = None
                prev2_cx = None
                for t in range(NT):
                    pbase = 0 if tilecount % 2 == 0 else 3
                    c0 = 0
                    while c0 < ncols:
                        n_ = min(PROJ_N, 512 - (c0 % 512), ncols - c0)
                        bank = pbase + c0 // 512
                        wkeys = [('wbuf', j) for j in range(c0 // 128, (c0 + n_) // 128)]
                        for c in range(8):
                            S.op('pe', lambda e, c=c, t=t, c0=c0, n_=n_, pbase=pbase, Wv=Wv: e.matmul(
                                PSALL[:, pbase * 512 + c0: pbase * 512 + c0 + n_], lhsT=hT_ap(c, t), rhs=Wv[:, c, c0:c0 + n_],
                                start=(c == 0), stop=(c == 7)),
                                reads=[('hT', t, c)] + wkeys, writes=[pk(bank)])
                        c0 += n_
                    if l == 0:
                        rope = (ropeA[:, t * 16:(t + 1) * 16], ropeA[:, 512 + t * 16: 512 + (t + 1) * 16], 16)
                        ropekey = 'ropeA'
                    else:
                        rbuf = tilecount % 2
                        S.op('sp', lambda e, rbuf=rbuf, t=t: e.dma_start(
                            out=ropeBt[:, rbuf * 128:(rbuf + 1) * 128].rearrange("p (a b) -> p a b", a=2),
                            in_=ropeB_in.rearrange("p (a t b) -> p a t b", a=2, t=32)[:, :, t, :]),
                            writes=[('ropeB', rbuf)], dma=('ropeB', rbuf))
                        rope = (ropeBt[:, rbuf * 128: rbuf * 128 + 64], ropeBt[:, rbuf * 128 + 64: rbuf * 128 + 128], 64)
                        ropekey = ('ropeB', rbuf)
                    if prev_cx is not None:
                        proj_tail(prev_cx)
                    cx_ = proj_head(pbase, nqk, nv, gain_ops, t, rope, ropekey, 0, tilecount)
                    if prev2_cx is not None:
                        proj_tail_b(prev2_cx)
                    proj_head2(cx_)
                    prev2_cx = prev_cx
                    prev_cx = cx_
                    tilecount += 1
                proj_tail(prev_cx)
                proj_tail_b(prev2_cx)
                proj_tail_b(prev_cx)
                if it_i + 1 < len(iters):
                    emit_iter_load(iters[it_i + 1])
                if stop == 'p2proj':
                    continue
                hd = it['hd']
                jobs = []
                if l == 0:
                    for T in range(NT):
                        blocks = []
                        for g, (window, dil) in enumerate(A_GROUPS):
                            for dl in range(-9, 10):
                                if (g, dl) not in MASK_TABLE:
                                    continue
                                kb = T + dl
                                if kb < 0 or kb >= NT:
                                    continue
                                blocks.append((qk_ap(g, T * 128), [('qk', g, T)], qk_ap(3 + g, kb * 128), [('qk', 3 + g, kb)],
                                               v_ap(g, kb), [('v', g, kb)], MASK_TABLE[(g, dl)]))
                        jobs.append((blocks, T, hd))
                else:
                    for j in range(4):
                        for T in range(NT):
                            blocks = []
                            for kb in range(NT):
                                blocks.append((qk_ap(j, T * 128), [('qk', j, T)], qk_ap(4, kb * 128), [('qk', 4, kb)],
                                               v_ap(0, kb), [('v', 0, kb)], None))
                            jobs.append((blocks, T, hd * 4 + j))
                for blocks, T, hcol in jobs:
                    def fin(ab, T=T, hcol=hcol):
                        ob_i = fcount[0] % 4
                        fcount[0] += 1
                        rc_ap = rc[:, ob_i:ob_i + 1]
                        ob_ap = ob[:, ob_i * 128:(ob_i + 1) * 128]
                        S.op('dve', lambda e: e.reciprocal(out=rc_ap, in_=PS[ab][:, 128:129]),
                             reads=[pk(ab)], writes=[('rc', ob_i)])
                        S.op('dve', lambda e: e.tensor_scalar(out=ob_ap, in0=PS[ab][:, 0:128], scalar1=rc_ap, scalar2=None,
                                                              op0=ALU.mult),
                             reads=[pk(ab), ('rc', ob_i)], writes=[('ob', ob_i)])
                        S.op('sp', lambda e: e.dma_start(out=o_scr[T * 128:(T + 1) * 128, hcol * 128:(hcol + 1) * 128], in_=ob_ap),
                             reads=[('ob', ob_i)], writes=[('oscr', T)], dma=('ob', ob_i))
                    attention_job(blocks, fin)
                att_flush_all()
            att_cfg['banks'] = [2, 3]
            att_cfg['depth'] = 1
            S.barrier()
            if stop in ('p2', 'p2proj'):
                break

            Wg = R[:, 0:16384].rearrange("p (c n) -> p c n", c=8)
            Wo = R[:, 16384:28672].rearrange("p (c n) -> p c n", c=12)
            gcol0 = 9216 if l == 0 else 1536
            for c_ in range(8):
                load_weight_rows(w_in[c_ * 128:(c_ + 1) * 128, gcol0:gcol0 + 1024], Wg[:, c_, 0:1024], ('wg', c_, 0))
            for c_ in range(8):
                load_weight_rows(w_in[c_ * 128:(c_ + 1) * 128, gcol0 + 1024:gcol0 + 2048], Wg[:, c_, 1024:2048], ('wg', c_, 1))
            for cc_ in range(12):
                load_weight_rows(w_out[l, cc_ * 128:(cc_ + 1) * 128, :], Wo[:, cc_, :], ('wo', cc_))
            KmT3 = KmT[:].rearrange("p (h k) -> p h k", h=4)
            att_cfg['banks'] = [2, 0, 5]
            att_cfg['group'] = 4
            att_cfg['pt'] = None
            att_cfg['depth'] = 2
            att_cfg['oa'] = [4, 7]

            def gate_chain(t, j, gb, o_j, o_keys):
                th_ap = th[:, j * 512:(j + 1) * 512]
                S.op('act', lambda e: e.activation(out=th_ap, in_=PS[gb][:, 0:512], func=ACT.Tanh, scale=0.5),
                     reads=[pk(gb)], writes=[('th', j)])
                S.op('dve', lambda e: e.scalar_tensor_tensor(out=th_ap, in0=th_ap, scalar=1.0, in1=PS[gb][:, 0:512],
                                                             op0=ALU.add, op1=ALU.mult),
                     reads=[pk(gb), ('th', j)], writes=[('th', j)])
                yb_ap = yb[:, j * 512:(j + 1) * 512]
                S.op('dve', lambda e: e.scalar_tensor_tensor(out=yb_ap, in0=th_ap, scalar=0.5, in1=o_j, op0=ALU.mult, op1=ALU.mult),
                     reads=[('th', j)] + o_keys, writes=[('yb', j)])

            def gate_proj(t, j, gb):
                for c in range(8):
                    S.op('pe', lambda e, c=c: e.matmul(PS[gb][:, 0:512], lhsT=hT_ap(c, t), rhs=Wg[:, c, 512 + j * 512: 1024 + j * 512],
                                                       start=(c == 0), stop=(c == 7)),
                         reads=[('hT', t, c), ('wg', c, (512 + j * 512) // 1024)], writes=[pk(gb)])

            def y_transposes(t, j, tpb, eng):
                b = t % 2
                yb_ap = yb[:, j * 512:(j + 1) * 512]
                yTb = yT[:, b * 1536:(b + 1) * 1536]
                for i in range(4):
                    S.op('pe', lambda e, i=i: e.transpose(out=PSB[tpb][:, i * 128:(i + 1) * 128],
                                                         in_=yb_ap[:, i * 128:(i + 1) * 128], identity=idb[:]),
                         reads=[('yb', j), 'idb'], writes=[pk(tpb)])
                if eng == 'act':
                    S.op('act', lambda e: e.activation(out=yTb[:, j * 512:(j + 1) * 512], in_=PSB[tpb][:, 0:512], func=ACT.Copy),
                         reads=[pk(tpb)], writes=[('yT', b, j)])
                else:
                    S.op('dve', lambda e: e.tensor_copy(out=yTb[:, j * 512:(j + 1) * 512], in_=PSB[tpb][:, 0:512]),
                         reads=[pk(tpb)], writes=[('yT', b, j)])

            def stage_A(t):
                b = t % 2
                xt = xt_ap(b)
                ot_ap = ot[:, b * 1024:(b + 1) * 1024]
                S.op('sp', lambda e, x_src=x_src: e.dma_start(out=xt, in_=x_src[t * 128:(t + 1) * 128, :]),
                     writes=[('xt', b)], dma=('xt', b))
                S.op('sp', lambda e: e.dma_start(out=ot_ap, in_=o_scr[t * 128:(t + 1) * 128, :]),
                     reads=[('oscr', t)], writes=[('ot', b)], dma=('ot', b))
                bank = 0
                for c in range(8):
                    S.op('pe', lambda e, c=c: e.matmul(PS[bank][:, 0:512], lhsT=hT_ap(c, t), rhs=Wg[:, c, 0:512],
                                                       start=(c == 0), stop=(c == 7)),
                         reads=[('hT', t, c), ('wg', c, 0)], writes=[pk(bank)])
                gate_proj(t, 0, 1)
                gate_proj(t, 1, 3)
                src = PS[bank][:, 0:512]
                rs_ap, rs_key = rms_stats(src, 4, 128, [pk(bank)], True, 1.0 / E)
                tf3 = tmpf[:, b * 768: b * 768 + 512].rearrange("p (a b) -> p a b", a=4)
                S.op('dve', lambda e: e.tensor_tensor(out=tf3, in0=src.rearrange("p (a b) -> p a b", a=4),
                                                      in1=rs_ap.unsqueeze(2).to_broadcast([128, 4, 128]), op=ALU.mult),
                     reads=[pk(bank), rs_key], writes=[('tmpf', b, 0)])
                qn3 = qn[:, b * 768: b * 768 + 512].rearrange("p (a b) -> p a b", a=4)
                gsrc = gains[:, g_q_mem * 128:(g_q_mem + 1) * 128].unsqueeze(1).to_broadcast([128, 4, 128])
                S.op('pool', lambda e: e.tensor_tensor(out=qn3, in0=tf3, in1=gsrc, op=ALU.mult),
                     reads=[('tmpf', b, 0), 'gains'], writes=[('qn', b, 'a')])
                gate_chain(t, 0, 1, ot_ap[:, 0:512], [('ot', b)])
                gate_chain(t, 1, 3, ot_ap[:, 512:1024], [('ot', b)])

            def stage_B(t):
                b = t % 2
                qn_ap = qn[:, b * 768: b * 768 + 512]
                tb = 6
                for hd in range(4):
                    S.op('pe', lambda e, hd=hd: e.transpose(out=PSB[tb][:, hd * 128:(hd + 1) * 128],
                                                           in_=qn_ap[:, hd * 128:(hd + 1) * 128], identity=idb[:]),
                         reads=[('qn', b, 'a'), 'idb'], writes=[pk(tb)])
                S.op('act', lambda e: e.activation(out=qmT[:], in_=PSB[tb][:, 0:512], func=ACT.Copy),
                     reads=[pk(tb)], writes=['qmT'])
                y_transposes(t, 0, 7, 'dve')
                gate_proj(t, 2, 1)
                th2 = th[:, 1024:1536]
                S.op('act', lambda e: e.activation(out=th2, in_=PS[1][:, 0:512], func=ACT.Tanh, scale=0.5),
                     reads=[pk(1)], writes=[('th', 2)])
                S.op('dve', lambda e: e.scalar_tensor_tensor(out=th2, in0=th2, scalar=1.0, in1=PS[1][:, 0:512],
                                                             op0=ALU.add, op1=ALU.mult),
                     reads=[pk(1), ('th', 2)], writes=[('th', 2)])
                y_transposes(t, 1, 6, 'act')
                for hd in range(4):
                    blocks = []
                    for kb in range(2):
                        blocks.append((qmT[:, hd * 128:(hd + 1) * 128], ['qmT'], KmT3[:, hd, kb * 128:(kb + 1) * 128], ['KmT'],
                                       Vm[:, (kb * 4 + hd) * 130:(kb * 4 + hd) * 130 + 129], ['Vm'], None))

                    def fin(ab, hd=hd):
                        ob_i = fcount[0] % 4
                        fcount[0] += 1
                        rc_ap = rc[:, 4 + ob_i:5 + ob_i]
                        S.op('dve', lambda e: e.reciprocal(out=rc_ap, in_=PS[ab][:, 128:129]),
                             reads=[pk(ab)], writes=[('rcm', ob_i)])
                        S.op('act', lambda e: e.activation(out=om[:, hd * 128:(hd + 1) * 128], in_=PS[ab][:, 0:128], func=ACT.Copy,
                                                           scale=rc_ap),
                             reads=[pk(ab), ('rcm', ob_i)], writes=[('om', hd)])
                    attention_job(blocks, fin)
                att_flush_all()
                yb2 = yb[:, 1024:1536]
                S.op('dve', lambda e: e.scalar_tensor_tensor(out=yb2, in0=th2, scalar=0.5, in1=om[:, 0:512], op0=ALU.mult, op1=ALU.mult),
                     reads=[('th', 2)] + [('om', hd) for hd in range(4)], writes=[('yb', 2)])

            def stage_B2(t):
                y_transposes(t, 2, 7, 'act')

            def stage_C(t):
                b = t % 2
                xt = xt_ap(b)
                x1t = x1t_ap(b)
                yTb = yT[:, b * 1536:(b + 1) * 1536]
                for nb_ in range(2):
                    obk = [2, 0][nb_]
                    for cc in range(12):
                        S.op('pe', lambda e, cc=cc, nb_=nb_, obk=obk: e.matmul(
                            PS[obk][:, 0:512], lhsT=yTb[:, cc * 128:(cc + 1) * 128], rhs=Wo[:, cc, nb_ * 512:(nb_ + 1) * 512],
                            start=(cc == 0), stop=(cc == 11)),
                            reads=[('yT', b, cc // 4), ('wo', cc)], writes=[pk(obk)])
                    S.op('dve', lambda e, nb_=nb_, obk=obk: e.tensor_tensor(
                        out=x1t[:, nb_ * 512:(nb_ + 1) * 512], in0=PS[obk][:, 0:512], in1=xt[:, nb_ * 512:(nb_ + 1) * 512], op=ALU.add),
                        reads=[pk(obk), ('xt', b)], writes=[('x1t', b, nb_)])
                S.op('sp', lambda e, x_dst=x_dst: e.dma_start(out=x_dst[t * 128:(t + 1) * 128, :], in_=x1t),
                     reads=[('x1t', b, 0), ('x1t', b, 1)], writes=[('xdst', t)], dma=('x1t', b))

            def _bind(f, **kw):
                return f

            if stop != 'p3w':
                stage_A(0)
                for t in range(NT):
                    stage_B(t)
                    if t + 1 < NT:
                        stage_A(t + 1)
                    stage_B2(t)
                    stage_C(t)
            att_cfg['oa'] = OA
            S.barrier()
        S.emit(nc)
    return nc, S


def _host_consts(norm_g, mem_norm_g, mem_qn_g, mem_kn_g, qn_a, kn_a, qn_b, kn_b):
    ng = np.zeros((128, 32), np.float32)
    for l in range(2):
        ng[:, l * 8:(l + 1) * 8] = norm_g[l].reshape(8, 128).T
        ng[:, 16 + l * 8:16 + (l + 1) * 8] = mem_norm_g[l].reshape(8, 128).T
    rows = [qn_a[0, 0], qn_a[0, 1], qn_a[0, 2], kn_a[0, 0], kn_a[0, 1], kn_a[0, 2], qn_b[0], kn_b[0],
            mem_qn_g[0], mem_qn_g[1], mem_kn_g[0], mem_kn_g[1]]
    gains = np.ascontiguousarray(np.broadcast_to(np.concatenate(rows)[None, :], (128, 12 * 128))).astype(np.float32)
    pos = np.arange(S_TOK, dtype=np.float32)
    invA = (np.float32(500000.0) ** (-np.arange(0, 32, 2, dtype=np.float32) / np.float32(32))).astype(np.float32)
    angA = pos[:, None] * invA[None, :]
    row = np.repeat(np.arange(64, dtype=np.float32), 64)
    col = np.tile(np.arange(64, dtype=np.float32), 64)
    invB = (np.float32(10000.0) ** (-np.arange(0, 64, 2, dtype=np.float32) / np.float32(64))).astype(np.float32)
    angB = np.concatenate([row[:, None] * invB[None, :], col[:, None] * invB[None, :]], axis=-1)

    def lay(a):
        return a.reshape(32, 128, -1).transpose(1, 0, 2)
    ropeA = np.stack([lay(np.cos(angA)), lay(np.sin(angA))], axis=1).reshape(128, -1).astype(np.float32)
    ropeB = np.stack([lay(np.cos(angB)), lay(np.sin(angB))], axis=1).reshape(128, -1).astype(np.float32)
    return dict(ng=ng, gains=gains, ropeA=np.ascontiguousarray(ropeA), ropeB=np.ascontiguousarray(ropeB),
                ident=np.eye(128, dtype=np.float32), masks=np.ascontiguousarray(MASKS_NP.reshape(128, -1)))


_NC_CACHE = {}


def kernel(x, mem, norm_g, mem_norm_g, w_mem_kv, mem_qn_g, mem_kn_g, w_out,
           w_in_a, qn_a, kn_a, w_in_b, qn_b, kn_b):
    f = lambda a: np.ascontiguousarray(np.asarray(a, dtype=np.float32))
    x = f(x)
    mem = f(mem)
    consts = _host_consts(f(norm_g), f(mem_norm_g), f(mem_qn_g), f(mem_kn_g), f(qn_a), f(kn_a), f(qn_b), f(kn_b))
    shared = dict(w_in_a=f(w_in_a)[0], w_in_b=f(w_in_b)[0], w_out=f(w_out), w_mem_kv=f(w_mem_kv), **consts)
    if 'nc' not in _NC_CACHE:
        _NC_CACHE['nc'] = build()[0]
    nc = _NC_CACHE['nc']
    n = x.shape[0]
    in_maps = [dict(x=x[b], mem=mem[b], **shared) for b in range(n)]
    res = run_bass_kernel_spmd(nc, in_maps, core_ids=list(range(n)))
    return np.stack([np.asarray(r["y_out"], dtype=np.float32) for r in res.results], axis=0)
```

```python
import numpy as np
import concourse.bass as bass
import concourse.mybir as mybir
from concourse.bass_utils import run_bass_kernel_spmd

F32 = mybir.dt.float32
BF16 = mybir.dt.bfloat16
ALU = mybir.AluOpType
ACT = mybir.ActivationFunctionType
AX = mybir.AxisListType

ENGS = ['pe', 'act', 'dve', 'pool', 'sp']

S_TOK = 4096
NT = 32
D = 1024
E = 128
EPS = 1e-6
SCALE = float(E) ** -0.5
A_GROUPS = ((128, 1), (512, 4), (2048, 16))
IN_A = 11264
IN_B = 3584

PROJ_N = 512


class Sched:
    def __init__(self):
        self.ops = []
        self.state = {}
        self.last_on = {e: None for e in ENGS}
        self.dma_last = {}

    def op(self, eng, fn, reads=(), writes=(), dma=None):
        i = len(self.ops)
        deps = set()
        for k in reads:
            excl = isinstance(k, tuple) and k[0] == 'ps'
            st = self.state.setdefault(k, [None, []])
            if st[0] is not None:
                deps.add(st[0])
            if excl:
                for r in st[1]:
                    if self.ops[r]['eng'] != eng:
                        deps.add(r)
        for k in writes:
            st = self.state.setdefault(k, [None, []])
            if st[0] is not None:
                deps.add(st[0])
            for r in st[1]:
                deps.add(r)
        for k in reads:
            self.state[k][1].append(i)
        for k in writes:
            self.state[k] = [i, []]
        if dma is not None:
            p = self.dma_last.get(dma)
            if p is not None:
                deps.add(p)
            self.dma_last[dma] = i
        deps.discard(i)
        if eng == 'pe':
            deps = {d for d in deps if self.ops[d]['eng'] != 'pe'}
        best = {}
        keep = set()
        for d in deps:
            od = self.ops[d]
            if od['dma'] is not None:
                keep.add(d)
            elif d > best.get(od['eng'], -1):
                best[od['eng']] = d
        deps = keep | set(best.values())
        self.ops.append(dict(eng=eng, fn=fn, deps=deps, dma=dma))
        if fn is not None:
            self.last_on[eng] = i
        return i

    def barrier(self):
        lasts = [v for v in self.last_on.values() if v is not None]
        lasts += list(self.dma_last.values())
        lasts = set(lasts)
        for e in ENGS:
            deps = {d for d in lasts if not (self.ops[d]['eng'] == e and self.ops[d]['dma'] is None)}
            self.ops.append(dict(eng=e, fn=None, deps=deps, dma=None))
        self.state = {}

    def emit(self, nc):
        ops = self.ops
        needed = set()
        for o in ops:
            needed |= o['deps']
        dma_keys = []
        seen = set()
        for o in ops:
            if o['dma'] is not None and o['dma'] not in seen:
                seen.add(o['dma'])
                dma_keys.append(o['dma'])
        sem_ctx = []
        sems = {}
        for n_i, name in enumerate(['pe', 'act', 'dve', 'pool'] + [('dma', k) for k in dma_keys]):
            cm = nc.semaphore("s%d" % n_i)
            sems[name] = cm.__enter__()
            sem_ctx.append(cm)
        cnt = {}
        for i, o in enumerate(ops):
            if o['dma'] is not None:
                key = ('dma', o['dma'])
                cnt[key] = cnt.get(key, 0) + 16
                o['sem'], o['val'], o['inc'] = key, cnt[key], 16
            elif i in needed:
                assert o['fn'] is not None
                key = o['eng']
                cnt[key] = cnt.get(key, 0) + 1
                o['sem'], o['val'], o['inc'] = key, cnt[key], 1
            else:
                o['sem'] = None
        self.maxval = dict(cnt)

        def run(eng_name, e):
            waited = {}
            for o in ops:
                if o['eng'] != eng_name:
                    continue
                need = {}
                for d in o['deps']:
                    od = ops[d]
                    s, v = od['sem'], od['val']
                    if v > need.get(s, 0):
                        need[s] = v
                for s, v in need.items():
                    if waited.get(s, 0) < v:
                        e.wait_ge(sems[s], v)
                        waited[s] = v
                if o['fn'] is None:
                    continue
                ins = o['fn'](e)
                if o['sem'] is not None:
                    ins.then_inc(sems[o['sem']], o['inc'])

        with nc.Block() as block:
            @block.tensor
            def _(e):
                run('pe', e)

            @block.scalar
            def _(e):
                run('act', e)

            @block.vector
            def _(e):
                run('dve', e)

            @block.gpsimd
            def _(e):
                run('pool', e)

            @block.sync
            def _(e):
                run('sp', e)
        for cm in reversed(sem_ctx):
            cm.__exit__(None, None, None)


def _mask_tables():
    masks = []
    index = {}
    table = {}
    kk = np.arange(128)[:, None]
    qq = np.arange(128)[None, :]
    for g, (window, dil) in enumerate(A_GROUPS):
        hw = window // 2
        dmax = hw // 128 + (1 if hw % 128 else 0)
        dmax = max(dmax, 1)
        for dl in range(-dmax, dmax + 1):
            diff = 128 * dl + kk - qq
            m = ((diff % dil) == 0) & (np.abs(diff) <= hw)
            if not m.any():
                continue
            key = m.tobytes()
            if key not in index:
                index[key] = len(masks)
                masks.append(m.astype(np.float32))
            table[(g, dl)] = index[key]
    return np.stack(masks, axis=1), table


MASKS_NP, MASK_TABLE = _mask_tables()
NM = MASKS_NP.shape[1]


def build(n_layers=2, debug=False, stop=None):
    stop_spec = stop
    nc = bass.Bass("TRN2", target_bir_lowering=False)
    dk = "ExternalOutput" if debug else "Internal"
    x_in = nc.dram_tensor("x", [S_TOK, D], F32, kind="ExternalInput").ap()
    mem_in = nc.dram_tensor("mem", [256, D], F32, kind="ExternalInput").ap()
    w_in_a = nc.dram_tensor("w_in_a", [D, IN_A], F32, kind="ExternalInput").ap()
    w_in_b = nc.dram_tensor("w_in_b", [D, IN_B], F32, kind="ExternalInput").ap()
    w_out = nc.dram_tensor("w_out", [2, 1536, D], F32, kind="ExternalInput").ap()
    w_kv = nc.dram_tensor("w_mem_kv", [2, D, 1024], F32, kind="ExternalInput").ap()
    ng_in = nc.dram_tensor("ng", [128, 32], F32, kind="ExternalInput").ap()
    gains_in = nc.dram_tensor("gains", [128, 12 * 128], F32, kind="ExternalInput").ap()
    ropeA_in = nc.dram_tensor("ropeA", [128, 2 * 32 * 16], F32, kind="ExternalInput").ap()
    ropeB_in = nc.dram_tensor("ropeB", [128, 2 * 32 * 64], F32, kind="ExternalInput").ap()
    ident_in = nc.dram_tensor("ident", [128, 128], F32, kind="ExternalInput").ap()
    masks_in = nc.dram_tensor("masks", [128, NM * 128], F32, kind="ExternalInput").ap()
    out = nc.dram_tensor("y_out", [S_TOK, D], F32, kind="ExternalOutput").ap()
    x1_scr = nc.dram_tensor("x1_scr", [S_TOK, D], F32, kind=dk).ap()
    o_scr = nc.dram_tensor("o_scr", [S_TOK, D], BF16, kind=dk).ap()

    S = Sched()
    R_N = 6 * 4096 + 3 * 32 * 130
    import contextlib
    with contextlib.ExitStack() as es:
        def sb(name, shape, dt):
            return es.enter_context(nc.sbuf_tensor(name, shape, dt))

        hT = sb("hT", [128, 8 * 4096], BF16)
        R = sb("R", [128, R_N], BF16)
        Wbuf = sb("Wbuf", [128, 8 * 1152], BF16)
        Wbf = Wbuf.bitcast(F32)
        Wst = sb("Wst", [128, 2 * 8 * 128], F32)
        ropeA = sb("ropeA_t", [128, 2 * 32 * 16], F32)
        ropeBt = ropeA[:, 0:256]
        gains = sb("gains_t", [128, 12 * 128], F32)
        maskb = sb("maskb", [128, NM * 128], BF16)
        idf = sb("idf", [128, 128], F32)
        idb = sb("idb", [128, 128], BF16)
        ngt = sb("ngt", [128, 32], F32)
        mh = sb("mh", [128, 16], F32)
        KmT = sb("KmT", [128, 4 * 256], BF16)
        Vm = sb("Vm", [128, 2 * 4 * 130], BF16)
        ssA = sb("ssA", [128, 2 * 8], F32)
        msA = sb("msA", [128, 2 * 8], F32)
        rsA = sb("rsA", [128, 2 * 8], F32)
        sq = sb("sq", [128, 2 * 768], F32)
        tmpf = sb("tmpf", [128, 2 * 768], F32)
        U = sb("U", [128, 2560], F32)
        Ub = U.bitcast(BF16)
        qn = sb("qn", [128, 2 * 768], BF16)
        pt = sb("pt", [128, 3 * 512], BF16)
        rc = sb("rc", [128, 8], F32)
        ob = Ub[:, 3072:3584]
        th = Wbf[:, 0:1536]
        ot = Ub[:, 3072:5120]
        Wbb = Wbuf
        om = Wbb[:, 3072:3584]
        yb = Wbb[:, 3584:5120]
        yT = Ub[:, 0:3072]
        qmT = Wbb[:, 5120:5632]

        PSALL = es.enter_context(nc.psum_tensor("psall", [128, 4096], F32))
        PSBALL = PSALL.bitcast(BF16)
        PS = [PSALL[:, i * 512:(i + 1) * 512] for i in range(8)]
        PSB = [PSBALL[:, i * 1024:(i + 1) * 1024] for i in range(8)]
        PJ = [0, 1]
        SC = [2, 3]
        OA = [4, 5]
        TP = [6, 7]

        def pk(i):
            return ('ps', i)

        Rf = R.bitcast(F32)
        QK_OFF = 0
        V_OFF = 6 * 4096
        XT_OFF_F = 28672 // 2
        def xt_ap(b):
            return Rf[:, XT_OFF_F + b * 1024: XT_OFF_F + (b + 1) * 1024]

        def x1t_ap(b):
            return Rf[:, XT_OFF_F + 2048 + b * 1024: XT_OFF_F + 2048 + (b + 1) * 1024]

        def qk_ap(slot, t0, n=128):
            o = QK_OFF + slot * 4096 + t0
            return R[:, o:o + n]

        def v_ap(slot, kb, n=129):
            o = V_OFF + (slot * 32 + kb) * 130
            return R[:, o:o + n]

        def hT_ap(c, t):
            o = c * 4096 + t * 128
            return hT[:, o:o + 128]

        def wst_ap(slot, nchunk=8):
            return Wst[:, slot * 1024: slot * 1024 + nchunk * 128].rearrange("p (c n) -> p c n", n=128)

        S.op('sp', lambda e: e.dma_start(out=idf[:], in_=ident_in), writes=['idf'], dma='c0')
        S.op('sp', lambda e: e.dma_start(out=gains[:], in_=gains_in), writes=['gains'], dma='c1')
        S.op('sp', lambda e: e.dma_start(out=ngt[:], in_=ng_in), writes=['ngt'], dma='c2')
        S.op('sp', lambda e: e.dma_start(out=ropeA[:], in_=ropeA_in), writes=['ropeA'], dma='c3')
        S.op('sp', lambda e: e.dma_start(out=Rf[:, 0:NM * 128], in_=masks_in), writes=['mstage'], dma='c4')
        S.op('pool', lambda e: e.tensor_copy(out=idb[:], in_=idf[:]), reads=['idf'], writes=['idb'])
        S.op('pool', lambda e: e.tensor_scalar(out=maskb[:], in0=Rf[:, 0:NM * 128], scalar1=-1.0, scalar2=30000.0,
                                               op0=ALU.add, op1=ALU.mult), reads=['mstage'], writes=['maskb'])
        S.op('pool', lambda e: e.memset(mh[:, 0:8], -0.5), writes=['mh'])
        S.op('pool', lambda e: e.memset(mh[:, 8:16], EPS), writes=['mhe'])
        S.barrier()
        stopped = (stop == 'init')

        wcount = [0]

        def load_weight_piece(src_ap, nchunk, dst_ap, dst_key):
            slot = wcount[0] % 2
            wcount[0] += 1
            S.op('sp', lambda e: e.dma_start(out=wst_ap(slot, nchunk), in_=src_ap.rearrange("(c p) n -> p c n", p=128)),
                 writes=[('wst', slot)], dma=('wst', slot))
            S.op('pool', lambda e: e.tensor_copy(out=dst_ap, in_=wst_ap(slot, nchunk)),
                 reads=[('wst', slot)], writes=[dst_key])

        rcount = [0]

        def load_weight_rows(src_ap, dst_ap, dst_key):
            slot = wcount[0] % 2
            wcount[0] += 1
            eng = 'dve' if rcount[0] % 2 == 0 else 'act'
            rcount[0] += 1
            st = Wst[:, slot * 1024:(slot + 1) * 1024]
            S.op('sp', lambda e: e.dma_start(out=st, in_=src_ap), writes=[('wst', slot)], dma=('wst', slot))
            if eng == 'dve':
                S.op('dve', lambda e: e.tensor_copy(out=dst_ap, in_=st), reads=[('wst', slot)], writes=[dst_key])
            else:
                S.op('act', lambda e: e.activation(out=dst_ap, in_=st, func=ACT.Copy), reads=[('wst', slot)], writes=[dst_key])

        sidx = [0]

        def rms_stats(src_ap, n_groups, glen, src_keys, src_is_psum, inv_n, use_ln=False):
            b = sidx[0] % 2
            sidx[0] += 1
            n = n_groups * glen
            sq_ap = sq[:, b * 768: b * 768 + n]
            ss_ap = ssA[:, b * 8: b * 8 + n_groups]
            ms_ap = msA[:, b * 8: b * 8 + n_groups]
            rs_ap = rsA[:, b * 8: b * 8 + n_groups]
            S.op('act', lambda e: e.activation(out=sq_ap, in_=src_ap, func=ACT.Square),
                 reads=src_keys, writes=[('sq', b), ('sqb', b)])
            S.op('dve', lambda e: e.tensor_reduce(out=ss_ap, in_=sq_ap.rearrange("p (a b) -> p a b", a=n_groups),
                                                  axis=AX.X, op=ALU.add),
                 reads=[('sq', b), ('sqb', b)], writes=[('ss', b)])
            if use_ln:
                S.op('act', lambda e: e.activation(out=ms_ap, in_=ss_ap, func=ACT.Ln, scale=inv_n, bias=mh[:, 8:9]),
                     reads=[('ss', b), 'mhe'], writes=[('ms', b)])
                S.op('act', lambda e: e.activation(out=rs_ap, in_=ms_ap, func=ACT.Exp, scale=-0.5),
                     reads=[('ms', b)], writes=[('rs', b)])
                return rs_ap, ('rs', b)
            S.op('dve', lambda e: e.tensor_scalar(out=ms_ap, in0=ss_ap, scalar1=inv_n, scalar2=EPS,
                                                  op0=ALU.mult, op1=ALU.add),
                 reads=[('ss', b)], writes=[('ms', b)])
            S.op('pool', lambda e: e.tensor_tensor(out=rs_ap, in0=ms_ap, in1=mh[:, 0:n_groups], op=ALU.pow),
                 reads=[('ms', b), 'mh'], writes=[('rs', b)])
            return rs_ap, ('rs', b)

        ntt_pending = [None]

        def ntt_flush():
            if ntt_pending[0] is not None:
                ntt_pending[0]()
                ntt_pending[0] = None

        def norm_transpose_tile(src_dram_ap, t, gcol, dst_fn, dst_keys, tcount):
            b = tcount % 4
            xt = xt_ap(b) if b < 2 else x1t_ap(b - 2)
            S.op('sp', lambda e: e.dma_start(out=xt, in_=src_dram_ap), writes=[('xt', b)], dma=('xt', b))
            col = (b % 2) * 8 + 4 + b // 2
            ss_ap = ssA[:, col:col + 1]
            ms_ap = msA[:, col:col + 1]
            rs_ap = rsA[:, col:col + 1]
            S.op('act', lambda e: e.activation(out=sq[:, 0:1024], in_=xt, func=ACT.Square, accum_out=ss_ap),
                 reads=[('xt', b)], writes=[('sq', 0), ('sq', 1), ('sqb', 0), ('sqb', 1), ('ss4', b)])
            S.op('act', lambda e: e.activation(out=ms_ap, in_=ss_ap, func=ACT.Ln, scale=1.0 / D, bias=mh[:, 8:9]),
                 reads=[('ss4', b), 'mhe'], writes=[('ms4', b)])
            S.op('act', lambda e: e.activation(out=rs_ap, in_=ms_ap, func=ACT.Exp, scale=-0.5),
                 reads=[('ms4', b)], writes=[('rs4', b)])
            S.op('dve', lambda e: e.tensor_scalar(out=xt, in0=xt, scalar1=rs_ap, scalar2=None, op0=ALU.mult),
                 reads=[('xt', b), ('rs4', b)], writes=[('xt', b)])

            def tail():
                for half in range(2):
                    bank = (0 if b % 2 == 0 else 2) + half
                    for j in range(4):
                        c = half * 4 + j
                        S.op('pe', lambda e, c=c, j=j, bank=bank: e.transpose(out=PS[bank][:, j * 128:(j + 1) * 128],
                                                                              in_=xt[:, c * 128:(c + 1) * 128], identity=idf[:]),
                             reads=[('xt', b), 'idf'], writes=[pk(bank)])
                    gsrc = ngt[:, gcol + half * 4: gcol + half * 4 + 4].unsqueeze(2).to_broadcast([128, 4, 128])
                    S.op('dve', lambda e, half=half, bank=bank, gsrc=gsrc: e.tensor_tensor(
                        out=dst_fn(half), in0=PS[bank][:, 0:512].rearrange("p (c k) -> p c k", c=4), in1=gsrc, op=ALU.mult),
                        reads=[pk(bank), 'ngt'], writes=dst_keys(half))
            ntt_flush()
            ntt_pending[0] = tail

        def proj_head(pbase, nqk, nv, gain_ops, t, rope, ropekey, vslot0, ucount):
            banks = sorted(set((pbase * 512 + i * 128) // 512 for i in range(nqk + nv)))
            bkeys = [pk(bk) for bk in banks]
            c0 = pbase * 512
            src = PSALL[:, c0:c0 + nqk * 128]
            for vi in range(nv):
                S.op('act', lambda e, vi=vi: e.activation(out=v_ap(vslot0 + vi, t, 128),
                                                          in_=PSALL[:, c0 + (nqk + vi) * 128: c0 + (nqk + vi + 1) * 128], func=ACT.Copy),
                     reads=bkeys, writes=[('v', vslot0 + vi, t)])
            sb_i = sidx[0] % 2
            sidx[0] += 1
            n = nqk * 128
            sq_ap = sq[:, sb_i * 768: sb_i * 768 + n]
            ss_ap = ssA[:, sb_i * 8: sb_i * 8 + nqk]
            S.op('act', lambda e: e.activation(out=sq_ap, in_=src, func=ACT.Square),
                 reads=bkeys, writes=[('sq', sb_i), ('sqb', sb_i)])
            S.op('dve', lambda e: e.tensor_reduce(out=ss_ap, in_=sq_ap.rearrange("p (a b) -> p a b", a=nqk),
                                                  axis=AX.X, op=ALU.add),
                 reads=[('sq', sb_i), ('sqb', sb_i)], writes=[('ss', sb_i)])
            b = ucount % 2
            tf = tmpf[:, b * 768: b * 768 + nqk * 128]
            tf3 = tf.rearrange("p (a b) -> p a b", a=nqk)
            TK = [('tmpf', b)]
            return dict(nqk=nqk, gain_ops=gain_ops, t=t, rope=rope, ropekey=ropekey, ucount=ucount, b=b, tf3=tf3, TK=TK,
                        sb_i=sb_i, src=src, bkeys=bkeys)

        def proj_head2(cx):
            nqk, sb_i, src, bkeys, tf3, TK = (cx[k_] for k_ in ('nqk', 'sb_i', 'src', 'bkeys', 'tf3', 'TK'))
            ss_ap = ssA[:, sb_i * 8: sb_i * 8 + nqk]
            ms_ap = msA[:, sb_i * 8: sb_i * 8 + nqk]
            rs_ap = rsA[:, sb_i * 8: sb_i * 8 + nqk]
            S.op('act', lambda e: e.activation(out=ms_ap, in_=ss_ap, func=ACT.Ln, scale=1.0 / E, bias=mh[:, 8:9]),
                 reads=[('ss', sb_i), 'mhe'], writes=[('ms', sb_i)])
            S.op('act', lambda e: e.activation(out=rs_ap, in_=ms_ap, func=ACT.Exp, scale=-0.5),
                 reads=[('ms', sb_i)], writes=[('rs', sb_i)])
            S.op('dve', lambda e: e.tensor_tensor(out=tf3, in0=src.rearrange("p (a b) -> p a b", a=nqk),
                                                  in1=rs_ap.unsqueeze(2).to_broadcast([128, nqk, 128]), op=ALU.mult),
                 reads=bkeys + [('rs', sb_i)], writes=TK)

        def proj_tail(cx):
            nqk, gain_ops, t, rope, ropekey, ucount, b, tf3, TK = (cx[k_] for k_ in
                                                                   ('nqk', 'gain_ops', 't', 'rope', 'ropekey', 'ucount', 'b', 'tf3', 'TK'))
            b3 = ucount % 3
            qn_ap = qn[:, b3 * 768: b3 * 768 + nqk * 128] if b3 < 2 else Ub[:, 3584:3584 + nqk * 128]
            qn3 = qn_ap.rearrange("p (a b) -> p a b", a=nqk)
            QA, QB, QC = ('qn3', b3, 'a'), ('qn3', b3, 'b'), ('qn3', b3, 'c')
            cx['qn_ap'] = qn_ap
            cx['QK3'] = [QA, QB, QC]
            for (b0, nb_, gsrc) in gain_ops:
                S.op('dve', lambda e, b0=b0, nb_=nb_, gsrc=gsrc: e.tensor_tensor(out=tf3[:, b0:b0 + nb_, :], in0=tf3[:, b0:b0 + nb_, :],
                                                                            in1=gsrc, op=ALU.mult),
                     reads=TK + ['gains'], writes=TK)
            cos_ap, sin_ap, Rr = rope
            cosb = cos_ap.unsqueeze(1).to_broadcast([128, nqk, Rr])
            sinb = sin_ap.unsqueeze(1).to_broadcast([128, nqk, Rr])
            x1 = tf3[:, :, 0:Rr]
            x2 = tf3[:, :, Rr:2 * Rr]
            nr = nqk * Rr
            ra3 = sq[:, b * 768: b * 768 + nr].rearrange("p (a b) -> p a b", a=nqk)
            rb3 = sq[:, b * 768 + 384: b * 768 + 384 + nr].rearrange("p (a b) -> p a b", a=nqk)
            ra4 = U[:, b * 768: b * 768 + nr].rearrange("p (a b) -> p a b", a=nqk)
            rb4 = U[:, b * 768 + 384: b * 768 + 384 + nr].rearrange("p (a b) -> p a b", a=nqk)
            SQK = ('sq', b)
            S.op('dve', lambda e: e.tensor_tensor(out=ra3, in0=x1, in1=cosb, op=ALU.mult),
                 reads=TK + [ropekey], writes=[SQK])
            S.op('pool', lambda e: e.tensor_tensor(out=rb3, in0=x2, in1=sinb, op=ALU.mult),
                 reads=TK + [ropekey], writes=[('sqb', b)])
            S.op('dve', lambda e: e.tensor_tensor(out=ra4, in0=x2, in1=cosb, op=ALU.mult),
                 reads=TK + [ropekey], writes=[('ra4', b)])
            S.op('pool', lambda e: e.tensor_tensor(out=rb4, in0=x1, in1=sinb, op=ALU.mult),
                 reads=TK + [ropekey], writes=[('rb4', b)])
            S.op('dve', lambda e: e.tensor_tensor(out=qn3[:, :, 0:Rr], in0=ra3, in1=rb3, op=ALU.subtract),
                 reads=[SQK, ('sqb', b)], writes=[QA])
            S.op('pool', lambda e: e.tensor_tensor(out=qn3[:, :, Rr:2 * Rr], in0=ra4, in1=rb4, op=ALU.add),
                 reads=[('ra4', b), ('rb4', b)], writes=[QB])
            if 2 * Rr < 128:
                S.op('pool', lambda e: e.tensor_copy(out=qn3[:, :, 2 * Rr:128], in_=tf3[:, :, 2 * Rr:128]),
                     reads=TK, writes=[QC])

        def proj_tail_b(cx):
            nqk, t, ucount, qn_ap = cx['nqk'], cx['t'], cx['ucount'], cx['qn_ap']
            QA, QB, QC = cx['QK3']
            tb = TP[ucount % 2]
            for i in range(nqk):
                S.op('pe', lambda e, i=i: e.transpose(out=PSB[tb][:, i * 128:(i + 1) * 128],
                                                     in_=qn_ap[:, i * 128:(i + 1) * 128], identity=idb[:]),
                     reads=[QA, QB, QC, 'idb'], writes=[pk(tb)])
            qkdst = R[:, 0:nqk * 4096].rearrange("p (s k) -> p s k", s=nqk)[:, :, t * 128:(t + 1) * 128]
            S.op('act', lambda e: e.activation(out=qkdst, in_=PSB[tb][:, 0:nqk * 128].rearrange("p (s k) -> p s k", s=nqk),
                                               func=ACT.Copy),
                 reads=[pk(tb)], writes=[('qk', sl, t) for sl in range(nqk)])

        acount = [0]
        ptcount = [0]
        sccount = [0]
        fcount = [0]
        pend = []
        att_cfg = dict(banks=[2, 3], depth=1)

        def att_flush_one():
            q = pend.pop(0)
            pb, ab = q['pb'], q['ab']
            p_base = q['p_base']
            acc = PS[ab][:, 0:129]
            for (i, v_ap_, v_keys, first, last) in q['pv']:
                S.op('pe', lambda e, i=i, v_ap_=v_ap_, first=first, last=last, p_base=p_base, acc=acc: e.matmul(
                    acc, lhsT=p_base[:, i * 128:(i + 1) * 128], rhs=v_ap_, start=first, stop=last),
                    reads=[('pt', pb)] + list(v_keys), writes=[pk(ab)])
            if q['fin'] is not None:
                q['fin'](ab)

        def att_flush_all():
            while pend:
                att_flush_one()

        def attention_job(blocks, fin):
            oa_ = att_cfg.get('oa', OA)
            ab = oa_[acount[0] % len(oa_)]
            acount[0] += 1
            nb = len(blocks)
            bi = 0
            grp = att_cfg.get('group', 4)
            while bi < nb:
                quad = blocks[bi:bi + grp]
                n = len(quad)
                banks = att_cfg['banks']
                sbk = banks[sccount[0] % len(banks)]
                sccount[0] += 1
                sbl = list(sbk) if isinstance(sbk, tuple) else [sbk]
                sb0 = sbl[0]
                skeys = [pk(x_) for x_ in sbl]
                pb = ptcount[0] % 3
                ptcount[0] += 1
                for i, blk in enumerate(quad):
                    q_ap, q_keys, k_ap, k_keys, v_ap_, v_keys, mi = blk
                    o_ap = PSALL[:, sb0 * 512 + i * 128: sb0 * 512 + (i + 1) * 128]
                    bkey = pk(sb0 + i // 4)
                    S.op('pe', lambda e, o_ap=o_ap, k_ap=k_ap, q_ap=q_ap, mi=mi: e.matmul(
                        o_ap, lhsT=k_ap, rhs=q_ap, start=True, stop=(mi is None)),
                         reads=list(q_keys) + list(k_keys), writes=[bkey])
                    if mi is not None:
                        S.op('pe', lambda e, o_ap=o_ap, mi=mi: e.matmul(
                            o_ap, lhsT=idb[:], rhs=maskb[:, mi * 128:(mi + 1) * 128], start=False, stop=True),
                             reads=['idb', 'maskb'], writes=[bkey])
                ptb = att_cfg.get('pt', None)
                if ptb is None:
                    p_base = pt[:, pb * 512:(pb + 1) * 512]
                else:
                    p_base = ptb(pb)
                p_ap = p_base[:, 0:n * 128]
                S.op('act', lambda e, p_ap=p_ap, sb0=sb0, n=n: e.activation(out=p_ap, in_=PSALL[:, sb0 * 512: sb0 * 512 + n * 128],
                                                                           func=ACT.Exp, scale=SCALE),
                     reads=skeys[:(n + 3) // 4], writes=[('pt', pb)])
                pv = []
                for i, blk in enumerate(quad):
                    gi = bi + i
                    pv.append((i, blk[4], blk[5], gi == 0, gi == nb - 1))
                bi += n
                pend.append(dict(pb=pb, ab=ab, pv=pv, fin=(fin if bi >= nb else None), p_base=p_base))
                while len(pend) > att_cfg['depth']:
                    att_flush_one()

        for l in range(n_layers):
            if stopped:
                break
            if stop_spec is not None and ':' in stop_spec:
                stop = stop_spec.split(':')[1] if int(stop_spec.split(':')[0]) == l else None
            x_src = x_in if l == 0 else x1_scr
            x_dst = x1_scr if (l == 0 and n_layers > 1) else out
            w_in = w_in_a if l == 0 else w_in_b
            g_q_mem = 8 + l
            g_k_mem = 10 + l

            Wkv = R[:, 0:8192].rearrange("p (c n) -> p c n", c=8)
            memT = R[:, 8192:8192 + 2048]
            for c_ in range(8):
                load_weight_rows(w_kv[l, c_ * 128:(c_ + 1) * 128, :], Wkv[:, c_, :], ('wkv', c_))
            for mt in range(2):
                norm_transpose_tile(mem_in[mt * 128:(mt + 1) * 128, :], mt, 16 + l * 8,
                                    lambda half, mt=mt: memT[:, half * 1024:(half + 1) * 1024].rearrange(
                                        "p (c k) -> p c k", c=4)[:, :, mt * 128:(mt + 1) * 128],
                                    lambda half, mt=mt: [('memT', mt, half * 4 + j_) for j_ in range(4)], mt)
            ntt_flush()
            S.op('pool', lambda e: e.memset(Vm[:], 1.0), writes=['Vm'])
            for mt in range(2):
                for half in range(2):
                    bank = PJ[half]
                    for c in range(8):
                        S.op('pe', lambda e, c=c, half=half, bank=bank, mt=mt: e.matmul(
                            PS[bank][:, 0:512], lhsT=memT[:, c * 256 + mt * 128: c * 256 + (mt + 1) * 128],
                            rhs=Wkv[:, c, half * 512:(half + 1) * 512], start=(c == 0), stop=(c == 7)),
                            reads=[('memT', mt, c), ('wkv', c)], writes=[pk(bank)])
                bank = PJ[0]
                src = PS[bank][:, 0:512]
                rs_ap, rs_key = rms_stats(src, 4, 128, [pk(bank)], True, 1.0 / E)
                tf3 = tmpf[:, 0:512].rearrange("p (a b) -> p a b", a=4)
                S.op('dve', lambda e, src=src, rs_ap=rs_ap, tf3=tf3: e.tensor_tensor(
                    out=tf3, in0=src.rearrange("p (a b) -> p a b", a=4),
                    in1=rs_ap.unsqueeze(2).to_broadcast([128, 4, 128]), op=ALU.mult),
                    reads=[pk(bank), rs_key], writes=[('tmpf', 0, 0)])
                qn3 = qn[:, 0:512].rearrange("p (a b) -> p a b", a=4)
                gsrc = gains[:, g_k_mem * 128:(g_k_mem + 1) * 128].unsqueeze(1).to_broadcast([128, 4, 128])
                S.op('pool', lambda e, qn3=qn3, tf3=tf3, gsrc=gsrc: e.tensor_tensor(out=qn3, in0=tf3, in1=gsrc, op=ALU.mult),
                     reads=[('tmpf', 0, 0), 'gains'], writes=[('qn', 0, 'a')])
                tb = TP[0]
                for hd in range(4):
                    S.op('pe', lambda e, hd=hd: e.transpose(out=PSB[tb][:, hd * 128:(hd + 1) * 128],
                                                           in_=qn[:, hd * 128:(hd + 1) * 128], identity=idb[:]),
                         reads=[('qn', 0, 'a'), 'idb'], writes=[pk(tb)])
                S.op('act', lambda e, mt=mt: e.activation(
                    out=KmT[:].rearrange("p (h k) -> p h k", h=4)[:, :, mt * 128:(mt + 1) * 128],
                    in_=PSB[tb][:, 0:512].rearrange("p (h k) -> p h k", h=4), func=ACT.Copy),
                    reads=[pk(tb)], writes=['KmT'])
                bank = PJ[1]
                S.op('dve', lambda e, mt=mt, bank=bank: e.tensor_copy(
                    out=Vm[:, mt * 520:(mt + 1) * 520].rearrange("p (h k) -> p h k", h=4)[:, :, 0:128],
                    in_=PS[bank][:, 0:512].rearrange("p (h k) -> p h k", h=4)),
                    reads=[pk(bank)], writes=['Vm'])
            if stop == 'M':
                S.barrier()
                break

            for t in range(NT):
                norm_transpose_tile(x_src[t * 128:(t + 1) * 128, :], t, l * 8,
                                    lambda half, t=t: hT[:, half * 16384:(half + 1) * 16384].rearrange(
                                        "p (c k) -> p c k", c=4)[:, :, t * 128:(t + 1) * 128],
                                    lambda half, t=t: [('hT', t, half * 4 + j_) for j_ in range(4)], t)
            ntt_flush()
            S.barrier()
            if stop == 'p1':
                break

            S.op('pool', lambda e: e.memset(R[:, V_OFF:R_N], 1.0), writes=['Vall'])
            S.barrier()
            if l == 0:
                iters = [dict(cols=[((s_ * 3 + g) * 8 + h) * 128 for s_ in range(3) for g in range(3)], nqk=6, nv=3, hd=h)
                         for h in range(8)]
                gain_ops = [(0, 6, gains[:, 0:768].rearrange("p (a b) -> p a b", a=6))]
            else:
                iters = [dict(cols=[(kvh * 4 + j) * 128 for j in range(4)] + [1024 + kvh * 128, 1280 + kvh * 128], nqk=5, nv=1, hd=kvh)
                         for kvh in range(2)]
                gain_ops = [(0, 4, gains[:, 6 * 128:7 * 128].unsqueeze(1).to_broadcast([128, 4, 128])),
                            (4, 1, gains[:, 7 * 128:8 * 128].unsqueeze(1))]
            if l == 0:
                att_cfg['banks'] = [0, 1, 2, 3]
                att_cfg['group'] = 4
                att_cfg['pt'] = None
            else:
                att_cfg['banks'] = [(0, 1), (2, 3), (6, 7)]
                att_cfg['group'] = 8
                att_cfg['pt'] = lambda pb_: R[:, 5 * 4096 + pb_ * 1024: 5 * 4096 + (pb_ + 1) * 1024]
            att_cfg['depth'] = 2

            def emit_iter_load(it):
                ncols = (it['nqk'] + it['nv']) * 128
                Wv_ = Wbuf[:, 0:8 * ncols].rearrange("p (c n) -> p c n", c=8)
                for bi_, col in enumerate(it['cols']):
                    load_weight_piece(w_in[:, col:col + 128], 8, Wv_[:, :, bi_ * 128:(bi_ + 1) * 128], ('wbuf', bi_))

            emit_iter_load(iters[0])
            tilecount = 0
            for it_i, it in enumerate(iters):
                nqk, nv = it['nqk'], it['nv']
                nblk = nqk + nv
                ncols = nblk * 128
                Wv = Wbuf[:, 0:8 * ncols].rearrange("p (c n) -> p c n", c=8)
                prev_cx = None
                prev2_cx = None
                for t in range(NT):
                    pbase = 0 if tilecount % 2 == 0 else 3
                    c0 = 0
                    while c0 < ncols:
                        n_ = min(PROJ_N, 512 - (c0 % 512), ncols - c0)
                        bank = pbase + c0 // 512
                        wkeys = [('wbuf', j) for j in range(c0 // 128, (c0 + n_) // 128)]
                        for c in range(8):
                            S.op('pe', lambda e, c=c, t=t, c0=c0, n_=n_, pbase=pbase, Wv=Wv: e.matmul(
                                PSALL[:, pbase * 512 + c0: pbase * 512 + c0 + n_], lhsT=hT_ap(c, t), rhs=Wv[:, c, c0:c0 + n_],
                                start=(c == 0), stop=(c == 7)),
                                reads=[('hT', t, c)] + wkeys, writes=[pk(bank)])
                        c0 += n_
                    if l == 0:
                        rope = (ropeA[:, t * 16:(t + 1) * 16], ropeA[:, 512 + t * 16: 512 + (t + 1) * 16], 16)
                        ropekey = 'ropeA'
                    else:
                        rbuf = tilecount % 2
                        S.op('sp', lambda e, rbuf=rbuf, t=t: e.dma_start(
                            out=ropeBt[:, rbuf * 128:(rbuf + 1) * 128].rearrange("p (a b) -> p a b", a=2),
                            in_=ropeB_in.rearrange("p (a t b) -> p a t b", a=2, t=32)[:, :, t, :]),
                            writes=[('ropeB', rbuf)], dma=('ropeB', rbuf))
                        rope = (ropeBt[:, rbuf * 128: rbuf * 128 + 64], ropeBt[:, rbuf * 128 + 64: rbuf * 128 + 128], 64)
                        ropekey = ('ropeB', rbuf)
                    if prev_cx is not None:
                        proj_tail(prev_cx)
                    cx_ = proj_head(pbase, nqk, nv, gain_ops, t, rope, ropekey, 0, tilecount)
                    if prev2_cx is not None:
                        proj_tail_b(prev2_cx)
                    proj_head2(cx_)
                    prev2_cx = prev_cx
                    prev_cx = cx_
                    tilecount += 1
                proj_tail(prev_cx)
                proj_tail_b(prev2_cx)
                proj_tail_b(prev_cx)
                if it_i + 1 < len(iters):
                    emit_iter_load(iters[it_i + 1])
                if stop == 'p2proj':
                    continue
                hd = it['hd']
                jobs = []
                if l == 0:
                    for T in range(NT):
                        blocks = []
                        for g, (window, dil) in enumerate(A_GROUPS):
                            for dl in range(-9, 10):
                                if (g, dl) not in MASK_TABLE:
                                    continue
                                kb = T + dl
                                if kb < 0 or kb >= NT:
                                    continue
                                blocks.append((qk_ap(g, T * 128), [('qk', g, T)], qk_ap(3 + g, kb * 128), [('qk', 3 + g, kb)],
                                               v_ap(g, kb), [('v', g, kb)], MASK_TABLE[(g, dl)]))
                        jobs.append((blocks, T, hd))
                else:
                    for j in range(4):
                        for T in range(NT):
                            blocks = []
                            for kb in range(NT):
                                blocks.append((qk_ap(j, T * 128), [('qk', j, T)], qk_ap(4, kb * 128), [('qk', 4, kb)],
                                               v_ap(0, kb), [('v', 0, kb)], None))
                            jobs.append((blocks, T, hd * 4 + j))
                for blocks, T, hcol in jobs:
                    def fin(ab, T=T, hcol=hcol):
                        ob_i = fcount[0] % 4
                        fcount[0] += 1
                        rc_ap = rc[:, ob_i:ob_i + 1]
                        ob_ap = ob[:, ob_i * 128:(ob_i + 1) * 128]
                        S.op('dve', lambda e: e.reciprocal(out=rc_ap, in_=PS[ab][:, 128:129]),
                             reads=[pk(ab)], writes=[('rc', ob_i)])
                        S.op('dve', lambda e: e.tensor_scalar(out=ob_ap, in0=PS[ab][:, 0:128], scalar1=rc_ap, scalar2=None,
                                                              op0=ALU.mult),
                             reads=[pk(ab), ('rc', ob_i)], writes=[('ob', ob_i)])
                        S.op('sp', lambda e: e.dma_start(out=o_scr[T * 128:(T + 1) * 128, hcol * 128:(hcol + 1) * 128], in_=ob_ap),
                             reads=[('ob', ob_i)], writes=[('oscr', T)], dma=('ob', ob_i))
                    attention_job(blocks, fin)
                att_flush_all()
            att_cfg['banks'] = [2, 3]
            att_cfg['depth'] = 1
            S.barrier()
            if stop in ('p2', 'p2proj'):
                break

            Wg = R[:, 0:16384].rearrange("p (c n) -> p c n", c=8)
            Wo = R[:, 16384:28672].rearrange("p (c n) -> p c n", c=12)
            gcol0 = 9216 if l == 0 else 1536
            for c_ in range(8):
                load_weight_rows(w_in[c_ * 128:(c_ + 1) * 128, gcol0:gcol0 + 1024], Wg[:, c_, 0:1024], ('wg', c_, 0))
            for c_ in range(8):
                load_weight_rows(w_in[c_ * 128:(c_ + 1) * 128, gcol0 + 1024:gcol0 + 2048], Wg[:, c_, 1024:2048], ('wg', c_, 1))
            for cc_ in range(12):
                load_weight_rows(w_out[l, cc_ * 128:(cc_ + 1) * 128, :], Wo[:, cc_, :], ('wo', cc_))
            KmT3 = KmT[:].rearrange("p (h k) -> p h k", h=4)
            att_cfg['banks'] = [2, 0, 5]
            att_cfg['group'] = 4
            att_cfg['pt'] = None
            att_cfg['depth'] = 2
            att_cfg['oa'] = [4, 7]

            def gate_chain(t, j, gb, o_j, o_keys):
                th_ap = th[:, j * 512:(j + 1) * 512]
                S.op('act', lambda e: e.activation(out=th_ap, in_=PS[gb][:, 0:512], func=ACT.Tanh, scale=0.5),
                     reads=[pk(gb)], writes=[('th', j)])
                S.op('dve', lambda e: e.scalar_tensor_tensor(out=th_ap, in0=th_ap, scalar=1.0, in1=PS[gb][:, 0:512],
                                                             op0=ALU.add, op1=ALU.mult),
                     reads=[pk(gb), ('th', j)], writes=[('th', j)])
                yb_ap = yb[:, j * 512:(j + 1) * 512]
                S.op('dve', lambda e: e.scalar_tensor_tensor(out=yb_ap, in0=th_ap, scalar=0.5, in1=o_j, op0=ALU.mult, op1=ALU.mult),
                     reads=[('th', j)] + o_keys, writes=[('yb', j)])

            def gate_proj(t, j, gb):
                for c in range(8):
                    S.op('pe', lambda e, c=c: e.matmul(PS[gb][:, 0:512], lhsT=hT_ap(c, t), rhs=Wg[:, c, 512 + j * 512: 1024 + j * 512],
                                                       start=(c == 0), stop=(c == 7)),
                         reads=[('hT', t, c), ('wg', c, (512 + j * 512) // 1024)], writes=[pk(gb)])

            def y_transposes(t, j, tpb, eng):
                b = t % 2
                yb_ap = yb[:, j * 512:(j + 1) * 512]
                yTb = yT[:, b * 1536:(b + 1) * 1536]
                for i in range(4):
                    S.op('pe', lambda e, i=i: e.transpose(out=PSB[tpb][:, i * 128:(i + 1) * 128],
                                                         in_=yb_ap[:, i * 128:(i + 1) * 128], identity=idb[:]),
                         reads=[('yb', j), 'idb'], writes=[pk(tpb)])
                if eng == 'act':
                    S.op('act', lambda e: e.activation(out=yTb[:, j * 512:(j + 1) * 512], in_=PSB[tpb][:, 0:512], func=ACT.Copy),
                         reads=[pk(tpb)], writes=[('yT', b, j)])
                else:
                    S.op('dve', lambda e: e.tensor_copy(out=yTb[:, j * 512:(j + 1) * 512], in_=PSB[tpb][:, 0:512]),
                         reads=[pk(tpb)], writes=[('yT', b, j)])

            def stage_A(t):
                b = t % 2
                xt = xt_ap(b)
                ot_ap = ot[:, b * 1024:(b + 1) * 1024]
                S.op('sp', lambda e, x_src=x_src: e.dma_start(out=xt, in_=x_src[t * 128:(t + 1) * 128, :]),
                     writes=[('xt', b)], dma=('xt', b))
                S.op('sp', lambda e: e.dma_start(out=ot_ap, in_=o_scr[t * 128:(t + 1) * 128, :]),
                     reads=[('oscr', t)], writes=[('ot', b)], dma=('ot', b))
                bank = 0
                for c in range(8):
                    S.op('pe', lambda e, c=c: e.matmul(PS[bank][:, 0:512], lhsT=hT_ap(c, t), rhs=Wg[:, c, 0:512],
                                                       start=(c == 0), stop=(c == 7)),
                         reads=[('hT', t, c), ('wg', c, 0)], writes=[pk(bank)])
                gate_proj(t, 0, 1)
                gate_proj(t, 1, 3)
                src = PS[bank][:, 0:512]
                rs_ap, rs_key = rms_stats(src, 4, 128, [pk(bank)], True, 1.0 / E)
                tf3 = tmpf[:, b * 768: b * 768 + 512].rearrange("p (a b) -> p a b", a=4)
                S.op('dve', lambda e: e.tensor_tensor(out=tf3, in0=src.rearrange("p (a b) -> p a b", a=4),
                                                      in1=rs_ap.unsqueeze(2).to_broadcast([128, 4, 128]), op=ALU.mult),
                     reads=[pk(bank), rs_key], writes=[('tmpf', b, 0)])
                qn3 = qn[:, b * 768: b * 768 + 512].rearrange("p (a b) -> p a b", a=4)
                gsrc = gains[:, g_q_mem * 128:(g_q_mem + 1) * 128].unsqueeze(1).to_broadcast([128, 4, 128])
                S.op('pool', lambda e: e.tensor_tensor(out=qn3, in0=tf3, in1=gsrc, op=ALU.mult),
                     reads=[('tmpf', b, 0), 'gains'], writes=[('qn', b, 'a')])
                gate_chain(t, 0, 1, ot_ap[:, 0:512], [('ot', b)])
                gate_chain(t, 1, 3, ot_ap[:, 512:1024], [('ot', b)])

            def stage_B(t):
                b = t % 2
                qn_ap = qn[:, b * 768: b * 768 + 512]
                tb = 6
                for hd in range(4):
                    S.op('pe', lambda e, hd=hd: e.transpose(out=PSB[tb][:, hd * 128:(hd + 1) * 128],
                                                           in_=qn_ap[:, hd * 128:(hd + 1) * 128], identity=idb[:]),
                         reads=[('qn', b, 'a'), 'idb'], writes=[pk(tb)])
                S.op('act', lambda e: e.activation(out=qmT[:], in_=PSB[tb][:, 0:512], func=ACT.Copy),
                     reads=[pk(tb)], writes=['qmT'])
                y_transposes(t, 0, 7, 'dve')
                gate_proj(t, 2, 1)
                th2 = th[:, 1024:1536]
                S.op('act', lambda e: e.activation(out=th2, in_=PS[1][:, 0:512], func=ACT.Tanh, scale=0.5),
                     reads=[pk(1)], writes=[('th', 2)])
                S.op('dve', lambda e: e.scalar_tensor_tensor(out=th2, in0=th2, scalar=1.0, in1=PS[1][:, 0:512],
                                                             op0=ALU.add, op1=ALU.mult),
                     reads=[pk(1), ('th', 2)], writes=[('th', 2)])
                y_transposes(t, 1, 6, 'act')
                for hd in range(4):
                    blocks = []
                    for kb in range(2):
                        blocks.append((qmT[:, hd * 128:(hd + 1) * 128], ['qmT'], KmT3[:, hd, kb * 128:(kb + 1) * 128], ['KmT'],
                                       Vm[:, (kb * 4 + hd) * 130:(kb * 4 + hd) * 130 + 129], ['Vm'], None))

                    def fin(ab, hd=hd):
                        ob_i = fcount[0] % 4
                        fcount[0] += 1
                        rc_ap = rc[:, 4 + ob_i:5 + ob_i]
                        S.op('dve', lambda e: e.reciprocal(out=rc_ap, in_=PS[ab][:, 128:129]),
                             reads=[pk(ab)], writes=[('rcm', ob_i)])
                        S.op('act', lambda e: e.activation(out=om[:, hd * 128:(hd + 1) * 128], in_=PS[ab][:, 0:128], func=ACT.Copy,
                                                           scale=rc_ap),
                             reads=[pk(ab), ('rcm', ob_i)], writes=[('om', hd)])
                    attention_job(blocks, fin)
                att_flush_all()
                yb2 = yb[:, 1024:1536]
                S.op('dve', lambda e: e.scalar_tensor_tensor(out=yb2, in0=th2, scalar=0.5, in1=om[:, 0:512], op0=ALU.mult, op1=ALU.mult),
                     reads=[('th', 2)] + [('om', hd) for hd in range(4)], writes=[('yb', 2)])

            def stage_B2(t):
                y_transposes(t, 2, 7, 'act')

            def stage_C(t):
                b = t % 2
                xt = xt_ap(b)
                x1t = x1t_ap(b)
                yTb = yT[:, b * 1536:(b + 1) * 1536]
                for nb_ in range(2):
                    obk = [2, 0][nb_]
                    for cc in range(12):
                        S.op('pe', lambda e, cc=cc, nb_=nb_, obk=obk: e.matmul(
                            PS[obk][:, 0:512], lhsT=yTb[:, cc * 128:(cc + 1) * 128], rhs=Wo[:, cc, nb_ * 512:(nb_ + 1) * 512],
                            start=(cc == 0), stop=(cc == 11)),
                            reads=[('yT', b, cc // 4), ('wo', cc)], writes=[pk(obk)])
                    S.op('dve', lambda e, nb_=nb_, obk=obk: e.tensor_tensor(
                        out=x1t[:, nb_ * 512:(nb_ + 1) * 512], in0=PS[obk][:, 0:512], in1=xt[:, nb_ * 512:(nb_ + 1) * 512], op=ALU.add),
                        reads=[pk(obk), ('xt', b)], writes=[('x1t', b, nb_)])
                S.op('sp', lambda e, x_dst=x_dst: e.dma_start(out=x_dst[t * 128:(t + 1) * 128, :], in_=x1t),
                     reads=[('x1t', b, 0), ('x1t', b, 1)], writes=[('xdst', t)], dma=('x1t', b))

            def _bind(f, **kw):
                return f

            if stop != 'p3w':
                stage_A(0)
                for t in range(NT):
                    stage_B(t)
                    if t + 1 < NT:
                        stage_A(t + 1)
                    stage_B2(t)
                    stage_C(t)
            att_cfg['oa'] = OA
            S.barrier()
        S.emit(nc)
    return nc, S


def _host_consts(norm_g, mem_norm_g, mem_qn_g, mem_kn_g, qn_a, kn_a, qn_b, kn_b):
    ng = np.zeros((128, 32), np.float32)
    for l in range(2):
        ng[:, l * 8:(l + 1) * 8] = norm_g[l].reshape(8, 128).T
        ng[:, 16 + l * 8:16 + (l + 1) * 8] = mem_norm_g[l].reshape(8, 128).T
    rows = [qn_a[0, 0], qn_a[0, 1], qn_a[0, 2], kn_a[0, 0], kn_a[0, 1], kn_a[0, 2], qn_b[0], kn_b[0],
            mem_qn_g[0], mem_qn_g[1], mem_kn_g[0], mem_kn_g[1]]
    gains = np.ascontiguousarray(np.broadcast_to(np.concatenate(rows)[None, :], (128, 12 * 128))).astype(np.float32)
    pos = np.arange(S_TOK, dtype=np.float32)
    invA = (np.float32(500000.0) ** (-np.arange(0, 32, 2, dtype=np.float32) / np.float32(32))).astype(np.float32)
    angA = pos[:, None] * invA[None, :]
    row = np.repeat(np.arange(64, dtype=np.float32), 64)
    col = np.tile(np.arange(64, dtype=np.float32), 64)
    invB = (np.float32(10000.0) ** (-np.arange(0, 64, 2, dtype=np.float32) / np.float32(64))).astype(np.float32)
    angB = np.concatenate([row[:, None] * invB[None, :], col[:, None] * invB[None, :]], axis=-1)

    def lay(a):
        return a.reshape(32, 128, -1).transpose(1, 0, 2)
    ropeA = np.stack([lay(np.cos(angA)), lay(np.sin(angA))], axis=1).reshape(128, -1).astype(np.float32)
    ropeB = np.stack([lay(np.cos(angB)), lay(np.sin(angB))], axis=1).reshape(128, -1).astype(np.float32)
    return dict(ng=ng, gains=gains, ropeA=np.ascontiguousarray(ropeA), ropeB=np.ascontiguousarray(ropeB),
                ident=np.eye(128, dtype=np.float32), masks=np.ascontiguousarray(MASKS_NP.reshape(128, -1)))


_NC_CACHE = {}


def kernel(x, mem, norm_g, mem_norm_g, w_mem_kv, mem_qn_g, mem_kn_g, w_out,
           w_in_a, qn_a, kn_a, w_in_b, qn_b, kn_b):
    f = lambda a: np.ascontiguousarray(np.asarray(a, dtype=np.float32))
    x = f(x)
    mem = f(mem)
    consts = _host_consts(f(norm_g), f(mem_norm_g), f(mem_qn_g), f(mem_kn_g), f(qn_a), f(kn_a), f(qn_b), f(kn_b))
    shared = dict(w_in_a=f(w_in_a)[0], w_in_b=f(w_in_b)[0], w_out=f(w_out), w_mem_kv=f(w_mem_kv), **consts)
    if 'nc' not in _NC_CACHE:
        _NC_CACHE['nc'] = build()[0]
    nc = _NC_CACHE['nc']
    n = x.shape[0]
    in_maps = [dict(x=x[b], mem=mem[b], **shared) for b in range(n)]
    res = run_bass_kernel_spmd(nc, in_maps, core_ids=list(range(n)))
    return np.stack([np.asarray(r["y_out"], dtype=np.float32) for r in res.results], axis=0)
```

```python
import numpy as np
import concourse.bass as bass
import concourse.mybir as mybir
from concourse.bass_utils import run_bass_kernel_spmd

F32 = mybir.dt.float32
BF16 = mybir.dt.bfloat16
ALU = mybir.AluOpType
ACT = mybir.ActivationFunctionType
AX = mybir.AxisListType

ENGS = ['pe', 'act', 'dve', 'pool', 'sp']

S_TOK = 4096
NT = 32
D = 1024
E = 128
EPS = 1e-6
SCALE = float(E) ** -0.5
A_GROUPS = ((128, 1), (512, 4), (2048, 16))
IN_A = 11264
IN_B = 3584

PROJ_N = 512


class Sched:
    def __init__(self):
        self.ops = []
        self.state = {}
        self.last_on = {e: None for e in ENGS}
        self.dma_last = {}

    def op(self, eng, fn, reads=(), writes=(), dma=None):
        i = len(self.ops)
        deps = set()
        for k in reads:
            excl = isinstance(k, tuple) and k[0] == 'ps'
            st = self.state.setdefault(k, [None, []])
            if st[0] is not None:
                deps.add(st[0])
            if excl:
                for r in st[1]:
                    if self.ops[r]['eng'] != eng:
                        deps.add(r)
        for k in writes:
            st = self.state.setdefault(k, [None, []])
            if st[0] is not None:
                deps.add(st[0])
            for r in st[1]:
                deps.add(r)
        for k in reads:
            self.state[k][1].append(i)
        for k in writes:
            self.state[k] = [i, []]
        if dma is not None:
            p = self.dma_last.get(dma)
            if p is not None:
                deps.add(p)
            self.dma_last[dma] = i
        deps.discard(i)
        if eng == 'pe':
            deps = {d for d in deps if self.ops[d]['eng'] != 'pe'}
        best = {}
        keep = set()
        for d in deps:
            od = self.ops[d]
            if od['dma'] is not None:
                keep.add(d)
            elif d > best.get(od['eng'], -1):
                best[od['eng']] = d
        deps = keep | set(best.values())
        self.ops.append(dict(eng=eng, fn=fn, deps=deps, dma=dma))
        if fn is not None:
            self.last_on[eng] = i
        return i

    def barrier(self):
        lasts = [v for v in self.last_on.values() if v is not None]
        lasts += list(self.dma_last.values())
        lasts = set(lasts)
        for e in ENGS:
            deps = {d for d in lasts if not (self.ops[d]['eng'] == e and self.ops[d]['dma'] is None)}
            self.ops.append(dict(eng=e, fn=None, deps=deps, dma=None))
        self.state = {}

    def emit(self, nc):
        ops = self.ops
        needed = set()
        for o in ops:
            needed |= o['deps']
        dma_keys = []
        seen = set()
        for o in ops:
            if o['dma'] is not None and o['dma'] not in seen:
                seen.add(o['dma'])
                dma_keys.append(o['dma'])
        sem_ctx = []
        sems = {}
        for n_i, name in enumerate(['pe', 'act', 'dve', 'pool'] + [('dma', k) for k in dma_keys]):
            cm = nc.semaphore("s%d" % n_i)
            sems[name] = cm.__enter__()
            sem_ctx.append(cm)
        cnt = {}
        for i, o in enumerate(ops):
            if o['dma'] is not None:
                key = ('dma', o['dma'])
                cnt[key] = cnt.get(key, 0) + 16
                o['sem'], o['val'], o['inc'] = key, cnt[key], 16
            elif i in needed:
                assert o['fn'] is not None
                key = o['eng']
                cnt[key] = cnt.get(key, 0) + 1
                o['sem'], o['val'], o['inc'] = key, cnt[key], 1
            else:
                o['sem'] = None
        self.maxval = dict(cnt)

        def run(eng_name, e):
            waited = {}
            for o in ops:
                if o['eng'] != eng_name:
                    continue
                need = {}
                for d in o['deps']:
                    od = ops[d]
                    s, v = od['sem'], od['val']
                    if v > need.get(s, 0):
                        need[s] = v
                for s, v in need.items():
                    if waited.get(s, 0) < v:
                        e.wait_ge(sems[s], v)
                        waited[s] = v
                if o['fn'] is None:
                    continue
                ins = o['fn'](e)
                if o['sem'] is not None:
                    ins.then_inc(sems[o['sem']], o['inc'])

        with nc.Block() as block:
            @block.tensor
            def _(e):
                run('pe', e)

            @block.scalar
            def _(e):
                run('act', e)

            @block.vector
            def _(e):
                run('dve', e)

            @block.gpsimd
            def _(e):
                run('pool', e)

            @block.sync
            def _(e):
                run('sp', e)
        for cm in reversed(sem_ctx):
            cm.__exit__(None, None, None)


def _mask_tables():
    masks = []
    index = {}
    table = {}
    kk = np.arange(128)[:, None]
    qq = np.arange(128)[None, :]
    for g, (window, dil) in enumerate(A_GROUPS):
        hw = window // 2
        dmax = hw // 128 + (1 if hw % 128 else 0)
        dmax = max(dmax, 1)
        for dl in range(-dmax, dmax + 1):
            diff = 128 * dl + kk - qq
            m = ((diff % dil) == 0) & (np.abs(diff) <= hw)
            if not m.any():
                continue
            key = m.tobytes()
            if key not in index:
                index[key] = len(masks)
                masks.append(m.astype(np.float32))
            table[(g, dl)] = index[key]
    return np.stack(masks, axis=1), table


MASKS_NP, MASK_TABLE = _mask_tables()
NM = MASKS_NP.shape[1]


def build(n_layers=2, debug=False, stop=None):
    stop_spec = stop
    nc = bass.Bass("TRN2", target_bir_lowering=False)
    dk = "ExternalOutput" if debug else "Internal"
    x_in = nc.dram_tensor("x", [S_TOK, D], F32, kind="ExternalInput").ap()
    mem_in = nc.dram_tensor("mem", [256, D], F32, kind="ExternalInput").ap()
    w_in_a = nc.dram_tensor("w_in_a", [D, IN_A], F32, kind="ExternalInput").ap()
    w_in_b = nc.dram_tensor("w_in_b", [D, IN_B], F32, kind="ExternalInput").ap()
    w_out = nc.dram_tensor("w_out", [2, 1536, D], F32, kind="ExternalInput").ap()
    w_kv = nc.dram_tensor("w_mem_kv", [2, D, 1024], F32, kind="ExternalInput").ap()
    ng_in = nc.dram_tensor("ng", [128, 32], F32, kind="ExternalInput").ap()
    gains_in = nc.dram_tensor("gains", [128, 12 * 128], F32, kind="ExternalInput").ap()
    ropeA_in = nc.dram_tensor("ropeA", [128, 2 * 32 * 16], F32, kind="ExternalInput").ap()
    ropeB_in = nc.dram_tensor("ropeB", [128, 2 * 32 * 64], F32, kind="ExternalInput").ap()
    ident_in = nc.dram_tensor("ident", [128, 128], F32, kind="ExternalInput").ap()
    masks_in = nc.dram_tensor("masks", [128, NM * 128], F32, kind="ExternalInput").ap()
    out = nc.dram_tensor("y_out", [S_TOK, D], F32, kind="ExternalOutput").ap()
    x1_scr = nc.dram_tensor("x1_scr", [S_TOK, D], F32, kind=dk).ap()
    o_scr = nc.dram_tensor("o_scr", [S_TOK, D], BF16, kind=dk).ap()

    S = Sched()
    R_N = 6 * 4096 + 3 * 32 * 130
    import contextlib
    with contextlib.ExitStack() as es:
        def sb(name, shape, dt):
            return es.enter_context(nc.sbuf_tensor(name, shape, dt))

        hT = sb("hT", [128, 8 * 4096], BF16)
        R = sb("R", [128, R_N], BF16)
        Wbuf = sb("Wbuf", [128, 8 * 1152], BF16)
        Wbf = Wbuf.bitcast(F32)
        Wst = sb("Wst", [128, 2 * 8 * 128], F32)
        ropeA = sb("ropeA_t", [128, 2 * 32 * 16], F32)
        ropeBt = ropeA[:, 0:256]
        gains = sb("gains_t", [128, 12 * 128], F32)
        maskb = sb("maskb", [128, NM * 128], BF16)
        idf = sb("idf", [128, 128], F32)
        idb = sb("idb", [128, 128], BF16)
        ngt = sb("ngt", [128, 32], F32)
        mh = sb("mh", [128, 16], F32)
        KmT = sb("KmT", [128, 4 * 256], BF16)
        Vm = sb("Vm", [128, 2 * 4 * 130], BF16)
        ssA = sb("ssA", [128, 2 * 8], F32)
        msA = sb("msA", [128, 2 * 8], F32)
        rsA = sb("rsA", [128, 2 * 8], F32)
        sq = sb("sq", [128, 2 * 768], F32)
        tmpf = sb("tmpf", [128, 2 * 768], F32)
        U = sb("U", [128, 2560], F32)
        Ub = U.bitcast(BF16)
        qn = sb("qn", [128, 2 * 768], BF16)
        pt = sb("pt", [128, 3 * 512], BF16)
        rc = sb("rc", [128, 8], F32)
        ob = Ub[:, 3072:3584]
        th = Wbf[:, 0:1536]
        ot = Ub[:, 3072:5120]
        Wbb = Wbuf
        om = Wbb[:, 3072:3584]
        yb = Wbb[:, 3584:5120]
        yT = Ub[:, 0:3072]
        qmT = Wbb[:, 5120:5632]

        PSALL = es.enter_context(nc.psum_tensor("psall", [128, 4096], F32))
        PSBALL = PSALL.bitcast(BF16)
        PS = [PSALL[:, i * 512:(i + 1) * 512] for i in range(8)]
        PSB = [PSBALL[:, i * 1024:(i + 1) * 1024] for i in range(8)]
        PJ = [0, 1]
        SC = [2, 3]
        OA = [4, 5]
        TP = [6, 7]

        def pk(i):
            return ('ps', i)

        Rf = R.bitcast(F32)
        QK_OFF = 0
        V_OFF = 6 * 4096
        XT_OFF_F = 28672 // 2
        def xt_ap(b):
            return Rf[:, XT_OFF_F + b * 1024: XT_OFF_F + (b + 1) * 1024]

        def x1t_ap(b):
            return Rf[:, XT_OFF_F + 2048 + b * 1024: XT_OFF_F + 2048 + (b + 1) * 1024]

        def qk_ap(slot, t0, n=128):
            o = QK_OFF + slot * 4096 + t0
            return R[:, o:o + n]

        def v_ap(slot, kb, n=129):
            o = V_OFF + (slot * 32 + kb) * 130
            return R[:, o:o + n]

        def hT_ap(c, t):
            o = c * 4096 + t * 128
            return hT[:, o:o + 128]

        def wst_ap(slot, nchunk=8):
            return Wst[:, slot * 1024: slot * 1024 + nchunk * 128].rearrange("p (c n) -> p c n", n=128)

        S.op('sp', lambda e: e.dma_start(out=idf[:], in_=ident_in), writes=['idf'], dma='c0')
        S.op('sp', lambda e: e.dma_start(out=gains[:], in_=gains_in), writes=['gains'], dma='c1')
        S.op('sp', lambda e: e.dma_start(out=ngt[:], in_=ng_in), writes=['ngt'], dma='c2')
        S.op('sp', lambda e: e.dma_start(out=ropeA[:], in_=ropeA_in), writes=['ropeA'], dma='c3')
        S.op('sp', lambda e: e.dma_start(out=Rf[:, 0:NM * 128], in_=masks_in), writes=['mstage'], dma='c4')
        S.op('pool', lambda e: e.tensor_copy(out=idb[:], in_=idf[:]), reads=['idf'], writes=['idb'])
        S.op('pool', lambda e: e.tensor_scalar(out=maskb[:], in0=Rf[:, 0:NM * 128], scalar1=-1.0, scalar2=30000.0,
                                               op0=ALU.add, op1=ALU.mult), reads=['mstage'], writes=['maskb'])
        S.op('pool', lambda e: e.memset(mh[:, 0:8], -0.5), writes=['mh'])
        S.op('pool', lambda e: e.memset(mh[:, 8:16], EPS), writes=['mhe'])
        S.barrier()
        stopped = (stop == 'init')

        wcount = [0]

        def load_weight_piece(src_ap, nchunk, dst_ap, dst_key):
            slot = wcount[0] % 2
            wcount[0] += 1
            S.op('sp', lambda e: e.dma_start(out=wst_ap(slot, nchunk), in_=src_ap.rearrange("(c p) n -> p c n", p=128)),
                 writes=[('wst', slot)], dma=('wst', slot))
            S.op('pool', lambda e: e.tensor_copy(out=dst_ap, in_=wst_ap(slot, nchunk)),
                 reads=[('wst', slot)], writes=[dst_key])

        rcount = [0]

        def load_weight_rows(src_ap, dst_ap, dst_key):
            slot = wcount[0] % 2
            wcount[0] += 1
            eng = 'dve' if rcount[0] % 2 == 0 else 'act'
            rcount[0] += 1
            st = Wst[:, slot * 1024:(slot + 1) * 1024]
            S.op('sp', lambda e: e.dma_start(out=st, in_=src_ap), writes=[('wst', slot)], dma=('wst', slot))
            if eng == 'dve':
                S.op('dve', lambda e: e.tensor_copy(out=dst_ap, in_=st), reads=[('wst', slot)], writes=[dst_key])
            else:
                S.op('act', lambda e: e.activation(out=dst_ap, in_=st, func=ACT.Copy), reads=[('wst', slot)], writes=[dst_key])

        sidx = [0]

        def rms_stats(src_ap, n_groups, glen, src_keys, src_is_psum, inv_n, use_ln=False):
            b = sidx[0] % 2
            sidx[0] += 1
            n = n_groups * glen
            sq_ap = sq[:, b * 768: b * 768 + n]
            ss_ap = ssA[:, b * 8: b * 8 + n_groups]
            ms_ap = msA[:, b * 8: b * 8 + n_groups]
            rs_ap = rsA[:, b * 8: b * 8 + n_groups]
            S.op('act', lambda e: e.activation(out=sq_ap, in_=src_ap, func=ACT.Square),
                 reads=src_keys, writes=[('sq', b), ('sqb', b)])
            S.op('dve', lambda e: e.tensor_reduce(out=ss_ap, in_=sq_ap.rearrange("p (a b) -> p a b", a=n_groups),
                                                  axis=AX.X, op=ALU.add),
                 reads=[('sq', b), ('sqb', b)], writes=[('ss', b)])
            if use_ln:
                S.op('act', lambda e: e.activation(out=ms_ap, in_=ss_ap, func=ACT.Ln, scale=inv_n, bias=mh[:, 8:9]),
                     reads=[('ss', b), 'mhe'], writes=[('ms', b)])
                S.op('act', lambda e: e.activation(out=rs_ap, in_=ms_ap, func=ACT.Exp, scale=-0.5),
                     reads=[('ms', b)], writes=[('rs', b)])
                return rs_ap, ('rs', b)
            S.op('dve', lambda e: e.tensor_scalar(out=ms_ap, in0=ss_ap, scalar1=inv_n, scalar2=EPS,
                                                  op0=ALU.mult, op1=ALU.add),
                 reads=[('ss', b)], writes=[('ms', b)])
            S.op('pool', lambda e: e.tensor_tensor(out=rs_ap, in0=ms_ap, in1=mh[:, 0:n_groups], op=ALU.pow),
                 reads=[('ms', b), 'mh'], writes=[('rs', b)])
            return rs_ap, ('rs', b)

        ntt_pending = [None]

        def ntt_flush():
            if ntt_pending[0] is not None:
                ntt_pending[0]()
                ntt_pending[0] = None

        def norm_transpose_tile(src_dram_ap, t, gcol, dst_fn, dst_keys, tcount):
            b = tcount % 4
            xt = xt_ap(b) if b < 2 else x1t_ap(b - 2)
            S.op('sp', lambda e: e.dma_start(out=xt, in_=src_dram_ap), writes=[('xt', b)], dma=('xt', b))
            col = (b % 2) * 8 + 4 + b // 2
            ss_ap = ssA[:, col:col + 1]
            ms_ap = msA[:, col:col + 1]
            rs_ap = rsA[:, col:col + 1]
            S.op('act', lambda e: e.activation(out=sq[:, 0:1024], in_=xt, func=ACT.Square, accum_out=ss_ap),
                 reads=[('xt', b)], writes=[('sq', 0), ('sq', 1), ('sqb', 0), ('sqb', 1), ('ss4', b)])
            S.op('act', lambda e: e.activation(out=ms_ap, in_=ss_ap, func=ACT.Ln, scale=1.0 / D, bias=mh[:, 8:9]),
                 reads=[('ss4', b), 'mhe'], writes=[('ms4', b)])
            S.op('act', lambda e: e.activation(out=rs_ap, in_=ms_ap, func=ACT.Exp, scale=-0.5),
                 reads=[('ms4', b)], writes=[('rs4', b)])
            S.op('dve', lambda e: e.tensor_scalar(out=xt, in0=xt, scalar1=rs_ap, scalar2=None, op0=ALU.mult),
                 reads=[('xt', b), ('rs4', b)], writes=[('xt', b)])

            def tail():
                for half in range(2):
                    bank = (0 if b % 2 == 0 else 2) + half
                    for j in range(4):
                        c = half * 4 + j
                        S.op('pe', lambda e, c=c, j=j, bank=bank: e.transpose(out=PS[bank][:, j * 128:(j + 1) * 128],
                                                                              in_=xt[:, c * 128:(c + 1) * 128], identity=idf[:]),
                             reads=[('xt', b), 'idf'], writes=[pk(bank)])
                    gsrc = ngt[:, gcol + half * 4: gcol + half * 4 + 4].unsqueeze(2).to_broadcast([128, 4, 128])
                    S.op('dve', lambda e, half=half, bank=bank, gsrc=gsrc: e.tensor_tensor(
                        out=dst_fn(half), in0=PS[bank][:, 0:512].rearrange("p (c k) -> p c k", c=4), in1=gsrc, op=ALU.mult),
                        reads=[pk(bank), 'ngt'], writes=dst_keys(half))
            ntt_flush()
            ntt_pending[0] = tail

        def proj_head(pbase, nqk, nv, gain_ops, t, rope, ropekey, vslot0, ucount):
            banks = sorted(set((pbase * 512 + i * 128) // 512 for i in range(nqk + nv)))
            bkeys = [pk(bk) for bk in banks]
            c0 = pbase * 512
            src = PSALL[:, c0:c0 + nqk * 128]
            for vi in range(nv):
                S.op('act', lambda e, vi=vi: e.activation(out=v_ap(vslot0 + vi, t, 128),
                                                          in_=PSALL[:, c0 + (nqk + vi) * 128: c0 + (nqk + vi + 1) * 128], func=ACT.Copy),
                     reads=bkeys, writes=[('v', vslot0 + vi, t)])
            sb_i = sidx[0] % 2
            sidx[0] += 1
            n = nqk * 128
            sq_ap = sq[:, sb_i * 768: sb_i * 768 + n]
            ss_ap = ssA[:, sb_i * 8: sb_i * 8 + nqk]
            S.op('act', lambda e: e.activation(out=sq_ap, in_=src, func=ACT.Square),
                 reads=bkeys, writes=[('sq', sb_i), ('sqb', sb_i)])
            S.op('dve', lambda e: e.tensor_reduce(out=ss_ap, in_=sq_ap.rearrange("p (a b) -> p a b", a=nqk),
                                                  axis=AX.X, op=ALU.add),
                 reads=[('sq', sb_i), ('sqb', sb_i)], writes=[('ss', sb_i)])
            b = ucount % 2
            tf = tmpf[:, b * 768: b * 768 + nqk * 128]
            tf3 = tf.rearrange("p (a b) -> p a b", a=nqk)
            TK = [('tmpf', b)]
            return dict(nqk=nqk, gain_ops=gain_ops, t=t, rope=rope, ropekey=ropekey, ucount=ucount, b=b, tf3=tf3, TK=TK,
                        sb_i=sb_i, src=src, bkeys=bkeys)

        def proj_head2(cx):
            nqk, sb_i, src, bkeys, tf3, TK = (cx[k_] for k_ in ('nqk', 'sb_i', 'src', 'bkeys', 'tf3', 'TK'))
            ss_ap = ssA[:, sb_i * 8: sb_i * 8 + nqk]
            ms_ap = msA[:, sb_i * 8: sb_i * 8 + nqk]
            rs_ap = rsA[:, sb_i * 8: sb_i * 8 + nqk]
            S.op('act', lambda e: e.activation(out=ms_ap, in_=ss_ap, func=ACT.Ln, scale=1.0 / E, bias=mh[:, 8:9]),
                 reads=[('ss', sb_i), 'mhe'], writes=[('ms', sb_i)])
            S.op('act', lambda e: e.activation(out=rs_ap, in_=ms_ap, func=ACT.Exp, scale=-0.5),
                 reads=[('ms', sb_i)], writes=[('rs', sb_i)])
            S.op('dve', lambda e: e.tensor_tensor(out=tf3, in0=src.rearrange("p (a b) -> p a b", a=nqk),
                                                  in1=rs_ap.unsqueeze(2).to_broadcast([128, nqk, 128]), op=ALU.mult),
                 reads=bkeys + [('rs', sb_i)], writes=TK)

        def proj_tail(cx):
            nqk, gain_ops, t, rope, ropekey, ucount, b, tf3, TK = (cx[k_] for k_ in
                                                                   ('nqk', 'gain_ops', 't', 'rope', 'ropekey', 'ucount', 'b', 'tf3', 'TK'))
            b3 = ucount % 3
            qn_ap = qn[:, b3 * 768: b3 * 768 + nqk * 128] if b3 < 2 else Ub[:, 3584:3584 + nqk * 128]
            qn3 = qn_ap.rearrange("p (a b) -> p a b", a=nqk)
            QA, QB, QC = ('qn3', b3, 'a'), ('qn3', b3, 'b'), ('qn3', b3, 'c')
            cx['qn_ap'] = qn_ap
            cx['QK3'] = [QA, QB, QC]
            for (b0, nb_, gsrc) in gain_ops:
                S.op('dve', lambda e, b0=b0, nb_=nb_, gsrc=gsrc: e.tensor_tensor(out=tf3[:, b0:b0 + nb_, :], in0=tf3[:, b0:b0 + nb_, :],
                                                                            in1=gsrc, op=ALU.mult),
                     reads=TK + ['gains'], writes=TK)
            cos_ap, sin_ap, Rr = rope
            cosb = cos_ap.unsqueeze(1).to_broadcast([128, nqk, Rr])
            sinb = sin_ap.unsqueeze(1).to_broadcast([128, nqk, Rr])
            x1 = tf3[:, :, 0:Rr]
            x2 = tf3[:, :, Rr:2 * Rr]
            nr = nqk * Rr
            ra3 = sq[:, b * 768: b * 768 + nr].rearrange("p (a b) -> p a b", a=nqk)
            rb3 = sq[:, b * 768 + 384: b * 768 + 384 + nr].rearrange("p (a b) -> p a b", a=nqk)
            ra4 = U[:, b * 768: b * 768 + nr].rearrange("p (a b) -> p a b", a=nqk)
            rb4 = U[:, b * 768 + 384: b * 768 + 384 + nr].rearrange("p (a b) -> p a b", a=nqk)
            SQK = ('sq', b)
            S.op('dve', lambda e: e.tensor_tensor(out=ra3, in0=x1, in1=cosb, op=ALU.mult),
                 reads=TK + [ropekey], writes=[SQK])
            S.op('pool', lambda e: e.tensor_tensor(out=rb3, in0=x2, in1=sinb, op=ALU.mult),
                 reads=TK + [ropekey], writes=[('sqb', b)])
            S.op('dve', lambda e: e.tensor_tensor(out=ra4, in0=x2, in1=cosb, op=ALU.mult),
                 reads=TK + [ropekey], writes=[('ra4', b)])
            S.op('pool', lambda e: e.tensor_tensor(out=rb4, in0=x1, in1=sinb, op=ALU.mult),
                 reads=TK + [ropekey], writes=[('rb4', b)])
            S.op('dve', lambda e: e.tensor_tensor(out=qn3[:, :, 0:Rr], in0=ra3, in1=rb3, op=ALU.subtract),
                 reads=[SQK, ('sqb', b)], writes=[QA])
            S.op('pool', lambda e: e.tensor_tensor(out=qn3[:, :, Rr:2 * Rr], in0=ra4, in1=rb4, op=ALU.add),
                 reads=[('ra4', b), ('rb4', b)], writes=[QB])
            if 2 * Rr < 128:
                S.op('pool', lambda e: e.tensor_copy(out=qn3[:, :, 2 * Rr:128], in_=tf3[:, :, 2 * Rr:128]),
                     reads=TK, writes=[QC])

        def proj_tail_b(cx):
            nqk, t, ucount, qn_ap = cx['nqk'], cx['t'], cx['ucount'], cx['qn_ap']
            QA, QB, QC = cx['QK3']
            tb = TP[ucount % 2]
            for i in range(nqk):
                S.op('pe', lambda e, i=i: e.transpose(out=PSB[tb][:, i * 128:(i + 1) * 128],
                                                     in_=qn_ap[:, i * 128:(i + 1) * 128], identity=idb[:]),
                     reads=[QA, QB, QC, 'idb'], writes=[pk(tb)])
            qkdst = R[:, 0:nqk * 4096].rearrange("p (s k) -> p s k", s=nqk)[:, :, t * 128:(t + 1) * 128]
            S.op('act', lambda e: e.activation(out=qkdst, in_=PSB[tb][:, 0:nqk * 128].rearrange("p (s k) -> p s k", s=nqk),
                                               func=ACT.Copy),
                 reads=[pk(tb)], writes=[('qk', sl, t) for sl in range(nqk)])

        acount = [0]
        ptcount = [0]
        sccount = [0]
        fcount = [0]
        pend = []
        att_cfg = dict(banks=[2, 3], depth=1)

        def att_flush_one():
            q = pend.pop(0)
            pb, ab = q['pb'], q['ab']
            p_base = q['p_base']
            acc = PS[ab][:, 0:129]
            for (i, v_ap_, v_keys, first, last) in q['pv']:
                S.op('pe', lambda e, i=i, v_ap_=v_ap_, first=first, last=last, p_base=p_base, acc=acc: e.matmul(
                    acc, lhsT=p_base[:, i * 128:(i + 1) * 128], rhs=v_ap_, start=first, stop=last),
                    reads=[('pt', pb)] + list(v_keys), writes=[pk(ab)])
            if q['fin'] is not None:
                q['fin'](ab)

        def att_flush_all():
            while pend:
                att_flush_one()

        def attention_job(blocks, fin):
            oa_ = att_cfg.get('oa', OA)
            ab = oa_[acount[0] % len(oa_)]
            acount[0] += 1
            nb = len(blocks)
            bi = 0
            grp = att_cfg.get('group', 4)
            while bi < nb:
                quad = blocks[bi:bi + grp]
                n = len(quad)
                banks = att_cfg['banks']
                sbk = banks[sccount[0] % len(banks)]
                sccount[0] += 1
                sbl = list(sbk) if isinstance(sbk, tuple) else [sbk]
                sb0 = sbl[0]
                skeys = [pk(x_) for x_ in sbl]
                pb = ptcount[0] % 3
                ptcount[0] += 1
                for i, blk in enumerate(quad):
                    q_ap, q_keys, k_ap, k_keys, v_ap_, v_keys, mi = blk
                    o_ap = PSALL[:, sb0 * 512 + i * 128: sb0 * 512 + (i + 1) * 128]
                    bkey = pk(sb0 + i // 4)
                    S.op('pe', lambda e, o_ap=o_ap, k_ap=k_ap, q_ap=q_ap, mi=mi: e.matmul(
                        o_ap, lhsT=k_ap, rhs=q_ap, start=True, stop=(mi is None)),
                         reads=list(q_keys) + list(k_keys), writes=[bkey])
                    if mi is not None:
                        S.op('pe', lambda e, o_ap=o_ap, mi=mi: e.matmul(
                            o_ap, lhsT=idb[:], rhs=maskb[:, mi * 128:(mi + 1) * 128], start=False, stop=True),
                             reads=['idb', 'maskb'], writes=[bkey])
                ptb = att_cfg.get('pt', None)
                if ptb is None:
                    p_base = pt[:, pb * 512:(pb + 1) * 512]
                else:
                    p_base = ptb(pb)
                p_ap = p_base[:, 0:n * 128]
                S.op('act', lambda e, p_ap=p_ap, sb0=sb0, n=n: e.activation(out=p_ap, in_=PSALL[:, sb0 * 512: sb0 * 512 + n * 128],
                                                                           func=ACT.Exp, scale=SCALE),
                     reads=skeys[:(n + 3) // 4], writes=[('pt', pb)])
                pv = []
                for i, blk in enumerate(quad):
                    gi = bi + i
                    pv.append((i, blk[4], blk[5], gi == 0, gi == nb - 1))
                bi += n
                pend.append(dict(pb=pb, ab=ab, pv=pv, fin=(fin if bi >= nb else None), p_base=p_base))
                while len(pend) > att_cfg['depth']:
                    att_flush_one()

        for l in range(n_layers):
            if stopped:
                break
            if stop_spec is not None and ':' in stop_spec:
                stop = stop_spec.split(':')[1] if int(stop_spec.split(':')[0]) == l else None
            x_src = x_in if l == 0 else x1_scr
            x_dst = x1_scr if (l == 0 and n_layers > 1) else out
            w_in = w_in_a if l == 0 else w_in_b
            g_q_mem = 8 + l
            g_k_mem = 10 + l

            Wkv = R[:, 0:8192].rearrange("p (c n) -> p c n", c=8)
            memT = R[:, 8192:8192 + 2048]
            for c_ in range(8):
                load_weight_rows(w_kv[l, c_ * 128:(c_ + 1) * 128, :], Wkv[:, c_, :], ('wkv', c_))
            for mt in range(2):
                norm_transpose_tile(mem_in[mt * 128:(mt + 1) * 128, :], mt, 16 + l * 8,
                                    lambda half, mt=mt: memT[:, half * 1024:(half + 1) * 1024].rearrange(
                                        "p (c k) -> p c k", c=4)[:, :, mt * 128:(mt + 1) * 128],
                                    lambda half, mt=mt: [('memT', mt, half * 4 + j_) for j_ in range(4)], mt)
            ntt_flush()
            S.op('pool', lambda e: e.memset(Vm[:], 1.0), writes=['Vm'])
            for mt in range(2):
                for half in range(2):
                    bank = PJ[half]
                    for c in range(8):
                        S.op('pe', lambda e, c=c, half=half, bank=bank, mt=mt: e.matmul(
                            PS[bank][:, 0:512], lhsT=memT[:, c * 256 + mt * 128: c * 256 + (mt + 1) * 128],
                            rhs=Wkv[:, c, half * 512:(half + 1) * 512], start=(c == 0), stop=(c == 7)),
                            reads=[('memT', mt, c), ('wkv', c)], writes=[pk(bank)])
                bank = PJ[0]
                src = PS[bank][:, 0:512]
                rs_ap, rs_key = rms_stats(src, 4, 128, [pk(bank)], True, 1.0 / E)
                tf3 = tmpf[:, 0:512].rearrange("p (a b) -> p a b", a=4)
                S.op('dve', lambda e, src=src, rs_ap=rs_ap, tf3=tf3: e.tensor_tensor(
                    out=tf3, in0=src.rearrange("p (a b) -> p a b", a=4),
                    in1=rs_ap.unsqueeze(2).to_broadcast([128, 4, 128]), op=ALU.mult),
                    reads=[pk(bank), rs_key], writes=[('tmpf', 0, 0)])
                qn3 = qn[:, 0:512].rearrange("p (a b) -> p a b", a=4)
                gsrc = gains[:, g_k_mem * 128:(g_k_mem + 1) * 128].unsqueeze(1).to_broadcast([128, 4, 128])
                S.op('pool', lambda e, qn3=qn3, tf3=tf3, gsrc=gsrc: e.tensor_tensor(out=qn3, in0=tf3, in1=gsrc, op=ALU.mult),
                     reads=[('tmpf', 0, 0), 'gains'], writes=[('qn', 0, 'a')])
                tb = TP[0]
                for hd in range(4):
                    S.op('pe', lambda e, hd=hd: e.transpose(out=PSB[tb][:, hd * 128:(hd + 1) * 128],
                                                           in_=qn[:, hd * 128:(hd + 1) * 128], identity=idb[:]),
                         reads=[('qn', 0, 'a'), 'idb'], writes=[pk(tb)])
                S.op('act', lambda e, mt=mt: e.activation(
                    out=KmT[:].rearrange("p (h k) -> p h k", h=4)[:, :, mt * 128:(mt + 1) * 128],
                    in_=PSB[tb][:, 0:512].rearrange("p (h k) -> p h k", h=4), func=ACT.Copy),
                    reads=[pk(tb)], writes=['KmT'])
                bank = PJ[1]
                S.op('dve', lambda e, mt=mt, bank=bank: e.tensor_copy(
                    out=Vm[:, mt * 520:(mt + 1) * 520].rearrange("p (h k) -> p h k", h=4)[:, :, 0:128],
                    in_=PS[bank][:, 0:512].rearrange("p (h k) -> p h k", h=4)),
                    reads=[pk(bank)], writes=['Vm'])
            if stop == 'M':
                S.barrier()
                break

            if l == 0:
                first_cols = [((s_ * 3 + g) * 8 + 0) * 128 for s_ in range(3) for g in range(3)]
            else:
                first_cols = [j * 128 for j in range(4)] + [1024, 1280]
            Wv0_ = Wbuf[:, 0:8 * len(first_cols) * 128].rearrange("p (c n) -> p c n", c=8)
            first_pending = list(enumerate(first_cols))
            for t in range(NT):
                norm_transpose_tile(x_src[t * 128:(t + 1) * 128, :], t, l * 8,
                                    lambda half, t=t: hT[:, half * 16384:(half + 1) * 16384].rearrange(
                                        "p (c k) -> p c k", c=4)[:, :, t * 128:(t + 1) * 128],
                                    lambda half, t=t: [('hT', t, half * 4 + j_) for j_ in range(4)], t)
                if t % 3 == 2 and first_pending:
                    bi_, col_ = first_pending.pop(0)
                    load_weight_piece(w_in[:, col_:col_ + 128], 8, Wv0_[:, :, bi_ * 128:(bi_ + 1) * 128], ('wbuf', bi_))
            ntt_flush()
            while first_pending:
                bi_, col_ = first_pending.pop(0)
                load_weight_piece(w_in[:, col_:col_ + 128], 8, Wv0_[:, :, bi_ * 128:(bi_ + 1) * 128], ('wbuf', bi_))
            S.barrier()
            if stop == 'p1':
                break

            S.op('pool', lambda e: e.memset(R[:, V_OFF:R_N], 1.0), writes=['Vall'])
            S.barrier()
            if l == 0:
                iters = [dict(cols=[((s_ * 3 + g) * 8 + h) * 128 for s_ in range(3) for g in range(3)], nqk=6, nv=3, hd=h)
                         for h in range(8)]
                gain_ops = [(0, 6, gains[:, 0:768].rearrange("p (a b) -> p a b", a=6))]
            else:
                iters = [dict(cols=[(kvh * 4 + j) * 128 for j in range(4)] + [1024 + kvh * 128, 1280 + kvh * 128], nqk=5, nv=1, hd=kvh)
                         for kvh in range(2)]
                gain_ops = [(0, 4, gains[:, 6 * 128:7 * 128].unsqueeze(1).to_broadcast([128, 4, 128])),
                            (4, 1, gains[:, 7 * 128:8 * 128].unsqueeze(1))]
            if l == 0:
                att_cfg['banks'] = [0, 1, 2, 3]
                att_cfg['group'] = 4
                att_cfg['pt'] = None
            else:
                att_cfg['banks'] = [(0, 1), (2, 3), (6, 7)]
                att_cfg['group'] = 8
                att_cfg['pt'] = lambda pb_: R[:, 5 * 4096 + pb_ * 1024: 5 * 4096 + (pb_ + 1) * 1024]
            att_cfg['depth'] = 2

            def emit_iter_load(it):
                ncols = (it['nqk'] + it['nv']) * 128
                Wv_ = Wbuf[:, 0:8 * ncols].rearrange("p (c n) -> p c n", c=8)
                for bi_, col in enumerate(it['cols']):
                    load_weight_piece(w_in[:, col:col + 128], 8, Wv_[:, :, bi_ * 128:(bi_ + 1) * 128], ('wbuf', bi_))

            tilecount = 0
            for it_i, it in enumerate(iters):
                nqk, nv = it['nqk'], it['nv']
                nblk = nqk + nv
                ncols = nblk * 128
                Wv = Wbuf[:, 0:8 * ncols].rearrange("p (c n) -> p c n", c=8)
                prev_cx = None
                prev2_cx = None
                for t in range(NT):
                    pbase = 0 if tilecount % 2 == 0 else 3
                    c0 = 0
                    while c0 < ncols:
                        n_ = min(PROJ_N, 512 - (c0 % 512), ncols - c0)
                        bank = pbase + c0 // 512
                        wkeys = [('wbuf', j) for j in range(c0 // 128, (c0 + n_) // 128)]
                        for c in range(8):
                            S.op('pe', lambda e, c=c, t=t, c0=c0, n_=n_, pbase=pbase, Wv=Wv: e.matmul(
                                PSALL[:, pbase * 512 + c0: pbase * 512 + c0 + n_], lhsT=hT_ap(c, t), rhs=Wv[:, c, c0:c0 + n_],
                                start=(c == 0), stop=(c == 7)),
                                reads=[('hT', t, c)] + wkeys, writes=[pk(bank)])
                        c0 += n_
                    if l == 0:
                        rope = (ropeA[:, t * 16:(t + 1) * 16], ropeA[:, 512 + t * 16: 512 + (t + 1) * 16], 16)
                        ropekey = 'ropeA'
                    else:
                        rbuf = tilecount % 2
                        S.op('sp', lambda e, rbuf=rbuf, t=t: e.dma_start(
                            out=ropeBt[:, rbuf * 128:(rbuf + 1) * 128].rearrange("p (a b) -> p a b", a=2),
                            in_=ropeB_in.rearrange("p (a t b) -> p a t b", a=2, t=32)[:, :, t, :]),
                            writes=[('ropeB', rbuf)], dma=('ropeB', rbuf))
                        rope = (ropeBt[:, rbuf * 128: rbuf * 128 + 64], ropeBt[:, rbuf * 128 + 64: rbuf * 128 + 128], 64)
                        ropekey = ('ropeB', rbuf)
                    if prev_cx is not None:
                        proj_tail(prev_cx)
                    cx_ = proj_head(pbase, nqk, nv, gain_ops, t, rope, ropekey, 0, tilecount)
                    if prev2_cx is not None:
                        proj_tail_b(prev2_cx)
                    proj_head2(cx_)
                    prev2_cx = prev_cx
                    prev_cx = cx_
                    tilecount += 1
                proj_tail(prev_cx)
                proj_tail_b(prev2_cx)
                proj_tail_b(prev_cx)
                if it_i + 1 < len(iters):
                    emit_iter_load(iters[it_i + 1])
                if stop == 'p2proj':
                    continue
                hd = it['hd']
                jobs = []
                if l == 0:
                    for T in range(NT):
                        blocks = []
                        for g, (window, dil) in enumerate(A_GROUPS):
                            for dl in range(-9, 10):
                                if (g, dl) not in MASK_TABLE:
                                    continue
                                kb = T + dl
                                if kb < 0 or kb >= NT:
                                    continue
                                blocks.append((qk_ap(g, T * 128), [('qk', g, T)], qk_ap(3 + g, kb * 128), [('qk', 3 + g, kb)],
                                               v_ap(g, kb), [('v', g, kb)], MASK_TABLE[(g, dl)]))
                        jobs.append((blocks, T, hd))
                else:
                    for j in range(4):
                        for T in range(NT):
                            blocks = []
                            for kb in range(NT):
                                blocks.append((qk_ap(j, T * 128), [('qk', j, T)], qk_ap(4, kb * 128), [('qk', 4, kb)],
                                               v_ap(0, kb), [('v', 0, kb)], None))
                            jobs.append((blocks, T, hd * 4 + j))
                for blocks, T, hcol in jobs:
                    def fin(ab, T=T, hcol=hcol):
                        ob_i = fcount[0] % 4
                        fcount[0] += 1
                        rc_ap = rc[:, ob_i:ob_i + 1]
                        ob_ap = ob[:, ob_i * 128:(ob_i + 1) * 128]
                        S.op('dve', lambda e: e.reciprocal(out=rc_ap, in_=PS[ab][:, 128:129]),
                             reads=[pk(ab)], writes=[('rc', ob_i)])
                        S.op('dve', lambda e: e.tensor_scalar(out=ob_ap, in0=PS[ab][:, 0:128], scalar1=rc_ap, scalar2=None,
                                                              op0=ALU.mult),
                             reads=[pk(ab), ('rc', ob_i)], writes=[('ob', ob_i)])
                        S.op('sp', lambda e: e.dma_start(out=o_scr[T * 128:(T + 1) * 128, hcol * 128:(hcol + 1) * 128], in_=ob_ap),
                             reads=[('ob', ob_i)], writes=[('oscr', T)], dma=('ob', ob_i))
                    attention_job(blocks, fin)
                att_flush_all()
            att_cfg['banks'] = [2, 3]
            att_cfg['depth'] = 1
            S.barrier()
            if stop in ('p2', 'p2proj'):
                break

            Wg = R[:, 0:16384].rearrange("p (c n) -> p c n", c=8)
            Wo = R[:, 16384:28672].rearrange("p (c n) -> p c n", c=12)
            gcol0 = 9216 if l == 0 else 1536
            for c_ in range(8):
                load_weight_rows(w_in[c_ * 128:(c_ + 1) * 128, gcol0:gcol0 + 1024], Wg[:, c_, 0:1024], ('wg', c_, 0))
            for c_ in range(8):
                load_weight_rows(w_in[c_ * 128:(c_ + 1) * 128, gcol0 + 1024:gcol0 + 2048], Wg[:, c_, 1024:2048], ('wg', c_, 1))
            for cc_ in range(12):
                load_weight_rows(w_out[l, cc_ * 128:(cc_ + 1) * 128, :], Wo[:, cc_, :], ('wo', cc_))
            KmT3 = KmT[:].rearrange("p (h k) -> p h k", h=4)
            att_cfg['banks'] = [2, 0, 5]
            att_cfg['group'] = 4
            att_cfg['pt'] = None
            att_cfg['depth'] = 2
            att_cfg['oa'] = [4, 7]

            def gate_chain(t, j, gb, o_j, o_keys):
                th_ap = th[:, j * 512:(j + 1) * 512]
                S.op('act', lambda e: e.activation(out=th_ap, in_=PS[gb][:, 0:512], func=ACT.Tanh, scale=0.5),
                     reads=[pk(gb)], writes=[('th', j)])
                S.op('dve', lambda e: e.scalar_tensor_tensor(out=th_ap, in0=th_ap, scalar=1.0, in1=PS[gb][:, 0:512],
                                                             op0=ALU.add, op1=ALU.mult),
                     reads=[pk(gb), ('th', j)], writes=[('th', j)])
                yb_ap = yb[:, j * 512:(j + 1) * 512]
                S.op('dve', lambda e: e.scalar_tensor_tensor(out=yb_ap, in0=th_ap, scalar=0.5, in1=o_j, op0=ALU.mult, op1=ALU.mult),
                     reads=[('th', j)] + o_keys, writes=[('yb', j)])

            def gate_proj(t, j, gb):
                for c in range(8):
                    S.op('pe', lambda e, c=c: e.matmul(PS[gb][:, 0:512], lhsT=hT_ap(c, t), rhs=Wg[:, c, 512 + j * 512: 1024 + j * 512],
                                                       start=(c == 0), stop=(c == 7)),
                         reads=[('hT', t, c), ('wg', c, (512 + j * 512) // 1024)], writes=[pk(gb)])

            def y_transposes(t, j, tpb, eng):
                b = t % 2
                yb_ap = yb[:, j * 512:(j + 1) * 512]
                yTb = yT[:, b * 1536:(b + 1) * 1536]
                for i in range(4):
                    S.op('pe', lambda e, i=i: e.transpose(out=PSB[tpb][:, i * 128:(i + 1) * 128],
                                                         in_=yb_ap[:, i * 128:(i + 1) * 128], identity=idb[:]),
                         reads=[('yb', j), 'idb'], writes=[pk(tpb)])
                if eng == 'act':
                    S.op('act', lambda e: e.activation(out=yTb[:, j * 512:(j + 1) * 512], in_=PSB[tpb][:, 0:512], func=ACT.Copy),
                         reads=[pk(tpb)], writes=[('yT', b, j)])
                else:
                    S.op('dve', lambda e: e.tensor_copy(out=yTb[:, j * 512:(j + 1) * 512], in_=PSB[tpb][:, 0:512]),
                         reads=[pk(tpb)], writes=[('yT', b, j)])

            def stage_A(t):
                b = t % 2
                xt = xt_ap(b)
                ot_ap = ot[:, b * 1024:(b + 1) * 1024]
                S.op('sp', lambda e, x_src=x_src: e.dma_start(out=xt, in_=x_src[t * 128:(t + 1) * 128, :]),
                     writes=[('xt', b)], dma=('xt', b))
                S.op('sp', lambda e: e.dma_start(out=ot_ap, in_=o_scr[t * 128:(t + 1) * 128, :]),
                     reads=[('oscr', t)], writes=[('ot', b)], dma=('ot', b))
                bank = 0
                for c in range(8):
                    S.op('pe', lambda e, c=c: e.matmul(PS[bank][:, 0:512], lhsT=hT_ap(c, t), rhs=Wg[:, c, 0:512],
                                                       start=(c == 0), stop=(c == 7)),
                         reads=[('hT', t, c), ('wg', c, 0)], writes=[pk(bank)])
                gate_proj(t, 0, 1)
                gate_proj(t, 1, 3)
                src = PS[bank][:, 0:512]
                rs_ap, rs_key = rms_stats(src, 4, 128, [pk(bank)], True, 1.0 / E)
                tf3 = tmpf[:, b * 768: b * 768 + 512].rearrange("p (a b) -> p a b", a=4)
                S.op('dve', lambda e: e.tensor_tensor(out=tf3, in0=src.rearrange("p (a b) -> p a b", a=4),
                                                      in1=rs_ap.unsqueeze(2).to_broadcast([128, 4, 128]), op=ALU.mult),
                     reads=[pk(bank), rs_key], writes=[('tmpf', b, 0)])
                qn3 = qn[:, b * 768: b * 768 + 512].rearrange("p (a b) -> p a b", a=4)
                gsrc = gains[:, g_q_mem * 128:(g_q_mem + 1) * 128].unsqueeze(1).to_broadcast([128, 4, 128])
                S.op('pool', lambda e: e.tensor_tensor(out=qn3, in0=tf3, in1=gsrc, op=ALU.mult),
                     reads=[('tmpf', b, 0), 'gains'], writes=[('qn', b, 'a')])
                gate_chain(t, 0, 1, ot_ap[:, 0:512], [('ot', b)])
                gate_chain(t, 1, 3, ot_ap[:, 512:1024], [('ot', b)])

            def stage_B(t):
                b = t % 2
                qn_ap = qn[:, b * 768: b * 768 + 512]
                tb = 6
                for hd in range(4):
                    S.op('pe', lambda e, hd=hd: e.transpose(out=PSB[tb][:, hd * 128:(hd + 1) * 128],
                                                           in_=qn_ap[:, hd * 128:(hd + 1) * 128], identity=idb[:]),
                         reads=[('qn', b, 'a'), 'idb'], writes=[pk(tb)])
                S.op('act', lambda e: e.activation(out=qmT[:], in_=PSB[tb][:, 0:512], func=ACT.Copy),
                     reads=[pk(tb)], writes=['qmT'])
                y_transposes(t, 0, 7, 'dve')
                gate_proj(t, 2, 1)
                th2 = th[:, 1024:1536]
                S.op('act', lambda e: e.activation(out=th2, in_=PS[1][:, 0:512], func=ACT.Tanh, scale=0.5),
                     reads=[pk(1)], writes=[('th', 2)])
                S.op('dve', lambda e: e.scalar_tensor_tensor(out=th2, in0=th2, scalar=1.0, in1=PS[1][:, 0:512],
                                                             op0=ALU.add, op1=ALU.mult),
                     reads=[pk(1), ('th', 2)], writes=[('th', 2)])
                y_transposes(t, 1, 6, 'act')
                for hd in range(4):
                    blocks = []
                    for kb in range(2):
                        blocks.append((qmT[:, hd * 128:(hd + 1) * 128], ['qmT'], KmT3[:, hd, kb * 128:(kb + 1) * 128], ['KmT'],
                                       Vm[:, (kb * 4 + hd) * 130:(kb * 4 + hd) * 130 + 129], ['Vm'], None))

                    def fin(ab, hd=hd):
                        ob_i = fcount[0] % 4
                        fcount[0] += 1
                        rc_ap = rc[:, 4 + ob_i:5 + ob_i]
                        S.op('dve', lambda e: e.reciprocal(out=rc_ap, in_=PS[ab][:, 128:129]),
                             reads=[pk(ab)], writes=[('rcm', ob_i)])
                        S.op('act', lambda e: e.activation(out=om[:, hd * 128:(hd + 1) * 128], in_=PS[ab][:, 0:128], func=ACT.Copy,
                                                           scale=rc_ap),
                             reads=[pk(ab), ('rcm', ob_i)], writes=[('om', hd)])
                    attention_job(blocks, fin)
                att_flush_all()
                yb2 = yb[:, 1024:1536]
                S.op('dve', lambda e: e.scalar_tensor_tensor(out=yb2, in0=th2, scalar=0.5, in1=om[:, 0:512], op0=ALU.mult, op1=ALU.mult),
                     reads=[('th', 2)] + [('om', hd) for hd in range(4)], writes=[('yb', 2)])

            def stage_B2(t):
                y_transposes(t, 2, 7, 'act')

            def stage_C(t):
                b = t % 2
                xt = xt_ap(b)
                x1t = x1t_ap(b)
                yTb = yT[:, b * 1536:(b + 1) * 1536]
                for nb_ in range(2):
                    obk = [2, 0][nb_]
                    for cc in range(12):
                        S.op('pe', lambda e, cc=cc, nb_=nb_, obk=obk: e.matmul(
                            PS[obk][:, 0:512], lhsT=yTb[:, cc * 128:(cc + 1) * 128], rhs=Wo[:, cc, nb_ * 512:(nb_ + 1) * 512],
                            start=(cc == 0), stop=(cc == 11)),
                            reads=[('yT', b, cc // 4), ('wo', cc)], writes=[pk(obk)])
                    S.op('dve', lambda e, nb_=nb_, obk=obk: e.tensor_tensor(
                        out=x1t[:, nb_ * 512:(nb_ + 1) * 512], in0=PS[obk][:, 0:512], in1=xt[:, nb_ * 512:(nb_ + 1) * 512], op=ALU.add),
                        reads=[pk(obk), ('xt', b)], writes=[('x1t', b, nb_)])
                S.op('sp', lambda e, x_dst=x_dst: e.dma_start(out=x_dst[t * 128:(t + 1) * 128, :], in_=x1t),
                     reads=[('x1t', b, 0), ('x1t', b, 1)], writes=[('xdst', t)], dma=('x1t', b))

            def _bind(f, **kw):
                return f

            if stop != 'p3w':
                stage_A(0)
                for t in range(NT):
                    stage_B(t)
                    if t + 1 < NT:
                        stage_A(t + 1)
                    stage_B2(t)
                    stage_C(t)
            att_cfg['oa'] = OA
            S.barrier()
        S.emit(nc)
    return nc, S


def _host_consts(norm_g, mem_norm_g, mem_qn_g, mem_kn_g, qn_a, kn_a, qn_b, kn_b):
    ng = np.zeros((128, 32), np.float32)
    for l in range(2):
        ng[:, l * 8:(l + 1) * 8] = norm_g[l].reshape(8, 128).T
        ng[:, 16 + l * 8:16 + (l + 1) * 8] = mem_norm_g[l].reshape(8, 128).T
    rows = [qn_a[0, 0], qn_a[0, 1], qn_a[0, 2], kn_a[0, 0], kn_a[0, 1], kn_a[0, 2], qn_b[0], kn_b[0],
            mem_qn_g[0], mem_qn_g[1], mem_kn_g[0], mem_kn_g[1]]
    gains = np.ascontiguousarray(np.broadcast_to(np.concatenate(rows)[None, :], (128, 12 * 128))).astype(np.float32)
    pos = np.arange(S_TOK, dtype=np.float32)
    invA = (np.float32(500000.0) ** (-np.arange(0, 32, 2, dtype=np.float32) / np.float32(32))).astype(np.float32)
    angA = pos[:, None] * invA[None, :]
    row = np.repeat(np.arange(64, dtype=np.float32), 64)
    col = np.tile(np.arange(64, dtype=np.float32), 64)
    invB = (np.float32(10000.0) ** (-np.arange(0, 64, 2, dtype=np.float32) / np.float32(64))).astype(np.float32)
    angB = np.concatenate([row[:, None] * invB[None, :], col[:, None] * invB[None, :]], axis=-1)

    def lay(a):
        return a.reshape(32, 128, -1).transpose(1, 0, 2)
    ropeA = np.stack([lay(np.cos(angA)), lay(np.sin(angA))], axis=1).reshape(128, -1).astype(np.float32)
    ropeB = np.stack([lay(np.cos(angB)), lay(np.sin(angB))], axis=1).reshape(128, -1).astype(np.float32)
    return dict(ng=ng, gains=gains, ropeA=np.ascontiguousarray(ropeA), ropeB=np.ascontiguousarray(ropeB),
                ident=np.eye(128, dtype=np.float32), masks=np.ascontiguousarray(MASKS_NP.reshape(128, -1)))


_NC_CACHE = {}


def kernel(x, mem, norm_g, mem_norm_g, w_mem_kv, mem_qn_g, mem_kn_g, w_out,
           w_in_a, qn_a, kn_a, w_in_b, qn_b, kn_b):
    f = lambda a: np.ascontiguousarray(np.asarray(a, dtype=np.float32))
    x = f(x)
    mem = f(mem)
    consts = _host_consts(f(norm_g), f(mem_norm_g), f(mem_qn_g), f(mem_kn_g), f(qn_a), f(kn_a), f(qn_b), f(kn_b))
    shared = dict(w_in_a=f(w_in_a)[0], w_in_b=f(w_in_b)[0], w_out=f(w_out), w_mem_kv=f(w_mem_kv), **consts)
    if 'nc' not in _NC_CACHE:
        _NC_CACHE['nc'] = build()[0]
    nc = _NC_CACHE['nc']
    n = x.shape[0]
    in_maps = [dict(x=x[b], mem=mem[b], **shared) for b in range(n)]
    res = run_bass_kernel_spmd(nc, in_maps, core_ids=list(range(n)))
    return np.stack([np.asarray(r["y_out"], dtype=np.float32) for r in res.results], axis=0)
```

```python
import numpy as np
import concourse.bass as bass
import concourse.mybir as mybir
from concourse.bass_utils import run_bass_kernel_spmd

F32 = mybir.dt.float32
BF16 = mybir.dt.bfloat16
ALU = mybir.AluOpType
ACT = mybir.ActivationFunctionType
AX = mybir.AxisListType

ENGS = ['pe', 'act', 'dve', 'pool', 'sp']

S_TOK = 4096
NT = 32
D = 1024
E = 128
EPS = 1e-6
SCALE = float(E) ** -0.5
A_GROUPS = ((128, 1), (512, 4), (2048, 16))
IN_A = 11264
IN_B = 3584

PROJ_N = 512


class Sched:
    def __init__(self):
        self.ops = []
        self.state = {}
        self.last_on = {e: None for e in ENGS}
        self.dma_last = {}

    def op(self, eng, fn, reads=(), writes=(), dma=None):
        i = len(self.ops)
        deps = set()
        for k in reads:
            excl = isinstance(k, tuple) and k[0] == 'ps'
            st = self.state.setdefault(k, [None, []])
            if st[0] is not None:
                deps.add(st[0])
            if excl:
                for r in st[1]:
                    if self.ops[r]['eng'] != eng:
                        deps.add(r)
        for k in writes:
            st = self.state.setdefault(k, [None, []])
            if st[0] is not None:
                deps.add(st[0])
            for r in st[1]:
                deps.add(r)
        for k in reads:
            self.state[k][1].append(i)
        for k in writes:
            self.state[k] = [i, []]
        if dma is not None:
            p = self.dma_last.get(dma)
            if p is not None:
                deps.add(p)
            self.dma_last[dma] = i
        deps.discard(i)
        if eng == 'pe':
            deps = {d for d in deps if self.ops[d]['eng'] != 'pe'}
        best = {}
        keep = set()
        for d in deps:
            od = self.ops[d]
            if od['dma'] is not None:
                keep.add(d)
            elif d > best.get(od['eng'], -1):
                best[od['eng']] = d
        deps = keep | set(best.values())
        self.ops.append(dict(eng=eng, fn=fn, deps=deps, dma=dma))
        if fn is not None:
            self.last_on[eng] = i
        return i

    def barrier(self):
        lasts = [v for v in self.last_on.values() if v is not None]
        lasts += list(self.dma_last.values())
        lasts = set(lasts)
        for e in ENGS:
            deps = {d for d in lasts if not (self.ops[d]['eng'] == e and self.ops[d]['dma'] is None)}
            self.ops.append(dict(eng=e, fn=None, deps=deps, dma=None))
        self.state = {}

    def emit(self, nc):
        ops = self.ops
        needed = set()
        for o in ops:
            needed |= o['deps']
        dma_keys = []
        seen = set()
        for o in ops:
            if o['dma'] is not None and o['dma'] not in seen:
                seen.add(o['dma'])
                dma_keys.append(o['dma'])
        sem_ctx = []
        sems = {}
        for n_i, name in enumerate(['pe', 'act', 'dve', 'pool'] + [('dma', k) for k in dma_keys]):
            cm = nc.semaphore("s%d" % n_i)
            sems[name] = cm.__enter__()
            sem_ctx.append(cm)
        cnt = {}
        for i, o in enumerate(ops):
            if o['dma'] is not None:
                key = ('dma', o['dma'])
                cnt[key] = cnt.get(key, 0) + 16
                o['sem'], o['val'], o['inc'] = key, cnt[key], 16
            elif i in needed:
                assert o['fn'] is not None
                key = o['eng']
                cnt[key] = cnt.get(key, 0) + 1
                o['sem'], o['val'], o['inc'] = key, cnt[key], 1
            else:
                o['sem'] = None
        self.maxval = dict(cnt)

        def run(eng_name, e):
            waited = {}
            for o in ops:
                if o['eng'] != eng_name:
                    continue
                need = {}
                for d in o['deps']:
                    od = ops[d]
                    s, v = od['sem'], od['val']
                    if v > need.get(s, 0):
                        need[s] = v
                for s, v in need.items():
                    if waited.get(s, 0) < v:
                        e.wait_ge(sems[s], v)
                        waited[s] = v
                if o['fn'] is None:
                    continue
                ins = o['fn'](e)
                if o['sem'] is not None:
                    ins.then_inc(sems[o['sem']], o['inc'])

        with nc.Block() as block:
            @block.tensor
            def _(e):
                run('pe', e)

            @block.scalar
            def _(e):
                run('act', e)

            @block.vector
            def _(e):
                run('dve', e)

            @block.gpsimd
            def _(e):
                run('pool', e)

            @block.sync
            def _(e):
                run('sp', e)
        for cm in reversed(sem_ctx):
            cm.__exit__(None, None, None)


def _mask_tables():
    masks = []
    index = {}
    table = {}
    kk = np.arange(128)[:, None]
    qq = np.arange(128)[None, :]
    for g, (window, dil) in enumerate(A_GROUPS):
        hw = window // 2
        dmax = hw // 128 + (1 if hw % 128 else 0)
        dmax = max(dmax, 1)
        for dl in range(-dmax, dmax + 1):
            diff = 128 * dl + kk - qq
            m = ((diff % dil) == 0) & (np.abs(diff) <= hw)
            if not m.any():
                continue
            key = m.tobytes()
            if key not in index:
                index[key] = len(masks)
                masks.append(m.astype(np.float32))
            table[(g, dl)] = index[key]
    return np.stack(masks, axis=1), table


MASKS_NP, MASK_TABLE = _mask_tables()
NM = MASKS_NP.shape[1]


def build(n_layers=2, debug=False, stop=None):
    stop_spec = stop
    nc = bass.Bass("TRN2", target_bir_lowering=False)
    dk = "ExternalOutput" if debug else "Internal"
    x_in = nc.dram_tensor("x", [S_TOK, D], F32, kind="ExternalInput").ap()
    mem_in = nc.dram_tensor("mem", [256, D], F32, kind="ExternalInput").ap()
    w_in_a = nc.dram_tensor("w_in_a", [D, IN_A], F32, kind="ExternalInput").ap()
    w_in_b = nc.dram_tensor("w_in_b", [D, IN_B], F32, kind="ExternalInput").ap()
    w_out = nc.dram_tensor("w_out", [2, 1536, D], F32, kind="ExternalInput").ap()
    w_kv = nc.dram_tensor("w_mem_kv", [2, D, 1024], F32, kind="ExternalInput").ap()
    ng_in = nc.dram_tensor("ng", [128, 32], F32, kind="ExternalInput").ap()
    gains_in = nc.dram_tensor("gains", [128, 12 * 128], F32, kind="ExternalInput").ap()
    ropeA_in = nc.dram_tensor("ropeA", [128, 2 * 32 * 16], F32, kind="ExternalInput").ap()
    ropeB_in = nc.dram_tensor("ropeB", [128, 2 * 32 * 64], F32, kind="ExternalInput").ap()
    ident_in = nc.dram_tensor("ident", [128, 128], F32, kind="ExternalInput").ap()
    masks_in = nc.dram_tensor("masks", [128, NM * 128], F32, kind="ExternalInput").ap()
    out = nc.dram_tensor("y_out", [S_TOK, D], F32, kind="ExternalOutput").ap()
    x1_scr = nc.dram_tensor("x1_scr", [S_TOK, D], F32, kind=dk).ap()
    o_scr = nc.dram_tensor("o_scr", [S_TOK, D], BF16, kind=dk).ap()

    S = Sched()
    R_N = 6 * 4096 + 3 * 32 * 130
    import contextlib
    with contextlib.ExitStack() as es:
        def sb(name, shape, dt):
            return es.enter_context(nc.sbuf_tensor(name, shape, dt))

        hT = sb("hT", [128, 8 * 4096], BF16)
        R = sb("R", [128, R_N], BF16)
        Wbuf = sb("Wbuf", [128, 8 * 1152], BF16)
        Wbf = Wbuf.bitcast(F32)
        Wst = sb("Wst", [128, 2 * 8 * 128], F32)
        ropeA = sb("ropeA_t", [128, 2 * 32 * 16], F32)
        ropeBt = ropeA[:, 0:256]
        gains = sb("gains_t", [128, 12 * 128], F32)
        maskb = sb("maskb", [128, NM * 128], BF16)
        idf = sb("idf", [128, 128], F32)
        idb = sb("idb", [128, 128], BF16)
        ngt = sb("ngt", [128, 32], F32)
        mh = sb("mh", [128, 16], F32)
        KmT = sb("KmT", [128, 4 * 256], BF16)
        Vm = sb("Vm", [128, 2 * 4 * 130], BF16)
        ssA = sb("ssA", [128, 2 * 8], F32)
        msA = sb("msA", [128, 2 * 8], F32)
        rsA = sb("rsA", [128, 2 * 8], F32)
        sq = sb("sq", [128, 2 * 768], F32)
        tmpf = sb("tmpf", [128, 2 * 768], F32)
        U = sb("U", [128, 2560], F32)
        Ub = U.bitcast(BF16)
        qn = sb("qn", [128, 2 * 768], BF16)
        pt = sb("pt", [128, 3 * 512], BF16)
        rc = sb("rc", [128, 8], F32)
        ob = Ub[:, 3072:3584]
        th = Wbf[:, 0:1536]
        ot = Ub[:, 3072:5120]
        Wbb = Wbuf
        om = Wbb[:, 3072:3584]
        yb = Wbb[:, 3584:5120]
        yT = Ub[:, 0:3072]
        qmT = Wbb[:, 5120:5632]

        PSALL = es.enter_context(nc.psum_tensor("psall", [128, 4096], F32))
        PSBALL = PSALL.bitcast(BF16)
        PS = [PSALL[:, i * 512:(i + 1) * 512] for i in range(8)]
        PSB = [PSBALL[:, i * 1024:(i + 1) * 1024] for i in range(8)]
        PJ = [0, 1]
        SC = [2, 3]
        OA = [4, 5]
        TP = [6, 7]

        def pk(i):
            return ('ps', i)

        Rf = R.bitcast(F32)
        QK_OFF = 0
        V_OFF = 6 * 4096
        XT_OFF_F = 28672 // 2
        def xt_ap(b):
            return Rf[:, XT_OFF_F + b * 1024: XT_OFF_F + (b + 1) * 1024]

        def x1t_ap(b):
            return Rf[:, XT_OFF_F + 2048 + b * 1024: XT_OFF_F + 2048 + (b + 1) * 1024]

        def qk_ap(slot, t0, n=128):
            o = QK_OFF + slot * 4096 + t0
            return R[:, o:o + n]

        def v_ap(slot, kb, n=129):
            o = V_OFF + (slot * 32 + kb) * 130
            return R[:, o:o + n]

        def hT_ap(c, t):
            o = c * 4096 + t * 128
            return hT[:, o:o + 128]

        def wst_ap(slot, nchunk=8):
            return Wst[:, slot * 1024: slot * 1024 + nchunk * 128].rearrange("p (c n) -> p c n", n=128)

        S.op('sp', lambda e: e.dma_start(out=idf[:], in_=ident_in), writes=['idf'], dma='c0')
        S.op('sp', lambda e: e.dma_start(out=gains[:], in_=gains_in), writes=['gains'], dma='c1')
        S.op('sp', lambda e: e.dma_start(out=ngt[:], in_=ng_in), writes=['ngt'], dma='c2')
        S.op('sp', lambda e: e.dma_start(out=ropeA[:], in_=ropeA_in), writes=['ropeA'], dma='c3')
        S.op('sp', lambda e: e.dma_start(out=Rf[:, 0:NM * 128], in_=masks_in), writes=['mstage'], dma='c4')
        S.op('pool', lambda e: e.tensor_copy(out=idb[:], in_=idf[:]), reads=['idf'], writes=['idb'])
        S.op('pool', lambda e: e.tensor_scalar(out=maskb[:], in0=Rf[:, 0:NM * 128], scalar1=-1.0, scalar2=30000.0,
                                               op0=ALU.add, op1=ALU.mult), reads=['mstage'], writes=['maskb'])
        S.op('pool', lambda e: e.memset(mh[:, 0:8], -0.5), writes=['mh'])
        S.op('pool', lambda e: e.memset(mh[:, 8:16], EPS), writes=['mhe'])
        S.barrier()
        stopped = (stop == 'init')

        wcount = [0]

        def load_weight_piece(src_ap, nchunk, dst_ap, dst_key):
            slot = wcount[0] % 2
            wcount[0] += 1
            S.op('sp', lambda e: e.dma_start(out=wst_ap(slot, nchunk), in_=src_ap.rearrange("(c p) n -> p c n", p=128)),
                 writes=[('wst', slot)], dma=('wst', slot))
            S.op('pool', lambda e: e.tensor_copy(out=dst_ap, in_=wst_ap(slot, nchunk)),
                 reads=[('wst', slot)], writes=[dst_key])

        rcount = [0]

        def load_weight_rows(src_ap, dst_ap, dst_key):
            slot = wcount[0] % 2
            wcount[0] += 1
            eng = 'dve' if rcount[0] % 2 == 0 else 'act'
            rcount[0] += 1
            st = Wst[:, slot * 1024:(slot + 1) * 1024]
            S.op('sp', lambda e: e.dma_start(out=st, in_=src_ap), writes=[('wst', slot)], dma=('wst', slot))
            if eng == 'dve':
                S.op('dve', lambda e: e.tensor_copy(out=dst_ap, in_=st), reads=[('wst', slot)], writes=[dst_key])
            else:
                S.op('act', lambda e: e.activation(out=dst_ap, in_=st, func=ACT.Copy), reads=[('wst', slot)], writes=[dst_key])

        sidx = [0]

        def rms_stats(src_ap, n_groups, glen, src_keys, src_is_psum, inv_n, use_ln=False):
            b = sidx[0] % 2
            sidx[0] += 1
            n = n_groups * glen
            sq_ap = sq[:, b * 768: b * 768 + n]
            ss_ap = ssA[:, b * 8: b * 8 + n_groups]
            ms_ap = msA[:, b * 8: b * 8 + n_groups]
            rs_ap = rsA[:, b * 8: b * 8 + n_groups]
            S.op('act', lambda e: e.activation(out=sq_ap, in_=src_ap, func=ACT.Square),
                 reads=src_keys, writes=[('sq', b), ('sqb', b)])
            S.op('dve', lambda e: e.tensor_reduce(out=ss_ap, in_=sq_ap.rearrange("p (a b) -> p a b", a=n_groups),
                                                  axis=AX.X, op=ALU.add),
                 reads=[('sq', b), ('sqb', b)], writes=[('ss', b)])
            if use_ln:
                S.op('act', lambda e: e.activation(out=ms_ap, in_=ss_ap, func=ACT.Ln, scale=inv_n, bias=mh[:, 8:9]),
                     reads=[('ss', b), 'mhe'], writes=[('ms', b)])
                S.op('act', lambda e: e.activation(out=rs_ap, in_=ms_ap, func=ACT.Exp, scale=-0.5),
                     reads=[('ms', b)], writes=[('rs', b)])
                return rs_ap, ('rs', b)
            S.op('dve', lambda e: e.tensor_scalar(out=ms_ap, in0=ss_ap, scalar1=inv_n, scalar2=EPS,
                                                  op0=ALU.mult, op1=ALU.add),
                 reads=[('ss', b)], writes=[('ms', b)])
            S.op('pool', lambda e: e.tensor_tensor(out=rs_ap, in0=ms_ap, in1=mh[:, 0:n_groups], op=ALU.pow),
                 reads=[('ms', b), 'mh'], writes=[('rs', b)])
            return rs_ap, ('rs', b)

        ntt_pending = [None]

        def ntt_flush():
            if ntt_pending[0] is not None:
                ntt_pending[0]()
                ntt_pending[0] = None

        def norm_transpose_tile(src_dram_ap, t, gcol, dst_fn, dst_keys, tcount):
            b = tcount % 4
            xt = xt_ap(b) if b < 2 else x1t_ap(b - 2)
            S.op('sp', lambda e: e.dma_start(out=xt, in_=src_dram_ap), writes=[('xt', b)], dma=('xt', b))
            col = (b % 2) * 8 + 4 + b // 2
            ss_ap = ssA[:, col:col + 1]
            ms_ap = msA[:, col:col + 1]
            rs_ap = rsA[:, col:col + 1]
            S.op('act', lambda e: e.activation(out=sq[:, 0:1024], in_=xt, func=ACT.Square, accum_out=ss_ap),
                 reads=[('xt', b)], writes=[('sq', 0), ('sq', 1), ('sqb', 0), ('sqb', 1), ('ss4', b)])
            S.op('act', lambda e: e.activation(out=ms_ap, in_=ss_ap, func=ACT.Ln, scale=1.0 / D, bias=mh[:, 8:9]),
                 reads=[('ss4', b), 'mhe'], writes=[('ms4', b)])
            S.op('act', lambda e: e.activation(out=rs_ap, in_=ms_ap, func=ACT.Exp, scale=-0.5),
                 reads=[('ms4', b)], writes=[('rs4', b)])
            S.op('dve', lambda e: e.tensor_scalar(out=xt, in0=xt, scalar1=rs_ap, scalar2=None, op0=ALU.mult),
                 reads=[('xt', b), ('rs4', b)], writes=[('xt', b)])

            def tail():
                for half in range(2):
                    bank = (0 if b % 2 == 0 else 2) + half
                    for j in range(4):
                        c = half * 4 + j
                        S.op('pe', lambda e, c=c, j=j, bank=bank: e.transpose(out=PS[bank][:, j * 128:(j + 1) * 128],
                                                                              in_=xt[:, c * 128:(c + 1) * 128], identity=idf[:]),
                             reads=[('xt', b), 'idf'], writes=[pk(bank)])
                    gsrc = ngt[:, gcol + half * 4: gcol + half * 4 + 4].unsqueeze(2).to_broadcast([128, 4, 128])
                    S.op('dve', lambda e, half=half, bank=bank, gsrc=gsrc: e.tensor_tensor(
                        out=dst_fn(half), in0=PS[bank][:, 0:512].rearrange("p (c k) -> p c k", c=4), in1=gsrc, op=ALU.mult),
                        reads=[pk(bank), 'ngt'], writes=dst_keys(half))
            ntt_flush()
            ntt_pending[0] = tail

        def proj_head(pbase, nqk, nv, gain_ops, t, rope, ropekey, vslot0, ucount):
            banks = sorted(set((pbase * 512 + i * 128) // 512 for i in range(nqk + nv)))
            bkeys = [pk(bk) for bk in banks]
            c0 = pbase * 512
            src = PSALL[:, c0:c0 + nqk * 128]
            for vi in range(nv):
                S.op('act', lambda e, vi=vi: e.activation(out=v_ap(vslot0 + vi, t, 128),
                                                          in_=PSALL[:, c0 + (nqk + vi) * 128: c0 + (nqk + vi + 1) * 128], func=ACT.Copy),
                     reads=bkeys, writes=[('v', vslot0 + vi, t)])
            sb_i = sidx[0] % 2
            sidx[0] += 1
            n = nqk * 128
            sq_ap = sq[:, sb_i * 768: sb_i * 768 + n]
            ss_ap = ssA[:, sb_i * 8: sb_i * 8 + nqk]
            if att_cfg.get('ss_eng', 'dve') == 'act':
                for i in range(nqk):
                    S.op('act', lambda e, i=i: e.activation(out=sq_ap[:, i * 128:(i + 1) * 128], in_=src[:, i * 128:(i + 1) * 128],
                                                            func=ACT.Square, accum_out=ss_ap[:, i:i + 1]),
                         reads=bkeys, writes=[('sq', sb_i), ('sqb', sb_i), ('ss', sb_i)])
            else:
                S.op('act', lambda e: e.activation(out=sq_ap, in_=src, func=ACT.Square),
                     reads=bkeys, writes=[('sq', sb_i), ('sqb', sb_i)])
                S.op('dve', lambda e: e.tensor_reduce(out=ss_ap, in_=sq_ap.rearrange("p (a b) -> p a b", a=nqk),
                                                      axis=AX.X, op=ALU.add),
                     reads=[('sq', sb_i), ('sqb', sb_i)], writes=[('ss', sb_i)])
            b = ucount % 2
            tf = tmpf[:, b * 768: b * 768 + nqk * 128]
            tf3 = tf.rearrange("p (a b) -> p a b", a=nqk)
            TK = [('tmpf', b)]
            return dict(nqk=nqk, gain_ops=gain_ops, t=t, rope=rope, ropekey=ropekey, ucount=ucount, b=b, tf3=tf3, TK=TK,
                        sb_i=sb_i, src=src, bkeys=bkeys)

        def proj_head2(cx):
            nqk, sb_i, src, bkeys, tf3, TK = (cx[k_] for k_ in ('nqk', 'sb_i', 'src', 'bkeys', 'tf3', 'TK'))
            ss_ap = ssA[:, sb_i * 8: sb_i * 8 + nqk]
            ms_ap = msA[:, sb_i * 8: sb_i * 8 + nqk]
            rs_ap = rsA[:, sb_i * 8: sb_i * 8 + nqk]
            S.op('act', lambda e: e.activation(out=ms_ap, in_=ss_ap, func=ACT.Ln, scale=1.0 / E, bias=mh[:, 8:9]),
                 reads=[('ss', sb_i), 'mhe'], writes=[('ms', sb_i)])
            S.op('act', lambda e: e.activation(out=rs_ap, in_=ms_ap, func=ACT.Exp, scale=-0.5),
                 reads=[('ms', sb_i)], writes=[('rs', sb_i)])
            S.op('dve', lambda e: e.tensor_tensor(out=tf3, in0=src.rearrange("p (a b) -> p a b", a=nqk),
                                                  in1=rs_ap.unsqueeze(2).to_broadcast([128, nqk, 128]), op=ALU.mult),
                 reads=bkeys + [('rs', sb_i)], writes=TK)

        def proj_tail(cx):
            nqk, gain_ops, t, rope, ropekey, ucount, b, tf3, TK = (cx[k_] for k_ in
                                                                   ('nqk', 'gain_ops', 't', 'rope', 'ropekey', 'ucount', 'b', 'tf3', 'TK'))
            b3 = ucount % 3
            qn_ap = qn[:, b3 * 768: b3 * 768 + nqk * 128] if b3 < 2 else Ub[:, 3584:3584 + nqk * 128]
            qn3 = qn_ap.rearrange("p (a b) -> p a b", a=nqk)
            QA, QB, QC = ('qn3', b3, 'a'), ('qn3', b3, 'b'), ('qn3', b3, 'c')
            cx['qn_ap'] = qn_ap
            cx['QK3'] = [QA, QB, QC]
            for (b0, nb_, gsrc) in gain_ops:
                S.op('dve', lambda e, b0=b0, nb_=nb_, gsrc=gsrc: e.tensor_tensor(out=tf3[:, b0:b0 + nb_, :], in0=tf3[:, b0:b0 + nb_, :],
                                                                            in1=gsrc, op=ALU.mult),
                     reads=TK + ['gains'], writes=TK)
            cos_ap, sin_ap, Rr = rope
            cosb = cos_ap.unsqueeze(1).to_broadcast([128, nqk, Rr])
            sinb = sin_ap.unsqueeze(1).to_broadcast([128, nqk, Rr])
            x1 = tf3[:, :, 0:Rr]
            x2 = tf3[:, :, Rr:2 * Rr]
            nr = nqk * Rr
            ra3 = sq[:, b * 768: b * 768 + nr].rearrange("p (a b) -> p a b", a=nqk)
            rb3 = sq[:, b * 768 + 384: b * 768 + 384 + nr].rearrange("p (a b) -> p a b", a=nqk)
            ra4 = U[:, b * 768: b * 768 + nr].rearrange("p (a b) -> p a b", a=nqk)
            rb4 = U[:, b * 768 + 384: b * 768 + 384 + nr].rearrange("p (a b) -> p a b", a=nqk)
            SQK = ('sq', b)
            S.op('dve', lambda e: e.tensor_tensor(out=ra3, in0=x1, in1=cosb, op=ALU.mult),
                 reads=TK + [ropekey], writes=[SQK])
            S.op('pool', lambda e: e.tensor_tensor(out=rb3, in0=x2, in1=sinb, op=ALU.mult),
                 reads=TK + [ropekey], writes=[('sqb', b)])
            S.op('dve', lambda e: e.tensor_tensor(out=ra4, in0=x2, in1=cosb, op=ALU.mult),
                 reads=TK + [ropekey], writes=[('ra4', b)])
            S.op('pool', lambda e: e.tensor_tensor(out=rb4, in0=x1, in1=sinb, op=ALU.mult),
                 reads=TK + [ropekey], writes=[('rb4', b)])
            S.op('dve', lambda e: e.tensor_tensor(out=qn3[:, :, 0:Rr], in0=ra3, in1=rb3, op=ALU.subtract),
                 reads=[SQK, ('sqb', b)], writes=[QA])
            S.op('pool', lambda e: e.tensor_tensor(out=qn3[:, :, Rr:2 * Rr], in0=ra4, in1=rb4, op=ALU.add),
                 reads=[('ra4', b), ('rb4', b)], writes=[QB])
            if 2 * Rr < 128:
                S.op('pool', lambda e: e.tensor_copy(out=qn3[:, :, 2 * Rr:128], in_=tf3[:, :, 2 * Rr:128]),
                     reads=TK, writes=[QC])

        def proj_tail_b(cx):
            nqk, t, ucount, qn_ap = cx['nqk'], cx['t'], cx['ucount'], cx['qn_ap']
            QA, QB, QC = cx['QK3']
            tb = TP[ucount % 2]
            for i in range(nqk):
                S.op('pe', lambda e, i=i: e.transpose(out=PSB[tb][:, i * 128:(i + 1) * 128],
                                                     in_=qn_ap[:, i * 128:(i + 1) * 128], identity=idb[:]),
                     reads=[QA, QB, QC, 'idb'], writes=[pk(tb)])
            qkdst = R[:, 0:nqk * 4096].rearrange("p (s k) -> p s k", s=nqk)[:, :, t * 128:(t + 1) * 128]
            S.op('act', lambda e: e.activation(out=qkdst, in_=PSB[tb][:, 0:nqk * 128].rearrange("p (s k) -> p s k", s=nqk),
                                               func=ACT.Copy),
                 reads=[pk(tb)], writes=[('qk', sl, t) for sl in range(nqk)])

        acount = [0]
        ptcount = [0]
        sccount = [0]
        fcount = [0]
        pend = []
        att_cfg = dict(banks=[2, 3], depth=1)

        def att_flush_one():
            q = pend.pop(0)
            pb, ab = q['pb'], q['ab']
            p_base = q['p_base']
            acc = PS[ab][:, 0:129]
            for (i, v_ap_, v_keys, first, last) in q['pv']:
                S.op('pe', lambda e, i=i, v_ap_=v_ap_, first=first, last=last, p_base=p_base, acc=acc: e.matmul(
                    acc, lhsT=p_base[:, i * 128:(i + 1) * 128], rhs=v_ap_, start=first, stop=last),
                    reads=[('pt', pb)] + list(v_keys), writes=[pk(ab)])
            if q['fin'] is not None:
                q['fin'](ab)

        def att_flush_all():
            while pend:
                att_flush_one()

        def attention_job(blocks, fin):
            oa_ = att_cfg.get('oa', OA)
            ab = oa_[acount[0] % len(oa_)]
            acount[0] += 1
            nb = len(blocks)
            bi = 0
            grp = att_cfg.get('group', 4)
            while bi < nb:
                quad = blocks[bi:bi + grp]
                n = len(quad)
                banks = att_cfg['banks']
                sbk = banks[sccount[0] % len(banks)]
                sccount[0] += 1
                sbl = list(sbk) if isinstance(sbk, tuple) else [sbk]
                sb0 = sbl[0]
                skeys = [pk(x_) for x_ in sbl]
                pb = ptcount[0] % 3
                ptcount[0] += 1
                for i, blk in enumerate(quad):
                    q_ap, q_keys, k_ap, k_keys, v_ap_, v_keys, mi = blk
                    o_ap = PSALL[:, sb0 * 512 + i * 128: sb0 * 512 + (i + 1) * 128]
                    bkey = pk(sb0 + i // 4)
                    S.op('pe', lambda e, o_ap=o_ap, k_ap=k_ap, q_ap=q_ap, mi=mi: e.matmul(
                        o_ap, lhsT=k_ap, rhs=q_ap, start=True, stop=(mi is None)),
                         reads=list(q_keys) + list(k_keys), writes=[bkey])
                    if mi is not None:
                        S.op('pe', lambda e, o_ap=o_ap, mi=mi: e.matmul(
                            o_ap, lhsT=idb[:], rhs=maskb[:, mi * 128:(mi + 1) * 128], start=False, stop=True),
                             reads=['idb', 'maskb'], writes=[bkey])
                ptb = att_cfg.get('pt', None)
                if ptb is None:
                    p_base = pt[:, pb * 512:(pb + 1) * 512]
                else:
                    p_base = ptb(pb)
                p_ap = p_base[:, 0:n * 128]
                S.op('act', lambda e, p_ap=p_ap, sb0=sb0, n=n: e.activation(out=p_ap, in_=PSALL[:, sb0 * 512: sb0 * 512 + n * 128],
                                                                           func=ACT.Exp, scale=SCALE),
                     reads=skeys[:(n + 3) // 4], writes=[('pt', pb)])
                pv = []
                for i, blk in enumerate(quad):
                    gi = bi + i
                    pv.append((i, blk[4], blk[5], gi == 0, gi == nb - 1))
                bi += n
                pend.append(dict(pb=pb, ab=ab, pv=pv, fin=(fin if bi >= nb else None), p_base=p_base))
                while len(pend) > att_cfg['depth']:
                    att_flush_one()

        for l in range(n_layers):
            if stopped:
                break
            if stop_spec is not None and ':' in stop_spec:
                stop = stop_spec.split(':')[1] if int(stop_spec.split(':')[0]) == l else None
            x_src = x_in if l == 0 else x1_scr
            x_dst = x1_scr if (l == 0 and n_layers > 1) else out
            w_in = w_in_a if l == 0 else w_in_b
            g_q_mem = 8 + l
            g_k_mem = 10 + l

            Wkv = R[:, 0:8192].rearrange("p (c n) -> p c n", c=8)
            memT = R[:, 8192:8192 + 2048]
            for c_ in range(8):
                load_weight_rows(w_kv[l, c_ * 128:(c_ + 1) * 128, :], Wkv[:, c_, :], ('wkv', c_))
            for mt in range(2):
                norm_transpose_tile(mem_in[mt * 128:(mt + 1) * 128, :], mt, 16 + l * 8,
                                    lambda half, mt=mt: memT[:, half * 1024:(half + 1) * 1024].rearrange(
                                        "p (c k) -> p c k", c=4)[:, :, mt * 128:(mt + 1) * 128],
                                    lambda half, mt=mt: [('memT', mt, half * 4 + j_) for j_ in range(4)], mt)
            ntt_flush()
            S.op('pool', lambda e: e.memset(Vm[:], 1.0), writes=['Vm'])
            for mt in range(2):
                for half in range(2):
                    bank = PJ[half]
                    for c in range(8):
                        S.op('pe', lambda e, c=c, half=half, bank=bank, mt=mt: e.matmul(
                            PS[bank][:, 0:512], lhsT=memT[:, c * 256 + mt * 128: c * 256 + (mt + 1) * 128],
                            rhs=Wkv[:, c, half * 512:(half + 1) * 512], start=(c == 0), stop=(c == 7)),
                            reads=[('memT', mt, c), ('wkv', c)], writes=[pk(bank)])
                bank = PJ[0]
                src = PS[bank][:, 0:512]
                rs_ap, rs_key = rms_stats(src, 4, 128, [pk(bank)], True, 1.0 / E)
                tf3 = tmpf[:, 0:512].rearrange("p (a b) -> p a b", a=4)
                S.op('dve', lambda e, src=src, rs_ap=rs_ap, tf3=tf3: e.tensor_tensor(
                    out=tf3, in0=src.rearrange("p (a b) -> p a b", a=4),
                    in1=rs_ap.unsqueeze(2).to_broadcast([128, 4, 128]), op=ALU.mult),
                    reads=[pk(bank), rs_key], writes=[('tmpf', 0, 0)])
                qn3 = qn[:, 0:512].rearrange("p (a b) -> p a b", a=4)
                gsrc = gains[:, g_k_mem * 128:(g_k_mem + 1) * 128].unsqueeze(1).to_broadcast([128, 4, 128])
                S.op('pool', lambda e, qn3=qn3, tf3=tf3, gsrc=gsrc: e.tensor_tensor(out=qn3, in0=tf3, in1=gsrc, op=ALU.mult),
                     reads=[('tmpf', 0, 0), 'gains'], writes=[('qn', 0, 'a')])
                tb = TP[0]
                for hd in range(4):
                    S.op('pe', lambda e, hd=hd: e.transpose(out=PSB[tb][:, hd * 128:(hd + 1) * 128],
                                                           in_=qn[:, hd * 128:(hd + 1) * 128], identity=idb[:]),
                         reads=[('qn', 0, 'a'), 'idb'], writes=[pk(tb)])
                S.op('act', lambda e, mt=mt: e.activation(
                    out=KmT[:].rearrange("p (h k) -> p h k", h=4)[:, :, mt * 128:(mt + 1) * 128],
                    in_=PSB[tb][:, 0:512].rearrange("p (h k) -> p h k", h=4), func=ACT.Copy),
                    reads=[pk(tb)], writes=['KmT'])
                bank = PJ[1]
                S.op('dve', lambda e, mt=mt, bank=bank: e.tensor_copy(
                    out=Vm[:, mt * 520:(mt + 1) * 520].rearrange("p (h k) -> p h k", h=4)[:, :, 0:128],
                    in_=PS[bank][:, 0:512].rearrange("p (h k) -> p h k", h=4)),
                    reads=[pk(bank)], writes=['Vm'])
            if stop == 'M':
                S.barrier()
                break

            if l == 0:
                first_cols = [((s_ * 3 + g) * 8 + 0) * 128 for s_ in range(3) for g in range(3)]
            else:
                first_cols = [j * 128 for j in range(4)] + [1024, 1280]
            Wv0_ = Wbuf[:, 0:8 * len(first_cols) * 128].rearrange("p (c n) -> p c n", c=8)
            first_pending = list(enumerate(first_cols))
            for t in range(NT):
                norm_transpose_tile(x_src[t * 128:(t + 1) * 128, :], t, l * 8,
                                    lambda half, t=t: hT[:, half * 16384:(half + 1) * 16384].rearrange(
                                        "p (c k) -> p c k", c=4)[:, :, t * 128:(t + 1) * 128],
                                    lambda half, t=t: [('hT', t, half * 4 + j_) for j_ in range(4)], t)
                if t % 3 == 2 and first_pending:
                    bi_, col_ = first_pending.pop(0)
                    load_weight_piece(w_in[:, col_:col_ + 128], 8, Wv0_[:, :, bi_ * 128:(bi_ + 1) * 128], ('wbuf', bi_))
            ntt_flush()
            while first_pending:
                bi_, col_ = first_pending.pop(0)
                load_weight_piece(w_in[:, col_:col_ + 128], 8, Wv0_[:, :, bi_ * 128:(bi_ + 1) * 128], ('wbuf', bi_))
            S.barrier()
            if stop == 'p1':
                break

            S.op('pool', lambda e: e.memset(R[:, V_OFF:R_N], 1.0), writes=['Vall'])
            S.barrier()
            if l == 0:
                iters = [dict(cols=[((s_ * 3 + g) * 8 + h) * 128 for s_ in range(3) for g in range(3)], nqk=6, nv=3, hd=h)
                         for h in range(8)]
                gain_ops = [(0, 6, gains[:, 0:768].rearrange("p (a b) -> p a b", a=6))]
            else:
                iters = [dict(cols=[(kvh * 4 + j) * 128 for j in range(4)] + [1024 + kvh * 128, 1280 + kvh * 128], nqk=5, nv=1, hd=kvh)
                         for kvh in range(2)]
                gain_ops = [(0, 4, gains[:, 6 * 128:7 * 128].unsqueeze(1).to_broadcast([128, 4, 128])),
                            (4, 1, gains[:, 7 * 128:8 * 128].unsqueeze(1))]
            att_cfg['ss_eng'] = 'dve' if l == 0 else 'act'
            if l == 0:
                att_cfg['banks'] = [0, 1, 2, 3]
                att_cfg['group'] = 4
                att_cfg['pt'] = None
            else:
                att_cfg['banks'] = [(0, 1), (2, 3), (6, 7)]
                att_cfg['group'] = 8
                att_cfg['pt'] = lambda pb_: R[:, 5 * 4096 + pb_ * 1024: 5 * 4096 + (pb_ + 1) * 1024]
            att_cfg['depth'] = 2

            def emit_iter_load(it):
                ncols = (it['nqk'] + it['nv']) * 128
                Wv_ = Wbuf[:, 0:8 * ncols].rearrange("p (c n) -> p c n", c=8)
                for bi_, col in enumerate(it['cols']):
                    load_weight_piece(w_in[:, col:col + 128], 8, Wv_[:, :, bi_ * 128:(bi_ + 1) * 128], ('wbuf', bi_))

            tilecount = 0
            for it_i, it in enumerate(iters):
                nqk, nv = it['nqk'], it['nv']
                nblk = nqk + nv
                ncols = nblk * 128
                Wv = Wbuf[:, 0:8 * ncols].rearrange("p (c n) -> p c n", c=8)
                prev_cx = None
                prev2_cx = None
                for t in range(NT):
                    pbase = 0 if tilecount % 2 == 0 else 3
                    c0 = 0
                    while c0 < ncols:
                        n_ = min(PROJ_N, 512 - (c0 % 512), ncols - c0)
                        bank = pbase + c0 // 512
                        wkeys = [('wbuf', j) for j in range(c0 // 128, (c0 + n_) // 128)]
                        for c in range(8):
                            S.op('pe', lambda e, c=c, t=t, c0=c0, n_=n_, pbase=pbase, Wv=Wv: e.matmul(
                                PSALL[:, pbase * 512 + c0: pbase * 512 + c0 + n_], lhsT=hT_ap(c, t), rhs=Wv[:, c, c0:c0 + n_],
                                start=(c == 0), stop=(c == 7)),
                                reads=[('hT', t, c)] + wkeys, writes=[pk(bank)])
                        c0 += n_
                    if l == 0:
                        rope = (ropeA[:, t * 16:(t + 1) * 16], ropeA[:, 512 + t * 16: 512 + (t + 1) * 16], 16)
                        ropekey = 'ropeA'
                    else:
                        rbuf = tilecount % 2
                        S.op('sp', lambda e, rbuf=rbuf, t=t: e.dma_start(
                            out=ropeBt[:, rbuf * 128:(rbuf + 1) * 128].rearrange("p (a b) -> p a b", a=2),
                            in_=ropeB_in.rearrange("p (a t b) -> p a t b", a=2, t=32)[:, :, t, :]),
                            writes=[('ropeB', rbuf)], dma=('ropeB', rbuf))
                        rope = (ropeBt[:, rbuf * 128: rbuf * 128 + 64], ropeBt[:, rbuf * 128 + 64: rbuf * 128 + 128], 64)
                        ropekey = ('ropeB', rbuf)
                    if prev_cx is not None:
                        proj_tail(prev_cx)
                    cx_ = proj_head(pbase, nqk, nv, gain_ops, t, rope, ropekey, 0, tilecount)
                    if prev2_cx is not None:
                        proj_tail_b(prev2_cx)
                    proj_head2(cx_)
                    prev2_cx = prev_cx
                    prev_cx = cx_
                    tilecount += 1
                proj_tail(prev_cx)
                proj_tail_b(prev2_cx)
                proj_tail_b(prev_cx)
                if it_i + 1 < len(iters):
                    emit_iter_load(iters[it_i + 1])
                if stop == 'p2proj':
                    continue
                hd = it['hd']
                jobs = []
                if l == 0:
                    for T in range(NT):
                        blocks = []
                        for g, (window, dil) in enumerate(A_GROUPS):
                            for dl in range(-9, 10):
                                if (g, dl) not in MASK_TABLE:
                                    continue
                                kb = T + dl
                                if kb < 0 or kb >= NT:
                                    continue
                                blocks.append((qk_ap(g, T * 128), [('qk', g, T)], qk_ap(3 + g, kb * 128), [('qk', 3 + g, kb)],
                                               v_ap(g, kb), [('v', g, kb)], MASK_TABLE[(g, dl)]))
                        jobs.append((blocks, T, hd))
                else:
                    for j in range(4):
                        for T in range(NT):
                            blocks = []
                            for kb in range(NT):
                                blocks.append((qk_ap(j, T * 128), [('qk', j, T)], qk_ap(4, kb * 128), [('qk', 4, kb)],
                                               v_ap(0, kb), [('v', 0, kb)], None))
                            jobs.append((blocks, T, hd * 4 + j))
                for blocks, T, hcol in jobs:
                    def fin(ab, T=T, hcol=hcol):
                        ob_i = fcount[0] % 4
                        fcount[0] += 1
                        rc_ap = rc[:, ob_i:ob_i + 1]
                        ob_ap = ob[:, ob_i * 128:(ob_i + 1) * 128]
                        S.op('dve', lambda e: e.reciprocal(out=rc_ap, in_=PS[ab][:, 128:129]),
                             reads=[pk(ab)], writes=[('rc', ob_i)])
                        S.op('dve', lambda e: e.tensor_scalar(out=ob_ap, in0=PS[ab][:, 0:128], scalar1=rc_ap, scalar2=None,
                                                              op0=ALU.mult),
                             reads=[pk(ab), ('rc', ob_i)], writes=[('ob', ob_i)])
                        S.op('sp', lambda e: e.dma_start(out=o_scr[T * 128:(T + 1) * 128, hcol * 128:(hcol + 1) * 128], in_=ob_ap),
                             reads=[('ob', ob_i)], writes=[('oscr', T)], dma=('ob', ob_i))
                    attention_job(blocks, fin)
                att_flush_all()
            att_cfg['banks'] = [2, 3]
            att_cfg['depth'] = 1
            S.barrier()
            if stop in ('p2', 'p2proj'):
                break

            Wg = R[:, 0:16384].rearrange("p (c n) -> p c n", c=8)
            Wo = R[:, 16384:28672].rearrange("p (c n) -> p c n", c=12)
            gcol0 = 9216 if l == 0 else 1536
            for c_ in range(8):
                load_weight_rows(w_in[c_ * 128:(c_ + 1) * 128, gcol0:gcol0 + 1024], Wg[:, c_, 0:1024], ('wg', c_, 0))
            for c_ in range(8):
                load_weight_rows(w_in[c_ * 128:(c_ + 1) * 128, gcol0 + 1024:gcol0 + 2048], Wg[:, c_, 1024:2048], ('wg', c_, 1))
            for cc_ in range(12):
                load_weight_rows(w_out[l, cc_ * 128:(cc_ + 1) * 128, :], Wo[:, cc_, :], ('wo', cc_))
            KmT3 = KmT[:].rearrange("p (h k) -> p h k", h=4)
            att_cfg['banks'] = [2, 0, 5]
            att_cfg['group'] = 4
            att_cfg['pt'] = None
            att_cfg['depth'] = 2
            att_cfg['oa'] = [4, 7]

            def gate_chain(t, j, gb, o_j, o_keys):
                th_ap = th[:, j * 512:(j + 1) * 512]
                S.op('act', lambda e: e.activation(out=th_ap, in_=PS[gb][:, 0:512], func=ACT.Tanh, scale=0.5),
                     reads=[pk(gb)], writes=[('th', j)])
                S.op('dve', lambda e: e.scalar_tensor_tensor(out=th_ap, in0=th_ap, scalar=1.0, in1=PS[gb][:, 0:512],
                                                             op0=ALU.add, op1=ALU.mult),
                     reads=[pk(gb), ('th', j)], writes=[('th', j)])
                yb_ap = yb[:, j * 512:(j + 1) * 512]
                S.op('dve', lambda e: e.scalar_tensor_tensor(out=yb_ap, in0=th_ap, scalar=0.5, in1=o_j, op0=ALU.mult, op1=ALU.mult),
                     reads=[('th', j)] + o_keys, writes=[('yb', j)])

            def gate_proj(t, j, gb):
                for c in range(8):
                    S.op('pe', lambda e, c=c: e.matmul(PS[gb][:, 0:512], lhsT=hT_ap(c, t), rhs=Wg[:, c, 512 + j * 512: 1024 + j * 512],
                                                       start=(c == 0), stop=(c == 7)),
                         reads=[('hT', t, c), ('wg', c, (512 + j * 512) // 1024)], writes=[pk(gb)])

            def y_transposes(t, j, tpb, eng):
                b = t % 2
                yb_ap = yb[:, j * 512:(j + 1) * 512]
                yTb = yT[:, b * 1536:(b + 1) * 1536]
                for i in range(4):
                    S.op('pe', lambda e, i=i: e.transpose(out=PSB[tpb][:, i * 128:(i + 1) * 128],
                                                         in_=yb_ap[:, i * 128:(i + 1) * 128], identity=idb[:]),
                         reads=[('yb', j), 'idb'], writes=[pk(tpb)])
                if eng == 'act':
                    S.op('act', lambda e: e.activation(out=yTb[:, j * 512:(j + 1) * 512], in_=PSB[tpb][:, 0:512], func=ACT.Copy),
                         reads=[pk(tpb)], writes=[('yT', b, j)])
                else:
                    S.op('dve', lambda e: e.tensor_copy(out=yTb[:, j * 512:(j + 1) * 512], in_=PSB[tpb][:, 0:512]),
                         reads=[pk(tpb)], writes=[('yT', b, j)])

            def stage_A(t):
                b = t % 2
                xt = xt_ap(b)
                ot_ap = ot[:, b * 1024:(b + 1) * 1024]
                S.op('sp', lambda e, x_src=x_src: e.dma_start(out=xt, in_=x_src[t * 128:(t + 1) * 128, :]),
                     writes=[('xt', b)], dma=('xt', b))
                S.op('sp', lambda e: e.dma_start(out=ot_ap, in_=o_scr[t * 128:(t + 1) * 128, :]),
                     reads=[('oscr', t)], writes=[('ot', b)], dma=('ot', b))
                bank = 0
                for c in range(8):
                    S.op('pe', lambda e, c=c: e.matmul(PS[bank][:, 0:512], lhsT=hT_ap(c, t), rhs=Wg[:, c, 0:512],
                                                       start=(c == 0), stop=(c == 7)),
                         reads=[('hT', t, c), ('wg', c, 0)], writes=[pk(bank)])
                gate_proj(t, 0, 1)
                gate_proj(t, 1, 3)
                src = PS[bank][:, 0:512]
                rs_ap, rs_key = rms_stats(src, 4, 128, [pk(bank)], True, 1.0 / E)
                tf3 = tmpf[:, b * 768: b * 768 + 512].rearrange("p (a b) -> p a b", a=4)
                S.op('dve', lambda e: e.tensor_tensor(out=tf3, in0=src.rearrange("p (a b) -> p a b", a=4),
                                                      in1=rs_ap.unsqueeze(2).to_broadcast([128, 4, 128]), op=ALU.mult),
                     reads=[pk(bank), rs_key], writes=[('tmpf', b, 0)])
                qn3 = qn[:, b * 768: b * 768 + 512].rearrange("p (a b) -> p a b", a=4)
                gsrc = gains[:, g_q_mem * 128:(g_q_mem + 1) * 128].unsqueeze(1).to_broadcast([128, 4, 128])
                S.op('pool', lambda e: e.tensor_tensor(out=qn3, in0=tf3, in1=gsrc, op=ALU.mult),
                     reads=[('tmpf', b, 0), 'gains'], writes=[('qn', b, 'a')])
                gate_chain(t, 0, 1, ot_ap[:, 0:512], [('ot', b)])
                gate_chain(t, 1, 3, ot_ap[:, 512:1024], [('ot', b)])

            def stage_B(t):
                b = t % 2
                qn_ap = qn[:, b * 768: b * 768 + 512]
                tb = 6
                for hd in range(4):
                    S.op('pe', lambda e, hd=hd: e.transpose(out=PSB[tb][:, hd * 128:(hd + 1) * 128],
                                                           in_=qn_ap[:, hd * 128:(hd + 1) * 128], identity=idb[:]),
                         reads=[('qn', b, 'a'), 'idb'], writes=[pk(tb)])
                S.op('act', lambda e: e.activation(out=qmT[:], in_=PSB[tb][:, 0:512], func=ACT.Copy),
                     reads=[pk(tb)], writes=['qmT'])
                y_transposes(t, 0, 7, 'dve')
                gate_proj(t, 2, 1)
                th2 = th[:, 1024:1536]
                S.op('act', lambda e: e.activation(out=th2, in_=PS[1][:, 0:512], func=ACT.Tanh, scale=0.5),
                     reads=[pk(1)], writes=[('th', 2)])
                S.op('dve', lambda e: e.scalar_tensor_tensor(out=th2, in0=th2, scalar=1.0, in1=PS[1][:, 0:512],
                                                             op0=ALU.add, op1=ALU.mult),
                     reads=[pk(1), ('th', 2)], writes=[('th', 2)])
                y_transposes(t, 1, 6, 'act')
                for hd in range(4):
                    blocks = []
                    for kb in range(2):
                        blocks.append((qmT[:, hd * 128:(hd + 1) * 128], ['qmT'], KmT3[:, hd, kb * 128:(kb + 1) * 128], ['KmT'],
                                       Vm[:, (kb * 4 + hd) * 130:(kb * 4 + hd) * 130 + 129], ['Vm'], None))

                    def fin(ab, hd=hd):
                        ob_i = fcount[0] % 4
                        fcount[0] += 1
                        rc_ap = rc[:, 4 + ob_i:5 + ob_i]
                        S.op('dve', lambda e: e.reciprocal(out=rc_ap, in_=PS[ab][:, 128:129]),
                             reads=[pk(ab)], writes=[('rcm', ob_i)])
                        S.op('act', lambda e: e.activation(out=om[:, hd * 128:(hd + 1) * 128], in_=PS[ab][:, 0:128], func=ACT.Copy,
                                                           scale=rc_ap),
                             reads=[pk(ab), ('rcm', ob_i)], writes=[('om', hd)])
                    attention_job(blocks, fin)
                att_flush_all()
                yb2 = yb[:, 1024:1536]
                S.op('dve', lambda e: e.scalar_tensor_tensor(out=yb2, in0=th2, scalar=0.5, in1=om[:, 0:512], op0=ALU.mult, op1=ALU.mult),
                     reads=[('th', 2)] + [('om', hd) for hd in range(4)], writes=[('yb', 2)])

            def stage_B2(t):
                y_transposes(t, 2, 7, 'act')

            def stage_C(t):
                b = t % 2
                xt = xt_ap(b)
                x1t = x1t_ap(b)
                yTb = yT[:, b * 1536:(b + 1) * 1536]
                for nb_ in range(2):
                    obk = [2, 0][nb_]
                    for cc in range(12):
                        S.op('pe', lambda e, cc=cc, nb_=nb_, obk=obk: e.matmul(
                            PS[obk][:, 0:512], lhsT=yTb[:, cc * 128:(cc + 1) * 128], rhs=Wo[:, cc, nb_ * 512:(nb_ + 1) * 512],
                            start=(cc == 0), stop=(cc == 11)),
                            reads=[('yT', b, cc // 4), ('wo', cc)], writes=[pk(obk)])
                    S.op('dve', lambda e, nb_=nb_, obk=obk: e.tensor_tensor(
                        out=x1t[:, nb_ * 512:(nb_ + 1) * 512], in0=PS[obk][:, 0:512], in1=xt[:, nb_ * 512:(nb_ + 1) * 512], op=ALU.add),
                        reads=[pk(obk), ('xt', b)], writes=[('x1t', b, nb_)])
                S.op('sp', lambda e, x_dst=x_dst: e.dma_start(out=x_dst[t * 128:(t + 1) * 128, :], in_=x1t),
                     reads=[('x1t', b, 0), ('x1t', b, 1)], writes=[('xdst', t)], dma=('x1t', b))

            def _bind(f, **kw):
                return f

            if stop != 'p3w':
                stage_A(0)
                for t in range(NT):
                    stage_B(t)
                    if t + 1 < NT:
                        stage_A(t + 1)
                    stage_B2(t)
                    stage_C(t)
            att_cfg['oa'] = OA
            S.barrier()
        S.emit(nc)
    return nc, S


def _host_consts(norm_g, mem_norm_g, mem_qn_g, mem_kn_g, qn_a, kn_a, qn_b, kn_b):
    ng = np.zeros((128, 32), np.float32)
    for l in range(2):
        ng[:, l * 8:(l + 1) * 8] = norm_g[l].reshape(8, 128).T
        ng[:, 16 + l * 8:16 + (l + 1) * 8] = mem_norm_g[l].reshape(8, 128).T
    rows = [qn_a[0, 0], qn_a[0, 1], qn_a[0, 2], kn_a[0, 0], kn_a[0, 1], kn_a[0, 2], qn_b[0], kn_b[0],
            mem_qn_g[0], mem_qn_g[1], mem_kn_g[0], mem_kn_g[1]]
    gains = np.ascontiguousarray(np.broadcast_to(np.concatenate(rows)[None, :], (128, 12 * 128))).astype(np.float32)
    pos = np.arange(S_TOK, dtype=np.float32)
    invA = (np.float32(500000.0) ** (-np.arange(0, 32, 2, dtype=np.float32) / np.float32(32))).astype(np.float32)
    angA = pos[:, None] * invA[None, :]
    row = np.repeat(np.arange(64, dtype=np.float32), 64)
    col = np.tile(np.arange(64, dtype=np.float32), 64)
    invB = (np.float32(10000.0) ** (-np.arange(0, 64, 2, dtype=np.float32) / np.float32(64))).astype(np.float32)
    angB = np.concatenate([row[:, None] * invB[None, :], col[:, None] * invB[None, :]], axis=-1)

    def lay(a):
        return a.reshape(32, 128, -1).transpose(1, 0, 2)
    ropeA = np.stack([lay(np.cos(angA)), lay(np.sin(angA))], axis=1).reshape(128, -1).astype(np.float32)
    ropeB = np.stack([lay(np.cos(angB)), lay(np.sin(angB))], axis=1).reshape(128, -1).astype(np.float32)
    return dict(ng=ng, gains=gains, ropeA=np.ascontiguousarray(ropeA), ropeB=np.ascontiguousarray(ropeB),
                ident=np.eye(128, dtype=np.float32), masks=np.ascontiguousarray(MASKS_NP.reshape(128, -1)))


_NC_CACHE = {}


def kernel(x, mem, norm_g, mem_norm_g, w_mem_kv, mem_qn_g, mem_kn_g, w_out,
           w_in_a, qn_a, kn_a, w_in_b, qn_b, kn_b):
    f = lambda a: np.ascontiguousarray(np.asarray(a, dtype=np.float32))
    x = f(x)
    mem = f(mem)
    consts = _host_consts(f(norm_g), f(mem_norm_g), f(mem_qn_g), f(mem_kn_g), f(qn_a), f(kn_a), f(qn_b), f(kn_b))
    shared = dict(w_in_a=f(w_in_a)[0], w_in_b=f(w_in_b)[0], w_out=f(w_out), w_mem_kv=f(w_mem_kv), **consts)
    if 'nc' not in _NC_CACHE:
        _NC_CACHE['nc'] = build()[0]
    nc = _NC_CACHE['nc']
    n = x.shape[0]
    in_maps = [dict(x=x[b], mem=mem[b], **shared) for b in range(n)]
    res = run_bass_kernel_spmd(nc, in_maps, core_ids=list(range(n)))
    return np.stack([np.asarray(r["y_out"], dtype=np.float32) for r in res.results], axis=0)
```
